# Optimizing a Trainium2 kernel written in Bass

```python
import jax, jax.numpy as jnp
from jax import lax
import numpy as np

D_MODEL = 2048
BATCH = 2
SEQ = 8192
DEPTH = 1

CTX_LEN = 256
GRID_W = 64

RW_HEAD = 64
RW_WIDTH = D_MODEL // 2
RW_HEADS = RW_WIDTH // RW_HEAD
DECAY_LORA = 64
AAA_LORA = 64
CONV_W = 3
GN_EPS = 64e-5

MLA_HEADS = 8
QK_NOPE = 128
QK_ROPE = 64
QK_DIM = QK_NOPE + QK_ROPE
V_HEAD = 128
MLA_WIDTH = MLA_HEADS * V_HEAD
Q_LORA = 512
KV_LORA = 256
ROPE_BASE = 10000.0
Q_BLOCK = 128

EPS = 1e-6

IN_WIDTHS = (3 * RW_WIDTH,
             RW_WIDTH,
             2 * DECAY_LORA,
             2 * AAA_LORA,
             Q_LORA,
             KV_LORA,
             QK_ROPE,
             MLA_WIDTH,
             2 * D_MODEL)
IN_WIDTH = sum(IN_WIDTHS)

kernel_name = "hybrid_rwkv7_mla_prefix_dit_block"

f32 = jnp.float32


def rmsnorm(x, w):
    xf = x.astype(f32)
    y = xf * lax.rsqrt(jnp.mean(xf * xf, axis=-1, keepdims=True) + EPS)
    return (y * w.astype(f32)).astype(x.dtype)


def modulation(cond, w_mod, b_mod):
    m = (jax.nn.silu(cond) @ w_mod + b_mod).reshape(-1, 1, 3 * D_MODEL)
    return jnp.split(m, 3, axis=-1)


def split_in(u):
    offs, acc = [], 0
    for w in IN_WIDTHS[:-1]:
        acc += w
        offs.append(acc)
    return jnp.split(u, offs, axis=-1)


def project_stream(h, norm_w, shift, scale, w_in):
    hm = rmsnorm(h, norm_w) * (1 + scale) + shift
    return split_in(hm @ w_in)


def centred_short_conv(u, w):
    up = jnp.pad(u, ((0, 0), (1, 1), (0, 0)))
    return up[:, :-2] * w[0] + up[:, 1:-1] * w[1] + up[:, 2:] * w[2]


def rwkv_prepare(u_rkv, u_wd, u_ad, conv_w, w0, w_decay_up, a0, w_a_up, k_k, k_a):
    B, L, _ = u_rkv.shape
    r, k, v = jnp.split(centred_short_conv(u_rkv, conv_w), 3, axis=-1)
    wd = jnp.tanh(u_wd.reshape(B, L, 2, DECAY_LORA))
    w_log = (w0 + jnp.einsum("bldr,drc->bldc", wd, w_decay_up)).astype(f32)
    decay = jnp.exp(-jnp.exp(-jax.nn.softplus(-w_log) - 0.5))
    a = jax.nn.sigmoid((a0 + jnp.einsum("bldr,drc->bldc",
                                        u_ad.reshape(B, L, 2, AAA_LORA), w_a_up)).astype(f32))
    kk = (k * k_k).reshape(B, L, RW_HEADS, RW_HEAD).astype(f32)
    kk = (kk * lax.rsqrt(jnp.sum(kk * kk, -1, keepdims=True) + 1e-12)).reshape(B, L, RW_WIDTH)
    k_mod = k[:, :, None, :].astype(f32) * (1 + (a - 1) * k_a.astype(f32))
    return r, v, kk, decay, a, k_mod


def to_scan_layout(t):
    B, L = t.shape[:2]
    t = t.reshape(B, L, 2, RW_HEADS, RW_HEAD).transpose(1, 2, 0, 3, 4).astype(f32)
    return jnp.stack([t[:, 0], jnp.flip(t[:, 1], 0)], axis=1)


def rwkv7_step(S, xs):
    r, w, k, v, kk, akk = xs
    sa = jnp.einsum("dbhij,dbhj->dbhi", S, -kk)
    S = S * w[..., None, :] + sa[..., :, None] * akk[..., None, :] + v[..., :, None] * k[..., None, :]
    return S, jnp.einsum("dbhij,dbhj->dbhi", S, r)


def rwkv_scan(r, v, kk, decay, a, k_mod, S0):
    B, L, _ = r.shape
    both = lambda t: jnp.broadcast_to(t[:, :, None, :], (B, L, 2, RW_WIDTH))
    xs = (to_scan_layout(both(r)), to_scan_layout(decay), to_scan_layout(k_mod),
          to_scan_layout(both(v)), to_scan_layout(both(kk)),
          to_scan_layout(a * kk[:, :, None, :]))
    S, ys = lax.scan(rwkv7_step, S0, xs)
    y = ys[:, 0] + jnp.flip(ys[:, 1], 0)
    return S, y.transpose(1, 0, 2, 3)


def rwkv_finish(r, v, k_mod, y, z, r_k, ln_w, ln_b):
    B, L, _ = r.shape
    mu = jnp.mean(y, -1, keepdims=True)
    var = jnp.mean(jnp.square(y - mu), -1, keepdims=True)
    yn = ((y - mu) * lax.rsqrt(var + GN_EPS)).reshape(B, L, RW_WIDTH) * ln_w + ln_b
    coef = jnp.einsum("blhn,bldhn,hn->blh", r.reshape(B, L, RW_HEADS, RW_HEAD).astype(f32),
                      k_mod.reshape(B, L, 2, RW_HEADS, RW_HEAD), r_k.astype(f32))
    bonus = (coef[..., None] * v.reshape(B, L, RW_HEADS, RW_HEAD).astype(f32)).reshape(B, L, RW_WIDTH)
    return (yn + bonus).astype(z.dtype) * jax.nn.silu(z)


def axial_rope_tables(rows):
    r_pos = jnp.broadcast_to(jnp.arange(rows)[:, None], (rows, GRID_W)).reshape(-1).astype(f32)
    c_pos = jnp.broadcast_to(jnp.arange(GRID_W)[None, :], (rows, GRID_W)).reshape(-1).astype(f32)
    axis_dim = QK_ROPE // 2
    inv = jnp.power(ROPE_BASE, -jnp.arange(0, axis_dim, 2, dtype=f32) / axis_dim)
    ang_r, ang_c = r_pos[:, None] * inv, c_pos[:, None] * inv
    return jnp.cos(ang_r), jnp.sin(ang_r), jnp.cos(ang_c), jnp.sin(ang_c)


def rope_axis(x, cos, sin):
    x1, x2 = jnp.split(x, 2, axis=-1)
    cos, sin = cos[None, :, None, :], sin[None, :, None, :]
    return jnp.concatenate([x1 * cos - x2 * sin, x2 * cos + x1 * sin], axis=-1)


def apply_axial_rope(x, tables):
    cos_r, sin_r, cos_c, sin_c = tables
    nope, rot = x[..., :QK_NOPE], x[..., QK_NOPE:].astype(f32)
    xr, xc = jnp.split(rot, 2, axis=-1)
    rot = jnp.concatenate([rope_axis(xr, cos_r, sin_r), rope_axis(xc, cos_c, sin_c)], axis=-1)
    return jnp.concatenate([nope, rot.astype(x.dtype)], axis=-1)


def mla_qkv(q_dn, kv_dn, k_rope, q_norm_w, w_uq, kv_norm_w, w_ukv, q_gain, k_gain, rope):
    B, L, _ = q_dn.shape
    q = (rmsnorm(q_dn, q_norm_w) @ w_uq).reshape(B, L, MLA_HEADS, QK_DIM)
    kv = (rmsnorm(kv_dn, kv_norm_w) @ w_ukv).reshape(B, L, MLA_HEADS, QK_NOPE + V_HEAD)
    k_nope, v = kv[..., :QK_NOPE], kv[..., QK_NOPE:]
    k = jnp.concatenate([k_nope, jnp.broadcast_to(k_rope[:, :, None, :], (B, L, MLA_HEADS, QK_ROPE))], -1)
    q, k = rmsnorm(q, q_gain), rmsnorm(k, k_gain)
    if rope is not None:
        q, k = apply_axial_rope(q, rope), apply_axial_rope(k, rope)
    return q, k, v


def attend(q, k, v):
    s = jnp.einsum("bqhd,bkhd->bhqk", q, k).astype(f32) * (QK_DIM ** -0.5)
    p = jax.nn.softmax(s, axis=-1).astype(v.dtype)
    return jnp.einsum("bhqk,bkhd->bqhd", p, v)


def blocked_attend(q, k, v):
    B, L, H, Dq = q.shape
    qb = q.reshape(B, L // Q_BLOCK, Q_BLOCK, H, Dq).transpose(1, 0, 2, 3, 4)
    out = lax.map(lambda qi: attend(qi, k, v), qb)
    return out.transpose(1, 0, 2, 3, 4).reshape(B, L, H * V_HEAD)


def merge_branches(o_rwkv, o_mla, gate_logits, w_br_r, w_br_m, w_out):
    g_r, g_m = jnp.split(jax.nn.sigmoid(gate_logits), 2, axis=-1)
    return (g_r * (o_rwkv @ w_br_r) + g_m * (o_mla @ w_br_m)) @ w_out


def setup_inputs(seed: int = 0) -> dict:
    key = jax.random.key(seed)
    ks = iter(jax.random.split(key, 32))
    nrm = lambda shape, s: jax.random.normal(next(ks), shape, f32) * s
    D = D_MODEL
    conv_centre = jnp.array([0.0, 1.0, 0.0], f32)[None, :, None]
    return {
        "x": nrm((BATCH, SEQ, D), 1.0),
        "c": nrm((BATCH, D), 1.0),
        "ctx": nrm((BATCH, CTX_LEN, D), 1.0),
        "c_ctx": nrm((D,), 1.0),
        "norm_w": 1.0 + nrm((DEPTH, D), 0.05),
        "w_mod": nrm((DEPTH, D, 3 * D), 0.5 * D ** -0.5),
        "b_mod": nrm((DEPTH, 3 * D), 0.01),
        "w_in": nrm((DEPTH, D, IN_WIDTH), D ** -0.5),
        "conv_rkv": conv_centre + nrm((DEPTH, CONV_W, 3 * RW_WIDTH), 0.2),
        "w0": jax.random.uniform(next(ks), (DEPTH, 2, RW_WIDTH), f32, -4.0, 1.0),
        "w_decay_up": nrm((DEPTH, 2, DECAY_LORA, RW_WIDTH), 0.1 * DECAY_LORA ** -0.5),
        "a0": nrm((DEPTH, 2, RW_WIDTH), 0.5),
        "w_a_up": nrm((DEPTH, 2, AAA_LORA, RW_WIDTH), 0.5 * AAA_LORA ** -0.5),
        "k_k": 0.85 + nrm((DEPTH, RW_WIDTH), 0.05),
        "k_a": 1.0 + nrm((DEPTH, RW_WIDTH), 0.05),
        "r_k": nrm((DEPTH, RW_HEADS, RW_HEAD), 0.1),
        "ln_x_w": 1.0 + nrm((DEPTH, RW_WIDTH), 0.05),
        "ln_x_b": nrm((DEPTH, RW_WIDTH), 0.01),
        "q_norm_w": 1.0 + nrm((DEPTH, Q_LORA), 0.05),
        "w_uq": nrm((DEPTH, Q_LORA, MLA_HEADS * QK_DIM), Q_LORA ** -0.5),
        "kv_norm_w": 1.0 + nrm((DEPTH, KV_LORA), 0.05),
        "w_ukv": nrm((DEPTH, KV_LORA, MLA_HEADS * (QK_NOPE + V_HEAD)), KV_LORA ** -0.5),
        "q_gain": 1.0 + nrm((DEPTH, QK_DIM), 0.05),
        "k_gain": 1.0 + nrm((DEPTH, QK_DIM), 0.05),
        "w_branch_rwkv": nrm((DEPTH, RW_WIDTH, D), RW_WIDTH ** -0.5),
        "w_branch_mla": nrm((DEPTH, MLA_WIDTH, D), MLA_WIDTH ** -0.5),
        "w_out": nrm((DEPTH, D, D), D ** -0.5),
    }


def reference(x, c, ctx, c_ctx, norm_w, w_mod, b_mod, w_in, conv_rkv, w0, w_decay_up, a0,
              w_a_up, k_k, k_a, r_k, ln_x_w, ln_x_b, q_norm_w, w_uq, kv_norm_w, w_ukv,
              q_gain, k_gain, w_branch_rwkv, w_branch_mla, w_out):
    B, L, _ = x.shape
    ROWS = L // GRID_W
    rope = axial_rope_tables(ROWS)
    for i in range(DEPTH):
        last = i == DEPTH - 1
        sh_x, sc_x, g_x = modulation(c, w_mod[i], b_mod[i])
        sh_c, sc_c, g_c = modulation(c_ctx, w_mod[i], b_mod[i])
        (rkv_x, zr_x, wd_x, ad_x, qd_x, kvd_x, kr_x, zm_x, mg_x) = project_stream(x, norm_w[i], sh_x, sc_x, w_in[i])
        (rkv_c, zr_c, wd_c, ad_c, qd_c, kvd_c, kr_c, zm_c, mg_c) = project_stream(ctx, norm_w[i], sh_c, sc_c, w_in[i])

        rw = (conv_rkv[i], w0[i], w_decay_up[i], a0[i], w_a_up[i], k_k[i], k_a[i])
        r_c, v_c, kk_c, dec_c, a_c, km_c = rwkv_prepare(rkv_c, wd_c, ad_c, *rw)
        r_x, v_x, kk_x, dec_x, a_x, km_x = rwkv_prepare(rkv_x, wd_x, ad_x, *rw)
        S0 = jnp.zeros((2, B, RW_HEADS, RW_HEAD, RW_HEAD), f32)
        S_ctx, y_c = rwkv_scan(r_c, v_c, kk_c, dec_c, a_c, km_c, S0)
        _, y_x = rwkv_scan(r_x, v_x, kk_x, dec_x, a_x, km_x, S_ctx)
        o_rw_x = rwkv_finish(r_x, v_x, km_x, y_x, zr_x, r_k[i], ln_x_w[i], ln_x_b[i])

        ml = (q_norm_w[i], w_uq[i], kv_norm_w[i], w_ukv[i], q_gain[i], k_gain[i])
        q_c, k_c, v_c_att = mla_qkv(qd_c, kvd_c, kr_c, *ml, None)
        q_x, k_x, v_x_att = mla_qkv(qd_x, kvd_x, kr_x, *ml, rope)
        att_x = blocked_attend(q_x, jnp.concatenate([k_x, k_c], axis=1),
                               jnp.concatenate([v_x_att, v_c_att], axis=1))
        o_ml_x = att_x * jax.nn.silu(zm_x)

        x_new = x + g_x * merge_branches(o_rw_x, o_ml_x, mg_x, w_branch_rwkv[i], w_branch_mla[i], w_out[i])
        if not last:
            o_rw_c = rwkv_finish(r_c, v_c, km_c, y_c, zr_c, r_k[i], ln_x_w[i], ln_x_b[i])
            o_ml_c = attend(q_c, k_c, v_c_att).reshape(B, -1, MLA_WIDTH) * jax.nn.silu(zm_c)
            ctx = ctx + g_c * merge_branches(o_rw_c, o_ml_c, mg_c, w_branch_rwkv[i], w_branch_mla[i], w_out[i])
        x = x_new
    return x
```

```python
import os
from contextlib import ExitStack
import numpy as np
import ml_dtypes
import concourse.bass as bass
import concourse.mybir as mybir
from concourse.bass_utils import run_bass_kernel_spmd

F32 = mybir.dt.float32
BF16 = mybir.dt.bfloat16
ALU = mybir.AluOpType
AF = mybir.ActivationFunctionType

NT = 8448
NX = 8192
NCTX = 256
D = 2048
WC = 2368
GS = 256
C0 = float(np.exp(-0.5))
EPS = 1e-6
GN_EPS = 64e-5


class Tok:
    __slots__ = ("sem", "val", "key")

    def __init__(self, sem, val, key):
        self.sem = sem
        self.val = val
        self.key = key


class DSem:
    def __init__(self, sem):
        self.sem = sem
        self.cnt = 0


class Buf:
    def __init__(self, name, const=False):
        self.name = name
        self.w = None
        self.r = []
        self.const = const
        self.dsem = None
        self.dcnt = 0


class Sched:
    ENG = ["pe", "act", "dve", "pool", "sp"]

    def __init__(self, nc, stack):
        self.nc = nc
        self.stack = stack
        self.plan = {e: [] for e in self.ENG}
        self.ecnt = {e: 0 for e in self.ENG}
        self.esem = {}
        self.waited = {e: {} for e in self.ENG}
        self.nsem = 0
        for e in ("pe", "act", "dve", "pool"):
            self.esem[e] = self._newsem("e_" + e)
        self.dbufs = []
        self.free_dsems = []
        self.ninst = 0

    def _newsem(self, name):
        self.nsem += 1
        return self.stack.enter_context(self.nc.semaphore(f"{name}_{self.nsem}"))

    def _waits(self, eng, toks):
        for t in toks:
            if t is None:
                continue
            if self.waited[eng].get(t.key, 0) >= t.val:
                continue
            self.waited[eng][t.key] = t.val
            self.plan[eng].append(lambda e, sem=t.sem, v=t.val: e.wait_ge(sem, v))

    def _deps(self, reads, writes):
        deps = []
        for b in reads:
            deps.append(b.w)
        for b in writes:
            deps.append(b.w)
            deps.extend(b.r)
        return deps

    def _mark(self, tok, reads, writes):
        for b in reads:
            if not b.const:
                b.r.append(tok)
        for b in writes:
            b.w = tok
            b.r = []

    def op(self, eng, fn, reads=(), writes=()):
        self._waits(eng, self._deps(reads, writes))
        self.ecnt[eng] += 1
        self.ninst += 1
        tok = Tok(self.esem[eng], self.ecnt[eng], "e_" + eng)
        self.plan[eng].append(lambda e, fn=fn, sem=tok.sem: fn(e).then_inc(sem, 1))
        self._mark(tok, reads, writes)
        return tok

    def _dsem(self, b):
        if b.dsem is None:
            if self.free_dsems:
                b.dsem = self.free_dsems.pop()
            else:
                b.dsem = DSem(self._newsem("d"))
            self.dbufs.append(b)

    def end_phase(self):
        for b in self.dbufs:
            if not os.environ.get("MK_NORECYCLE"):
                self.free_dsems.append(b.dsem)
            b.dsem = None
            b.w = None
            b.r = []
        self.dbufs = []

    def dma(self, q, out_ap, in_ap, reads=(), writes=(), semof=None, **kw):
        self._waits(q, self._deps(reads, writes))
        b = semof
        self._dsem(b)
        ds = b.dsem
        ds.cnt += 16
        tok = Tok(ds.sem, ds.cnt, "d%d" % id(ds))
        self.plan[q].append(
            lambda e, o=out_ap, i=in_ap, sem=ds.sem, kw=kw: e.dma_start(out=o, in_=i, **kw).then_inc(sem, 16)
        )
        self._mark(tok, reads, writes)
        return tok

    def collective(self, kind, groups, in_ap, out_ap, reads, writes, semof):
        q = "pool"
        self._waits(q, self._deps(reads, writes))
        b = semof
        self._dsem(b)
        ds = b.dsem
        ds.cnt += 1
        tok = Tok(ds.sem, ds.cnt, "d%d" % id(ds))
        self.plan[q].append(
            lambda e, sem=ds.sem: e.collective_compute(
                kind, ALU.bypass, replica_groups=groups, ins=[in_ap], outs=[out_ap]
            ).then_inc(sem, 1)
        )
        self._mark(tok, reads, writes)
        return tok

    def barrier(self):
        toks = []
        for e in ("pe", "act", "dve", "pool"):
            if self.ecnt[e] > 0:
                toks.append(Tok(self.esem[e], self.ecnt[e], "e_" + e))
        for b in self.dbufs:
            toks.append(Tok(b.dsem.sem, b.dsem.cnt, "d%d" % id(b.dsem)))
        for e in self.ENG:
            self._waits(e, toks)

    def emit(self):
        plan = self.plan
        with self.nc.Block() as block:

            @block.tensor
            def _(e):
                for f in plan["pe"]:
                    f(e)

            @block.scalar
            def _(e):
                for f in plan["act"]:
                    f(e)

            @block.vector
            def _(e):
                for f in plan["dve"]:
                    f(e)

            @block.gpsimd
            def _(e):
                for f in plan["pool"]:
                    f(e)

            @block.sync
            def _(e):
                for f in plan["sp"]:
                    f(e)

        self.plan = {e: [] for e in self.ENG}


class Ctx:
    def __init__(self, nc, stack):
        self.nc = nc
        self.stack = stack
        self.n = 0

    def sb(self, shape, dt, name=None):
        self.n += 1
        name = (name or "t") + f"_{self.n}"
        t = self.stack.enter_context(self.nc.sbuf_tensor(name, list(shape), dt))
        return t, Buf(name)

    def ps(self, shape, dt, name=None):
        self.n += 1
        name = (name or "p") + f"_{self.n}"
        t = self.stack.enter_context(self.nc.psum_tensor(name, list(shape), dt))
        return t, Buf(name)

    def sub(self):
        c = Ctx(self.nc, ExitStack())
        c.n = self.n + 1000
        return c


def _host_consts():
    idx = np.arange(64)
    us = (idx[:, None] < idx[None, :]).astype(np.float32)
    ui = (idx[:, None] <= idx[None, :]).astype(np.float32)
    ls = (idx[:, None] > idx[None, :]).astype(np.float32)
    li = (idx[:, None] >= idx[None, :]).astype(np.float32)
    masks = np.zeros((128, 4, 128), np.float32)
    for i, m in enumerate((us, ui, ls, li)):
        masks[0:64, i, 0:64] = m
        masks[64:128, i, 64:128] = m
    bones = np.zeros((128, 128), np.float32)
    bones[0:64, 0:64] = 1
    bones[64:128, 64:128] = 1
    sel = np.zeros((2, 2, 128), np.float32)
    sel[0, 0, :] = 1
    sel[1, 1, :] = 1
    reset = np.ones((128, 512), np.float32)
    reset[:, ::64] = 0
    rows = np.repeat(np.arange(128), 64).astype(np.float32)
    cols = np.tile(np.arange(64), 128).astype(np.float32)
    inv = np.power(np.float32(10000.0), -np.arange(0, 32, 2, dtype=np.float32) / np.float32(32)).astype(np.float32)
    ang = np.zeros((64, NX), np.float32)
    for d in range(64):
        pos = rows if d < 32 else cols
        ang[d] = pos * inv[d % 16]
    cos = np.cos(ang.astype(np.float64)).astype(np.float32)
    sin = np.sin(ang.astype(np.float64)).astype(np.float32)
    P = np.zeros((64, 64), np.float32)
    for d in range(64):
        if d % 32 < 16:
            P[d, d + 16] = -1
        else:
            P[d, d - 16] = 1
    return dict(
        ident=np.eye(128, dtype=np.float32), masks=masks, bones=bones, onesf=np.ones((128, 128), np.float32),
        sel=sel, reset=reset, rope_cos=cos, rope_sin=sin, ropePT=np.ascontiguousarray(P.T),
    )


def _host_inputs(inp):
    f = lambda a: np.ascontiguousarray(a, dtype=np.float32)
    consts = _host_consts()
    w_in = inp["w_in"][0]
    offs = np.cumsum([0, 3072, 1024, 128, 128, 512, 256, 64, 1024, 4096])
    o_rkv, o_zr, o_wd, o_ad, o_qd, o_kvd, o_kr, o_zm, o_mg = offs[:9]
    maps = []
    for c in range(8):
        b, g = c // 4, c % 4
        ch = slice(256 * g, 256 * g + 256)
        cols = np.concatenate([
            o_rkv + np.arange(256 * g, 256 * g + 256),
            o_rkv + 1024 + np.arange(256 * g, 256 * g + 256),
            o_rkv + 2048 + np.arange(256 * g, 256 * g + 256),
            o_zr + np.arange(256 * g, 256 * g + 256),
            o_wd + np.arange(128), o_ad + np.arange(128),
            o_qd + np.arange(512), o_kvd + np.arange(256), o_kr + np.arange(64),
            o_zm + np.arange(256 * g, 256 * g + 256),
        ])
        assert cols.size == WC
        conv = inp["conv_rkv"][0]
        conv_fm = np.zeros((128, 6, 3), np.float32)
        for kind in range(3):
            for cc in range(2):
                cidx = kind * 1024 + 256 * g + cc * 128 + np.arange(128)
                conv_fm[:, kind * 2 + cc, :] = conv[:, cidx].T
        chanv = np.zeros((128, 2, 10), np.float32)
        for cc in range(2):
            cidx = 256 * g + cc * 128 + np.arange(128)
            chanv[:, cc, 0] = inp["w0"][0, 0, cidx]
            chanv[:, cc, 1] = inp["w0"][0, 1, cidx]
            chanv[:, cc, 2] = inp["a0"][0, 0, cidx]
            chanv[:, cc, 3] = inp["a0"][0, 1, cidx]
            chanv[:, cc, 4] = inp["k_k"][0, cidx]
            chanv[:, cc, 5] = inp["k_a"][0, cidx]
            chanv[:, cc, 6] = inp["ln_x_w"][0, cidx]
            chanv[:, cc, 7] = inp["ln_x_b"][0, cidx]
            chanv[:, cc, 8] = inp["r_k"][0].reshape(-1)[cidx]
        wlora = np.zeros((128, 2, 256), np.float32)
        for d in range(2):
            wlora[64 * d:64 * d + 64, 0, :] = inp["w_decay_up"][0, d][:, ch]
            wlora[64 * d:64 * d + 64, 1, :] = inp["w_a_up"][0, d][:, ch]
        mlav = np.zeros((128, 10), np.float32)
        mlav[:, 0:4] = inp["q_norm_w"][0].reshape(4, 128).T
        mlav[:, 4:6] = inp["kv_norm_w"][0].reshape(2, 128).T
        mlav[:, 6] = inp["q_gain"][0][:128]
        mlav[:, 7] = inp["k_gain"][0][:128]
        mlav[0:64, 8] = inp["q_gain"][0][128:]
        mlav[0:64, 9] = inp["k_gain"][0][128:]
        hq = [2 * g, 2 * g + 1]
        w_uq_c = np.concatenate([inp["w_uq"][0][:, h * 192:(h + 1) * 192] for h in hq], axis=1)
        w_ukv_c = np.concatenate([inp["w_ukv"][0][:, h * 256:(h + 1) * 256] for h in hq], axis=1)
        cfm = np.stack([inp["c"][b].reshape(16, 128).T, inp["c_ctx"].reshape(16, 128).T], axis=-1)
        selq = np.zeros((128, 4), np.float32)
        selq[:, g] = 1
        m = dict(
            xs=f(np.concatenate([inp["ctx"][b], inp["x"][b]], axis=0)),
            xm=f(inp["x"][b, 2048 * g:2048 * g + 2048]),
            cfm=f(cfm), norm_w_row=f(np.stack([inp["norm_w"][0]] * 2)), b_mod_row=f(np.stack([inp["b_mod"][0]] * 2)),
            w_mod=f(inp["w_mod"][0]), w_in_core=f(w_in[:, cols]), w_in_mg=f(w_in[:, o_mg:o_mg + 4096]),
            conv_fm=f(conv_fm), chanv=f(chanv), wlora=f(wlora), mlav=f(mlav), w_uq_c=f(w_uq_c), w_ukv_c=f(w_ukv_c),
            w_br_r=f(inp["w_branch_rwkv"][0]), w_br_m=f(inp["w_branch_mla"][0]), w_out=f(inp["w_out"][0]),
            selq=f(selq),
        )
        m.update({k: f(v) for k, v in consts.items()})
        maps.append(m)
    return maps


INPUT_SHAPES = dict(
    xs=[NT, D], xm=[2048, D], cfm=[128, 16, 2], norm_w_row=[2, D], b_mod_row=[2, 3 * D], w_mod=[D, 3 * D],
    w_in_core=[D, WC], w_in_mg=[D, 4096], conv_fm=[128, 6, 3], chanv=[128, 2, 10], wlora=[128, 2, 256],
    mlav=[128, 10], w_uq_c=[512, 384], w_ukv_c=[256, 512], w_br_r=[1024, D], w_br_m=[1024, D], w_out=[D, D],
    selq=[128, 4], ident=[128, 128], masks=[128, 4, 128], bones=[128, 128], onesf=[128, 128], sel=[2, 2, 128],
    reset=[128, 512], rope_cos=[64, NX], rope_sin=[64, NX], ropePT=[64, 64],
)


def build(debug=(), upto="C"):
    nc = bass.Bass("TRN2", target_bir_lowering=False)
    I = {k: nc.dram_tensor(k, s, F32, kind="ExternalInput").ap() for k, s in INPUT_SHAPES.items()}
    out = nc.dram_tensor("out", [2048, D], F32, kind="ExternalOutput").ap()

    def scratch(name, shape, dt):
        if name in debug:
            return nc.dram_tensor(name, shape, dt, kind="ExternalOutput").ap()
        return nc.dram_tensor(name, shape, dt).ap()

    BC = scratch("BC", [5, 128, D], F32)
    U_rkv = scratch("U_rkv", [768, NT], F32)
    G_zr = scratch("G_zr", [256, NX], F32)
    SG = scratch("SG", [512, NT], F32)
    AA = scratch("AA", [512, NT], F32)
    QN = scratch("QN", [256, NX], BF16)
    QR = scratch("QR", [128, NX], BF16)
    KN = scratch("KN", [256, NT], BF16)
    KR = scratch("KR", [64, NT], BF16)
    VT = scratch("VT", [2, 128, NT], BF16)
    G_zm = scratch("G_zm", [256, NX], F32)
    KSD = scratch("KSD", [128, 132], F32)
    YD = [scratch("YD0", [256, NX], F32), scratch("YD1", [256, NX], F32)]
    BD = [scratch("BD0", [256, NX], F32), scratch("BD1", [256, NX], F32)]
    DBG = [scratch(f"DBG{i}", [128, 512], BF16 if i < 2 else F32) for i in range(4)]
    OXs = [scratch(f"OX{j}", [64, NX], BF16) for j in range(8)]
    OGs = [scratch(f"OG{j}", [256, NX], BF16) for j in range(8)]

    v3 = lambda ap, p=128: ap.rearrange("(c p) n -> p c n", p=p)
    U_v, Gzr_v, SG_v, AA_v = v3(U_rkv), v3(G_zr), v3(SG), v3(AA)
    QN_v, QR_v, KN_v, Gzm_v = v3(QN), v3(QR, 64), v3(KN), v3(G_zm)

    with ExitStack() as top:
        S = Sched(nc, top)
        T = Ctx(nc, top)

        identb, Bidentb = T.sb([128, 128], BF16, "identb")
        msk, Bmsk = T.sb([128, 4, 128], BF16, "msk")
        bones, Bbones = T.sb([128, 128], F32, "bones")
        onesf, Bonesf = T.sb([128, 128], F32, "onesf")
        selt, Bselt = T.sb([2, 2, 128], F32, "selt")
        resetm, Bresetm = T.sb([128, 512], F32, "resetm")
        ropePT, BropePT = T.sb([64, 64], F32, "ropePT")
        epsc, Bepsc = T.sb([128, 4], F32, "epsc")
        convw, Bconvw = T.sb([128, 6, 3], F32, "convw")
        chanv, Bchanv = T.sb([128, 2, 10], F32, "chanv")
        mlav, Bmlav = T.sb([128, 10], F32, "mlav")
        KS, BKS = T.sb([128, 132], F32, "KS")
        S.dma("pool", identb[:], I["ident"], writes=[Bidentb], semof=Bidentb)
        S.dma("pool", msk[:], I["masks"], writes=[Bmsk], semof=Bmsk)
        for t_, b_, k_ in ((bones, Bbones, "bones"), (onesf, Bonesf, "onesf"), (selt, Bselt, "sel"), (resetm, Bresetm, "reset"),
                           (ropePT, BropePT, "ropePT"), (convw, Bconvw, "conv_fm"), (chanv, Bchanv, "chanv"), (mlav, Bmlav, "mlav")):
            S.dma("sp", t_[:], I[k_], writes=[b_], semof=b_)
        S.op("pool", lambda e: e.memset(epsc[:, 0:1], EPS), writes=[Bepsc])
        S.op("pool", lambda e: e.memset(epsc[:, 1:2], 1e-12), writes=[Bepsc])
        S.op("pool", lambda e: e.memset(epsc[:, 2:3], GN_EPS), writes=[Bepsc])
        S.op("pool", lambda e: e.memset(epsc[:, 3:4], 0.0), writes=[Bepsc])
        for b_ in (Bidentb, Bmsk, Bbones, Bonesf, Bselt, Bresetm, BropePT, Bepsc, Bconvw, Bchanv, Bmlav):
            b_.const = True

        with ExitStack() as p0s:
            P = Ctx(nc, p0s); P.n = 100
            cf, Bcf = P.sb([128, 16, 2], F32, "cf")
            sc, Bsc = P.sb([128, 16, 2], BF16, "sc")
            b2, Bb2 = P.sb([2, 3 * D], F32, "b2")
            nw2, Bnw2 = P.sb([2, D], F32, "nw2")
            mrow, Bmrow = P.sb([2, 3 * D], F32, "mrow")
            grow, Bgrow = P.sb([2, D], F32, "grow")
            wm = [P.sb([128, 16, 512], BF16, "wm") for _ in range(2)]
            bct = [P.sb([128, D], F32, "bct") for _ in range(2)]
            pm, Bpm = P.ps([128, 512], F32, "pm")
            pb = [P.ps([128, 512], F32, "pb") for _ in range(2)]
            S.dma("sp", cf[:], I["cfm"], writes=[Bcf], semof=Bcf)
            S.dma("sp", b2[:], I["b_mod_row"], writes=[Bb2], semof=Bb2)
            S.dma("sp", nw2[:], I["norm_w_row"], writes=[Bnw2], semof=Bnw2)
            S.op("act", lambda e: e.activation(out=sc[:], in_=cf[:], func=AF.Silu), reads=[Bcf], writes=[Bsc])
            wmod_v = I["w_mod"].rearrange("(kc p) n -> p kc n", p=128)
            for cb in range(12):
                wt, Bwt = wm[cb % 2]
                S.dma("pool", wt[:], wmod_v[:, :, cb * 512:(cb + 1) * 512], writes=[Bwt], semof=Bwt)

                def mmf(e, wt=wt):
                    for kc in range(16):
                        ins = e.matmul(pm[0:2, :], sc[:, kc, :], wt[:, kc, :], start=(kc == 0), stop=(kc == 15))
                    return ins
                S.op("pe", mmf, reads=[Bsc, Bwt], writes=[Bpm])
                S.op("act", lambda e, cb=cb: e.activation(out=mrow[0:2, cb * 512:(cb + 1) * 512], in_=pm[0:2, :], func=AF.Copy),
                     reads=[Bpm], writes=[Bmrow])
            S.op("pool", lambda e: e.tensor_tensor(out=mrow[:], in0=mrow[:], in1=b2[:], op=ALU.add), reads=[Bmrow, Bb2], writes=[Bmrow])
            S.op("dve", lambda e: e.scalar_tensor_tensor(out=grow[:], in0=mrow[:, D:2 * D], scalar=1.0, in1=nw2[:],
                                                          op0=ALU.add, op1=ALU.mult), reads=[Bmrow, Bnw2], writes=[Bgrow])
            plan0 = [(0, grow, Bgrow, 0, 0), (1, mrow, Bmrow, 0, 0), (2, mrow, Bmrow, 2 * D, 0), (3, grow, Bgrow, 0, 1), (4, mrow, Bmrow, 0, 1)]
            k = 0
            for (idx, src, Bsrc, off, si) in plan0:
                st, Bst = bct[idx % 2]
                for blk in range(4):
                    pt_, Bpt_ = pb[k % 2]
                    k += 1
                    S.op("pe", lambda e, pt_=pt_, src=src, off=off, blk=blk, si=si: e.matmul(
                        pt_[:, :], selt[0:2, si, :], src[0:2, off + blk * 512: off + (blk + 1) * 512], start=True, stop=True),
                        reads=[Bsrc, Bselt], writes=[Bpt_])
                    S.op("act", lambda e, pt_=pt_, st=st, blk=blk: e.activation(out=st[:, blk * 512:(blk + 1) * 512], in_=pt_[:, :], func=AF.Copy),
                         reads=[Bpt_], writes=[Bst])
                S.dma("sp", BC[idx], st[:], reads=[Bst], semof=Bst)
            S.barrier()
            S.emit()
            S.end_phase()
        if upto == "0":
            return nc

        with ExitStack() as pas:
            P = Ctx(nc, pas); P.n = 200
            W, BW = P.sb([128, 16, WC], BF16, "W")
            wlora, Bwlora = P.sb([128, 2, 256], BF16, "wlora")
            wuq, Bwuq = P.sb([128, 4, 384], BF16, "wuq")
            wukv, Bwukv = P.sb([128, 2, 512], BF16, "wukv")
            gain_bc, Bgain = P.sb([128, D], F32, "gain_bc")
            shift_bc, Bshift = P.sb([128, D], F32, "shift_bc")
            xt = [P.sb([128, D], F32, "xt") for _ in range(2)]
            hm = [P.sb([128, D], BF16, "hm") for _ in range(2)]
            ss = [P.sb([128, 4], F32, "ss") for _ in range(2)]
            hmT = [P.sb([128, 16, GS], BF16, "hmT") for _ in range(2)]
            urkv = [P.sb([128, GS], F32, "urkv") for _ in range(2)]
            gz = [P.sb([128, GS], F32, "gz") for _ in range(2)]
            sga = [P.sb([128, GS], F32, "sga") for _ in range(2)]
            twd, Btwd = P.sb([128, GS], BF16, "twd")
            adb, Badb = P.sb([128, GS], BF16, "adb")
            qd, Bqd = P.sb([128, 4, GS], F32, "qd")
            sqq, Bsqq = P.sb([128, 4, GS], F32, "sqq")
            qn, Bqn = P.sb([128, 4, GS], BF16, "qn")
            rq, Brq = P.sb([128, GS], F32, "rq")
            qno, Bqno = P.sb([128, GS], F32, "qno")
            qro, Bqro = P.sb([64, GS], F32, "qro")
            sqh, Bsqh = P.sb([128, 2, GS], F32, "sqh")
            rh, Brh = P.sb([128, GS], F32, "rh")
            qnf, Bqnf = P.sb([128, GS], BF16, "qnf")
            qrg, Bqrg = P.sb([64, GS], F32, "qrg")
            t1, Bt1 = P.sb([64, GS], F32, "t1")
            t2, Bt2 = P.sb([64, GS], F32, "t2")
            qrf, Bqrf = P.sb([64, GS], BF16, "qrf")
            cost, Bcost = P.sb([64, GS], F32, "cost")
            sint, Bsint = P.sb([64, GS], F32, "sint")
            kvd, Bkvd = P.sb([128, 2, GS], F32, "kvd")
            kvn, Bkvn = P.sb([128, 2, GS], BF16, "kvn")
            kro, Bkro = P.sb([64, GS], F32, "kro")
            vts, Bvts = P.sb([128, 2, GS], BF16, "vts")
            kst, Bkst = P.sb([128, 8], F32, "kst")
            pT = [P.ps([128, 1024], BF16, "pT") for _ in range(2)]
            po = [P.ps([128, 512], F32, "po") for _ in range(3)]
            pst, Bpst = P.ps([128, 512], F32, "pst")
            pv, Bpv = P.ps([128, 512], F32, "pv")
            pks, Bpks = P.ps([128, 512], F32, "pks")
            cnt = {"po": 0, "tile": 0, "u": 0, "g": 0, "s": 0}

            def next_po():
                cnt["po"] += 1
                return po[cnt["po"] % 3]

            win_v = I["w_in_core"].rearrange("(kc p) n -> p kc n", p=128)
            S.dma("pool", W[:, :, :], win_v[:, :, :], writes=[BW], semof=BW)
            S.dma("pool", wlora[:], I["wlora"], writes=[Bwlora], semof=Bwlora)
            S.dma("pool", wuq[:], I["w_uq_c"].rearrange("(kc p) n -> p kc n", p=128), writes=[Bwuq], semof=Bwuq)
            S.dma("pool", wukv[:], I["w_ukv_c"].rearrange("(kc p) n -> p kc n", p=128), writes=[Bwukv], semof=Bwukv)
            S.dma("sp", gain_bc[:], BC[3], writes=[Bgain], semof=Bgain)
            S.dma("sp", shift_bc[:], BC[4], writes=[Bshift], semof=Bshift)

            def prep_tile(row0, hslot, t):
                s = cnt["tile"] % 2
                cnt["tile"] += 1
                x_, Bx_ = xt[s]
                h_, Bh_ = hm[s]
                s_, Bs_ = ss[s]
                hT, BhT = hmT[hslot]
                S.dma("sp", x_[:], I["xs"][row0:row0 + 128, :], writes=[Bx_], semof=Bx_)
                S.op("act", lambda e: e.activation(out=h_[:], in_=x_[:], func=AF.Square, accum_out=s_[:, 0:1]), reads=[Bx_], writes=[Bh_, Bs_])
                S.op("act", lambda e: e.activation(out=s_[:, 1:2], in_=s_[:, 0:1], func=AF.Sqrt, scale=1.0 / D, bias=epsc[:, 0:1]),
                     reads=[Bs_, Bepsc], writes=[Bs_])
                S.op("dve", lambda e: e.reciprocal(out=s_[:, 2:3], in_=s_[:, 1:2]), reads=[Bs_], writes=[Bs_])
                S.op("dve", lambda e: e.scalar_tensor_tensor(out=x_[:], in0=x_[:], scalar=s_[:, 2:3], in1=gain_bc[:], op0=ALU.mult, op1=ALU.mult),
                     reads=[Bx_, Bs_, Bgain], writes=[Bx_])
                S.op("pool", lambda e: e.tensor_tensor(out=h_[:], in0=x_[:], in1=shift_bc[:], op=ALU.add), reads=[Bx_, Bshift], writes=[Bh_])
                for half in range(2):
                    p_, Bp_ = pT[half]

                    def trf(e, half=half, p_=p_):
                        for j in range(8):
                            kc = half * 8 + j
                            ins = e.transpose(p_[:, j * 128:(j + 1) * 128], h_[:, kc * 128:(kc + 1) * 128], identb[:])
                        return ins
                    S.op("pe", trf, reads=[Bh_, Bidentb], writes=[Bp_])
                    cp = (lambda e, half=half, p_=p_: e.activation(out=hT[:, half * 8:(half + 1) * 8, t * 128:(t + 1) * 128],
                                                                   in_=p_[:, :].rearrange("p (j n) -> p j n", n=128), func=AF.Copy)) if half == 0 else \
                         (lambda e, half=half, p_=p_: e.tensor_copy(out=hT[:, half * 8:(half + 1) * 8, t * 128:(t + 1) * 128],
                                                                    in_=p_[:, :].rearrange("p (j n) -> p j n", n=128)))
                    S.op("act" if half == 0 else "dve", cp, reads=[Bp_], writes=[BhT])

            def rstd_from(psum_ap, Bps, out_t, Bout, npart, N, inv_n):
                S.op("act", lambda e: e.activation(out=out_t[0:npart, 0:N], in_=psum_ap, func=AF.Sqrt, scale=inv_n, bias=epsc[0:npart, 0:1]),
                     reads=[Bps, Bepsc], writes=[Bout])
                S.op("dve", lambda e: e.reciprocal(out=out_t[0:npart, 0:N], in_=out_t[0:npart, 0:N]), reads=[Bout], writes=[Bout])

            def rope_apply(src, Bsrc, N, dst_dram):
                pr, Bpr = next_po()
                S.op("pe", lambda e: e.matmul(pr[0:64, 0:N], ropePT[0:64, 0:64], src[0:64, 0:N], start=True, stop=True),
                     reads=[Bsrc, BropePT], writes=[Bpr])
                S.op("dve", lambda e: e.tensor_tensor(out=t1[0:64, 0:N], in0=src[0:64, 0:N], in1=cost[0:64, 0:N], op=ALU.mult),
                     reads=[Bsrc, Bcost], writes=[Bt1])
                S.op("dve", lambda e: e.tensor_tensor(out=t2[0:64, 0:N], in0=pr[0:64, 0:N], in1=sint[0:64, 0:N], op=ALU.mult),
                     reads=[Bpr, Bsint], writes=[Bt2])
                S.op("pool", lambda e: e.tensor_tensor(out=qrf[0:64, 0:N], in0=t1[0:64, 0:N], in1=t2[0:64, 0:N], op=ALU.add),
                     reads=[Bt1, Bt2], writes=[Bqrf])
                S.dma("sp", dst_dram, qrf[0:64, 0:N], reads=[Bqrf], semof=Bqrf)

            def proj_group(gi, hslot, inter):
                isx = gi > 0
                N = GS
                n0 = 256 + (gi - 1) * GS if isx else 0
                xo = (gi - 1) * GS
                ntile = N // 128
                hT, BhT = hmT[hslot]
                inter = list(inter)

                def mm_chunk(col0, M):
                    p_, Bp_ = next_po()

                    def f(e):
                        for kc in range(16):
                            ins = e.matmul(p_[0:M, 0:N], W[:, kc, col0:col0 + M], hT[:, kc, 0:N], start=(kc == 0), stop=(kc == 15))
                        return ins
                    S.op("pe", f, reads=[BW, BhT], writes=[Bp_])
                    if inter:
                        inter.pop(0)()
                    return p_, Bp_

                if isx:
                    S.dma("sp", cost[:, 0:N], I["rope_cos"][:, xo:xo + N], writes=[Bcost], semof=Bcost)
                    S.dma("sp", sint[:, 0:N], I["rope_sin"][:, xo:xo + N], writes=[Bsint], semof=Bsint)
                for j in range(6):
                    p_, Bp_ = mm_chunk(j * 128, 128)
                    u_, Bu_ = urkv[cnt["u"] % 2]
                    cnt["u"] += 1
                    S.op("act", lambda e, p_=p_, u_=u_: e.activation(out=u_[:, 0:N], in_=p_[:, 0:N], func=AF.Copy), reads=[Bp_], writes=[Bu_])
                    S.dma("sp", U_v[:, j, n0:n0 + N], u_[:, 0:N], reads=[Bu_], semof=Bu_)
                if isx:
                    for j in range(2):
                        p_, Bp_ = mm_chunk(768 + j * 128, 128)
                        g_, Bg_ = gz[cnt["g"] % 2]
                        cnt["g"] += 1
                        S.op("act", lambda e, p_=p_, g_=g_: e.activation(out=g_[:, 0:N], in_=p_[:, 0:N], func=AF.Silu), reads=[Bp_], writes=[Bg_])
                        S.dma("sp", Gzr_v[:, j, xo:xo + N], g_[:, 0:N], reads=[Bg_], semof=Bg_)
                p_, Bp_ = mm_chunk(1024, 128)
                S.op("act", lambda e, p_=p_: e.activation(out=twd[:, 0:N], in_=p_[:, 0:N], func=AF.Tanh), reads=[Bp_], writes=[Btwd])
                p_, Bp_ = mm_chunk(1152, 128)
                S.op("act", lambda e, p_=p_: e.activation(out=adb[:, 0:N], in_=p_[:, 0:N], func=AF.Copy), reads=[Bp_], writes=[Badb])
                for which, (src, Bsrc, dst_v, cbase) in enumerate(((twd, Btwd, SG_v, 0), (adb, Badb, AA_v, 2))):
                    for d in range(2):
                        for cc in range(2):
                            p_, Bp_ = next_po()
                            S.op("pe", lambda e, p_=p_, src=src, d=d, cc=cc, which=which: e.matmul(
                                p_[:, 0:N], wlora[64 * d:64 * d + 64, which, cc * 128:(cc + 1) * 128], src[64 * d:64 * d + 64, 0:N],
                                start=True, stop=True), reads=[Bwlora, Bsrc], writes=[Bp_])
                            s_, Bs_ = sga[cnt["s"] % 2]
                            cnt["s"] += 1
                            S.op("act", lambda e, p_=p_, s_=s_, d=d, cc=cc, cbase=cbase: e.activation(
                                out=s_[:, 0:N], in_=p_[:, 0:N], func=AF.Sigmoid, bias=chanv[:, cc, cbase + d:cbase + d + 1]),
                                reads=[Bp_, Bchanv], writes=[Bs_])
                            S.dma("sp", dst_v[:, d * 2 + cc, n0:n0 + N], s_[:, 0:N], reads=[Bs_], semof=Bs_)
                if isx:
                    for j in range(4):
                        p_, Bp_ = mm_chunk(1280 + j * 128, 128)
                        S.op("act", lambda e, p_=p_, j=j: e.activation(out=qd[:, j, 0:N], in_=p_[:, 0:N], func=AF.Copy), reads=[Bp_], writes=[Bqd])
                    S.op("pool", lambda e: e.tensor_tensor(out=sqq[:, :, 0:N], in0=qd[:, :, 0:N], in1=qd[:, :, 0:N], op=ALU.mult), reads=[Bqd], writes=[Bsqq])

                    def ssq(e):
                        for j in range(4):
                            ins = e.matmul(pst[:, 0:N], onesf[:, :], sqq[:, j, 0:N], start=(j == 0), stop=(j == 3))
                        return ins
                    S.op("pe", ssq, reads=[Bsqq, Bonesf], writes=[Bpst])
                    rstd_from(pst[:, 0:N], Bpst, rq, Brq, 128, N, 1.0 / 512)
                    for j in range(4):
                        S.op("dve", lambda e, j=j: e.scalar_tensor_tensor(out=qn[:, j, 0:N], in0=qd[:, j, 0:N], scalar=mlav[:, j:j + 1], in1=rq[:, 0:N],
                                                                          op0=ALU.mult, op1=ALU.mult), reads=[Bqd, Bmlav, Brq], writes=[Bqn])
                    for h in range(2):
                        p1, Bp1 = next_po()
                        p2, Bp2 = next_po()

                        def qup(e, h=h, p1=p1, p2=p2):
                            for kc in range(4):
                                e.matmul(p1[:, 0:N], wuq[:, kc, h * 192:h * 192 + 128], qn[:, kc, 0:N], start=(kc == 0), stop=(kc == 3))
                            for kc in range(4):
                                ins = e.matmul(p2[0:64, 0:N], wuq[:, kc, h * 192 + 128:h * 192 + 192], qn[:, kc, 0:N], start=(kc == 0), stop=(kc == 3))
                            return ins
                        S.op("pe", qup, reads=[Bwuq, Bqn], writes=[Bp1, Bp2])
                        S.op("act", lambda e, p1=p1: e.activation(out=qno[:, 0:N], in_=p1[:, 0:N], func=AF.Copy), reads=[Bp1], writes=[Bqno])
                        S.op("act", lambda e, p2=p2: e.activation(out=qro[0:64, 0:N], in_=p2[0:64, 0:N], func=AF.Copy), reads=[Bp2], writes=[Bqro])
                        S.op("pool", lambda e: e.tensor_tensor(out=sqh[:, 0, 0:N], in0=qno[:, 0:N], in1=qno[:, 0:N], op=ALU.mult), reads=[Bqno], writes=[Bsqh])
                        S.op("pool", lambda e: e.tensor_tensor(out=sqh[0:64, 1, 0:N], in0=qro[0:64, 0:N], in1=qro[0:64, 0:N], op=ALU.mult), reads=[Bqro], writes=[Bsqh])

                        def ssh(e):
                            e.matmul(pst[:, 0:N], onesf[:, :], sqh[:, 0, 0:N], start=True, stop=False)
                            return e.matmul(pst[:, 0:N], onesf[0:64, :], sqh[0:64, 1, 0:N], start=False, stop=True)
                        S.op("pe", ssh, reads=[Bsqh, Bonesf], writes=[Bpst])
                        rstd_from(pst[:, 0:N], Bpst, rh, Brh, 128, N, 1.0 / 192)
                        S.op("dve", lambda e: e.scalar_tensor_tensor(out=qnf[:, 0:N], in0=qno[:, 0:N], scalar=mlav[:, 6:7], in1=rh[:, 0:N],
                                                                      op0=ALU.mult, op1=ALU.mult), reads=[Bqno, Bmlav, Brh], writes=[Bqnf])
                        S.dma("sp", QN_v[:, h, xo:xo + N], qnf[:, 0:N], reads=[Bqnf], semof=Bqnf)
                        S.op("dve", lambda e: e.scalar_tensor_tensor(out=qrg[0:64, 0:N], in0=qro[0:64, 0:N], scalar=mlav[0:64, 8:9], in1=rh[0:64, 0:N],
                                                                      op0=ALU.mult, op1=ALU.mult), reads=[Bqro, Bmlav, Brh], writes=[Bqrg])
                        rope_apply(qrg, Bqrg, N, QR_v[:, h, xo:xo + N])
                for j in range(2):
                    p_, Bp_ = mm_chunk(1792 + j * 128, 128)
                    S.op("act", lambda e, p_=p_, j=j: e.activation(out=kvd[:, j, 0:N], in_=p_[:, 0:N], func=AF.Copy), reads=[Bp_], writes=[Bkvd])
                p_, Bp_ = mm_chunk(2048, 64)
                S.op("act", lambda e, p_=p_: e.activation(out=kro[0:64, 0:N], in_=p_[0:64, 0:N], func=AF.Copy), reads=[Bp_], writes=[Bkro])
                S.op("pool", lambda e: e.tensor_tensor(out=sqq[:, 0:2, 0:N], in0=kvd[:, :, 0:N], in1=kvd[:, :, 0:N], op=ALU.mult), reads=[Bkvd], writes=[Bsqq])

                def sskv(e):
                    for j in range(2):
                        ins = e.matmul(pst[:, 0:N], onesf[:, :], sqq[:, j, 0:N], start=(j == 0), stop=(j == 1))
                    return ins
                S.op("pe", sskv, reads=[Bsqq, Bonesf], writes=[Bpst])
                rstd_from(pst[:, 0:N], Bpst, rq, Brq, 128, N, 1.0 / 256)
                for j in range(2):
                    S.op("dve", lambda e, j=j: e.scalar_tensor_tensor(out=kvn[:, j, 0:N], in0=kvd[:, j, 0:N], scalar=mlav[:, 4 + j:5 + j], in1=rq[:, 0:N],
                                                                      op0=ALU.mult, op1=ALU.mult), reads=[Bkvd, Bmlav, Brq], writes=[Bkvn])
                S.op("pool", lambda e: e.tensor_tensor(out=sqh[0:64, 1, 0:N], in0=kro[0:64, 0:N], in1=kro[0:64, 0:N], op=ALU.mult), reads=[Bkro], writes=[Bsqh])
                for h in range(2):
                    p1, Bp1 = next_po()

                    def kup(e, h=h, p1=p1):
                        for kc in range(2):
                            ins = e.matmul(p1[:, 0:N], wukv[:, kc, h * 256:h * 256 + 128], kvn[:, kc, 0:N], start=(kc == 0), stop=(kc == 1))
                        return ins
                    S.op("pe", kup, reads=[Bwukv, Bkvn], writes=[Bp1])
                    S.op("act", lambda e, p1=p1: e.activation(out=qno[:, 0:N], in_=p1[:, 0:N], func=AF.Copy), reads=[Bp1], writes=[Bqno])
                    S.op("pool", lambda e: e.tensor_tensor(out=sqh[:, 0, 0:N], in0=qno[:, 0:N], in1=qno[:, 0:N], op=ALU.mult), reads=[Bqno], writes=[Bsqh])

                    def kss(e, h=h):
                        for t in range(ntile):
                            c = t * 2 + h
                            e.matmul(pks[:, c:c + 1], sqh[:, 0, t * 128:(t + 1) * 128], onesf[:, 0:1], start=True, stop=False)
                            ins = e.matmul(pks[:, c:c + 1], sqh[0:64, 1, t * 128:(t + 1) * 128], onesf[0:64, 0:1], start=False, stop=True)
                        return ins
                    S.op("pe", kss, reads=[Bsqh, Bonesf], writes=[Bpks])
                    S.op("dve", lambda e: e.tensor_scalar(out=qnf[:, 0:N], in0=qno[:, 0:N], scalar1=mlav[:, 7:8], scalar2=None, op0=ALU.mult),
                         reads=[Bqno, Bmlav], writes=[Bqnf])
                    S.dma("sp", KN_v[:, h, n0:n0 + N], qnf[:, 0:N], reads=[Bqnf], semof=Bqnf)

                    def vup(e, h=h):
                        for t in range(ntile):
                            for kc in range(2):
                                ins = e.matmul(pv[:, t * 128:(t + 1) * 128], kvn[:, kc, t * 128:(t + 1) * 128], wukv[:, kc, h * 256 + 128:h * 256 + 256],
                                               start=(kc == 0), stop=(kc == 1))
                        return ins
                    S.op("pe", vup, reads=[Bwukv, Bkvn], writes=[Bpv])
                    S.op("act", lambda e, h=h: e.activation(out=vts[:, h, 0:N], in_=pv[:, 0:N], func=AF.Copy), reads=[Bpv], writes=[Bvts])
                    S.dma("sp", VT[h, :, n0:n0 + N], vts[:, h, 0:N], reads=[Bvts], semof=Bvts)
                nk = ntile * 2
                t0 = (n0 // 128) * 2
                S.op("act", lambda e: e.activation(out=kst[:, 0:nk], in_=pks[:, 0:nk], func=AF.Sqrt, scale=1.0 / 192, bias=epsc[:, 0:1]),
                     reads=[Bpks, Bepsc], writes=[Bkst])
                S.op("dve", lambda e: e.reciprocal(out=kst[:, 0:nk], in_=kst[:, 0:nk]), reads=[Bkst], writes=[Bkst])
                S.op("dve", lambda e: e.tensor_scalar(out=KS[:, t0:t0 + nk], in0=kst[:, 0:nk], scalar1=float(192 ** -0.5), scalar2=None, op0=ALU.mult),
                     reads=[Bkst], writes=[BKS])
                S.op("dve", lambda e: e.tensor_scalar(out=qrg[0:64, 0:N], in0=kro[0:64, 0:N], scalar1=mlav[0:64, 9:10], scalar2=None, op0=ALU.mult),
                     reads=[Bkro, Bmlav], writes=[Bqrg])
                if isx:
                    rope_apply(qrg, Bqrg, N, KR[:, n0:n0 + N])
                else:
                    S.op("pool", lambda e: e.tensor_copy(out=qrf[0:64, 0:N], in_=qrg[0:64, 0:N]), reads=[Bqrg], writes=[Bqrf])
                    S.dma("sp", KR[:, n0:n0 + N], qrf[0:64, 0:N], reads=[Bqrf], semof=Bqrf)
                if isx:
                    for j in range(2):
                        p_, Bp_ = mm_chunk(2112 + j * 128, 128)
                        g_, Bg_ = gz[cnt["g"] % 2]
                        cnt["g"] += 1
                        S.op("act", lambda e, p_=p_, g_=g_: e.activation(out=g_[:, 0:N], in_=p_[:, 0:N], func=AF.Silu), reads=[Bp_], writes=[Bg_])
                        S.dma("sp", Gzm_v[:, j, xo:xo + N], g_[:, 0:N], reads=[Bg_], semof=Bg_)
                while inter:
                    inter.pop(0)()

            NGRP = 1 + NX // GS
            NGRP = int(os.environ.get("MK_NGRP", NGRP))
            for t in range(GS // 128):
                prep_tile(t * 128, 0, t)
            S.dma("sp", gain_bc[:], BC[0], writes=[Bgain], reads=[], semof=Bgain)
            S.dma("sp", shift_bc[:], BC[1], writes=[Bshift], reads=[], semof=Bshift)
            for gi in range(NGRP):
                inter = []
                if gi + 1 < NGRP:
                    r0 = 256 + gi * GS
                    inter = [(lambda t=t, r0=r0, hs=(gi + 1) % 2: prep_tile(r0 + t * 128, hs, t)) for t in range(GS // 128)]
                proj_group(gi, gi % 2, inter)
            if "KSD" in debug:
                S.dma("sp", KSD, KS[:], reads=[BKS], semof=BKS)
            S.barrier()
            S.emit()
            S.end_phase()
        if upto == "A":
            return nc

        GR = 256
        NCH = GR // 64
        NXG = NX // GR
        U_k = U_rkv.rearrange("(k c p) n -> p k c n", k=3, c=2, p=128)
        YD_v = [v3(YD[d]) for d in range(2)]
        BD_v = [v3(BD[d]) for d in range(2)]
        BYD = [Buf("YD0"), Buf("YD1")]
        with ExitStack() as prs:
            P = Ctx(nc, prs); P.n = 300
            omka, Bomka = P.sb([128, 2], F32, "omka")
            S.op("dve", lambda e: e.tensor_scalar(out=omka[:, :], in0=chanv[:, :, 5], scalar1=-1.0, scalar2=1.0, op0=ALU.mult, op1=ALU.add),
                 reads=[Bchanv], writes=[Bomka])
            pNA = [P.ps([128, 512], F32, "pNA") for _ in range(2)]
            pB = [P.ps([128, 512], F32, "pB") for _ in range(2)]
            pC = [P.ps([128, 512], F32, "pC") for _ in range(2)]
            pTRt, BpTRt = P.ps([128, 1024], BF16, "pTR")
            pSTt, BpSTt = P.ps([128, 512], F32, "pST")
            BpTR = [Buf("pTR0"), Buf("pTR1")]
            BpST = [Buf("pST0"), Buf("pST1")]

            class CP:
                pass
            cps = []
            for cc in range(2):
                for d in range(2):
                    c_ = CP()
                    c_.cc, c_.d, c_.set = cc, d, len(cps) % 2
                    for nm, shp, dt in (("ub", [128, 3, GR + 2], F32), ("cv", [128, 3, GR], F32), ("sgt", [128, GR], F32), ("aat", [128, GR], F32),
                                        ("sq", [128, GR], F32), ("rs", [128, GR], F32), ("kk", [128, GR], F32), ("ff", [128, GR], F32),
                                        ("kmod", [128, GR], F32), ("akk", [128, GR], F32), ("Pc", [128, GR], F32), ("Ei", [128, GR], F32),
                                        ("Ee", [128, GR], F32), ("g", [128, GR], F32), ("gp", [128, GR], F32), ("gi", [128, GR], F32),
                                        ("gtot", [128, NCH], F32), ("AR", [128, NCH, 256], BF16), ("BE", [128, NCH, 128], BF16),
                                        ("KT", [128, NCH, 128], BF16), ("VB", [128, NCH, 128], BF16), ("NA", [128, 256], BF16),
                                        ("KA", [128, 256], BF16), ("A0", [128, 128], BF16), ("PW0", [128, 256], BF16), ("PW1", [128, 256], BF16),
                                        ("Tm0", [128, 128], BF16), ("Tm1", [128, 128], BF16), ("TR", [128, 384], BF16), ("Xb", [128, 128], BF16),
                                        ("Ub", [128, 128], BF16), ("H", [128, 128], F32), ("Hb", [128, 128], BF16), ("S1", [128, 128], F32),
                                        ("Yg", [128, GR], F32), ("pr", [128, GR], F32), ("bon", [128, GR], F32)):
                        t_, b_ = P.sb(shp, dt, nm)
                        setattr(c_, nm, t_)
                        setattr(c_, "B" + nm, b_)
                    for nm in ("AR", "BE", "KT", "VB", "H", "Hb"):
                        t_ = getattr(c_, nm)
                        b_ = getattr(c_, "B" + nm)
                        S.op("pool", lambda e, t_=t_: e.memset(t_[:], 0.0), writes=[b_])
                    cps.append(c_)

            c3 = lambda ap: ap.rearrange("p (c t) -> p c t", t=64)

            def prep(c, n0, N, s0, s1):
                cc, d = c.cc, c.d
                lo, hi = n0 - 1, n0 + N + 1
                dl, dh = 0, N + 2
                if n0 == s0:
                    S.op("pool", lambda e: e.memset(c.ub[:, :, 0:1], 0.0), writes=[c.Bub])
                    lo, dl = n0, 1
                if n0 + N == s1:
                    S.op("pool", lambda e: e.memset(c.ub[:, :, N + 1:N + 2], 0.0), writes=[c.Bub])
                    hi, dh = n0 + N, N + 1
                S.dma("sp", c.ub[:, :, dl:dh], U_k[:, :, cc, lo:hi], writes=[c.Bub], semof=c.Bub)
                S.dma("sp", c.sgt[:, 0:N], SG_v[:, d * 2 + cc, n0:n0 + N], writes=[c.Bsgt], semof=c.Bsgt)
                S.dma("sp", c.aat[:, 0:N], AA_v[:, d * 2 + cc, n0:n0 + N], writes=[c.Baat], semof=c.Baat)
                for kind in range(3):
                    ch = kind * 2 + cc
                    S.op("act", lambda e, kind=kind, ch=ch: e.activation(out=c.cv[:, kind, 0:N], in_=c.ub[:, kind, 1:N + 1], func=AF.Copy,
                                                                         scale=convw[:, ch, 1:2]), reads=[c.Bub, Bconvw], writes=[c.Bcv])
                    S.op("dve", lambda e, kind=kind, ch=ch: e.scalar_tensor_tensor(out=c.cv[:, kind, 0:N], in0=c.ub[:, kind, 0:N], scalar=convw[:, ch, 0:1],
                                                                                   in1=c.cv[:, kind, 0:N], op0=ALU.mult, op1=ALU.add),
                         reads=[c.Bub, Bconvw, c.Bcv], writes=[c.Bcv])
                    S.op("dve", lambda e, kind=kind, ch=ch: e.scalar_tensor_tensor(out=c.cv[:, kind, 0:N], in0=c.ub[:, kind, 2:N + 2], scalar=convw[:, ch, 2:3],
                                                                                   in1=c.cv[:, kind, 0:N], op0=ALU.mult, op1=ALU.add),
                         reads=[c.Bub, Bconvw, c.Bcv], writes=[c.Bcv])
                S.op("act", lambda e: e.activation(out=c.sq[:, 0:N], in_=c.cv[:, 1, 0:N], func=AF.Square, scale=chanv[:, cc, 4:5]),
                     reads=[c.Bcv, Bchanv], writes=[c.Bsq])
                st_ = pSTt[:, c.set * 256:c.set * 256 + N]
                Bst_ = BpST[c.set]
                S.op("pe", lambda e: e.matmul(st_, bones[:, :], c.sq[:, 0:N], start=True, stop=True), reads=[c.Bsq, Bbones], writes=[Bst_])
                S.op("act", lambda e: e.activation(out=c.rs[:, 0:N], in_=st_, func=AF.Sqrt, bias=epsc[:, 1:2], scale=1.0), reads=[Bst_, Bepsc], writes=[c.Brs])
                S.op("dve", lambda e: e.reciprocal(out=c.rs[:, 0:N], in_=c.rs[:, 0:N]), reads=[c.Brs], writes=[c.Brs])
                S.op("dve", lambda e: e.scalar_tensor_tensor(out=c.kk[:, 0:N], in0=c.cv[:, 1, 0:N], scalar=chanv[:, cc, 4:5], in1=c.rs[:, 0:N],
                                                              op0=ALU.mult, op1=ALU.mult), reads=[c.Bcv, Bchanv, c.Brs], writes=[c.Bkk])
                S.op("dve", lambda e: e.tensor_scalar(out=c.ff[:, 0:N], in0=c.aat[:, 0:N], scalar1=chanv[:, cc, 5:6], scalar2=omka[:, cc:cc + 1],
                                                       op0=ALU.mult, op1=ALU.add), reads=[c.Baat, Bchanv, Bomka], writes=[c.Bff])
                S.op("pool", lambda e: e.tensor_tensor(out=c.kmod[:, 0:N], in0=c.cv[:, 1, 0:N], in1=c.ff[:, 0:N], op=ALU.mult), reads=[c.Bcv, c.Bff], writes=[c.Bkmod])
                S.op("pool", lambda e: e.tensor_tensor(out=c.akk[:, 0:N], in0=c.aat[:, 0:N], in1=c.kk[:, 0:N], op=ALU.mult), reads=[c.Baat, c.Bkk], writes=[c.Bakk])
                S.op("dve", lambda e: e.scalar_tensor_tensor(out=c.pr[:, 0:N], in0=c.cv[:, 0, 0:N], scalar=chanv[:, cc, 8:9], in1=c.kmod[:, 0:N],
                                                              op0=ALU.mult, op1=ALU.mult), reads=[c.Bcv, Bchanv, c.Bkmod], writes=[c.Bpr])
                S.op("pe", lambda e: e.matmul(st_, bones[:, :], c.pr[:, 0:N], start=True, stop=True), reads=[c.Bpr, Bbones], writes=[Bst_])
                S.op("dve", lambda e: e.tensor_tensor(out=c.bon[:, 0:N], in0=st_, in1=c.cv[:, 2, 0:N], op=ALU.mult), reads=[Bst_, c.Bcv], writes=[c.Bbon])
                S.op("dve", lambda e: e.tensor_tensor_scan(out=c.Pc[:, 0:N], data0=resetm[:, 0:N], data1=c.sgt[:, 0:N], initial=0.0, op0=ALU.mult, op1=ALU.add),
                     reads=[Bresetm, c.Bsgt], writes=[c.BPc])
                nch = N // 64
                tot = c3(c.Pc[:, 0:N])[:, :, 63]
                if d == 0:
                    S.op("pool", lambda e: e.tensor_tensor(out=c.Ee[:, 0:N], in0=c.Pc[:, 0:N], in1=c.sgt[:, 0:N], op=ALU.subtract), reads=[c.BPc, c.Bsgt], writes=[c.BEe])
                    Ei, BEi = c.Pc, c.BPc
                else:
                    for k_ in range(nch):
                        S.op("dve", lambda e, k_=k_: e.tensor_scalar(out=c.Ee[:, k_ * 64:(k_ + 1) * 64], in0=c.Pc[:, k_ * 64:(k_ + 1) * 64], scalar1=-1.0,
                                                                     scalar2=c.Pc[:, k_ * 64 + 63:k_ * 64 + 64], op0=ALU.mult, op1=ALU.add),
                             reads=[c.BPc], writes=[c.BEe])
                    S.op("pool", lambda e: e.tensor_tensor(out=c.Ei[:, 0:N], in0=c.Ee[:, 0:N], in1=c.sgt[:, 0:N], op=ALU.add), reads=[c.BEe, c.Bsgt], writes=[c.BEi])
                    Ei, BEi = c.Ei, c.BEi
                S.op("act", lambda e: e.activation(out=c.g[:, 0:N], in_=Ei[:, 0:N], func=AF.Exp, scale=-C0), reads=[BEi], writes=[c.Bg])
                S.op("act", lambda e: e.activation(out=c.gp[:, 0:N], in_=c.Ee[:, 0:N], func=AF.Exp, scale=-C0), reads=[c.BEe], writes=[c.Bgp])
                S.op("act", lambda e: e.activation(out=c.gi[:, 0:N], in_=Ei[:, 0:N], func=AF.Exp, scale=C0), reads=[BEi], writes=[c.Bgi])
                S.op("act", lambda e: e.activation(out=c.gtot[:, 0:nch], in_=tot, func=AF.Exp, scale=-C0), reads=[c.BPc], writes=[c.Bgtot])
                for hh in range(2):
                    ps_ = slice(64 * hh, 64 * hh + 64)
                    o1 = slice(64 * hh, 64 * hh + 64)
                    o2 = slice(128 + 64 * hh, 128 + 64 * hh + 64)
                    S.op("dve", lambda e, ps_=ps_, o2=o2: e.tensor_tensor(out=c.AR[ps_, 0:nch, o2], in0=c3(c.cv[ps_, 0, 0:N]), in1=c3(c.g[ps_, 0:N]), op=ALU.mult),
                         reads=[c.Bcv, c.Bg], writes=[c.BAR])
                    S.op("dve", lambda e, ps_=ps_, o1=o1: e.scalar_tensor_tensor(out=c.AR[ps_, 0:nch, o1], in0=c3(c.kk[ps_, 0:N]), scalar=-1.0, in1=c3(c.gp[ps_, 0:N]),
                                                                                 op0=ALU.mult, op1=ALU.mult), reads=[c.Bkk, c.Bgp], writes=[c.BAR])
                    S.op("pool", lambda e, ps_=ps_, o1=o1: e.tensor_tensor(out=c.BE[ps_, 0:nch, o1], in0=c3(c.akk[ps_, 0:N]), in1=c3(c.gi[ps_, 0:N]), op=ALU.mult),
                         reads=[c.Bakk, c.Bgi], writes=[c.BBE])
                    S.op("pool", lambda e, ps_=ps_, o1=o1: e.tensor_tensor(out=c.KT[ps_, 0:nch, o1], in0=c3(c.kmod[ps_, 0:N]), in1=c3(c.gi[ps_, 0:N]), op=ALU.mult),
                         reads=[c.Bkmod, c.Bgi], writes=[c.BKT])
                    S.op("pool", lambda e, ps_=ps_, o1=o1: e.tensor_copy(out=c.VB[ps_, 0:nch, o1], in_=c3(c.cv[ps_, 2, 0:N])), reads=[c.Bcv], writes=[c.BVB])

            def pre(c, k):
                st = c.set
                pna, Bpna = pNA[st]
                pb_, Bpb_ = pB[st]
                mN = msk[:, 0:2, :] if c.d == 0 else msk[:, 2:4, :]
                mA = msk[:, 2, :] if c.d == 0 else msk[:, 0, :]

                def f1(e):
                    e.matmul(pna[:, 0:256], c.BE[:, k, :], c.AR[:, k, :], start=True, stop=True)
                    return e.matmul(pna[:, 256:512], c.KT[:, k, :], c.AR[:, k, :], start=True, stop=True)
                S.op("pe", f1, reads=[c.BBE, c.BKT, c.BAR], writes=[Bpna])
                S.op("dve", lambda e: e.tensor_tensor(out=c.NA[:, :], in0=pna[:, 0:256], in1=mN.rearrange("p a b -> p (a b)"), op=ALU.mult), reads=[Bpna, Bmsk], writes=[c.BNA])
                S.op("dve", lambda e: e.tensor_tensor(out=c.KA[:, :], in0=pna[:, 256:512], in1=mN.rearrange("p a b -> p (a b)"), op=ALU.mult), reads=[Bpna, Bmsk], writes=[c.BKA])
                S.op("pe", lambda e: e.matmul(pb_[:, 0:128], c.AR[:, k, 0:128], c.BE[:, k, :], start=True, stop=True), reads=[c.BAR, c.BBE], writes=[Bpb_])
                S.op("dve", lambda e: e.tensor_tensor(out=c.A0[:, :], in0=pb_[:, 0:128], in1=mA, op=ALU.mult), reads=[Bpb_, Bmsk], writes=[c.BA0])
                tr = pTRt[:, st * 384:st * 384 + 384]

                def f2(e):
                    e.transpose(tr[:, 0:128], c.BE[:, k, :], identb[:])
                    e.transpose(tr[:, 128:256], c.KT[:, k, :], identb[:])
                    return e.transpose(tr[:, 256:384], c.VB[:, k, :], identb[:])
                S.op("pe", f2, reads=[c.BBE, c.BKT, c.BVB, Bidentb], writes=[BpTR[st]])
                S.op("act", lambda e: e.activation(out=c.TR[:, :], in_=tr, func=AF.Copy), reads=[BpTR[st]], writes=[c.BTR])
                S.op("pool", lambda e: e.tensor_tensor(out=c.Tm0[:, :], in0=c.NA[:, 0:128], in1=identb[:, :], op=ALU.add), reads=[c.BNA, Bidentb], writes=[c.BTm0])
                Nk, BNk, Ak, BAk = c.NA[:, 0:128], c.BNA, c.A0[:, :], c.BA0
                Tc, BTc = c.Tm0, c.BTm0
                for lvl in range(5):
                    pw, Bpw = (c.PW0, c.BPW0) if lvl % 2 == 0 else (c.PW1, c.BPW1)
                    if lvl < 4:
                        def f3(e, Nk=Nk, Ak=Ak):
                            e.matmul(pb_[:, 128:256], Ak, Nk, start=True, stop=True)
                            return e.matmul(pb_[:, 256:384], Nk, Ak, start=True, stop=True)
                        S.op("pe", f3, reads=[BNk, BAk], writes=[Bpb_])
                        S.op("act", lambda e, pw=pw: e.activation(out=pw[:, :], in_=pb_[:, 128:384], func=AF.Copy), reads=[Bpb_], writes=[Bpw])
                    else:
                        S.op("pe", lambda e, Nk=Nk, Ak=Ak: e.matmul(pb_[:, 256:384], Nk, Ak, start=True, stop=True), reads=[BNk, BAk], writes=[Bpb_])
                        S.op("act", lambda e, pw=pw: e.activation(out=pw[:, 128:256], in_=pb_[:, 256:384], func=AF.Copy), reads=[Bpb_], writes=[Bpw])
                    Nk, BNk, Ak, BAk = pw[:, 0:128], Bpw, pw[:, 128:256], Bpw
                    Tn, BTn = (c.Tm1, c.BTm1) if lvl % 2 == 0 else (c.Tm0, c.BTm0)
                    S.op("pe", lambda e, Ak=Ak, Tc=Tc: e.matmul(pb_[:, 384:512], Ak, Tc[:, :], start=True, stop=True), reads=[BAk, BTc], writes=[Bpb_])
                    S.op("dve", lambda e, Tc=Tc, Tn=Tn: e.tensor_tensor(out=Tn[:, :], in0=pb_[:, 384:512], in1=Tc[:, :], op=ALU.add), reads=[Bpb_, BTc], writes=[BTn])
                    Tc, BTc = Tn, BTn
                c.Tfin, c.BTfin = Tc, BTc

            def stage1(c, k):
                pc_, Bpc_ = pC[c.set]

                def f(e):
                    e.matmul(pc_[:, 0:128], c.KA[:, 0:128], c.TR[:, 256:384], start=True, stop=False)
                    return e.matmul(pc_[:, 0:128], c.AR[:, k, 0:128], c.Hb[:, :], start=False, stop=True)
                S.op("pe", f, reads=[c.BKA, c.BTR, c.BAR, c.BHb], writes=[Bpc_])
                S.op("dve", lambda e: e.tensor_copy(out=c.Xb[:, :], in_=pc_[:, 0:128]), reads=[Bpc_], writes=[c.BXb])

            def stage2(c, k):
                pc_, Bpc_ = pC[c.set]
                S.op("pe", lambda e: e.matmul(pc_[:, 128:256], c.Tfin[:, :], c.Xb[:, :], start=True, stop=True), reads=[c.BTfin, c.BXb], writes=[Bpc_])
                S.op("act", lambda e: e.activation(out=c.Ub[:, :], in_=pc_[:, 128:256], func=AF.Copy), reads=[Bpc_], writes=[c.BUb])

            def stage3(c, k, want_y):
                pc_, Bpc_ = pC[c.set]

                def f(e):
                    e.matmul(pc_[:, 256:384], c.TR[:, 128:256], c.TR[:, 256:384], start=True, stop=False)
                    ins = e.matmul(pc_[:, 256:384], c.TR[:, 0:128], c.Ub[:, :], start=False, stop=True)
                    if want_y:
                        e.matmul(pc_[:, 384:512], c.Hb[:, :], c.AR[:, k, 128:256], start=True, stop=False)
                        e.matmul(pc_[:, 384:512], c.Ub[:, :], c.NA[:, 128:256], start=False, stop=False)
                        ins = e.matmul(pc_[:, 384:512], c.TR[:, 256:384], c.KA[:, 128:256], start=False, stop=True)
                    return ins
                S.op("pe", f, reads=[c.BTR, c.BUb, c.BHb, c.BAR, c.BNA, c.BKA], writes=[Bpc_])
                S.op("dve", lambda e: e.tensor_tensor(out=c.S1[:, :], in0=pc_[:, 256:384], in1=c.H[:, :], op=ALU.add), reads=[Bpc_, c.BH], writes=[c.BS1])
                S.op("dve", lambda e: e.tensor_scalar(out=c.H[:, :], in0=c.S1[:, :], scalar1=c.gtot[:, k:k + 1], scalar2=None, op0=ALU.mult),
                     reads=[c.BS1, c.Bgtot], writes=[c.BH])
                S.op("act", lambda e: e.activation(out=c.Hb[:, :], in_=c.S1[:, :], func=AF.Copy, scale=c.gtot[:, k:k + 1]), reads=[c.BS1, c.Bgtot], writes=[c.BHb])
                if want_y:
                    for hh in range(2):
                        ps_ = slice(64 * hh, 64 * hh + 64)
                        S.op("act", lambda e, ps_=ps_, hh=hh: e.activation(out=c.Yg[ps_, k * 64:(k + 1) * 64], in_=pc_[ps_, 384 + 64 * hh:384 + 64 * hh + 64], func=AF.Copy),
                             reads=[Bpc_], writes=[c.BYg])

            NSTEP = int(os.environ.get("MK_RSTEPS", 1 + NXG))
            for step in range(NSTEP):
                isx = step > 0
                for c in cps:
                    if not isx:
                        c.n0, c.xg = 0, None
                        prep(c, 0, GR, 0, 256)
                    else:
                        xg = (step - 1) if c.d == 0 else (NXG - step)
                        c.xg = xg
                        c.n0 = 256 + xg * GR
                        prep(c, c.n0, GR, 256, NT)
                for ci in range(NCH):
                    for c in cps:
                        c.k = ci if c.d == 0 else NCH - 1 - ci
                        pre(c, c.k)
                    for c in cps:
                        stage1(c, c.k)
                    for c in cps:
                        stage2(c, c.k)
                    for c in cps:
                        stage3(c, c.k, isx)
                if isx:
                    for c in cps:
                        xo = c.xg * GR
                        S.dma("sp", YD_v[c.d][:, c.cc, xo:xo + GR], c.Yg[:, :], reads=[c.BYg], semof=c.BYg)
                        S.dma("sp", BD_v[c.d][:, c.cc, xo:xo + GR], c.bon[:, :], reads=[c.Bbon], semof=c.Bbon)
            S.barrier()
            S.emit()
            S.end_phase()
        if upto == "R":
            return nc

        NF = 512
        BOX = Buf("OX")
        with ExitStack() as pfs:
            P = Ctx(nc, pfs); P.n = 400
            ld = [[P.sb([128, NF], F32, "fld") for _ in range(5)] for _ in range(2)]
            yy, Byy = P.sb([128, NF], F32, "yy")
            bs, Bbs = P.sb([128, NF], F32, "bs")
            ysq, Bysq = P.sb([128, NF], F32, "ysq")
            mm_, Bmm_ = P.sb([128, NF], F32, "mm")
            msq, Bmsq = P.sb([128, NF], F32, "msq")
            var, Bvar = P.sb([128, NF], F32, "var")
            yc, Byc = P.sb([128, NF], F32, "yc")
            ob = [P.sb([128, NF], BF16, "ob") for _ in range(2)]
            ps1 = [P.ps([128, 512], F32, "ps1") for _ in range(2)]
            ps2 = [P.ps([128, 512], F32, "ps2") for _ in range(2)]
            it = 0
            for cc in range(2):
                for ti in range(NX // NF):
                    xo = ti * NF
                    sl = it % 2
                    (y0, By0), (y1, By1), (b0, Bb0), (b1, Bb1), (gzt, Bgzt) = ld[sl]
                    S.dma("sp", y0[:], YD_v[0][:, cc, xo:xo + NF], writes=[By0], semof=By0)
                    S.dma("sp", y1[:], YD_v[1][:, cc, xo:xo + NF], writes=[By1], semof=By1)
                    S.dma("sp", b0[:], BD_v[0][:, cc, xo:xo + NF], writes=[Bb0], semof=Bb0)
                    S.dma("sp", b1[:], BD_v[1][:, cc, xo:xo + NF], writes=[Bb1], semof=Bb1)
                    S.dma("sp", gzt[:], Gzr_v[:, cc, xo:xo + NF], writes=[Bgzt], semof=Bgzt)
                    p1, Bp1 = ps1[sl]
                    p2, Bp2 = ps2[sl]
                    o_, Bo_ = ob[sl]
                    S.op("pool", lambda e, y0=y0, y1=y1: e.tensor_tensor(out=yy[:], in0=y0[:], in1=y1[:], op=ALU.add), reads=[By0, By1], writes=[Byy])
                    S.op("pool", lambda e, b0=b0, b1=b1: e.tensor_tensor(out=bs[:], in0=b0[:], in1=b1[:], op=ALU.add), reads=[Bb0, Bb1], writes=[Bbs])
                    S.op("pe", lambda e, p1=p1: e.matmul(p1[:, :], bones[:, :], yy[:], start=True, stop=True), reads=[Byy, Bbones], writes=[Bp1])
                    S.op("act", lambda e: e.activation(out=ysq[:], in_=yy[:], func=AF.Square), reads=[Byy], writes=[Bysq])
                    S.op("pe", lambda e, p2=p2: e.matmul(p2[:, :], bones[:, :], ysq[:], start=True, stop=True), reads=[Bysq, Bbones], writes=[Bp2])
                    S.op("dve", lambda e, p1=p1: e.tensor_scalar(out=mm_[:], in0=p1[:, :], scalar1=1.0 / 64, scalar2=None, op0=ALU.mult), reads=[Bp1], writes=[Bmm_])
                    S.op("pool", lambda e: e.tensor_tensor(out=msq[:], in0=mm_[:], in1=mm_[:], op=ALU.mult), reads=[Bmm_], writes=[Bmsq])
                    S.op("dve", lambda e, p2=p2: e.scalar_tensor_tensor(out=var[:], in0=p2[:, :], scalar=1.0 / 64, in1=msq[:], op0=ALU.mult, op1=ALU.subtract),
                         reads=[Bp2, Bmsq], writes=[Bvar])
                    S.op("act", lambda e: e.activation(out=var[:], in_=var[:], func=AF.Sqrt, bias=epsc[:, 2:3], scale=1.0), reads=[Bvar, Bepsc], writes=[Bvar])
                    S.op("dve", lambda e: e.reciprocal(out=var[:], in_=var[:]), reads=[Bvar], writes=[Bvar])
                    S.op("pool", lambda e: e.tensor_tensor(out=yc[:], in0=yy[:], in1=mm_[:], op=ALU.subtract), reads=[Byy, Bmm_], writes=[Byc])
                    S.op("pool", lambda e: e.tensor_tensor(out=yc[:], in0=yc[:], in1=var[:], op=ALU.mult), reads=[Byc, Bvar], writes=[Byc])
                    S.op("dve", lambda e, cc=cc: e.tensor_scalar(out=yc[:], in0=yc[:], scalar1=chanv[:, cc, 6:7], scalar2=chanv[:, cc, 7:8], op0=ALU.mult, op1=ALU.add),
                         reads=[Byc, Bchanv], writes=[Byc])
                    S.op("pool", lambda e: e.tensor_tensor(out=yc[:], in0=yc[:], in1=bs[:], op=ALU.add), reads=[Byc, Bbs], writes=[Byc])
                    S.op("dve", lambda e, o_=o_, gzt=gzt: e.tensor_tensor(out=o_[:], in0=yc[:], in1=gzt[:], op=ALU.mult), reads=[Byc, Bgzt], writes=[Bo_])
                    S.dma("sp", OXs[2 * cc][:, xo:xo + NF], o_[0:64, :], reads=[Bo_], semof=Bo_)
                    S.dma("sp", OXs[2 * cc + 1][:, xo:xo + NF], o_[64:128, :], reads=[Bo_], semof=Bo_)
                    it += 1
            S.barrier()
            S.emit()
            S.end_phase()
        if upto == "F":
            return nc

        QG = 512
        NKT = NT // 128
        with ExitStack() as pms:
            P = Ctx(nc, pms); P.n = 500
            Kn, BKn = P.sb([128, NT], BF16, "Kn")
            Kr, BKr = P.sb([64, NT], BF16, "Kr")
            Vt, BVt = P.sb([128, NKT, 128], BF16, "Vt")
            onesb, Bonesb = P.sb([128, 128], BF16, "onesb")
            Qn = [P.sb([128, QG], BF16, "Qn") for _ in range(2)]
            Qr = [P.sb([64, QG], BF16, "Qr") for _ in range(2)]
            gmt = [P.sb([128, QG], F32, "gmt") for _ in range(2)]
            Pt = [P.sb([128, QG], BF16, "Pt") for _ in range(3)]
            rl, Brl = P.sb([128, QG], F32, "rl")
            oo, Boo = P.sb([128, QG], F32, "oo")
            om = [P.sb([128, QG], BF16, "om") for _ in range(2)]
            pS = [P.ps([128, 512], F32, "pS") for _ in range(3)]
            pO = [P.ps([128, 512], F32, "pO") for _ in range(2)]
            pL = [P.ps([128, 512], F32, "pL") for _ in range(2)]
            S.op("pool", lambda e: e.memset(onesb[:], 1.0), writes=[Bonesb])
            S.dma("sp", Kr[:], KR, writes=[BKr], semof=BKr)
            NQG = int(os.environ.get("MK_NQG", NX // QG))
            def attn_group(h, qg, sl):
                qo = qg * QG
                qn_, Bqn_ = Qn[sl]
                qr_, Bqr_ = Qr[sl]
                gm_, Bgm_ = gmt[sl]
                po_, Bpo_ = pO[sl]
                pl_, Bpl_ = pL[sl]
                o_, Bo_ = om[sl]
                S.dma("sp", qn_[:], QN_v[:, h, qo:qo + QG], writes=[Bqn_], semof=Bqn_)
                S.dma("sp", qr_[:], QR_v[:, h, qo:qo + QG], writes=[Bqr_], semof=Bqr_)
                S.dma("sp", gm_[:], Gzm_v[:, h, qo:qo + QG], writes=[Bgm_], semof=Bgm_)

                def qk(kt):
                    ps_, Bps_ = pS[kt % 3]

                    def f(e):
                        e.matmul(ps_[:, :], Kn[:, kt * 128:(kt + 1) * 128], qn_[:, :], start=True, stop=False)
                        return e.matmul(ps_[:, :], Kr[0:64, kt * 128:(kt + 1) * 128], qr_[0:64, :], start=False, stop=True)
                    S.op("pe", f, reads=[BKn, BKr, Bqn_, Bqr_], writes=[Bps_])

                def ex_pv(kt):
                    ps_, Bps_ = pS[kt % 3]
                    pt_, Bpt_ = Pt[kt % 3]
                    S.op("act", lambda e: e.activation(out=pt_[:, :], in_=ps_[:, :], func=AF.Exp, scale=KS[:, kt * 2 + h:kt * 2 + h + 1]),
                         reads=[Bps_, BKS], writes=[Bpt_])

                    def pv_(e):
                        e.matmul(po_[:, :], Vt[:, kt, :], pt_[:, :], start=(kt == 0), stop=(kt == NKT - 1))
                        return e.matmul(pl_[:, :], onesb[:, :], pt_[:, :], start=(kt == 0), stop=(kt == NKT - 1))
                    S.op("pe", pv_, reads=[BVt, Bpt_, Bonesb], writes=[Bpo_, Bpl_])

                qk(0)
                for kt in range(NKT):
                    if kt + 1 < NKT:
                        qk(kt + 1)
                    ex_pv(kt)
                S.op("dve", lambda e: e.reciprocal(out=rl[:, :], in_=pl_[:, :]), reads=[Bpl_], writes=[Brl])
                S.op("dve", lambda e: e.tensor_tensor(out=oo[:, :], in0=po_[:, :], in1=rl[:, :], op=ALU.mult), reads=[Bpo_, Brl], writes=[Boo])
                S.op("pool", lambda e: e.tensor_tensor(out=o_[:, :], in0=oo[:, :], in1=gm_[:, :], op=ALU.mult), reads=[Boo, Bgm_], writes=[Bo_])
                S.dma("sp", OXs[4 + 2 * h][:, qo:qo + QG], o_[0:64, :], reads=[Bo_], semof=Bo_)
                S.dma("sp", OXs[5 + 2 * h][:, qo:qo + QG], o_[64:128, :], reads=[Bo_], semof=Bo_)

            gcount = 0
            for h in range(2):
                S.dma("sp", Kn[:], KN_v[:, h, :], writes=[BKn], semof=BKn)
                S.dma("sp", Vt[:], VT[h].rearrange("p (t d) -> p t d", d=128), writes=[BVt], semof=BVt)
                for qg in range(NQG):
                    attn_group(h, qg, gcount % 2)
                    gcount += 1
            S.barrier()
            S.emit()
            S.end_phase()
        if upto == "M":
            return nc

        BOG = Buf("OG")
        for j in range(8):
            S.collective("AllGather", [[0, 1, 2, 3], [4, 5, 6, 7]], OXs[j], OGs[j], reads=[BOX], writes=[BOG], semof=BOG)
        S.barrier()
        S.emit()
        S.end_phase()

        CG = 512
        with ExitStack() as pcs:
            P = Ctx(nc, pcs); P.n = 600
            selq, Bselq = P.sb([128, 4], F32, "selq")
            gain_bc, Bgain = P.sb([128, D], F32, "gain_c")
            shift_bc, Bshift = P.sb([128, D], F32, "shift_c")
            gate_bc, Bgate = P.sb([128, D], F32, "gate_c")
            xt = [P.sb([128, D], F32, "xtc") for _ in range(2)]
            hm = [P.sb([128, D], BF16, "hmc") for _ in range(2)]
            ss = [P.sb([128, 4], F32, "ssc") for _ in range(2)]
            hT, BhT = P.sb([128, 16, CG], BF16, "hTc")
            ldq = [P.sb([128, 16, CG], BF16, "ldq") for _ in range(2)]
            osel, Bosel = P.sb([128, 16, CG], BF16, "osel")
            wmg = [P.sb([128, 16, 256], BF16, "wmg") for _ in range(2)]
            wbr = [P.sb([128, 8, 256], BF16, "wbr") for _ in range(2)]
            sgr, Bsgr = P.sb([128, CG], F32, "sgr")
            sgm, Bsgm = P.sb([128, CG], F32, "sgm")
            tr_, Btr_ = P.sb([128, CG], F32, "tr")
            tm_, Btm_ = P.sb([128, CG], F32, "tm")
            merged, Bmerged = P.sb([128, 16, CG], BF16, "merged")
            wout = [P.sb([128, 16, 512], BF16, "wout") for _ in range(2)]
            xr = [P.sb([128, 512], F32, "xr") for _ in range(2)]
            res = [P.sb([128, 512], F32, "res") for _ in range(2)]
            pT = [P.ps([128, 1024], BF16, "pTc") for _ in range(2)]
            pg = [P.ps([128, 512], F32, "pg") for _ in range(4)]
            pout = [P.ps([128, 512], F32, "pout") for _ in range(2)]
            S.dma("sp", selq[:], I["selq"], writes=[Bselq], semof=Bselq)
            S.dma("sp", gain_bc[:], BC[0], writes=[Bgain], semof=Bgain)
            S.dma("sp", shift_bc[:], BC[1], writes=[Bshift], semof=Bshift)
            S.dma("sp", gate_bc[:], BC[2], writes=[Bgate], semof=Bgate)
            wmg_v = I["w_in_mg"].rearrange("(kc p) n -> p kc n", p=128)
            wbr_r_v = I["w_br_r"].rearrange("(kc p) n -> p kc n", p=128)
            wbr_m_v = I["w_br_m"].rearrange("(kc p) n -> p kc n", p=128)
            wout_v = I["w_out"].rearrange("(kc p) n -> p kc n", p=128)
            cnt = {"tile": 0, "w": 0, "o": 0, "r": 0}
            NCG = int(os.environ.get("MK_NCG", 2048 // CG))
            for gj in range(NCG):
                go = gj * CG
                for t in range(CG // 128):
                    s = cnt["tile"] % 2
                    cnt["tile"] += 1
                    x_, Bx_ = xt[s]
                    h_, Bh_ = hm[s]
                    s_, Bs_ = ss[s]
                    S.dma("sp", x_[:], I["xm"][go + t * 128:go + (t + 1) * 128, :], writes=[Bx_], semof=Bx_)
                    S.op("act", lambda e, x_=x_, h_=h_, s_=s_: e.activation(out=h_[:], in_=x_[:], func=AF.Square, accum_out=s_[:, 0:1]), reads=[Bx_], writes=[Bh_, Bs_])
                    S.op("act", lambda e, s_=s_: e.activation(out=s_[:, 1:2], in_=s_[:, 0:1], func=AF.Sqrt, scale=1.0 / D, bias=epsc[:, 0:1]), reads=[Bs_, Bepsc], writes=[Bs_])
                    S.op("dve", lambda e, s_=s_: e.reciprocal(out=s_[:, 2:3], in_=s_[:, 1:2]), reads=[Bs_], writes=[Bs_])
                    S.op("dve", lambda e, x_=x_, s_=s_: e.scalar_tensor_tensor(out=x_[:], in0=x_[:], scalar=s_[:, 2:3], in1=gain_bc[:], op0=ALU.mult, op1=ALU.mult),
                         reads=[Bx_, Bs_, Bgain], writes=[Bx_])
                    S.op("pool", lambda e, x_=x_, h_=h_: e.tensor_tensor(out=h_[:], in0=x_[:], in1=shift_bc[:], op=ALU.add), reads=[Bx_, Bshift], writes=[Bh_])
                    for half in range(2):
                        p_, Bp_ = pT[half]

                        def trf(e, half=half, p_=p_, h_=h_):
                            for j in range(8):
                                kc = half * 8 + j
                                ins = e.transpose(p_[:, j * 128:(j + 1) * 128], h_[:, kc * 128:(kc + 1) * 128], identb[:])
                            return ins
                        S.op("pe", trf, reads=[Bh_, Bidentb], writes=[Bp_])
                        S.op("act" if half == 0 else "dve",
                             (lambda e, half=half, p_=p_, t=t: e.activation(out=hT[:, half * 8:(half + 1) * 8, t * 128:(t + 1) * 128],
                                                                            in_=p_[:, :].rearrange("p (j n) -> p j n", n=128), func=AF.Copy)) if half == 0 else
                             (lambda e, half=half, p_=p_, t=t: e.tensor_copy(out=hT[:, half * 8:(half + 1) * 8, t * 128:(t + 1) * 128],
                                                                             in_=p_[:, :].rearrange("p (j n) -> p j n", n=128))),
                             reads=[Bp_], writes=[BhT])
                for q in range(4):
                    l_, Bl_ = ldq[q % 2]
                    for j in range(8):
                        S.dma("sp", l_[(j % 2) * 64:(j % 2) * 64 + 64, :, :].rearrange("p (r c) n -> p r c n", c=4)[:, :, j // 2, :],
                              OGs[j].rearrange("(r p) n -> p r n", p=64)[:, :, q * 2048 + go:q * 2048 + go + CG],
                              reads=[BOG], writes=[Bl_], semof=Bl_)
                    if q == 0:
                        S.op("dve", lambda e, l_=l_: e.tensor_scalar(out=osel[:], in0=l_[:], scalar1=selq[:, 0:1], scalar2=None, op0=ALU.mult),
                             reads=[Bl_, Bselq], writes=[Bosel])
                    else:
                        S.op("dve", lambda e, l_=l_, q=q: e.scalar_tensor_tensor(out=osel[:], in0=l_[:], scalar=selq[:, q:q + 1], in1=osel[:], op0=ALU.mult, op1=ALU.add),
                             reads=[Bl_, Bselq, Bosel], writes=[Bosel])
                for m in range(16):
                    w_, Bw_ = wmg[cnt["w"] % 2]
                    b_, Bb_ = wbr[cnt["w"] % 2]
                    cnt["w"] += 1
                    S.dma("pool", w_[:, :, 0:128], wmg_v[:, :, m * 128:(m + 1) * 128], writes=[Bw_], semof=Bw_)
                    S.dma("pool", w_[:, :, 128:256], wmg_v[:, :, 2048 + m * 128:2048 + (m + 1) * 128], writes=[Bw_], semof=Bw_)
                    S.dma("pool", b_[:, :, 0:128], wbr_r_v[:, :, m * 128:(m + 1) * 128], writes=[Bb_], semof=Bb_)
                    S.dma("pool", b_[:, :, 128:256], wbr_m_v[:, :, m * 128:(m + 1) * 128], writes=[Bb_], semof=Bb_)
                    (pgr, Bpgr), (pgm, Bpgm), (ppr, Bppr), (ppm, Bppm) = pg

                    def fg(e, w_=w_):
                        for kc in range(16):
                            e.matmul(pgr[:, 0:CG], w_[:, kc, 0:128], hT[:, kc, :], start=(kc == 0), stop=(kc == 15))
                        for kc in range(16):
                            ins = e.matmul(pgm[:, 0:CG], w_[:, kc, 128:256], hT[:, kc, :], start=(kc == 0), stop=(kc == 15))
                        return ins
                    S.op("pe", fg, reads=[Bw_, BhT], writes=[Bpgr, Bpgm])
                    S.op("act", lambda e: e.activation(out=sgr[:, :], in_=pgr[:, 0:CG], func=AF.Sigmoid), reads=[Bpgr], writes=[Bsgr])
                    S.op("act", lambda e: e.activation(out=sgm[:, :], in_=pgm[:, 0:CG], func=AF.Sigmoid), reads=[Bpgm], writes=[Bsgm])

                    def fb(e, b_=b_):
                        for j in range(8):
                            kc = (j // 2) * 4 + (j % 2)
                            e.matmul(ppr[:, 0:CG], b_[:, j, 0:128], osel[:, kc, :], start=(j == 0), stop=(j == 7))
                        for j in range(8):
                            kc = (j // 2) * 4 + 2 + (j % 2)
                            ins = e.matmul(ppm[:, 0:CG], b_[:, j, 128:256], osel[:, kc, :], start=(j == 0), stop=(j == 7))
                        return ins
                    S.op("pe", fb, reads=[Bb_, Bosel], writes=[Bppr, Bppm])
                    S.op("dve", lambda e: e.tensor_tensor(out=tr_[:, :], in0=ppr[:, 0:CG], in1=sgr[:, :], op=ALU.mult), reads=[Bppr, Bsgr], writes=[Btr_])
                    S.op("dve", lambda e: e.tensor_tensor(out=tm_[:, :], in0=ppm[:, 0:CG], in1=sgm[:, :], op=ALU.mult), reads=[Bppm, Bsgm], writes=[Btm_])
                    S.op("pool", lambda e, m=m: e.tensor_tensor(out=merged[:, m, :], in0=tr_[:, :], in1=tm_[:, :], op=ALU.add), reads=[Btr_, Btm_], writes=[Bmerged])
                for nb in range(4):
                    wo_, Bwo_ = wout[cnt["o"] % 2]
                    cnt["o"] += 1
                    S.dma("pool", wo_[:], wout_v[:, :, nb * 512:(nb + 1) * 512], writes=[Bwo_], semof=Bwo_)
                    for t in range(CG // 128):
                        r_ = cnt["r"] % 2
                        cnt["r"] += 1
                        po_, Bpo_ = pout[r_]
                        xr_, Bxr_ = xr[r_]
                        rs_, Brs_ = res[r_]
                        S.dma("sp", xr_[:], I["xm"][go + t * 128:go + (t + 1) * 128, nb * 512:(nb + 1) * 512], writes=[Bxr_], semof=Bxr_)

                        def fo(e, wo_=wo_, po_=po_, t=t):
                            for kc in range(16):
                                ins = e.matmul(po_[:, :], merged[:, kc, t * 128:(t + 1) * 128], wo_[:, kc, :], start=(kc == 0), stop=(kc == 15))
                            return ins
                        S.op("pe", fo, reads=[Bwo_, Bmerged], writes=[Bpo_])
                        S.op("dve", lambda e, po_=po_, rs_=rs_, nb=nb: e.tensor_tensor(out=rs_[:], in0=po_[:, :], in1=gate_bc[:, nb * 512:(nb + 1) * 512], op=ALU.mult),
                             reads=[Bpo_, Bgate], writes=[Brs_])
                        S.op("pool", lambda e, rs_=rs_, xr_=xr_: e.tensor_tensor(out=rs_[:], in0=rs_[:], in1=xr_[:], op=ALU.add), reads=[Brs_, Bxr_], writes=[Brs_])
                        S.dma("sp", out[go + t * 128:go + (t + 1) * 128, nb * 512:(nb + 1) * 512], rs_[:], reads=[Brs_], semof=Brs_)
            S.barrier()
            S.emit()
            S.end_phase()
    return nc


_NC_CACHE = {}


def kernel(**inputs):
    maps = _host_inputs(inputs)
    if "nc" not in _NC_CACHE:
        _NC_CACHE["nc"] = build()
    nc = _NC_CACHE["nc"]
    res = run_bass_kernel_spmd(nc, maps, core_ids=list(range(8)))
    outp = np.zeros((2, NX, D), np.float32)
    for c in range(8):
        b, g = c // 4, c % 4
        outp[b, 2048 * g:2048 * g + 2048] = res.results[c]["out"]
    return outp
```

```python
import os
from contextlib import ExitStack
import numpy as np
import ml_dtypes
import concourse.bass as bass
import concourse.mybir as mybir
from concourse.bass_utils import run_bass_kernel_spmd

F32 = mybir.dt.float32
BF16 = mybir.dt.bfloat16
ALU = mybir.AluOpType
AF = mybir.ActivationFunctionType

NT = 8448
NX = 8192
NCTX = 256
D = 2048
WC = 2368
GS = 256
C0 = float(np.exp(-0.5))
EPS = 1e-6
GN_EPS = 64e-5


class Tok:
    __slots__ = ("sem", "val", "key")

    def __init__(self, sem, val, key):
        self.sem = sem
        self.val = val
        self.key = key


class DSem:
    def __init__(self, sem, kind):
        self.sem = sem
        self.cnt = 0
        self.kind = kind


class Buf:
    def __init__(self, name, const=False):
        self.name = name
        self.w = None
        self.r = []
        self.const = const
        self.dsem = None
        self.dcnt = 0


class Sched:
    ENG = ["pe", "act", "dve", "pool", "sp"]

    def __init__(self, nc, stack):
        self.nc = nc
        self.stack = stack
        self.plan = {e: [] for e in self.ENG}
        self.ecnt = {e: 0 for e in self.ENG}
        self.esem = {}
        self.waited = {e: {} for e in self.ENG}
        self.nsem = 0
        for e in ("pe", "act", "dve", "pool"):
            self.esem[e] = self._newsem("e_" + e)
        self.dbufs = []
        self.free_dsems = {}
        self.ninst = 0

    def _newsem(self, name):
        self.nsem += 1
        return self.stack.enter_context(self.nc.semaphore(f"{name}_{self.nsem}"))

    def _waits(self, eng, toks):
        for t in toks:
            if t is None:
                continue
            if self.waited[eng].get(t.key, 0) >= t.val:
                continue
            self.waited[eng][t.key] = t.val
            self.plan[eng].append(lambda e, sem=t.sem, v=t.val: e.wait_ge(sem, v))

    def _deps(self, reads, writes):
        deps = []
        for b in reads:
            deps.append(b.w)
        for b in writes:
            deps.append(b.w)
            deps.extend(b.r)
        return deps

    def _mark(self, tok, reads, writes):
        for b in reads:
            if not b.const:
                b.r.append(tok)
        for b in writes:
            b.w = tok
            b.r = []

    def op(self, eng, fn, reads=(), writes=()):
        self._waits(eng, self._deps(reads, writes))
        self.ecnt[eng] += 1
        self.ninst += 1
        tok = Tok(self.esem[eng], self.ecnt[eng], "e_" + eng)
        self.plan[eng].append(lambda e, fn=fn, sem=tok.sem: fn(e).then_inc(sem, 1))
        self._mark(tok, reads, writes)
        return tok

    def _dsem(self, b, kind):
        if b.dsem is None:
            fl = self.free_dsems.setdefault(kind, [])
            if fl:
                b.dsem = fl.pop()
            else:
                b.dsem = DSem(self._newsem("d" + kind), kind)
            self.dbufs.append(b)
        assert b.dsem.kind == kind, (b.name, b.dsem.kind, kind)

    def end_phase(self):
        for b in self.dbufs:
            self.free_dsems.setdefault(b.dsem.kind, []).append(b.dsem)
            b.dsem = None
            b.w = None
            b.r = []
        self.dbufs = []

    def dma(self, q, out_ap, in_ap, reads=(), writes=(), semof=None, **kw):
        self._waits(q, self._deps(reads, writes))
        b = semof
        self._dsem(b, "sw" if q == "pool" else "hw")
        ds = b.dsem
        ds.cnt += 16
        tok = Tok(ds.sem, ds.cnt, "d%d" % id(ds))
        self.plan[q].append(
            lambda e, o=out_ap, i=in_ap, sem=ds.sem, kw=kw: e.dma_start(out=o, in_=i, **kw).then_inc(sem, 16)
        )
        self._mark(tok, reads, writes)
        return tok

    def collective(self, kind, groups, in_ap, out_ap, reads, writes, semof):
        q = "pool"
        self._waits(q, self._deps(reads, writes))
        b = semof
        self._dsem(b, "cc")
        ds = b.dsem
        ds.cnt += 1
        tok = Tok(ds.sem, ds.cnt, "d%d" % id(ds))
        self.plan[q].append(
            lambda e, sem=ds.sem: e.collective_compute(
                kind, ALU.bypass, replica_groups=groups, ins=[in_ap], outs=[out_ap]
            ).then_inc(sem, 1)
        )
        self._mark(tok, reads, writes)
        return tok

    def barrier(self):
        toks = []
        for e in ("pe", "act", "dve", "pool"):
            if self.ecnt[e] > 0:
                toks.append(Tok(self.esem[e], self.ecnt[e], "e_" + e))
        for b in self.dbufs:
            toks.append(Tok(b.dsem.sem, b.dsem.cnt, "d%d" % id(b.dsem)))
        for e in self.ENG:
            self._waits(e, toks)

    def emit(self):
        plan = self.plan
        with self.nc.Block() as block:

            @block.tensor
            def _(e):
                for f in plan["pe"]:
                    f(e)

            @block.scalar
            def _(e):
                for f in plan["act"]:
                    f(e)

            @block.vector
            def _(e):
                for f in plan["dve"]:
                    f(e)

            @block.gpsimd
            def _(e):
                for f in plan["pool"]:
                    f(e)

            @block.sync
            def _(e):
                for f in plan["sp"]:
                    f(e)

        self.plan = {e: [] for e in self.ENG}


class Ctx:
    def __init__(self, nc, stack):
        self.nc = nc
        self.stack = stack
        self.n = 0

    def sb(self, shape, dt, name=None):
        self.n += 1
        name = (name or "t") + f"_{self.n}"
        t = self.stack.enter_context(self.nc.sbuf_tensor(name, list(shape), dt))
        return t, Buf(name)

    def ps(self, shape, dt, name=None):
        self.n += 1
        name = (name or "p") + f"_{self.n}"
        t = self.stack.enter_context(self.nc.psum_tensor(name, list(shape), dt))
        return t, Buf(name)

    def sub(self):
        c = Ctx(self.nc, ExitStack())
        c.n = self.n + 1000
        return c


def _host_consts():
    idx = np.arange(64)
    us = (idx[:, None] < idx[None, :]).astype(np.float32)
    ui = (idx[:, None] <= idx[None, :]).astype(np.float32)
    ls = (idx[:, None] > idx[None, :]).astype(np.float32)
    li = (idx[:, None] >= idx[None, :]).astype(np.float32)
    masks = np.zeros((128, 4, 128), np.float32)
    for i, m in enumerate((us, ui, ls, li)):
        masks[0:64, i, 0:64] = m
        masks[64:128, i, 64:128] = m
    bones = np.zeros((128, 128), np.float32)
    bones[0:64, 0:64] = 1
    bones[64:128, 64:128] = 1
    sel = np.zeros((2, 2, 128), np.float32)
    sel[0, 0, :] = 1
    sel[1, 1, :] = 1
    reset = np.ones((128, 512), np.float32)
    reset[:, ::64] = 0
    rows = np.repeat(np.arange(128), 64).astype(np.float32)
    cols = np.tile(np.arange(64), 128).astype(np.float32)
    inv = np.power(np.float32(10000.0), -np.arange(0, 32, 2, dtype=np.float32) / np.float32(32)).astype(np.float32)
    ang = np.zeros((64, NX), np.float32)
    for d in range(64):
        pos = rows if d < 32 else cols
        ang[d] = pos * inv[d % 16]
    cos = np.cos(ang.astype(np.float64)).astype(np.float32)
    sin = np.sin(ang.astype(np.float64)).astype(np.float32)
    P = np.zeros((64, 64), np.float32)
    for d in range(64):
        if d % 32 < 16:
            P[d, d + 16] = -1
        else:
            P[d, d - 16] = 1
    return dict(
        ident=np.eye(128, dtype=np.float32), masks=masks, bones=bones, onesf=np.ones((128, 128), np.float32),
        sel=sel, reset=reset, rope_cos=cos, rope_sin=sin, ropePT=np.ascontiguousarray(P.T),
    )


def _host_inputs(inp):
    f = lambda a: np.ascontiguousarray(a, dtype=np.float32)
    consts = _host_consts()
    w_in = inp["w_in"][0]
    offs = np.cumsum([0, 3072, 1024, 128, 128, 512, 256, 64, 1024, 4096])
    o_rkv, o_zr, o_wd, o_ad, o_qd, o_kvd, o_kr, o_zm, o_mg = offs[:9]
    maps = []
    for c in range(8):
        b, g = c // 4, c % 4
        ch = slice(256 * g, 256 * g + 256)
        cols = np.concatenate([
            o_rkv + np.arange(256 * g, 256 * g + 256),
            o_rkv + 1024 + np.arange(256 * g, 256 * g + 256),
            o_rkv + 2048 + np.arange(256 * g, 256 * g + 256),
            o_zr + np.arange(256 * g, 256 * g + 256),
            o_wd + np.arange(128), o_ad + np.arange(128),
            o_qd + np.arange(512), o_kvd + np.arange(256), o_kr + np.arange(64),
            o_zm + np.arange(256 * g, 256 * g + 256),
        ])
        assert cols.size == WC
        conv = inp["conv_rkv"][0]
        conv_fm = np.zeros((128, 6, 3), np.float32)
        for kind in range(3):
            for cc in range(2):
                cidx = kind * 1024 + 256 * g + cc * 128 + np.arange(128)
                conv_fm[:, kind * 2 + cc, :] = conv[:, cidx].T
        chanv = np.zeros((128, 2, 10), np.float32)
        for cc in range(2):
            cidx = 256 * g + cc * 128 + np.arange(128)
            chanv[:, cc, 0] = inp["w0"][0, 0, cidx]
            chanv[:, cc, 1] = inp["w0"][0, 1, cidx]
            chanv[:, cc, 2] = inp["a0"][0, 0, cidx]
            chanv[:, cc, 3] = inp["a0"][0, 1, cidx]
            chanv[:, cc, 4] = inp["k_k"][0, cidx]
            chanv[:, cc, 5] = inp["k_a"][0, cidx]
            chanv[:, cc, 6] = inp["ln_x_w"][0, cidx]
            chanv[:, cc, 7] = inp["ln_x_b"][0, cidx]
            chanv[:, cc, 8] = inp["r_k"][0].reshape(-1)[cidx]
        wlora = np.zeros((128, 2, 256), np.float32)
        for d in range(2):
            wlora[64 * d:64 * d + 64, 0, :] = inp["w_decay_up"][0, d][:, ch]
            wlora[64 * d:64 * d + 64, 1, :] = inp["w_a_up"][0, d][:, ch]
        mlav = np.zeros((128, 10), np.float32)
        mlav[:, 0:4] = inp["q_norm_w"][0].reshape(4, 128).T
        mlav[:, 4:6] = inp["kv_norm_w"][0].reshape(2, 128).T
        mlav[:, 6] = inp["q_gain"][0][:128]
        mlav[:, 7] = inp["k_gain"][0][:128]
        mlav[0:64, 8] = inp["q_gain"][0][128:]
        mlav[0:64, 9] = inp["k_gain"][0][128:]
        hq = [2 * g, 2 * g + 1]
        w_uq_c = np.concatenate([inp["w_uq"][0][:, h * 192:(h + 1) * 192] for h in hq], axis=1)
        w_ukv_c = np.concatenate([inp["w_ukv"][0][:, h * 256:(h + 1) * 256] for h in hq], axis=1)
        cfm = np.stack([inp["c"][b].reshape(16, 128).T, inp["c_ctx"].reshape(16, 128).T], axis=-1)
        selq = np.zeros((128, 4), np.float32)
        selq[:, g] = 1
        m = dict(
            xs=f(np.concatenate([inp["ctx"][b], inp["x"][b]], axis=0)),
            xm=f(inp["x"][b, 2048 * g:2048 * g + 2048]),
            cfm=f(cfm), norm_w_row=f(np.stack([inp["norm_w"][0]] * 2)), b_mod_row=f(np.stack([inp["b_mod"][0]] * 2)),
            w_mod=f(inp["w_mod"][0]), w_in_core=f(w_in[:, cols]), w_in_mg=f(w_in[:, o_mg:o_mg + 4096]),
            conv_fm=f(conv_fm), chanv=f(chanv), wlora=f(wlora), mlav=f(mlav), w_uq_c=f(w_uq_c), w_ukv_c=f(w_ukv_c),
            w_br_r=f(inp["w_branch_rwkv"][0]), w_br_m=f(inp["w_branch_mla"][0]), w_out=f(inp["w_out"][0]),
            selq=f(selq),
        )
        m.update({k: f(v) for k, v in consts.items()})
        maps.append(m)
    return maps


INPUT_SHAPES = dict(
    xs=[NT, D], xm=[2048, D], cfm=[128, 16, 2], norm_w_row=[2, D], b_mod_row=[2, 3 * D], w_mod=[D, 3 * D],
    w_in_core=[D, WC], w_in_mg=[D, 4096], conv_fm=[128, 6, 3], chanv=[128, 2, 10], wlora=[128, 2, 256],
    mlav=[128, 10], w_uq_c=[512, 384], w_ukv_c=[256, 512], w_br_r=[1024, D], w_br_m=[1024, D], w_out=[D, D],
    selq=[128, 4], ident=[128, 128], masks=[128, 4, 128], bones=[128, 128], onesf=[128, 128], sel=[2, 2, 128],
    reset=[128, 512], rope_cos=[64, NX], rope_sin=[64, NX], ropePT=[64, 64],
)


def build(debug=(), upto="C"):
    nc = bass.Bass("TRN2", target_bir_lowering=False)
    I = {k: nc.dram_tensor(k, s, F32, kind="ExternalInput").ap() for k, s in INPUT_SHAPES.items()}
    out = nc.dram_tensor("out", [2048, D], F32, kind="ExternalOutput").ap()

    def scratch(name, shape, dt):
        if name in debug:
            return nc.dram_tensor(name, shape, dt, kind="ExternalOutput").ap()
        return nc.dram_tensor(name, shape, dt).ap()

    BC = scratch("BC", [5, 128, D], F32)
    U_rkv = scratch("U_rkv", [768, NT], F32)
    G_zr = scratch("G_zr", [256, NX], F32)
    SG = scratch("SG", [512, NT], F32)
    AA = scratch("AA", [512, NT], F32)
    QN = scratch("QN", [256, NX], BF16)
    QR = scratch("QR", [128, NX], BF16)
    KN = scratch("KN", [256, NT], BF16)
    KR = scratch("KR", [64, NT], BF16)
    VT = scratch("VT", [2, 128, NT], BF16)
    G_zm = scratch("G_zm", [256, NX], F32)
    KSD = scratch("KSD", [128, 132], F32)
    YD = [scratch("YD0", [256, NX], F32), scratch("YD1", [256, NX], F32)]
    BD = [scratch("BD0", [256, NX], F32), scratch("BD1", [256, NX], F32)]
    DBG = [scratch(f"DBG{i}", [128, 512], BF16 if i < 2 else F32) for i in range(4)]
    OXs = [scratch(f"OX{j}", [64, NX], BF16) for j in range(8)]
    OGs = [scratch(f"OG{j}", [256, NX], BF16) for j in range(8)]

    v3 = lambda ap, p=128: ap.rearrange("(c p) n -> p c n", p=p)
    U_v, Gzr_v, SG_v, AA_v = v3(U_rkv), v3(G_zr), v3(SG), v3(AA)
    QN_v, QR_v, KN_v, Gzm_v = v3(QN), v3(QR, 64), v3(KN), v3(G_zm)

    with ExitStack() as top:
        S = Sched(nc, top)
        T = Ctx(nc, top)

        identb, Bidentb = T.sb([128, 128], BF16, "identb")
        msk, Bmsk = T.sb([128, 4, 128], BF16, "msk")
        bones, Bbones = T.sb([128, 128], F32, "bones")
        onesf, Bonesf = T.sb([128, 128], F32, "onesf")
        selt, Bselt = T.sb([2, 2, 128], F32, "selt")
        resetm, Bresetm = T.sb([128, 512], F32, "resetm")
        ropePT, BropePT = T.sb([64, 64], F32, "ropePT")
        epsc, Bepsc = T.sb([128, 4], F32, "epsc")
        convw, Bconvw = T.sb([128, 6, 3], F32, "convw")
        chanv, Bchanv = T.sb([128, 2, 10], F32, "chanv")
        mlav, Bmlav = T.sb([128, 10], F32, "mlav")
        KS, BKS = T.sb([128, 132], F32, "KS")
        S.dma("pool", identb[:], I["ident"], writes=[Bidentb], semof=Bidentb)
        S.dma("pool", msk[:], I["masks"], writes=[Bmsk], semof=Bmsk)
        for t_, b_, k_ in ((bones, Bbones, "bones"), (onesf, Bonesf, "onesf"), (selt, Bselt, "sel"), (resetm, Bresetm, "reset"),
                           (ropePT, BropePT, "ropePT"), (convw, Bconvw, "conv_fm"), (chanv, Bchanv, "chanv"), (mlav, Bmlav, "mlav")):
            S.dma("sp", t_[:], I[k_], writes=[b_], semof=b_)
        S.op("pool", lambda e: e.memset(epsc[:, 0:1], EPS), writes=[Bepsc])
        S.op("pool", lambda e: e.memset(epsc[:, 1:2], 1e-12), writes=[Bepsc])
        S.op("pool", lambda e: e.memset(epsc[:, 2:3], GN_EPS), writes=[Bepsc])
        S.op("pool", lambda e: e.memset(epsc[:, 3:4], 0.0), writes=[Bepsc])
        for b_ in (Bidentb, Bmsk, Bbones, Bonesf, Bselt, Bresetm, BropePT, Bepsc, Bconvw, Bchanv, Bmlav):
            b_.const = True

        with ExitStack() as p0s:
            P = Ctx(nc, p0s); P.n = 100
            cf, Bcf = P.sb([128, 16, 2], F32, "cf")
            sc, Bsc = P.sb([128, 16, 2], BF16, "sc")
            b2, Bb2 = P.sb([2, 3 * D], F32, "b2")
            nw2, Bnw2 = P.sb([2, D], F32, "nw2")
            mrow, Bmrow = P.sb([2, 3 * D], F32, "mrow")
            grow, Bgrow = P.sb([2, D], F32, "grow")
            wm = [P.sb([128, 16, 512], BF16, "wm") for _ in range(2)]
            bct = [P.sb([128, D], F32, "bct") for _ in range(2)]
            pm, Bpm = P.ps([128, 512], F32, "pm")
            pb = [P.ps([128, 512], F32, "pb") for _ in range(2)]
            S.dma("sp", cf[:], I["cfm"], writes=[Bcf], semof=Bcf)
            S.dma("sp", b2[:], I["b_mod_row"], writes=[Bb2], semof=Bb2)
            S.dma("sp", nw2[:], I["norm_w_row"], writes=[Bnw2], semof=Bnw2)
            S.op("act", lambda e: e.activation(out=sc[:], in_=cf[:], func=AF.Silu), reads=[Bcf], writes=[Bsc])
            wmod_v = I["w_mod"].rearrange("(kc p) n -> p kc n", p=128)
            for cb in range(12):
                wt, Bwt = wm[cb % 2]
                S.dma("pool", wt[:], wmod_v[:, :, cb * 512:(cb + 1) * 512], writes=[Bwt], semof=Bwt)

                def mmf(e, wt=wt):
                    for kc in range(16):
                        ins = e.matmul(pm[0:2, :], sc[:, kc, :], wt[:, kc, :], start=(kc == 0), stop=(kc == 15))
                    return ins
                S.op("pe", mmf, reads=[Bsc, Bwt], writes=[Bpm])
                S.op("act", lambda e, cb=cb: e.activation(out=mrow[0:2, cb * 512:(cb + 1) * 512], in_=pm[0:2, :], func=AF.Copy),
                     reads=[Bpm], writes=[Bmrow])
            S.op("pool", lambda e: e.tensor_tensor(out=mrow[:], in0=mrow[:], in1=b2[:], op=ALU.add), reads=[Bmrow, Bb2], writes=[Bmrow])
            S.op("dve", lambda e: e.scalar_tensor_tensor(out=grow[:], in0=mrow[:, D:2 * D], scalar=1.0, in1=nw2[:],
                                                          op0=ALU.add, op1=ALU.mult), reads=[Bmrow, Bnw2], writes=[Bgrow])
            plan0 = [(0, grow, Bgrow, 0, 0), (1, mrow, Bmrow, 0, 0), (2, mrow, Bmrow, 2 * D, 0), (3, grow, Bgrow, 0, 1), (4, mrow, Bmrow, 0, 1)]
            k = 0
            for (idx, src, Bsrc, off, si) in plan0:
                st, Bst = bct[idx % 2]
                for blk in range(4):
                    pt_, Bpt_ = pb[k % 2]
                    k += 1
                    S.op("pe", lambda e, pt_=pt_, src=src, off=off, blk=blk, si=si: e.matmul(
                        pt_[:, :], selt[0:2, si, :], src[0:2, off + blk * 512: off + (blk + 1) * 512], start=True, stop=True),
                        reads=[Bsrc, Bselt], writes=[Bpt_])
                    S.op("act", lambda e, pt_=pt_, st=st, blk=blk: e.activation(out=st[:, blk * 512:(blk + 1) * 512], in_=pt_[:, :], func=AF.Copy),
                         reads=[Bpt_], writes=[Bst])
                S.dma("sp", BC[idx], st[:], reads=[Bst], semof=Bst)
            S.barrier()
            S.emit()
            S.end_phase()
        if upto == "0":
            return nc

        with ExitStack() as pas:
            P = Ctx(nc, pas); P.n = 200
            W, BW = P.sb([128, 16, WC], BF16, "W")
            wlora, Bwlora = P.sb([128, 2, 256], BF16, "wlora")
            wuq, Bwuq = P.sb([128, 4, 384], BF16, "wuq")
            wukv, Bwukv = P.sb([128, 2, 512], BF16, "wukv")
            gain_bc, Bgain = P.sb([128, D], F32, "gain_bc")
            shift_bc, Bshift = P.sb([128, D], F32, "shift_bc")
            xt = [P.sb([128, D], F32, "xt") for _ in range(2)]
            hm = [P.sb([128, D], BF16, "hm") for _ in range(2)]
            ss = [P.sb([128, 4], F32, "ss") for _ in range(2)]
            hmT = [P.sb([128, 16, GS], BF16, "hmT") for _ in range(2)]
            urkv = [P.sb([128, GS], F32, "urkv") for _ in range(2)]
            gz = [P.sb([128, GS], F32, "gz") for _ in range(2)]
            sga = [P.sb([128, GS], F32, "sga") for _ in range(2)]
            twd, Btwd = P.sb([128, GS], BF16, "twd")
            adb, Badb = P.sb([128, GS], BF16, "adb")
            qd, Bqd = P.sb([128, 4, GS], F32, "qd")
            sqq, Bsqq = P.sb([128, 4, GS], F32, "sqq")
            qn, Bqn = P.sb([128, 4, GS], BF16, "qn")
            rq, Brq = P.sb([128, GS], F32, "rq")
            qno, Bqno = P.sb([128, GS], F32, "qno")
            qro, Bqro = P.sb([64, GS], F32, "qro")
            sqh, Bsqh = P.sb([128, 2, GS], F32, "sqh")
            rh, Brh = P.sb([128, GS], F32, "rh")
            qnf, Bqnf = P.sb([128, GS], BF16, "qnf")
            qrg, Bqrg = P.sb([64, GS], F32, "qrg")
            t1, Bt1 = P.sb([64, GS], F32, "t1")
            t2, Bt2 = P.sb([64, GS], F32, "t2")
            qrf, Bqrf = P.sb([64, GS], BF16, "qrf")
            cost, Bcost = P.sb([64, GS], F32, "cost")
            sint, Bsint = P.sb([64, GS], F32, "sint")
            kvd, Bkvd = P.sb([128, 2, GS], F32, "kvd")
            kvn, Bkvn = P.sb([128, 2, GS], BF16, "kvn")
            kro, Bkro = P.sb([64, GS], F32, "kro")
            vts, Bvts = P.sb([128, 2, GS], BF16, "vts")
            kst, Bkst = P.sb([128, 8], F32, "kst")
            pT = [P.ps([128, 1024], BF16, "pT") for _ in range(2)]
            po = [P.ps([128, 512], F32, "po") for _ in range(3)]
            pst, Bpst = P.ps([128, 512], F32, "pst")
            pv, Bpv = P.ps([128, 512], F32, "pv")
            pks, Bpks = P.ps([128, 512], F32, "pks")
            cnt = {"po": 0, "tile": 0, "u": 0, "g": 0, "s": 0}

            def next_po():
                cnt["po"] += 1
                return po[cnt["po"] % 3]

            win_v = I["w_in_core"].rearrange("(kc p) n -> p kc n", p=128)
            S.dma("pool", W[:, :, :], win_v[:, :, :], writes=[BW], semof=BW)
            S.dma("pool", wlora[:], I["wlora"], writes=[Bwlora], semof=Bwlora)
            S.dma("pool", wuq[:], I["w_uq_c"].rearrange("(kc p) n -> p kc n", p=128), writes=[Bwuq], semof=Bwuq)
            S.dma("pool", wukv[:], I["w_ukv_c"].rearrange("(kc p) n -> p kc n", p=128), writes=[Bwukv], semof=Bwukv)
            S.dma("sp", gain_bc[:], BC[3], writes=[Bgain], semof=Bgain)
            S.dma("sp", shift_bc[:], BC[4], writes=[Bshift], semof=Bshift)

            def prep_tile(row0, hslot, t):
                s = cnt["tile"] % 2
                cnt["tile"] += 1
                x_, Bx_ = xt[s]
                h_, Bh_ = hm[s]
                s_, Bs_ = ss[s]
                hT, BhT = hmT[hslot]
                S.dma("sp", x_[:], I["xs"][row0:row0 + 128, :], writes=[Bx_], semof=Bx_)
                S.op("act", lambda e: e.activation(out=h_[:], in_=x_[:], func=AF.Square, accum_out=s_[:, 0:1]), reads=[Bx_], writes=[Bh_, Bs_])
                S.op("act", lambda e: e.activation(out=s_[:, 1:2], in_=s_[:, 0:1], func=AF.Sqrt, scale=1.0 / D, bias=epsc[:, 0:1]),
                     reads=[Bs_, Bepsc], writes=[Bs_])
                S.op("dve", lambda e: e.reciprocal(out=s_[:, 2:3], in_=s_[:, 1:2]), reads=[Bs_], writes=[Bs_])
                S.op("dve", lambda e: e.scalar_tensor_tensor(out=x_[:], in0=x_[:], scalar=s_[:, 2:3], in1=gain_bc[:], op0=ALU.mult, op1=ALU.mult),
                     reads=[Bx_, Bs_, Bgain], writes=[Bx_])
                S.op("pool", lambda e: e.tensor_tensor(out=h_[:], in0=x_[:], in1=shift_bc[:], op=ALU.add), reads=[Bx_, Bshift], writes=[Bh_])
                for half in range(2):
                    p_, Bp_ = pT[half]

                    def trf(e, half=half, p_=p_):
                        for j in range(8):
                            kc = half * 8 + j
                            ins = e.transpose(p_[:, j * 128:(j + 1) * 128], h_[:, kc * 128:(kc + 1) * 128], identb[:])
                        return ins
                    S.op("pe", trf, reads=[Bh_, Bidentb], writes=[Bp_])
                    cp = (lambda e, half=half, p_=p_: e.activation(out=hT[:, half * 8:(half + 1) * 8, t * 128:(t + 1) * 128],
                                                                   in_=p_[:, :].rearrange("p (j n) -> p j n", n=128), func=AF.Copy)) if half == 0 else \
                         (lambda e, half=half, p_=p_: e.tensor_copy(out=hT[:, half * 8:(half + 1) * 8, t * 128:(t + 1) * 128],
                                                                    in_=p_[:, :].rearrange("p (j n) -> p j n", n=128)))
                    S.op("act" if half == 0 else "dve", cp, reads=[Bp_], writes=[BhT])

            def rstd_from(psum_ap, Bps, out_t, Bout, npart, N, inv_n):
                S.op("act", lambda e: e.activation(out=out_t[0:npart, 0:N], in_=psum_ap, func=AF.Sqrt, scale=inv_n, bias=epsc[0:npart, 0:1]),
                     reads=[Bps, Bepsc], writes=[Bout])
                S.op("dve", lambda e: e.reciprocal(out=out_t[0:npart, 0:N], in_=out_t[0:npart, 0:N]), reads=[Bout], writes=[Bout])

            def rope_apply(src, Bsrc, N, dst_dram):
                pr, Bpr = next_po()
                S.op("pe", lambda e: e.matmul(pr[0:64, 0:N], ropePT[0:64, 0:64], src[0:64, 0:N], start=True, stop=True),
                     reads=[Bsrc, BropePT], writes=[Bpr])
                S.op("dve", lambda e: e.tensor_tensor(out=t1[0:64, 0:N], in0=src[0:64, 0:N], in1=cost[0:64, 0:N], op=ALU.mult),
                     reads=[Bsrc, Bcost], writes=[Bt1])
                S.op("dve", lambda e: e.tensor_tensor(out=t2[0:64, 0:N], in0=pr[0:64, 0:N], in1=sint[0:64, 0:N], op=ALU.mult),
                     reads=[Bpr, Bsint], writes=[Bt2])
                S.op("pool", lambda e: e.tensor_tensor(out=qrf[0:64, 0:N], in0=t1[0:64, 0:N], in1=t2[0:64, 0:N], op=ALU.add),
                     reads=[Bt1, Bt2], writes=[Bqrf])
                S.dma("sp", dst_dram, qrf[0:64, 0:N], reads=[Bqrf], semof=Bqrf)

            def proj_group(gi, hslot, inter):
                isx = gi > 0
                N = GS
                n0 = 256 + (gi - 1) * GS if isx else 0
                xo = (gi - 1) * GS
                ntile = N // 128
                hT, BhT = hmT[hslot]
                inter = list(inter)

                def mm_chunk(col0, M):
                    p_, Bp_ = next_po()

                    def f(e):
                        for kc in range(16):
                            ins = e.matmul(p_[0:M, 0:N], W[:, kc, col0:col0 + M], hT[:, kc, 0:N], start=(kc == 0), stop=(kc == 15))
                        return ins
                    S.op("pe", f, reads=[BW, BhT], writes=[Bp_])
                    if inter:
                        inter.pop(0)()
                    return p_, Bp_

                if isx:
                    S.dma("sp", cost[:, 0:N], I["rope_cos"][:, xo:xo + N], writes=[Bcost], semof=Bcost)
                    S.dma("sp", sint[:, 0:N], I["rope_sin"][:, xo:xo + N], writes=[Bsint], semof=Bsint)
                for j in range(6):
                    p_, Bp_ = mm_chunk(j * 128, 128)
                    u_, Bu_ = urkv[cnt["u"] % 2]
                    cnt["u"] += 1
                    S.op("act", lambda e, p_=p_, u_=u_: e.activation(out=u_[:, 0:N], in_=p_[:, 0:N], func=AF.Copy), reads=[Bp_], writes=[Bu_])
                    S.dma("sp", U_v[:, j, n0:n0 + N], u_[:, 0:N], reads=[Bu_], semof=Bu_)
                if isx:
                    for j in range(2):
                        p_, Bp_ = mm_chunk(768 + j * 128, 128)
                        g_, Bg_ = gz[cnt["g"] % 2]
                        cnt["g"] += 1
                        S.op("act", lambda e, p_=p_, g_=g_: e.activation(out=g_[:, 0:N], in_=p_[:, 0:N], func=AF.Silu), reads=[Bp_], writes=[Bg_])
                        S.dma("sp", Gzr_v[:, j, xo:xo + N], g_[:, 0:N], reads=[Bg_], semof=Bg_)
                p_, Bp_ = mm_chunk(1024, 128)
                S.op("act", lambda e, p_=p_: e.activation(out=twd[:, 0:N], in_=p_[:, 0:N], func=AF.Tanh), reads=[Bp_], writes=[Btwd])
                p_, Bp_ = mm_chunk(1152, 128)
                S.op("act", lambda e, p_=p_: e.activation(out=adb[:, 0:N], in_=p_[:, 0:N], func=AF.Copy), reads=[Bp_], writes=[Badb])
                for which, (src, Bsrc, dst_v, cbase) in enumerate(((twd, Btwd, SG_v, 0), (adb, Badb, AA_v, 2))):
                    for d in range(2):
                        for cc in range(2):
                            p_, Bp_ = next_po()
                            S.op("pe", lambda e, p_=p_, src=src, d=d, cc=cc, which=which: e.matmul(
                                p_[:, 0:N], wlora[64 * d:64 * d + 64, which, cc * 128:(cc + 1) * 128], src[64 * d:64 * d + 64, 0:N],
                                start=True, stop=True), reads=[Bwlora, Bsrc], writes=[Bp_])
                            s_, Bs_ = sga[cnt["s"] % 2]
                            cnt["s"] += 1
                            S.op("act", lambda e, p_=p_, s_=s_, d=d, cc=cc, cbase=cbase: e.activation(
                                out=s_[:, 0:N], in_=p_[:, 0:N], func=AF.Sigmoid, bias=chanv[:, cc, cbase + d:cbase + d + 1]),
                                reads=[Bp_, Bchanv], writes=[Bs_])
                            S.dma("sp", dst_v[:, d * 2 + cc, n0:n0 + N], s_[:, 0:N], reads=[Bs_], semof=Bs_)
                if isx:
                    for j in range(4):
                        p_, Bp_ = mm_chunk(1280 + j * 128, 128)
                        S.op("act", lambda e, p_=p_, j=j: e.activation(out=qd[:, j, 0:N], in_=p_[:, 0:N], func=AF.Copy), reads=[Bp_], writes=[Bqd])
                    S.op("pool", lambda e: e.tensor_tensor(out=sqq[:, :, 0:N], in0=qd[:, :, 0:N], in1=qd[:, :, 0:N], op=ALU.mult), reads=[Bqd], writes=[Bsqq])

                    def ssq(e):
                        for j in range(4):
                            ins = e.matmul(pst[:, 0:N], onesf[:, :], sqq[:, j, 0:N], start=(j == 0), stop=(j == 3))
                        return ins
                    S.op("pe", ssq, reads=[Bsqq, Bonesf], writes=[Bpst])
                    rstd_from(pst[:, 0:N], Bpst, rq, Brq, 128, N, 1.0 / 512)
                    for j in range(4):
                        S.op("dve", lambda e, j=j: e.scalar_tensor_tensor(out=qn[:, j, 0:N], in0=qd[:, j, 0:N], scalar=mlav[:, j:j + 1], in1=rq[:, 0:N],
                                                                          op0=ALU.mult, op1=ALU.mult), reads=[Bqd, Bmlav, Brq], writes=[Bqn])
                    for h in range(2):
                        p1, Bp1 = next_po()
                        p2, Bp2 = next_po()

                        def qup(e, h=h, p1=p1, p2=p2):
                            for kc in range(4):
                                e.matmul(p1[:, 0:N], wuq[:, kc, h * 192:h * 192 + 128], qn[:, kc, 0:N], start=(kc == 0), stop=(kc == 3))
                            for kc in range(4):
                                ins = e.matmul(p2[0:64, 0:N], wuq[:, kc, h * 192 + 128:h * 192 + 192], qn[:, kc, 0:N], start=(kc == 0), stop=(kc == 3))
                            return ins
                        S.op("pe", qup, reads=[Bwuq, Bqn], writes=[Bp1, Bp2])
                        S.op("act", lambda e, p1=p1: e.activation(out=qno[:, 0:N], in_=p1[:, 0:N], func=AF.Copy), reads=[Bp1], writes=[Bqno])
                        S.op("act", lambda e, p2=p2: e.activation(out=qro[0:64, 0:N], in_=p2[0:64, 0:N], func=AF.Copy), reads=[Bp2], writes=[Bqro])
                        S.op("pool", lambda e: e.tensor_tensor(out=sqh[:, 0, 0:N], in0=qno[:, 0:N], in1=qno[:, 0:N], op=ALU.mult), reads=[Bqno], writes=[Bsqh])
                        S.op("pool", lambda e: e.tensor_tensor(out=sqh[0:64, 1, 0:N], in0=qro[0:64, 0:N], in1=qro[0:64, 0:N], op=ALU.mult), reads=[Bqro], writes=[Bsqh])

                        def ssh(e):
                            e.matmul(pst[:, 0:N], onesf[:, :], sqh[:, 0, 0:N], start=True, stop=False)
                            return e.matmul(pst[:, 0:N], onesf[0:64, :], sqh[0:64, 1, 0:N], start=False, stop=True)
                        S.op("pe", ssh, reads=[Bsqh, Bonesf], writes=[Bpst])
                        rstd_from(pst[:, 0:N], Bpst, rh, Brh, 128, N, 1.0 / 192)
                        S.op("dve", lambda e: e.scalar_tensor_tensor(out=qnf[:, 0:N], in0=qno[:, 0:N], scalar=mlav[:, 6:7], in1=rh[:, 0:N],
                                                                      op0=ALU.mult, op1=ALU.mult), reads=[Bqno, Bmlav, Brh], writes=[Bqnf])
                        S.dma("sp", QN_v[:, h, xo:xo + N], qnf[:, 0:N], reads=[Bqnf], semof=Bqnf)
                        S.op("dve", lambda e: e.scalar_tensor_tensor(out=qrg[0:64, 0:N], in0=qro[0:64, 0:N], scalar=mlav[0:64, 8:9], in1=rh[0:64, 0:N],
                                                                      op0=ALU.mult, op1=ALU.mult), reads=[Bqro, Bmlav, Brh], writes=[Bqrg])
                        rope_apply(qrg, Bqrg, N, QR_v[:, h, xo:xo + N])
                for j in range(2):
                    p_, Bp_ = mm_chunk(1792 + j * 128, 128)
                    S.op("act", lambda e, p_=p_, j=j: e.activation(out=kvd[:, j, 0:N], in_=p_[:, 0:N], func=AF.Copy), reads=[Bp_], writes=[Bkvd])
                p_, Bp_ = mm_chunk(2048, 64)
                S.op("act", lambda e, p_=p_: e.activation(out=kro[0:64, 0:N], in_=p_[0:64, 0:N], func=AF.Copy), reads=[Bp_], writes=[Bkro])
                S.op("pool", lambda e: e.tensor_tensor(out=sqq[:, 0:2, 0:N], in0=kvd[:, :, 0:N], in1=kvd[:, :, 0:N], op=ALU.mult), reads=[Bkvd], writes=[Bsqq])

                def sskv(e):
                    for j in range(2):
                        ins = e.matmul(pst[:, 0:N], onesf[:, :], sqq[:, j, 0:N], start=(j == 0), stop=(j == 1))
                    return ins
                S.op("pe", sskv, reads=[Bsqq, Bonesf], writes=[Bpst])
                rstd_from(pst[:, 0:N], Bpst, rq, Brq, 128, N, 1.0 / 256)
                for j in range(2):
                    S.op("dve", lambda e, j=j: e.scalar_tensor_tensor(out=kvn[:, j, 0:N], in0=kvd[:, j, 0:N], scalar=mlav[:, 4 + j:5 + j], in1=rq[:, 0:N],
                                                                      op0=ALU.mult, op1=ALU.mult), reads=[Bkvd, Bmlav, Brq], writes=[Bkvn])
                S.op("pool", lambda e: e.tensor_tensor(out=sqh[0:64, 1, 0:N], in0=kro[0:64, 0:N], in1=kro[0:64, 0:N], op=ALU.mult), reads=[Bkro], writes=[Bsqh])
                for h in range(2):
                    p1, Bp1 = next_po()

                    def kup(e, h=h, p1=p1):
                        for kc in range(2):
                            ins = e.matmul(p1[:, 0:N], wukv[:, kc, h * 256:h * 256 + 128], kvn[:, kc, 0:N], start=(kc == 0), stop=(kc == 1))
                        return ins
                    S.op("pe", kup, reads=[Bwukv, Bkvn], writes=[Bp1])
                    S.op("act", lambda e, p1=p1: e.activation(out=qno[:, 0:N], in_=p1[:, 0:N], func=AF.Copy), reads=[Bp1], writes=[Bqno])
                    S.op("pool", lambda e: e.tensor_tensor(out=sqh[:, 0, 0:N], in0=qno[:, 0:N], in1=qno[:, 0:N], op=ALU.mult), reads=[Bqno], writes=[Bsqh])

                    def kss(e, h=h):
                        for t in range(ntile):
                            c = t * 2 + h
                            e.matmul(pks[:, c:c + 1], sqh[:, 0, t * 128:(t + 1) * 128], onesf[:, 0:1], start=True, stop=False)
                            ins = e.matmul(pks[:, c:c + 1], sqh[0:64, 1, t * 128:(t + 1) * 128], onesf[0:64, 0:1], start=False, stop=True)
                        return ins
                    S.op("pe", kss, reads=[Bsqh, Bonesf], writes=[Bpks])
                    S.op("dve", lambda e: e.tensor_scalar(out=qnf[:, 0:N], in0=qno[:, 0:N], scalar1=mlav[:, 7:8], scalar2=None, op0=ALU.mult),
                         reads=[Bqno, Bmlav], writes=[Bqnf])
                    S.dma("sp", KN_v[:, h, n0:n0 + N], qnf[:, 0:N], reads=[Bqnf], semof=Bqnf)

                    def vup(e, h=h):
                        for t in range(ntile):
                            for kc in range(2):
                                ins = e.matmul(pv[:, t * 128:(t + 1) * 128], kvn[:, kc, t * 128:(t + 1) * 128], wukv[:, kc, h * 256 + 128:h * 256 + 256],
                                               start=(kc == 0), stop=(kc == 1))
                        return ins
                    S.op("pe", vup, reads=[Bwukv, Bkvn], writes=[Bpv])
                    S.op("act", lambda e, h=h: e.activation(out=vts[:, h, 0:N], in_=pv[:, 0:N], func=AF.Copy), reads=[Bpv], writes=[Bvts])
                    S.dma("sp", VT[h, :, n0:n0 + N], vts[:, h, 0:N], reads=[Bvts], semof=Bvts)
                nk = ntile * 2
                t0 = (n0 // 128) * 2
                S.op("act", lambda e: e.activation(out=kst[:, 0:nk], in_=pks[:, 0:nk], func=AF.Sqrt, scale=1.0 / 192, bias=epsc[:, 0:1]),
                     reads=[Bpks, Bepsc], writes=[Bkst])
                S.op("dve", lambda e: e.reciprocal(out=kst[:, 0:nk], in_=kst[:, 0:nk]), reads=[Bkst], writes=[Bkst])
                S.op("dve", lambda e: e.tensor_scalar(out=KS[:, t0:t0 + nk], in0=kst[:, 0:nk], scalar1=float(192 ** -0.5), scalar2=None, op0=ALU.mult),
                     reads=[Bkst], writes=[BKS])
                S.op("dve", lambda e: e.tensor_scalar(out=qrg[0:64, 0:N], in0=kro[0:64, 0:N], scalar1=mlav[0:64, 9:10], scalar2=None, op0=ALU.mult),
                     reads=[Bkro, Bmlav], writes=[Bqrg])
                if isx:
                    rope_apply(qrg, Bqrg, N, KR[:, n0:n0 + N])
                else:
                    S.op("pool", lambda e: e.tensor_copy(out=qrf[0:64, 0:N], in_=qrg[0:64, 0:N]), reads=[Bqrg], writes=[Bqrf])
                    S.dma("sp", KR[:, n0:n0 + N], qrf[0:64, 0:N], reads=[Bqrf], semof=Bqrf)
                if isx:
                    for j in range(2):
                        p_, Bp_ = mm_chunk(2112 + j * 128, 128)
                        g_, Bg_ = gz[cnt["g"] % 2]
                        cnt["g"] += 1
                        S.op("act", lambda e, p_=p_, g_=g_: e.activation(out=g_[:, 0:N], in_=p_[:, 0:N], func=AF.Silu), reads=[Bp_], writes=[Bg_])
                        S.dma("sp", Gzm_v[:, j, xo:xo + N], g_[:, 0:N], reads=[Bg_], semof=Bg_)
                while inter:
                    inter.pop(0)()

            NGRP = 1 + NX // GS
            NGRP = int(os.environ.get("MK_NGRP", NGRP))
            for t in range(GS // 128):
                prep_tile(t * 128, 0, t)
            S.dma("sp", gain_bc[:], BC[0], writes=[Bgain], reads=[], semof=Bgain)
            S.dma("sp", shift_bc[:], BC[1], writes=[Bshift], reads=[], semof=Bshift)
            for gi in range(NGRP):
                inter = []
                if gi + 1 < NGRP:
                    r0 = 256 + gi * GS
                    inter = [(lambda t=t, r0=r0, hs=(gi + 1) % 2: prep_tile(r0 + t * 128, hs, t)) for t in range(GS // 128)]
                proj_group(gi, gi % 2, inter)
            if "KSD" in debug:
                S.dma("sp", KSD, KS[:], reads=[BKS], semof=BKS)
            S.barrier()
            S.emit()
            S.end_phase()
        if upto == "A":
            return nc

        GR = 256
        NCH = GR // 64
        NXG = NX // GR
        U_k = U_rkv.rearrange("(k c p) n -> p k c n", k=3, c=2, p=128)
        YD_v = [v3(YD[d]) for d in range(2)]
        BD_v = [v3(BD[d]) for d in range(2)]
        with ExitStack() as prs:
            P = Ctx(nc, prs); P.n = 300
            omka, Bomka = P.sb([128, 2], F32, "omka")
            S.op("dve", lambda e: e.tensor_scalar(out=omka[:, :], in0=chanv[:, :, 5], scalar1=-1.0, scalar2=1.0, op0=ALU.mult, op1=ALU.add),
                 reads=[Bchanv], writes=[Bomka])

            class CP:
                pass
            cps = []
            for cc in range(2):
                for d in range(2):
                    c_ = CP()
                    c_.cc, c_.d = cc, d
                    for nm, shp, dt in (("ub", [128, 3, GR + 2], F32), ("cv", [128, 3, GR], F32), ("sgt", [128, GR], F32), ("aat", [128, GR], F32),
                                        ("sq", [128, GR], F32), ("rs", [128, GR], F32), ("kk", [128, GR], F32), ("ff", [128, GR], F32),
                                        ("kmod", [128, GR], F32), ("akk", [128, GR], F32), ("Pc", [128, GR], F32), ("Ei", [128, GR], F32),
                                        ("Ee", [128, GR], F32), ("g", [128, GR], F32), ("gp", [128, GR], F32), ("gi", [128, GR], F32),
                                        ("NA", [128, 256], BF16),
                                        ("KA", [128, 256], BF16), ("A0", [128, 128], BF16), ("PW0", [128, 256], BF16), ("PW1", [128, 256], BF16),
                                        ("Tm0", [128, 128], BF16), ("Tm1", [128, 128], BF16), ("TR", [128, 384], BF16), ("Xb", [128, 128], BF16),
                                        ("Ub", [128, 128], BF16), ("H", [128, 128], F32), ("Hb", [128, 128], BF16), ("S1", [128, 128], F32),
                                        ("pr", [128, GR], F32), ("bon", [128, GR], F32)):
                        t_, b_ = P.sb(shp, dt, nm)
                        setattr(c_, nm, t_)
                        setattr(c_, "B" + nm, b_)
                    for nm, shp, dt in (("gtot", [128, NCH], F32), ("AR", [128, NCH, 256], BF16), ("BE", [128, NCH, 128], BF16),
                                        ("KT", [128, NCH, 128], BF16), ("VB", [128, NCH, 128], BF16), ("Yg", [128, GR], F32)):
                        lst = [P.sb(shp, dt, nm) for _ in range(2)]
                        setattr(c_, nm, [x[0] for x in lst])
                        setattr(c_, "B" + nm, [x[1] for x in lst])
                    c_.bk1, c_.Bbk1 = P.ps([128, 512], F32, "bk1")
                    c_.bk2, c_.BpAD = P.ps([128, 512], F32, "bk2")
                    c_.BpTT = c_.BpAD
                    for nm in ("H", "Hb"):
                        t_ = getattr(c_, nm)
                        b_ = getattr(c_, "B" + nm)
                        S.op("pool", lambda e, t_=t_: e.memset(t_[:], 0.0), writes=[b_])
                    for nm in ("AR", "BE", "KT", "VB"):
                        for sl_ in range(2):
                            t_ = getattr(c_, nm)[sl_]
                            b_ = getattr(c_, "B" + nm)[sl_]
                            S.op("pool", lambda e, t_=t_: e.memset(t_[:], 0.0), writes=[b_])
                    cps.append(c_)

            c3 = lambda ap: ap.rearrange("p (c t) -> p c t", t=64)

            def prep_gen(c, sl, n0, N, s0, s1, xo):
                cc, d = c.cc, c.d
                AR, BAR = c.AR[sl], c.BAR[sl]
                BE, BBE = c.BE[sl], c.BBE[sl]
                KT, BKT = c.KT[sl], c.BKT[sl]
                VB, BVB = c.VB[sl], c.BVB[sl]
                gtot, Bgtot = c.gtot[sl], c.Bgtot[sl]
                lo, hi = n0 - 1, n0 + N + 1
                dl, dh = 0, N + 2
                if n0 == s0:
                    S.op("pool", lambda e: e.memset(c.ub[:, :, 0:1], 0.0), writes=[c.Bub])
                    lo, dl = n0, 1
                if n0 + N == s1:
                    S.op("pool", lambda e: e.memset(c.ub[:, :, N + 1:N + 2], 0.0), writes=[c.Bub])
                    hi, dh = n0 + N, N + 1
                S.dma("sp", c.ub[:, :, dl:dh], U_k[:, :, cc, lo:hi], writes=[c.Bub], semof=c.Bub)
                S.dma("sp", c.sgt[:, 0:N], SG_v[:, d * 2 + cc, n0:n0 + N], writes=[c.Bsgt], semof=c.Bsgt)
                S.dma("sp", c.aat[:, 0:N], AA_v[:, d * 2 + cc, n0:n0 + N], writes=[c.Baat], semof=c.Baat)
                yield
                for kind in range(3):
                    ch = kind * 2 + cc
                    S.op("act", lambda e, kind=kind, ch=ch: e.activation(out=c.cv[:, kind, 0:N], in_=c.ub[:, kind, 1:N + 1], func=AF.Copy,
                                                                         scale=convw[:, ch, 1:2]), reads=[c.Bub, Bconvw], writes=[c.Bcv])
                S.op("dve", lambda e: e.tensor_tensor_scan(out=c.Pc[:, 0:N], data0=resetm[:, 0:N], data1=c.sgt[:, 0:N], initial=0.0, op0=ALU.mult, op1=ALU.add),
                     reads=[Bresetm, c.Bsgt], writes=[c.BPc])
                yield
                for tap in (0, 2):
                    for kind in range(3):
                        ch = kind * 2 + cc
                        S.op("dve", lambda e, kind=kind, ch=ch, tap=tap: e.scalar_tensor_tensor(
                            out=c.cv[:, kind, 0:N], in0=c.ub[:, kind, tap:tap + N], scalar=convw[:, ch, tap:tap + 1],
                            in1=c.cv[:, kind, 0:N], op0=ALU.mult, op1=ALU.add), reads=[c.Bub, Bconvw, c.Bcv], writes=[c.Bcv])
                    yield
                nch = N // 64
                tot = c3(c.Pc[:, 0:N])[:, :, 63]
                if d == 0:
                    S.op("pool", lambda e: e.tensor_tensor(out=c.Ee[:, 0:N], in0=c.Pc[:, 0:N], in1=c.sgt[:, 0:N], op=ALU.subtract), reads=[c.BPc, c.Bsgt], writes=[c.BEe])
                    Ei, BEi = c.Pc, c.BPc
                else:
                    for k_ in range(nch):
                        S.op("dve", lambda e, k_=k_: e.tensor_scalar(out=c.Ee[:, k_ * 64:(k_ + 1) * 64], in0=c.Pc[:, k_ * 64:(k_ + 1) * 64], scalar1=-1.0,
                                                                     scalar2=c.Pc[:, k_ * 64 + 63:k_ * 64 + 64], op0=ALU.mult, op1=ALU.add),
                             reads=[c.BPc], writes=[c.BEe])
                    S.op("pool", lambda e: e.tensor_tensor(out=c.Ei[:, 0:N], in0=c.Ee[:, 0:N], in1=c.sgt[:, 0:N], op=ALU.add), reads=[c.BEe, c.Bsgt], writes=[c.BEi])
                    Ei, BEi = c.Ei, c.BEi
                S.op("act", lambda e: e.activation(out=c.sq[:, 0:N], in_=c.cv[:, 1, 0:N], func=AF.Square, scale=chanv[:, cc, 4:5]),
                     reads=[c.Bcv, Bchanv], writes=[c.Bsq])
                yield
                st_ = c.bk1[:, 0:N]
                Bst_ = c.Bbk1
                S.op("pe", lambda e: e.matmul(st_, bones[:, :], c.sq[:, 0:N], start=True, stop=True), reads=[c.Bsq, Bbones], writes=[Bst_])
                S.op("act", lambda e: e.activation(out=c.rs[:, 0:N], in_=st_, func=AF.Sqrt, bias=epsc[:, 1:2], scale=1.0), reads=[Bepsc], writes=[c.Brs, Bst_])
                yield
                S.op("act", lambda e: e.activation(out=c.g[:, 0:N], in_=Ei[:, 0:N], func=AF.Exp, scale=-C0), reads=[BEi], writes=[c.Bg])
                S.op("act", lambda e: e.activation(out=c.gp[:, 0:N], in_=c.Ee[:, 0:N], func=AF.Exp, scale=-C0), reads=[c.BEe], writes=[c.Bgp])
                S.op("act", lambda e: e.activation(out=c.gi[:, 0:N], in_=Ei[:, 0:N], func=AF.Exp, scale=C0), reads=[BEi], writes=[c.Bgi])
                S.op("act", lambda e: e.activation(out=gtot[:, 0:nch], in_=tot, func=AF.Exp, scale=-C0), reads=[c.BPc], writes=[Bgtot])
                S.op("dve", lambda e: e.tensor_scalar(out=c.ff[:, 0:N], in0=c.aat[:, 0:N], scalar1=chanv[:, cc, 5:6], scalar2=omka[:, cc:cc + 1],
                                                       op0=ALU.mult, op1=ALU.add), reads=[c.Baat, Bchanv, Bomka], writes=[c.Bff])
                S.op("pool", lambda e: e.tensor_tensor(out=c.kmod[:, 0:N], in0=c.cv[:, 1, 0:N], in1=c.ff[:, 0:N], op=ALU.mult), reads=[c.Bcv, c.Bff], writes=[c.Bkmod])
                S.op("dve", lambda e: e.reciprocal(out=c.rs[:, 0:N], in_=c.rs[:, 0:N]), reads=[c.Brs], writes=[c.Brs])
                yield
                S.op("dve", lambda e: e.scalar_tensor_tensor(out=c.pr[:, 0:N], in0=c.cv[:, 0, 0:N], scalar=chanv[:, cc, 8:9], in1=c.kmod[:, 0:N],
                                                              op0=ALU.mult, op1=ALU.mult), reads=[c.Bcv, Bchanv, c.Bkmod], writes=[c.Bpr])
                S.op("dve", lambda e: e.scalar_tensor_tensor(out=c.kk[:, 0:N], in0=c.cv[:, 1, 0:N], scalar=chanv[:, cc, 4:5], in1=c.rs[:, 0:N],
                                                              op0=ALU.mult, op1=ALU.mult), reads=[c.Bcv, Bchanv, c.Brs], writes=[c.Bkk])
                yield
                S.op("pe", lambda e: e.matmul(st_, bones[:, :], c.pr[:, 0:N], start=True, stop=True), reads=[c.Bpr, Bbones], writes=[Bst_])
                S.op("dve", lambda e: e.tensor_tensor(out=c.bon[:, 0:N], in0=st_, in1=c.cv[:, 2, 0:N], op=ALU.mult), reads=[c.Bcv], writes=[c.Bbon, Bst_])
                if xo is not None:
                    S.dma("sp", BD_v[d][:, cc, xo:xo + N], c.bon[:, 0:N], reads=[c.Bbon], semof=c.Bbon)
                yield
                S.op("pool", lambda e: e.tensor_tensor(out=c.akk[:, 0:N], in0=c.aat[:, 0:N], in1=c.kk[:, 0:N], op=ALU.mult), reads=[c.Baat, c.Bkk], writes=[c.Bakk])
                for hh in range(2):
                    ps_ = slice(64 * hh, 64 * hh + 64)
                    o1 = slice(64 * hh, 64 * hh + 64)
                    o2 = slice(128 + 64 * hh, 128 + 64 * hh + 64)
                    S.op("dve", lambda e, ps_=ps_, o2=o2: e.tensor_tensor(out=AR[ps_, 0:nch, o2], in0=c3(c.cv[ps_, 0, 0:N]), in1=c3(c.g[ps_, 0:N]), op=ALU.mult),
                         reads=[c.Bcv, c.Bg], writes=[BAR])
                    S.op("dve", lambda e, ps_=ps_, o1=o1: e.scalar_tensor_tensor(out=AR[ps_, 0:nch, o1], in0=c3(c.kk[ps_, 0:N]), scalar=-1.0, in1=c3(c.gp[ps_, 0:N]),
                                                                                 op0=ALU.mult, op1=ALU.mult), reads=[c.Bkk, c.Bgp], writes=[BAR])
                    S.op("pool", lambda e, ps_=ps_, o1=o1: e.tensor_tensor(out=KT[ps_, 0:nch, o1], in0=c3(c.kmod[ps_, 0:N]), in1=c3(c.gi[ps_, 0:N]), op=ALU.mult),
                         reads=[c.Bkmod, c.Bgi], writes=[BKT])
                    S.op("pool", lambda e, ps_=ps_, o1=o1: e.tensor_copy(out=VB[ps_, 0:nch, o1], in_=c3(c.cv[ps_, 2, 0:N])), reads=[c.Bcv], writes=[BVB])
                yield
                for hh in range(2):
                    ps_ = slice(64 * hh, 64 * hh + 64)
                    o1 = slice(64 * hh, 64 * hh + 64)
                    S.op("pool", lambda e, ps_=ps_, o1=o1: e.tensor_tensor(out=BE[ps_, 0:nch, o1], in0=c3(c.akk[ps_, 0:N]), in1=c3(c.gi[ps_, 0:N]), op=ALU.mult),
                         reads=[c.Bakk, c.Bgi], writes=[BBE])
                yield

            def chunk_gen(c, sl, k, want_y):
                AR, BAR = c.AR[sl], c.BAR[sl]
                BE, BBE = c.BE[sl], c.BBE[sl]
                KT, BKT = c.KT[sl], c.BKT[sl]
                VB, BVB = c.VB[sl], c.BVB[sl]
                gtot, Bgtot = c.gtot[sl], c.Bgtot[sl]
                Yg, BYg = c.Yg[sl], c.BYg[sl]
                bk1, Bbk1, bk2, BpAD, BpTT = c.bk1, c.Bbk1, c.bk2, c.BpAD, c.BpTT
                mN = (msk[:, 0:2, :] if c.d == 0 else msk[:, 2:4, :]).rearrange("p a b -> p (a b)")
                mA = msk[:, 2, :] if c.d == 0 else msk[:, 0, :]

                def f1(e):
                    e.matmul(bk1[:, 0:256], BE[:, k, :], AR[:, k, :], start=True, stop=True)
                    return e.matmul(bk1[:, 256:512], KT[:, k, :], AR[:, k, :], start=True, stop=True)
                S.op("pe", f1, reads=[BBE, BKT, BAR], writes=[Bbk1])
                S.op("pe", lambda e: e.matmul(bk2[:, 0:128], AR[:, k, 0:128], BE[:, k, :], start=True, stop=True), reads=[BAR, BBE], writes=[BpAD])
                S.op("dve", lambda e: e.tensor_tensor(out=c.NA[:, :], in0=bk1[:, 0:256], in1=mN, op=ALU.mult), reads=[Bmsk], writes=[c.BNA, Bbk1])
                S.op("dve", lambda e: e.tensor_tensor(out=c.KA[:, :], in0=bk1[:, 256:512], in1=mN, op=ALU.mult), reads=[Bmsk], writes=[c.BKA, Bbk1])
                S.op("dve", lambda e: e.tensor_tensor(out=c.A0[:, :], in0=bk2[:, 0:128], in1=mA, op=ALU.mult), reads=[Bmsk], writes=[c.BA0, BpAD])
                yield
                def f2(e):
                    e.matmul(bk1[:, 0:128], BE[:, k, :], identb[:, :], start=True, stop=True)
                    e.matmul(bk1[:, 128:256], KT[:, k, :], identb[:, :], start=True, stop=True)
                    return e.matmul(bk1[:, 256:384], VB[:, k, :], identb[:, :], start=True, stop=True)
                S.op("pe", f2, reads=[BBE, BKT, BVB, Bidentb], writes=[Bbk1])
                S.op("act", lambda e: e.activation(out=c.TR[:, :], in_=bk1[:, 0:384], func=AF.Copy), reads=[], writes=[c.BTR, Bbk1])
                yield
                S.op("pool", lambda e: e.tensor_tensor(out=c.Tm0[:, :], in0=c.NA[:, 0:128], in1=identb[:, :], op=ALU.add), reads=[c.BNA, Bidentb], writes=[c.BTm0])
                Nk, BNk, Ak, BAk = c.NA[:, 0:128], c.BNA, c.A0[:, :], c.BA0
                Tc, BTc = c.Tm0, c.BTm0
                for lvl in range(5):
                    pw, Bpw = (c.PW0, c.BPW0) if lvl % 2 == 0 else (c.PW1, c.BPW1)
                    if lvl < 4:
                        def f3(e, Nk=Nk, Ak=Ak):
                            e.matmul(bk2[:, 0:128], Ak, Nk, start=True, stop=True)
                            return e.matmul(bk2[:, 128:256], Nk, Ak, start=True, stop=True)
                        S.op("pe", f3, reads=[BNk, BAk], writes=[BpAD])
                        S.op("act", lambda e, pw=pw: e.activation(out=pw[:, :], in_=bk2[:, 0:256], func=AF.Copy), reads=[BpAD], writes=[Bpw])
                    else:
                        S.op("pe", lambda e, Nk=Nk, Ak=Ak: e.matmul(bk2[:, 128:256], Nk, Ak, start=True, stop=True), reads=[BNk, BAk], writes=[BpAD])
                        S.op("act", lambda e, pw=pw: e.activation(out=pw[:, 128:256], in_=bk2[:, 128:256], func=AF.Copy), reads=[BpAD], writes=[Bpw])
                    yield
                    Nk, BNk, Ak, BAk = pw[:, 0:128], Bpw, pw[:, 128:256], Bpw
                    Tn, BTn = (c.Tm1, c.BTm1) if lvl % 2 == 0 else (c.Tm0, c.BTm0)
                    S.op("pe", lambda e, Ak=Ak, Tc=Tc: e.matmul(bk1[:, 384:512], Ak, Tc[:, :], start=True, stop=True), reads=[BAk, BTc], writes=[Bbk1])
                    S.op("dve", lambda e, Tc=Tc, Tn=Tn: e.tensor_tensor(out=Tn[:, :], in0=bk1[:, 384:512], in1=Tc[:, :], op=ALU.add), reads=[BTc], writes=[BTn, Bbk1])
                    Tc, BTc = Tn, BTn
                    yield
                Tf, BTf = Tc, BTc
                def fx(e):
                    e.matmul(bk1[:, 0:128], c.KA[:, 0:128], c.TR[:, 256:384], start=True, stop=False)
                    return e.matmul(bk1[:, 0:128], AR[:, k, 0:128], c.Hb[:, :], start=False, stop=True)
                S.op("pe", fx, reads=[c.BKA, c.BTR, BAR, c.BHb], writes=[Bbk1])
                S.op("dve", lambda e: e.tensor_copy(out=c.Xb[:, :], in_=bk1[:, 0:128]), reads=[Bbk1], writes=[c.BXb])
                yield
                S.op("pe", lambda e: e.matmul(bk1[:, 128:256], Tf[:, :], c.Xb[:, :], start=True, stop=True), reads=[BTf, c.BXb], writes=[Bbk1])
                S.op("act", lambda e: e.activation(out=c.Ub[:, :], in_=bk1[:, 128:256], func=AF.Copy), reads=[Bbk1], writes=[c.BUb])
                yield

                def fh(e):
                    e.matmul(bk1[:, 256:384], c.TR[:, 128:256], c.TR[:, 256:384], start=True, stop=False)
                    ins = e.matmul(bk1[:, 256:384], c.TR[:, 0:128], c.Ub[:, :], start=False, stop=True)
                    if want_y:
                        e.matmul(bk1[:, 384:512], c.Hb[:, :], AR[:, k, 128:256], start=True, stop=False)
                        e.matmul(bk1[:, 384:512], c.Ub[:, :], c.NA[:, 128:256], start=False, stop=False)
                        ins = e.matmul(bk1[:, 384:512], c.TR[:, 256:384], c.KA[:, 128:256], start=False, stop=True)
                    return ins
                S.op("pe", fh, reads=[c.BTR, c.BUb, c.BHb, BAR, c.BNA, c.BKA], writes=[Bbk1])
                S.op("dve", lambda e: e.tensor_tensor(out=c.S1[:, :], in0=bk1[:, 256:384], in1=c.H[:, :], op=ALU.add), reads=[Bbk1, c.BH], writes=[c.BS1])
                if want_y:
                    for hh in range(2):
                        ps_ = slice(64 * hh, 64 * hh + 64)
                        S.op("act", lambda e, ps_=ps_, hh=hh: e.activation(out=Yg[ps_, k * 64:(k + 1) * 64], in_=bk1[ps_, 384 + 64 * hh:384 + 64 * hh + 64], func=AF.Copy),
                             reads=[], writes=[BYg, Bbk1])
                yield
                S.op("act", lambda e: e.activation(out=c.Hb[:, :], in_=c.S1[:, :], func=AF.Copy, scale=gtot[:, k:k + 1]), reads=[c.BS1, Bgtot], writes=[c.BHb])
                S.op("dve", lambda e: e.tensor_scalar(out=c.H[:, :], in0=c.S1[:, :], scalar1=gtot[:, k:k + 1], scalar2=None, op0=ALU.mult),
                     reads=[c.BS1, Bgtot], writes=[c.BH])
                yield

            def step_info(c, step):
                if step == 0:
                    return 0, 0, 256, None
                xg = (step - 1) if c.d == 0 else (NXG - step)
                return 256 + xg * GR, 256, NT, xg * GR

            def run_rr(gens):
                alive = list(gens)
                while alive:
                    nxt = []
                    for g_ in alive:
                        try:
                            next(g_)
                            nxt.append(g_)
                        except StopIteration:
                            pass
                    alive = nxt

            NSTEP = int(os.environ.get("MK_RSTEPS", 1 + NXG))

            def mk_prep(c, step):
                n0, s0, s1, xo = step_info(c, step)
                return prep_gen(c, step % 2, n0, GR, s0, s1, xo)

            run_rr([mk_prep(c, 0) for c in cps])
            for step in range(NSTEP):
                isx = step > 0
                sl = step % 2

                def seq(c):
                    for ci in range(NCH):
                        k = ci if c.d == 0 else NCH - 1 - ci
                        yield from chunk_gen(c, sl, k, isx)
                    if isx:
                        xo = step_info(c, step)[3]
                        S.dma("sp", YD_v[c.d][:, c.cc, xo:xo + GR], c.Yg[sl][:, :], reads=[c.BYg[sl]], semof=c.BYg[sl])

                gens = [seq(c) for c in cps]
                preps = [mk_prep(c, step + 1) for c in cps] if step + 1 < NSTEP else []
                rnd = 0
                alive = gens
                while alive or preps:
                    nxt = []
                    for g_ in alive:
                        try:
                            next(g_)
                            nxt.append(g_)
                        except StopIteration:
                            pass
                    alive = nxt
                    rnd += 1
                    if preps and (rnd % 4 == 0 or not alive):
                        np_ = []
                        for g_ in preps:
                            try:
                                next(g_)
                                np_.append(g_)
                            except StopIteration:
                                pass
                        preps = np_
            S.barrier()
            S.emit()
            S.end_phase()
        if upto == "R":
            return nc

        NF = 512
        BOX = Buf("OX")
        with ExitStack() as pfs:
            P = Ctx(nc, pfs); P.n = 400
            ld = [[P.sb([128, NF], F32, "fld") for _ in range(5)] for _ in range(2)]
            yy, Byy = P.sb([128, NF], F32, "yy")
            bs, Bbs = P.sb([128, NF], F32, "bs")
            ysq, Bysq = P.sb([128, NF], F32, "ysq")
            mm_, Bmm_ = P.sb([128, NF], F32, "mm")
            msq, Bmsq = P.sb([128, NF], F32, "msq")
            var, Bvar = P.sb([128, NF], F32, "var")
            yc, Byc = P.sb([128, NF], F32, "yc")
            ob = [P.sb([128, NF], BF16, "ob") for _ in range(2)]
            ps1 = [P.ps([128, 512], F32, "ps1") for _ in range(2)]
            ps2 = [P.ps([128, 512], F32, "ps2") for _ in range(2)]
            it = 0
            for cc in range(2):
                for ti in range(NX // NF):
                    xo = ti * NF
                    sl = it % 2
                    (y0, By0), (y1, By1), (b0, Bb0), (b1, Bb1), (gzt, Bgzt) = ld[sl]
                    S.dma("sp", y0[:], YD_v[0][:, cc, xo:xo + NF], writes=[By0], semof=By0)
                    S.dma("sp", y1[:], YD_v[1][:, cc, xo:xo + NF], writes=[By1], semof=By1)
                    S.dma("sp", b0[:], BD_v[0][:, cc, xo:xo + NF], writes=[Bb0], semof=Bb0)
                    S.dma("sp", b1[:], BD_v[1][:, cc, xo:xo + NF], writes=[Bb1], semof=Bb1)
                    S.dma("sp", gzt[:], Gzr_v[:, cc, xo:xo + NF], writes=[Bgzt], semof=Bgzt)
                    p1, Bp1 = ps1[sl]
                    p2, Bp2 = ps2[sl]
                    o_, Bo_ = ob[sl]
                    S.op("pool", lambda e, y0=y0, y1=y1: e.tensor_tensor(out=yy[:], in0=y0[:], in1=y1[:], op=ALU.add), reads=[By0, By1], writes=[Byy])
                    S.op("pool", lambda e, b0=b0, b1=b1: e.tensor_tensor(out=bs[:], in0=b0[:], in1=b1[:], op=ALU.add), reads=[Bb0, Bb1], writes=[Bbs])
                    S.op("pe", lambda e, p1=p1: e.matmul(p1[:, :], bones[:, :], yy[:], start=True, stop=True), reads=[Byy, Bbones], writes=[Bp1])
                    S.op("act", lambda e: e.activation(out=ysq[:], in_=yy[:], func=AF.Square), reads=[Byy], writes=[Bysq])
                    S.op("pe", lambda e, p2=p2: e.matmul(p2[:, :], bones[:, :], ysq[:], start=True, stop=True), reads=[Bysq, Bbones], writes=[Bp2])
                    S.op("dve", lambda e, p1=p1: e.tensor_scalar(out=mm_[:], in0=p1[:, :], scalar1=1.0 / 64, scalar2=None, op0=ALU.mult), reads=[Bp1], writes=[Bmm_])
                    S.op("pool", lambda e: e.tensor_tensor(out=msq[:], in0=mm_[:], in1=mm_[:], op=ALU.mult), reads=[Bmm_], writes=[Bmsq])
                    S.op("dve", lambda e, p2=p2: e.scalar_tensor_tensor(out=var[:], in0=p2[:, :], scalar=1.0 / 64, in1=msq[:], op0=ALU.mult, op1=ALU.subtract),
                         reads=[Bp2, Bmsq], writes=[Bvar])
                    S.op("act", lambda e: e.activation(out=var[:], in_=var[:], func=AF.Sqrt, bias=epsc[:, 2:3], scale=1.0), reads=[Bvar, Bepsc], writes=[Bvar])
                    S.op("dve", lambda e: e.reciprocal(out=var[:], in_=var[:]), reads=[Bvar], writes=[Bvar])
                    S.op("pool", lambda e: e.tensor_tensor(out=yc[:], in0=yy[:], in1=mm_[:], op=ALU.subtract), reads=[Byy, Bmm_], writes=[Byc])
                    S.op("pool", lambda e: e.tensor_tensor(out=yc[:], in0=yc[:], in1=var[:], op=ALU.mult), reads=[Byc, Bvar], writes=[Byc])
                    S.op("dve", lambda e, cc=cc: e.tensor_scalar(out=yc[:], in0=yc[:], scalar1=chanv[:, cc, 6:7], scalar2=chanv[:, cc, 7:8], op0=ALU.mult, op1=ALU.add),
                         reads=[Byc, Bchanv], writes=[Byc])
                    S.op("pool", lambda e: e.tensor_tensor(out=yc[:], in0=yc[:], in1=bs[:], op=ALU.add), reads=[Byc, Bbs], writes=[Byc])
                    S.op("dve", lambda e, o_=o_, gzt=gzt: e.tensor_tensor(out=o_[:], in0=yc[:], in1=gzt[:], op=ALU.mult), reads=[Byc, Bgzt], writes=[Bo_])
                    S.dma("sp", OXs[2 * cc][:, xo:xo + NF], o_[0:64, :], reads=[Bo_], semof=Bo_)
                    S.dma("sp", OXs[2 * cc + 1][:, xo:xo + NF], o_[64:128, :], reads=[Bo_], semof=Bo_)
                    it += 1
            S.barrier()
            S.emit()
            S.end_phase()
        if upto == "F":
            return nc

        QG = 512
        NKT = NT // 128
        with ExitStack() as pms:
            P = Ctx(nc, pms); P.n = 500
            Kn, BKn = P.sb([128, NT], BF16, "Kn")
            Kr, BKr = P.sb([64, NT], BF16, "Kr")
            Vt, BVt = P.sb([128, NKT, 128], BF16, "Vt")
            onesb, Bonesb = P.sb([128, 128], BF16, "onesb")
            Qn = [P.sb([128, QG], BF16, "Qn") for _ in range(2)]
            Qr = [P.sb([64, QG], BF16, "Qr") for _ in range(2)]
            gmt = [P.sb([128, QG], F32, "gmt") for _ in range(2)]
            Pt = [P.sb([128, QG], BF16, "Pt") for _ in range(3)]
            rl, Brl = P.sb([128, QG], F32, "rl")
            oo, Boo = P.sb([128, QG], F32, "oo")
            om = [P.sb([128, QG], BF16, "om") for _ in range(2)]
            pS = [P.ps([128, 512], F32, "pS") for _ in range(3)]
            pO = [P.ps([128, 512], F32, "pO") for _ in range(2)]
            pL = [P.ps([128, 512], F32, "pL") for _ in range(2)]
            S.op("pool", lambda e: e.memset(onesb[:], 1.0), writes=[Bonesb])
            S.dma("sp", Kr[:], KR, writes=[BKr], semof=BKr)
            NQG = int(os.environ.get("MK_NQG", NX // QG))
            def attn_group(h, qg, sl):
                qo = qg * QG
                qn_, Bqn_ = Qn[sl]
                qr_, Bqr_ = Qr[sl]
                gm_, Bgm_ = gmt[sl]
                po_, Bpo_ = pO[sl]
                pl_, Bpl_ = pL[sl]
                o_, Bo_ = om[sl]
                S.dma("sp", qn_[:], QN_v[:, h, qo:qo + QG], writes=[Bqn_], semof=Bqn_)
                S.dma("sp", qr_[:], QR_v[:, h, qo:qo + QG], writes=[Bqr_], semof=Bqr_)
                S.dma("sp", gm_[:], Gzm_v[:, h, qo:qo + QG], writes=[Bgm_], semof=Bgm_)

                def qk(kt):
                    ps_, Bps_ = pS[kt % 3]

                    def f(e):
                        e.matmul(ps_[:, :], Kn[:, kt * 128:(kt + 1) * 128], qn_[:, :], start=True, stop=False)
                        return e.matmul(ps_[:, :], Kr[0:64, kt * 128:(kt + 1) * 128], qr_[0:64, :], start=False, stop=True)
                    S.op("pe", f, reads=[BKn, BKr, Bqn_, Bqr_], writes=[Bps_])

                def ex_pv(kt):
                    ps_, Bps_ = pS[kt % 3]
                    pt_, Bpt_ = Pt[kt % 3]
                    S.op("act", lambda e: e.activation(out=pt_[:, :], in_=ps_[:, :], func=AF.Exp, scale=KS[:, kt * 2 + h:kt * 2 + h + 1]),
                         reads=[Bps_, BKS], writes=[Bpt_])

                    def pv_(e):
                        e.matmul(po_[:, :], Vt[:, kt, :], pt_[:, :], start=(kt == 0), stop=(kt == NKT - 1))
                        return e.matmul(pl_[:, :], onesb[:, :], pt_[:, :], start=(kt == 0), stop=(kt == NKT - 1))
                    S.op("pe", pv_, reads=[BVt, Bpt_, Bonesb], writes=[Bpo_, Bpl_])

                qk(0)
                for kt in range(NKT):
                    if kt + 1 < NKT:
                        qk(kt + 1)
                    ex_pv(kt)
                S.op("dve", lambda e: e.reciprocal(out=rl[:, :], in_=pl_[:, :]), reads=[Bpl_], writes=[Brl])
                S.op("dve", lambda e: e.tensor_tensor(out=oo[:, :], in0=po_[:, :], in1=rl[:, :], op=ALU.mult), reads=[Bpo_, Brl], writes=[Boo])
                S.op("pool", lambda e: e.tensor_tensor(out=o_[:, :], in0=oo[:, :], in1=gm_[:, :], op=ALU.mult), reads=[Boo, Bgm_], writes=[Bo_])
                S.dma("sp", OXs[4 + 2 * h][:, qo:qo + QG], o_[0:64, :], reads=[Bo_], semof=Bo_)
                S.dma("sp", OXs[5 + 2 * h][:, qo:qo + QG], o_[64:128, :], reads=[Bo_], semof=Bo_)

            gcount = 0
            for h in range(2):
                S.dma("sp", Kn[:], KN_v[:, h, :], writes=[BKn], semof=BKn)
                S.dma("sp", Vt[:], VT[h].rearrange("p (t d) -> p t d", d=128), writes=[BVt], semof=BVt)
                for qg in range(NQG):
                    attn_group(h, qg, gcount % 2)
                    gcount += 1
            S.barrier()
            S.emit()
            S.end_phase()
        if upto == "M":
            return nc

        BOG = Buf("OG")
        for j in range(8):
            S.collective("AllGather", [[0, 1, 2, 3], [4, 5, 6, 7]], OXs[j], OGs[j], reads=[BOX], writes=[BOG], semof=BOG)
        S.barrier()
        S.emit()
        S.end_phase()

        CG = 512
        with ExitStack() as pcs:
            P = Ctx(nc, pcs); P.n = 600
            selq, Bselq = P.sb([128, 4], F32, "selq")
            gain_bc, Bgain = P.sb([128, D], F32, "gain_c")
            shift_bc, Bshift = P.sb([128, D], F32, "shift_c")
            gate_bc, Bgate = P.sb([128, D], F32, "gate_c")
            xt = [P.sb([128, D], F32, "xtc") for _ in range(2)]
            hm = [P.sb([128, D], BF16, "hmc") for _ in range(2)]
            ss = [P.sb([128, 4], F32, "ssc") for _ in range(2)]
            hT, BhT = P.sb([128, 16, CG], BF16, "hTc")
            ldq = [P.sb([128, 16, CG], BF16, "ldq") for _ in range(2)]
            osel, Bosel = P.sb([128, 16, CG], BF16, "osel")
            wmg = [P.sb([128, 16, 256], BF16, "wmg") for _ in range(2)]
            wbr = [P.sb([128, 8, 256], BF16, "wbr") for _ in range(2)]
            sgr, Bsgr = P.sb([128, CG], F32, "sgr")
            sgm, Bsgm = P.sb([128, CG], F32, "sgm")
            tr_, Btr_ = P.sb([128, CG], F32, "tr")
            tm_, Btm_ = P.sb([128, CG], F32, "tm")
            merged, Bmerged = P.sb([128, 16, CG], BF16, "merged")
            wout = [P.sb([128, 16, 512], BF16, "wout") for _ in range(2)]
            xr = [P.sb([128, 512], F32, "xr") for _ in range(2)]
            res = [P.sb([128, 512], F32, "res") for _ in range(2)]
            pT = [P.ps([128, 1024], BF16, "pTc") for _ in range(2)]
            pg = [P.ps([128, 512], F32, "pg") for _ in range(4)]
            pout = [P.ps([128, 512], F32, "pout") for _ in range(2)]
            S.dma("sp", selq[:], I["selq"], writes=[Bselq], semof=Bselq)
            S.dma("sp", gain_bc[:], BC[0], writes=[Bgain], semof=Bgain)
            S.dma("sp", shift_bc[:], BC[1], writes=[Bshift], semof=Bshift)
            S.dma("sp", gate_bc[:], BC[2], writes=[Bgate], semof=Bgate)
            wmg_v = I["w_in_mg"].rearrange("(kc p) n -> p kc n", p=128)
            wbr_r_v = I["w_br_r"].rearrange("(kc p) n -> p kc n", p=128)
            wbr_m_v = I["w_br_m"].rearrange("(kc p) n -> p kc n", p=128)
            wout_v = I["w_out"].rearrange("(kc p) n -> p kc n", p=128)
            cnt = {"tile": 0, "w": 0, "o": 0, "r": 0}
            NCG = int(os.environ.get("MK_NCG", 2048 // CG))
            for gj in range(NCG):
                go = gj * CG
                for t in range(CG // 128):
                    s = cnt["tile"] % 2
                    cnt["tile"] += 1
                    x_, Bx_ = xt[s]
                    h_, Bh_ = hm[s]
                    s_, Bs_ = ss[s]
                    S.dma("sp", x_[:], I["xm"][go + t * 128:go + (t + 1) * 128, :], writes=[Bx_], semof=Bx_)
                    S.op("act", lambda e, x_=x_, h_=h_, s_=s_: e.activation(out=h_[:], in_=x_[:], func=AF.Square, accum_out=s_[:, 0:1]), reads=[Bx_], writes=[Bh_, Bs_])
                    S.op("act", lambda e, s_=s_: e.activation(out=s_[:, 1:2], in_=s_[:, 0:1], func=AF.Sqrt, scale=1.0 / D, bias=epsc[:, 0:1]), reads=[Bs_, Bepsc], writes=[Bs_])
                    S.op("dve", lambda e, s_=s_: e.reciprocal(out=s_[:, 2:3], in_=s_[:, 1:2]), reads=[Bs_], writes=[Bs_])
                    S.op("dve", lambda e, x_=x_, s_=s_: e.scalar_tensor_tensor(out=x_[:], in0=x_[:], scalar=s_[:, 2:3], in1=gain_bc[:], op0=ALU.mult, op1=ALU.mult),
                         reads=[Bx_, Bs_, Bgain], writes=[Bx_])
                    S.op("pool", lambda e, x_=x_, h_=h_: e.tensor_tensor(out=h_[:], in0=x_[:], in1=shift_bc[:], op=ALU.add), reads=[Bx_, Bshift], writes=[Bh_])
                    for half in range(2):
                        p_, Bp_ = pT[half]

                        def trf(e, half=half, p_=p_, h_=h_):
                            for j in range(8):
                                kc = half * 8 + j
                                ins = e.transpose(p_[:, j * 128:(j + 1) * 128], h_[:, kc * 128:(kc + 1) * 128], identb[:])
                            return ins
                        S.op("pe", trf, reads=[Bh_, Bidentb], writes=[Bp_])
                        S.op("act" if half == 0 else "dve",
                             (lambda e, half=half, p_=p_, t=t: e.activation(out=hT[:, half * 8:(half + 1) * 8, t * 128:(t + 1) * 128],
                                                                            in_=p_[:, :].rearrange("p (j n) -> p j n", n=128), func=AF.Copy)) if half == 0 else
                             (lambda e, half=half, p_=p_, t=t: e.tensor_copy(out=hT[:, half * 8:(half + 1) * 8, t * 128:(t + 1) * 128],
                                                                             in_=p_[:, :].rearrange("p (j n) -> p j n", n=128))),
                             reads=[Bp_], writes=[BhT])
                for q in range(4):
                    l_, Bl_ = ldq[q % 2]
                    for j in range(8):
                        S.dma("sp", l_[(j % 2) * 64:(j % 2) * 64 + 64, :, :].rearrange("p (r c) n -> p r c n", c=4)[:, :, j // 2, :],
                              OGs[j].rearrange("(r p) n -> p r n", p=64)[:, :, q * 2048 + go:q * 2048 + go + CG],
                              reads=[BOG], writes=[Bl_], semof=Bl_)
                    if q == 0:
                        S.op("dve", lambda e, l_=l_: e.tensor_scalar(out=osel[:], in0=l_[:], scalar1=selq[:, 0:1], scalar2=None, op0=ALU.mult),
                             reads=[Bl_, Bselq], writes=[Bosel])
                    else:
                        S.op("dve", lambda e, l_=l_, q=q: e.scalar_tensor_tensor(out=osel[:], in0=l_[:], scalar=selq[:, q:q + 1], in1=osel[:], op0=ALU.mult, op1=ALU.add),
                             reads=[Bl_, Bselq, Bosel], writes=[Bosel])
                for m in range(16):
                    w_, Bw_ = wmg[cnt["w"] % 2]
                    b_, Bb_ = wbr[cnt["w"] % 2]
                    cnt["w"] += 1
                    S.dma("pool", w_[:, :, 0:128], wmg_v[:, :, m * 128:(m + 1) * 128], writes=[Bw_], semof=Bw_)
                    S.dma("pool", w_[:, :, 128:256], wmg_v[:, :, 2048 + m * 128:2048 + (m + 1) * 128], writes=[Bw_], semof=Bw_)
                    S.dma("pool", b_[:, :, 0:128], wbr_r_v[:, :, m * 128:(m + 1) * 128], writes=[Bb_], semof=Bb_)
                    S.dma("pool", b_[:, :, 128:256], wbr_m_v[:, :, m * 128:(m + 1) * 128], writes=[Bb_], semof=Bb_)
                    (pgr, Bpgr), (pgm, Bpgm), (ppr, Bppr), (ppm, Bppm) = pg

                    def fg(e, w_=w_):
                        for kc in range(16):
                            e.matmul(pgr[:, 0:CG], w_[:, kc, 0:128], hT[:, kc, :], start=(kc == 0), stop=(kc == 15))
                        for kc in range(16):
                            ins = e.matmul(pgm[:, 0:CG], w_[:, kc, 128:256], hT[:, kc, :], start=(kc == 0), stop=(kc == 15))
                        return ins
                    S.op("pe", fg, reads=[Bw_, BhT], writes=[Bpgr, Bpgm])
                    S.op("act", lambda e: e.activation(out=sgr[:, :], in_=pgr[:, 0:CG], func=AF.Sigmoid), reads=[Bpgr], writes=[Bsgr])
                    S.op("act", lambda e: e.activation(out=sgm[:, :], in_=pgm[:, 0:CG], func=AF.Sigmoid), reads=[Bpgm], writes=[Bsgm])

                    def fb(e, b_=b_):
                        for j in range(8):
                            kc = (j // 2) * 4 + (j % 2)
                            e.matmul(ppr[:, 0:CG], b_[:, j, 0:128], osel[:, kc, :], start=(j == 0), stop=(j == 7))
                        for j in range(8):
                            kc = (j // 2) * 4 + 2 + (j % 2)
                            ins = e.matmul(ppm[:, 0:CG], b_[:, j, 128:256], osel[:, kc, :], start=(j == 0), stop=(j == 7))
                        return ins
                    S.op("pe", fb, reads=[Bb_, Bosel], writes=[Bppr, Bppm])
                    S.op("dve", lambda e: e.tensor_tensor(out=tr_[:, :], in0=ppr[:, 0:CG], in1=sgr[:, :], op=ALU.mult), reads=[Bppr, Bsgr], writes=[Btr_])
                    S.op("dve", lambda e: e.tensor_tensor(out=tm_[:, :], in0=ppm[:, 0:CG], in1=sgm[:, :], op=ALU.mult), reads=[Bppm, Bsgm], writes=[Btm_])
                    S.op("pool", lambda e, m=m: e.tensor_tensor(out=merged[:, m, :], in0=tr_[:, :], in1=tm_[:, :], op=ALU.add), reads=[Btr_, Btm_], writes=[Bmerged])
                for nb in range(4):
                    wo_, Bwo_ = wout[cnt["o"] % 2]
                    cnt["o"] += 1
                    S.dma("pool", wo_[:], wout_v[:, :, nb * 512:(nb + 1) * 512], writes=[Bwo_], semof=Bwo_)
                    for t in range(CG // 128):
                        r_ = cnt["r"] % 2
                        cnt["r"] += 1
                        po_, Bpo_ = pout[r_]
                        xr_, Bxr_ = xr[r_]
                        rs_, Brs_ = res[r_]
                        S.dma("sp", xr_[:], I["xm"][go + t * 128:go + (t + 1) * 128, nb * 512:(nb + 1) * 512], writes=[Bxr_], semof=Bxr_)

                        def fo(e, wo_=wo_, po_=po_, t=t):
                            for kc in range(16):
                                ins = e.matmul(po_[:, :], merged[:, kc, t * 128:(t + 1) * 128], wo_[:, kc, :], start=(kc == 0), stop=(kc == 15))
                            return ins
                        S.op("pe", fo, reads=[Bwo_, Bmerged], writes=[Bpo_])
                        S.op("dve", lambda e, po_=po_, rs_=rs_, nb=nb: e.tensor_tensor(out=rs_[:], in0=po_[:, :], in1=gate_bc[:, nb * 512:(nb + 1) * 512], op=ALU.mult),
                             reads=[Bpo_, Bgate], writes=[Brs_])
                        S.op("pool", lambda e, rs_=rs_, xr_=xr_: e.tensor_tensor(out=rs_[:], in0=rs_[:], in1=xr_[:], op=ALU.add), reads=[Brs_, Bxr_], writes=[Brs_])
                        S.dma("sp", out[go + t * 128:go + (t + 1) * 128, nb * 512:(nb + 1) * 512], rs_[:], reads=[Brs_], semof=Brs_)
            S.barrier()
            S.emit()
            S.end_phase()
    return nc


_NC_CACHE = {}


def kernel(**inputs):
    maps = _host_inputs(inputs)
    if "nc" not in _NC_CACHE:
        _NC_CACHE["nc"] = build()
    nc = _NC_CACHE["nc"]
    res = run_bass_kernel_spmd(nc, maps, core_ids=list(range(8)))
    outp = np.zeros((2, NX, D), np.float32)
    for c in range(8):
        b, g = c // 4, c % 4
        outp[b, 2048 * g:2048 * g + 2048] = res.results[c]["out"]
    return outp
```

```python
import os
from contextlib import ExitStack
import numpy as np
import ml_dtypes
import concourse.bass as bass
import concourse.mybir as mybir
from concourse.bass_utils import run_bass_kernel_spmd

F32 = mybir.dt.float32
BF16 = mybir.dt.bfloat16
ALU = mybir.AluOpType
AF = mybir.ActivationFunctionType

NT = 8448
NX = 8192
NCTX = 256
D = 2048
WC = 2368
GS = 256
C0 = float(np.exp(-0.5))
EPS = 1e-6
GN_EPS = 64e-5


class Tok:
    __slots__ = ("sem", "val", "key")

    def __init__(self, sem, val, key):
        self.sem = sem
        self.val = val
        self.key = key


class DSem:
    def __init__(self, sem, kind):
        self.sem = sem
        self.cnt = 0
        self.kind = kind


class Buf:
    def __init__(self, name, const=False):
        self.name = name
        self.w = None
        self.r = []
        self.const = const
        self.dsem = None
        self.dcnt = 0


class Sched:
    ENG = ["pe", "act", "dve", "pool", "sp"]

    def __init__(self, nc, stack):
        self.nc = nc
        self.stack = stack
        self.plan = {e: [] for e in self.ENG}
        self.ecnt = {e: 0 for e in self.ENG}
        self.esem = {}
        self.waited = {e: {} for e in self.ENG}
        self.nsem = 0
        for e in ("pe", "act", "dve", "pool"):
            self.esem[e] = self._newsem("e_" + e)
        self.dbufs = []
        self.free_dsems = {}
        self.ninst = 0

    def _newsem(self, name):
        self.nsem += 1
        return self.stack.enter_context(self.nc.semaphore(f"{name}_{self.nsem}"))

    def _waits(self, eng, toks):
        for t in toks:
            if t is None:
                continue
            if self.waited[eng].get(t.key, 0) >= t.val:
                continue
            if eng == "pe" and t.key == "e_pe":
                continue
            self.waited[eng][t.key] = t.val
            self.plan[eng].append(lambda e, sem=t.sem, v=t.val: e.wait_ge(sem, v))

    def _deps(self, reads, writes):
        deps = []
        for b in reads:
            deps.append(b.w)
        for b in writes:
            deps.append(b.w)
            deps.extend(b.r)
        return deps

    def _mark(self, tok, reads, writes):
        for b in reads:
            if not b.const:
                b.r.append(tok)
        for b in writes:
            b.w = tok
            b.r = []

    def op(self, eng, fn, reads=(), writes=()):
        self._waits(eng, self._deps(reads, writes))
        self.ecnt[eng] += 1
        self.ninst += 1
        tok = Tok(self.esem[eng], self.ecnt[eng], "e_" + eng)
        self.plan[eng].append(lambda e, fn=fn, sem=tok.sem: fn(e).then_inc(sem, 1))
        self._mark(tok, reads, writes)
        return tok

    def _dsem(self, b, kind):
        if b.dsem is None:
            fl = self.free_dsems.setdefault(kind, [])
            if fl:
                b.dsem = fl.pop()
            else:
                b.dsem = DSem(self._newsem("d" + kind), kind)
            self.dbufs.append(b)
        assert b.dsem.kind == kind, (b.name, b.dsem.kind, kind)

    def end_phase(self):
        for b in self.dbufs:
            self.free_dsems.setdefault(b.dsem.kind, []).append(b.dsem)
            b.dsem = None
            b.w = None
            b.r = []
        self.dbufs = []

    def dma(self, q, out_ap, in_ap, reads=(), writes=(), semof=None, **kw):
        self._waits(q, self._deps(reads, writes))
        b = semof
        self._dsem(b, "sw" if q == "pool" else "hw")
        ds = b.dsem
        ds.cnt += 16
        tok = Tok(ds.sem, ds.cnt, "d%d" % id(ds))
        self.plan[q].append(
            lambda e, o=out_ap, i=in_ap, sem=ds.sem, kw=kw: e.dma_start(out=o, in_=i, **kw).then_inc(sem, 16)
        )
        self._mark(tok, reads, writes)
        return tok

    def collective(self, kind, groups, in_ap, out_ap, reads, writes, semof):
        q = "pool"
        self._waits(q, self._deps(reads, writes))
        b = semof
        self._dsem(b, "cc")
        ds = b.dsem
        ds.cnt += 1
        tok = Tok(ds.sem, ds.cnt, "d%d" % id(ds))
        self.plan[q].append(
            lambda e, sem=ds.sem: e.collective_compute(
                kind, ALU.bypass, replica_groups=groups, ins=[in_ap], outs=[out_ap]
            ).then_inc(sem, 1)
        )
        self._mark(tok, reads, writes)
        return tok

    def barrier(self):
        toks = []
        for e in ("pe", "act", "dve", "pool"):
            if self.ecnt[e] > 0:
                toks.append(Tok(self.esem[e], self.ecnt[e], "e_" + e))
        for b in self.dbufs:
            toks.append(Tok(b.dsem.sem, b.dsem.cnt, "d%d" % id(b.dsem)))
        for e in self.ENG:
            self._waits(e, toks)

    def emit(self):
        plan = self.plan
        with self.nc.Block() as block:

            @block.tensor
            def _(e):
                for f in plan["pe"]:
                    f(e)

            @block.scalar
            def _(e):
                for f in plan["act"]:
                    f(e)

            @block.vector
            def _(e):
                for f in plan["dve"]:
                    f(e)

            @block.gpsimd
            def _(e):
                for f in plan["pool"]:
                    f(e)

            @block.sync
            def _(e):
                for f in plan["sp"]:
                    f(e)

        self.plan = {e: [] for e in self.ENG}


class Ctx:
    def __init__(self, nc, stack):
        self.nc = nc
        self.stack = stack
        self.n = 0

    def sb(self, shape, dt, name=None):
        self.n += 1
        name = (name or "t") + f"_{self.n}"
        t = self.stack.enter_context(self.nc.sbuf_tensor(name, list(shape), dt))
        return t, Buf(name)

    def ps(self, shape, dt, name=None):
        self.n += 1
        name = (name or "p") + f"_{self.n}"
        t = self.stack.enter_context(self.nc.psum_tensor(name, list(shape), dt))
        return t, Buf(name)

    def sub(self):
        c = Ctx(self.nc, ExitStack())
        c.n = self.n + 1000
        return c


def _host_consts():
    idx = np.arange(64)
    us = (idx[:, None] < idx[None, :]).astype(np.float32)
    ui = (idx[:, None] <= idx[None, :]).astype(np.float32)
    ls = (idx[:, None] > idx[None, :]).astype(np.float32)
    li = (idx[:, None] >= idx[None, :]).astype(np.float32)
    masks = np.zeros((128, 4, 128), np.float32)
    for i, m in enumerate((us, ui, ls, li)):
        masks[0:64, i, 0:64] = m
        masks[64:128, i, 64:128] = m
    bones = np.zeros((128, 128), np.float32)
    bones[0:64, 0:64] = 1
    bones[64:128, 64:128] = 1
    sel = np.zeros((2, 2, 128), np.float32)
    sel[0, 0, :] = 1
    sel[1, 1, :] = 1
    reset = np.ones((128, 512), np.float32)
    reset[:, ::64] = 0
    rows = np.repeat(np.arange(128), 64).astype(np.float32)
    cols = np.tile(np.arange(64), 128).astype(np.float32)
    inv = np.power(np.float32(10000.0), -np.arange(0, 32, 2, dtype=np.float32) / np.float32(32)).astype(np.float32)
    ang = np.zeros((64, NX), np.float32)
    for d in range(64):
        pos = rows if d < 32 else cols
        ang[d] = pos * inv[d % 16]
    cos = np.cos(ang.astype(np.float64)).astype(np.float32)
    sin = np.sin(ang.astype(np.float64)).astype(np.float32)
    P = np.zeros((64, 64), np.float32)
    for d in range(64):
        if d % 32 < 16:
            P[d, d + 16] = -1
        else:
            P[d, d - 16] = 1
    return dict(
        ident=np.eye(128, dtype=np.float32), masks=masks, bones=bones, onesf=np.ones((128, 128), np.float32),
        sel=sel, reset=reset, rope_cos=cos, rope_sin=sin, ropePT=np.ascontiguousarray(P.T),
    )


def _host_inputs(inp):
    f = lambda a: np.ascontiguousarray(a, dtype=np.float32)
    consts = _host_consts()
    w_in = inp["w_in"][0]
    offs = np.cumsum([0, 3072, 1024, 128, 128, 512, 256, 64, 1024, 4096])
    o_rkv, o_zr, o_wd, o_ad, o_qd, o_kvd, o_kr, o_zm, o_mg = offs[:9]
    maps = []
    for c in range(8):
        b, g = c // 4, c % 4
        ch = slice(256 * g, 256 * g + 256)
        cols = np.concatenate([
            o_rkv + np.arange(256 * g, 256 * g + 256),
            o_rkv + 1024 + np.arange(256 * g, 256 * g + 256),
            o_rkv + 2048 + np.arange(256 * g, 256 * g + 256),
            o_zr + np.arange(256 * g, 256 * g + 256),
            o_wd + np.arange(128), o_ad + np.arange(128),
            o_qd + np.arange(512), o_kvd + np.arange(256), o_kr + np.arange(64),
            o_zm + np.arange(256 * g, 256 * g + 256),
        ])
        assert cols.size == WC
        conv = inp["conv_rkv"][0]
        conv_fm = np.zeros((128, 6, 3), np.float32)
        for kind in range(3):
            for cc in range(2):
                cidx = kind * 1024 + 256 * g + cc * 128 + np.arange(128)
                conv_fm[:, kind * 2 + cc, :] = conv[:, cidx].T
        chanv = np.zeros((128, 2, 10), np.float32)
        for cc in range(2):
            cidx = 256 * g + cc * 128 + np.arange(128)
            chanv[:, cc, 0] = inp["w0"][0, 0, cidx]
            chanv[:, cc, 1] = inp["w0"][0, 1, cidx]
            chanv[:, cc, 2] = inp["a0"][0, 0, cidx]
            chanv[:, cc, 3] = inp["a0"][0, 1, cidx]
            chanv[:, cc, 4] = inp["k_k"][0, cidx]
            chanv[:, cc, 5] = inp["k_a"][0, cidx]
            chanv[:, cc, 6] = inp["ln_x_w"][0, cidx]
            chanv[:, cc, 7] = inp["ln_x_b"][0, cidx]
            chanv[:, cc, 8] = inp["r_k"][0].reshape(-1)[cidx]
        wlora = np.zeros((128, 2, 256), np.float32)
        for d in range(2):
            wlora[64 * d:64 * d + 64, 0, :] = inp["w_decay_up"][0, d][:, ch]
            wlora[64 * d:64 * d + 64, 1, :] = inp["w_a_up"][0, d][:, ch]
        mlav = np.zeros((128, 10), np.float32)
        mlav[:, 0:4] = inp["q_norm_w"][0].reshape(4, 128).T
        mlav[:, 4:6] = inp["kv_norm_w"][0].reshape(2, 128).T
        mlav[:, 6] = inp["q_gain"][0][:128]
        mlav[:, 7] = inp["k_gain"][0][:128]
        mlav[0:64, 8] = inp["q_gain"][0][128:]
        mlav[0:64, 9] = inp["k_gain"][0][128:]
        hq = [2 * g, 2 * g + 1]
        w_uq_c = np.concatenate([inp["w_uq"][0][:, h * 192:(h + 1) * 192] for h in hq], axis=1)
        w_ukv_c = np.concatenate([inp["w_ukv"][0][:, h * 256:(h + 1) * 256] for h in hq], axis=1)
        cfm = np.stack([inp["c"][b].reshape(16, 128).T, inp["c_ctx"].reshape(16, 128).T], axis=-1)
        selq = np.zeros((128, 4), np.float32)
        selq[:, g] = 1
        m = dict(
            xs=f(np.concatenate([inp["ctx"][b], inp["x"][b]], axis=0)),
            xm=f(inp["x"][b, 2048 * g:2048 * g + 2048]),
            cfm=f(cfm), norm_w_row=f(np.stack([inp["norm_w"][0]] * 2)), b_mod_row=f(np.stack([inp["b_mod"][0]] * 2)),
            w_mod=f(inp["w_mod"][0]), w_in_core=f(w_in[:, cols]), w_in_mg=f(w_in[:, o_mg:o_mg + 4096]),
            conv_fm=f(conv_fm), chanv=f(chanv), wlora=f(wlora), mlav=f(mlav), w_uq_c=f(w_uq_c), w_ukv_c=f(w_ukv_c),
            w_br_r=f(inp["w_branch_rwkv"][0]), w_br_m=f(inp["w_branch_mla"][0]), w_out=f(inp["w_out"][0]),
            selq=f(selq),
        )
        m.update({k: f(v) for k, v in consts.items()})
        maps.append(m)
    return maps


INPUT_SHAPES = dict(
    xs=[NT, D], xm=[2048, D], cfm=[128, 16, 2], norm_w_row=[2, D], b_mod_row=[2, 3 * D], w_mod=[D, 3 * D],
    w_in_core=[D, WC], w_in_mg=[D, 4096], conv_fm=[128, 6, 3], chanv=[128, 2, 10], wlora=[128, 2, 256],
    mlav=[128, 10], w_uq_c=[512, 384], w_ukv_c=[256, 512], w_br_r=[1024, D], w_br_m=[1024, D], w_out=[D, D],
    selq=[128, 4], ident=[128, 128], masks=[128, 4, 128], bones=[128, 128], onesf=[128, 128], sel=[2, 2, 128],
    reset=[128, 512], rope_cos=[64, NX], rope_sin=[64, NX], ropePT=[64, 64],
)


def build(debug=(), upto="C"):
    nc = bass.Bass("TRN2", target_bir_lowering=False)
    I = {k: nc.dram_tensor(k, s, F32, kind="ExternalInput").ap() for k, s in INPUT_SHAPES.items()}
    out = nc.dram_tensor("out", [2048, D], F32, kind="ExternalOutput").ap()

    def scratch(name, shape, dt):
        if name in debug:
            return nc.dram_tensor(name, shape, dt, kind="ExternalOutput").ap()
        return nc.dram_tensor(name, shape, dt).ap()

    BC = scratch("BC", [5, 128, D], F32)
    U_rkv = scratch("U_rkv", [768, NT], F32)
    G_zr = scratch("G_zr", [256, NX], F32)
    SG = scratch("SG", [512, NT], F32)
    AA = scratch("AA", [512, NT], F32)
    QN = scratch("QN", [256, NX], BF16)
    QR = scratch("QR", [128, NX], BF16)
    KN = scratch("KN", [256, NT], BF16)
    KR = scratch("KR", [64, NT], BF16)
    VT = scratch("VT", [2, 128, NT], BF16)
    G_zm = scratch("G_zm", [256, NX], F32)
    KSD = scratch("KSD", [128, 132], F32)
    YD = [scratch("YD0", [256, NX], F32), scratch("YD1", [256, NX], F32)]
    BD = [scratch("BD0", [256, NX], F32), scratch("BD1", [256, NX], F32)]
    DBG = [scratch(f"DBG{i}", [128, 512], BF16 if i < 2 else F32) for i in range(4)]
    OXs = [scratch(f"OX{j}", [64, NX], BF16) for j in range(8)]
    OGs = [scratch(f"OG{j}", [256, NX], BF16) for j in range(8)]

    WMG_b = scratch("WMG_b", [16, 128, 16 * 256], BF16)
    WBR_b = scratch("WBR_b", [16, 128, 8 * 256], BF16)
    WOUT_b = scratch("WOUT_b", [4, 128, 16 * 512], BF16)
    v3 = lambda ap, p=128: ap.rearrange("(c p) n -> p c n", p=p)
    U_v, Gzr_v, SG_v, AA_v = v3(U_rkv), v3(G_zr), v3(SG), v3(AA)
    QN_v, QR_v, KN_v, Gzm_v = v3(QN), v3(QR, 64), v3(KN), v3(G_zm)

    with ExitStack() as top:
        S = Sched(nc, top)
        T = Ctx(nc, top)

        identb, Bidentb = T.sb([128, 128], BF16, "identb")
        msk, Bmsk = T.sb([128, 4, 128], BF16, "msk")
        bones, Bbones = T.sb([128, 128], F32, "bones")
        onesf, Bonesf = T.sb([128, 128], F32, "onesf")
        selt, Bselt = T.sb([2, 2, 128], F32, "selt")
        resetm, Bresetm = T.sb([128, 512], F32, "resetm")
        ropePT, BropePT = T.sb([64, 64], F32, "ropePT")
        epsc, Bepsc = T.sb([128, 4], F32, "epsc")
        convw, Bconvw = T.sb([128, 6, 3], F32, "convw")
        chanv, Bchanv = T.sb([128, 2, 10], F32, "chanv")
        mlav, Bmlav = T.sb([128, 10], F32, "mlav")
        KS, BKS = T.sb([128, 132], F32, "KS")
        S.dma("pool", identb[:], I["ident"], writes=[Bidentb], semof=Bidentb)
        S.dma("pool", msk[:], I["masks"], writes=[Bmsk], semof=Bmsk)
        for t_, b_, k_ in ((bones, Bbones, "bones"), (onesf, Bonesf, "onesf"), (selt, Bselt, "sel"), (resetm, Bresetm, "reset"),
                           (ropePT, BropePT, "ropePT"), (convw, Bconvw, "conv_fm"), (chanv, Bchanv, "chanv"), (mlav, Bmlav, "mlav")):
            S.dma("sp", t_[:], I[k_], writes=[b_], semof=b_)
        S.op("pool", lambda e: e.memset(epsc[:, 0:1], EPS), writes=[Bepsc])
        S.op("pool", lambda e: e.memset(epsc[:, 1:2], 1e-12), writes=[Bepsc])
        S.op("pool", lambda e: e.memset(epsc[:, 2:3], GN_EPS), writes=[Bepsc])
        S.op("pool", lambda e: e.memset(epsc[:, 3:4], 0.0), writes=[Bepsc])
        for b_ in (Bidentb, Bmsk, Bbones, Bonesf, Bselt, Bresetm, BropePT, Bepsc, Bconvw, Bchanv, Bmlav):
            b_.const = True

        with ExitStack() as p0s:
            P = Ctx(nc, p0s); P.n = 100
            cf, Bcf = P.sb([128, 16, 2], F32, "cf")
            sc, Bsc = P.sb([128, 16, 2], BF16, "sc")
            b2, Bb2 = P.sb([2, 3 * D], F32, "b2")
            nw2, Bnw2 = P.sb([2, D], F32, "nw2")
            mrow, Bmrow = P.sb([2, 3 * D], F32, "mrow")
            grow, Bgrow = P.sb([2, D], F32, "grow")
            wm = [P.sb([128, 16, 512], BF16, "wm") for _ in range(2)]
            bct = [P.sb([128, D], F32, "bct") for _ in range(2)]
            pm, Bpm = P.ps([128, 512], F32, "pm")
            pb = [P.ps([128, 512], F32, "pb") for _ in range(2)]
            S.dma("sp", cf[:], I["cfm"], writes=[Bcf], semof=Bcf)
            S.dma("sp", b2[:], I["b_mod_row"], writes=[Bb2], semof=Bb2)
            S.dma("sp", nw2[:], I["norm_w_row"], writes=[Bnw2], semof=Bnw2)
            S.op("act", lambda e: e.activation(out=sc[:], in_=cf[:], func=AF.Silu), reads=[Bcf], writes=[Bsc])
            wmod_v = I["w_mod"].rearrange("(kc p) n -> p kc n", p=128)
            for cb in range(12):
                wt, Bwt = wm[cb % 2]
                S.dma("pool", wt[:], wmod_v[:, :, cb * 512:(cb + 1) * 512], writes=[Bwt], semof=Bwt)

                def mmf(e, wt=wt):
                    for kc in range(16):
                        ins = e.matmul(pm[0:2, :], sc[:, kc, :], wt[:, kc, :], start=(kc == 0), stop=(kc == 15))
                    return ins
                S.op("pe", mmf, reads=[Bsc, Bwt], writes=[Bpm])
                S.op("act", lambda e, cb=cb: e.activation(out=mrow[0:2, cb * 512:(cb + 1) * 512], in_=pm[0:2, :], func=AF.Copy),
                     reads=[Bpm], writes=[Bmrow])
            S.op("pool", lambda e: e.tensor_tensor(out=mrow[:], in0=mrow[:], in1=b2[:], op=ALU.add), reads=[Bmrow, Bb2], writes=[Bmrow])
            S.op("dve", lambda e: e.scalar_tensor_tensor(out=grow[:], in0=mrow[:, D:2 * D], scalar=1.0, in1=nw2[:],
                                                          op0=ALU.add, op1=ALU.mult), reads=[Bmrow, Bnw2], writes=[Bgrow])
            plan0 = [(0, grow, Bgrow, 0, 0), (1, mrow, Bmrow, 0, 0), (2, mrow, Bmrow, 2 * D, 0), (3, grow, Bgrow, 0, 1), (4, mrow, Bmrow, 0, 1)]
            k = 0
            for (idx, src, Bsrc, off, si) in plan0:
                st, Bst = bct[idx % 2]
                for blk in range(4):
                    pt_, Bpt_ = pb[k % 2]
                    k += 1
                    S.op("pe", lambda e, pt_=pt_, src=src, off=off, blk=blk, si=si: e.matmul(
                        pt_[:, :], selt[0:2, si, :], src[0:2, off + blk * 512: off + (blk + 1) * 512], start=True, stop=True),
                        reads=[Bsrc, Bselt], writes=[Bpt_])
                    S.op("act", lambda e, pt_=pt_, st=st, blk=blk: e.activation(out=st[:, blk * 512:(blk + 1) * 512], in_=pt_[:, :], func=AF.Copy),
                         reads=[Bpt_], writes=[Bst])
                S.dma("sp", BC[idx], st[:], reads=[Bst], semof=Bst)
            S.barrier()
            S.emit()
            S.end_phase()
        if upto == "0":
            return nc

        with ExitStack() as pas:
            P = Ctx(nc, pas); P.n = 200
            W, BW = P.sb([128, 16, WC], BF16, "W")
            wlora, Bwlora = P.sb([128, 2, 256], BF16, "wlora")
            wuq, Bwuq = P.sb([128, 4, 384], BF16, "wuq")
            wukv, Bwukv = P.sb([128, 2, 512], BF16, "wukv")
            gain_bc, Bgain = P.sb([128, D], F32, "gain_bc")
            shift_bc, Bshift = P.sb([128, D], F32, "shift_bc")
            xt = [P.sb([128, D], F32, "xt") for _ in range(2)]
            hm = [P.sb([128, D], BF16, "hm") for _ in range(2)]
            ss = [P.sb([128, 4], F32, "ss") for _ in range(2)]
            hmT = [P.sb([128, 16, GS], BF16, "hmT") for _ in range(2)]
            urkv = [P.sb([128, GS], F32, "urkv") for _ in range(2)]
            gz = [P.sb([128, GS], F32, "gz") for _ in range(2)]
            sga = [P.sb([128, GS], F32, "sga") for _ in range(2)]
            twd, Btwd = P.sb([128, GS], BF16, "twd")
            adb, Badb = P.sb([128, GS], BF16, "adb")
            qd, Bqd = P.sb([128, 4, GS], F32, "qd")
            sqq, Bsqq = P.sb([128, 4, GS], F32, "sqq")
            qn, Bqn = P.sb([128, 4, GS], BF16, "qn")
            rq, Brq = P.sb([128, GS], F32, "rq")
            qno, Bqno = P.sb([128, GS], F32, "qno")
            qro, Bqro = P.sb([64, GS], F32, "qro")
            sqh, Bsqh = P.sb([128, 2, GS], F32, "sqh")
            rh, Brh = P.sb([128, GS], F32, "rh")
            qnf, Bqnf = P.sb([128, GS], BF16, "qnf")
            qrg, Bqrg = P.sb([64, GS], F32, "qrg")
            t1, Bt1 = P.sb([64, GS], F32, "t1")
            t2, Bt2 = P.sb([64, GS], F32, "t2")
            qrf, Bqrf = P.sb([64, GS], BF16, "qrf")
            cost, Bcost = P.sb([64, GS], F32, "cost")
            sint, Bsint = P.sb([64, GS], F32, "sint")
            kvd, Bkvd = P.sb([128, 2, GS], F32, "kvd")
            kvn, Bkvn = P.sb([128, 2, GS], BF16, "kvn")
            kro, Bkro = P.sb([64, GS], F32, "kro")
            vts, Bvts = P.sb([128, 2, GS], BF16, "vts")
            kst, Bkst = P.sb([128, 8], F32, "kst")
            pT = [P.ps([128, 1024], BF16, "pT") for _ in range(2)]
            po = [P.ps([128, 512], F32, "po") for _ in range(3)]
            pst, Bpst = P.ps([128, 512], F32, "pst")
            pv, Bpv = P.ps([128, 512], F32, "pv")
            pks, Bpks = P.ps([128, 512], F32, "pks")
            cnt = {"po": 0, "tile": 0, "u": 0, "g": 0, "s": 0}

            def next_po():
                cnt["po"] += 1
                return po[cnt["po"] % 3]

            win_v = I["w_in_core"].rearrange("(kc p) n -> p kc n", p=128)
            S.dma("pool", W[:, :, :], win_v[:, :, :], writes=[BW], semof=BW)
            S.dma("pool", wlora[:], I["wlora"], writes=[Bwlora], semof=Bwlora)
            S.dma("pool", wuq[:], I["w_uq_c"].rearrange("(kc p) n -> p kc n", p=128), writes=[Bwuq], semof=Bwuq)
            S.dma("pool", wukv[:], I["w_ukv_c"].rearrange("(kc p) n -> p kc n", p=128), writes=[Bwukv], semof=Bwukv)
            S.dma("sp", gain_bc[:], BC[3], writes=[Bgain], semof=Bgain)
            S.dma("sp", shift_bc[:], BC[4], writes=[Bshift], semof=Bshift)

            def prep_tile(row0, hslot, t):
                s = cnt["tile"] % 2
                cnt["tile"] += 1
                x_, Bx_ = xt[s]
                h_, Bh_ = hm[s]
                s_, Bs_ = ss[s]
                hT, BhT = hmT[hslot]
                S.dma("sp", x_[:], I["xs"][row0:row0 + 128, :], writes=[Bx_], semof=Bx_)
                S.op("act", lambda e: e.activation(out=h_[:], in_=x_[:], func=AF.Square, accum_out=s_[:, 0:1]), reads=[Bx_], writes=[Bh_, Bs_])
                S.op("act", lambda e: e.activation(out=s_[:, 1:2], in_=s_[:, 0:1], func=AF.Sqrt, scale=1.0 / D, bias=epsc[:, 0:1]),
                     reads=[Bs_, Bepsc], writes=[Bs_])
                S.op("dve", lambda e: e.reciprocal(out=s_[:, 2:3], in_=s_[:, 1:2]), reads=[Bs_], writes=[Bs_])
                S.op("dve", lambda e: e.scalar_tensor_tensor(out=x_[:], in0=x_[:], scalar=s_[:, 2:3], in1=gain_bc[:], op0=ALU.mult, op1=ALU.mult),
                     reads=[Bx_, Bs_, Bgain], writes=[Bx_])
                S.op("pool", lambda e: e.tensor_tensor(out=h_[:], in0=x_[:], in1=shift_bc[:], op=ALU.add), reads=[Bx_, Bshift], writes=[Bh_])
                for half in range(2):
                    p_, Bp_ = pT[half]

                    def trf(e, half=half, p_=p_):
                        for j in range(8):
                            kc = half * 8 + j
                            ins = e.transpose(p_[:, j * 128:(j + 1) * 128], h_[:, kc * 128:(kc + 1) * 128], identb[:])
                        return ins
                    S.op("pe", trf, reads=[Bh_, Bidentb], writes=[Bp_])
                    cp = (lambda e, half=half, p_=p_: e.activation(out=hT[:, half * 8:(half + 1) * 8, t * 128:(t + 1) * 128],
                                                                   in_=p_[:, :].rearrange("p (j n) -> p j n", n=128), func=AF.Copy)) if half == 0 else \
                         (lambda e, half=half, p_=p_: e.tensor_copy(out=hT[:, half * 8:(half + 1) * 8, t * 128:(t + 1) * 128],
                                                                    in_=p_[:, :].rearrange("p (j n) -> p j n", n=128)))
                    S.op("act" if half == 0 else "dve", cp, reads=[Bp_], writes=[BhT])

            def rstd_from(psum_ap, Bps, out_t, Bout, npart, N, inv_n):
                S.op("act", lambda e: e.activation(out=out_t[0:npart, 0:N], in_=psum_ap, func=AF.Sqrt, scale=inv_n, bias=epsc[0:npart, 0:1]),
                     reads=[Bps, Bepsc], writes=[Bout])
                S.op("dve", lambda e: e.reciprocal(out=out_t[0:npart, 0:N], in_=out_t[0:npart, 0:N]), reads=[Bout], writes=[Bout])

            def rope_apply(src, Bsrc, N, dst_dram):
                pr, Bpr = next_po()
                S.op("pe", lambda e: e.matmul(pr[0:64, 0:N], ropePT[0:64, 0:64], src[0:64, 0:N], start=True, stop=True),
                     reads=[Bsrc, BropePT], writes=[Bpr])
                S.op("dve", lambda e: e.tensor_tensor(out=t1[0:64, 0:N], in0=src[0:64, 0:N], in1=cost[0:64, 0:N], op=ALU.mult),
                     reads=[Bsrc, Bcost], writes=[Bt1])
                S.op("dve", lambda e: e.tensor_tensor(out=t2[0:64, 0:N], in0=pr[0:64, 0:N], in1=sint[0:64, 0:N], op=ALU.mult),
                     reads=[Bpr, Bsint], writes=[Bt2])
                S.op("pool", lambda e: e.tensor_tensor(out=qrf[0:64, 0:N], in0=t1[0:64, 0:N], in1=t2[0:64, 0:N], op=ALU.add),
                     reads=[Bt1, Bt2], writes=[Bqrf])
                S.dma("sp", dst_dram, qrf[0:64, 0:N], reads=[Bqrf], semof=Bqrf)

            def proj_group(gi, hslot, inter):
                isx = gi > 0
                N = GS
                n0 = 256 + (gi - 1) * GS if isx else 0
                xo = (gi - 1) * GS
                ntile = N // 128
                hT, BhT = hmT[hslot]
                inter = list(inter)

                def mm_chunk(col0, M):
                    p_, Bp_ = next_po()

                    def f(e):
                        for kc in range(16):
                            ins = e.matmul(p_[0:M, 0:N], W[:, kc, col0:col0 + M], hT[:, kc, 0:N], start=(kc == 0), stop=(kc == 15))
                        return ins
                    S.op("pe", f, reads=[BW, BhT], writes=[Bp_])
                    if inter:
                        inter.pop(0)()
                    return p_, Bp_

                if isx:
                    S.dma("sp", cost[:, 0:N], I["rope_cos"][:, xo:xo + N], writes=[Bcost], semof=Bcost)
                    S.dma("sp", sint[:, 0:N], I["rope_sin"][:, xo:xo + N], writes=[Bsint], semof=Bsint)
                for j in range(6):
                    p_, Bp_ = mm_chunk(j * 128, 128)
                    u_, Bu_ = urkv[cnt["u"] % 2]
                    cnt["u"] += 1
                    S.op("act", lambda e, p_=p_, u_=u_: e.activation(out=u_[:, 0:N], in_=p_[:, 0:N], func=AF.Copy), reads=[Bp_], writes=[Bu_])
                    S.dma("sp", U_v[:, j, n0:n0 + N], u_[:, 0:N], reads=[Bu_], semof=Bu_)
                if isx:
                    for j in range(2):
                        p_, Bp_ = mm_chunk(768 + j * 128, 128)
                        g_, Bg_ = gz[cnt["g"] % 2]
                        cnt["g"] += 1
                        S.op("act", lambda e, p_=p_, g_=g_: e.activation(out=g_[:, 0:N], in_=p_[:, 0:N], func=AF.Silu), reads=[Bp_], writes=[Bg_])
                        S.dma("sp", Gzr_v[:, j, xo:xo + N], g_[:, 0:N], reads=[Bg_], semof=Bg_)
                p_, Bp_ = mm_chunk(1024, 128)
                S.op("act", lambda e, p_=p_: e.activation(out=twd[:, 0:N], in_=p_[:, 0:N], func=AF.Tanh), reads=[Bp_], writes=[Btwd])
                p_, Bp_ = mm_chunk(1152, 128)
                S.op("act", lambda e, p_=p_: e.activation(out=adb[:, 0:N], in_=p_[:, 0:N], func=AF.Copy), reads=[Bp_], writes=[Badb])
                for which, (src, Bsrc, dst_v, cbase) in enumerate(((twd, Btwd, SG_v, 0), (adb, Badb, AA_v, 2))):
                    for d in range(2):
                        for cc in range(2):
                            p_, Bp_ = next_po()
                            S.op("pe", lambda e, p_=p_, src=src, d=d, cc=cc, which=which: e.matmul(
                                p_[:, 0:N], wlora[64 * d:64 * d + 64, which, cc * 128:(cc + 1) * 128], src[64 * d:64 * d + 64, 0:N],
                                start=True, stop=True), reads=[Bwlora, Bsrc], writes=[Bp_])
                            s_, Bs_ = sga[cnt["s"] % 2]
                            cnt["s"] += 1
                            S.op("act", lambda e, p_=p_, s_=s_, d=d, cc=cc, cbase=cbase: e.activation(
                                out=s_[:, 0:N], in_=p_[:, 0:N], func=AF.Sigmoid, bias=chanv[:, cc, cbase + d:cbase + d + 1]),
                                reads=[Bp_, Bchanv], writes=[Bs_])
                            S.dma("sp", dst_v[:, d * 2 + cc, n0:n0 + N], s_[:, 0:N], reads=[Bs_], semof=Bs_)
                if isx:
                    for j in range(4):
                        p_, Bp_ = mm_chunk(1280 + j * 128, 128)
                        S.op("act", lambda e, p_=p_, j=j: e.activation(out=qd[:, j, 0:N], in_=p_[:, 0:N], func=AF.Copy), reads=[Bp_], writes=[Bqd])
                    S.op("pool", lambda e: e.tensor_tensor(out=sqq[:, :, 0:N], in0=qd[:, :, 0:N], in1=qd[:, :, 0:N], op=ALU.mult), reads=[Bqd], writes=[Bsqq])

                    def ssq(e):
                        for j in range(4):
                            ins = e.matmul(pst[:, 0:N], onesf[:, :], sqq[:, j, 0:N], start=(j == 0), stop=(j == 3))
                        return ins
                    S.op("pe", ssq, reads=[Bsqq, Bonesf], writes=[Bpst])
                    rstd_from(pst[:, 0:N], Bpst, rq, Brq, 128, N, 1.0 / 512)
                    for j in range(4):
                        S.op("dve", lambda e, j=j: e.scalar_tensor_tensor(out=qn[:, j, 0:N], in0=qd[:, j, 0:N], scalar=mlav[:, j:j + 1], in1=rq[:, 0:N],
                                                                          op0=ALU.mult, op1=ALU.mult), reads=[Bqd, Bmlav, Brq], writes=[Bqn])
                    for h in range(2):
                        p1, Bp1 = next_po()
                        p2, Bp2 = next_po()

                        def qup(e, h=h, p1=p1, p2=p2):
                            for kc in range(4):
                                e.matmul(p1[:, 0:N], wuq[:, kc, h * 192:h * 192 + 128], qn[:, kc, 0:N], start=(kc == 0), stop=(kc == 3))
                            for kc in range(4):
                                ins = e.matmul(p2[0:64, 0:N], wuq[:, kc, h * 192 + 128:h * 192 + 192], qn[:, kc, 0:N], start=(kc == 0), stop=(kc == 3))
                            return ins
                        S.op("pe", qup, reads=[Bwuq, Bqn], writes=[Bp1, Bp2])
                        S.op("act", lambda e, p1=p1: e.activation(out=qno[:, 0:N], in_=p1[:, 0:N], func=AF.Copy), reads=[Bp1], writes=[Bqno])
                        S.op("act", lambda e, p2=p2: e.activation(out=qro[0:64, 0:N], in_=p2[0:64, 0:N], func=AF.Copy), reads=[Bp2], writes=[Bqro])
                        S.op("pool", lambda e: e.tensor_tensor(out=sqh[:, 0, 0:N], in0=qno[:, 0:N], in1=qno[:, 0:N], op=ALU.mult), reads=[Bqno], writes=[Bsqh])
                        S.op("pool", lambda e: e.tensor_tensor(out=sqh[0:64, 1, 0:N], in0=qro[0:64, 0:N], in1=qro[0:64, 0:N], op=ALU.mult), reads=[Bqro], writes=[Bsqh])

                        def ssh(e):
                            e.matmul(pst[:, 0:N], onesf[:, :], sqh[:, 0, 0:N], start=True, stop=False)
                            return e.matmul(pst[:, 0:N], onesf[0:64, :], sqh[0:64, 1, 0:N], start=False, stop=True)
                        S.op("pe", ssh, reads=[Bsqh, Bonesf], writes=[Bpst])
                        rstd_from(pst[:, 0:N], Bpst, rh, Brh, 128, N, 1.0 / 192)
                        S.op("dve", lambda e: e.scalar_tensor_tensor(out=qnf[:, 0:N], in0=qno[:, 0:N], scalar=mlav[:, 6:7], in1=rh[:, 0:N],
                                                                      op0=ALU.mult, op1=ALU.mult), reads=[Bqno, Bmlav, Brh], writes=[Bqnf])
                        S.dma("sp", QN_v[:, h, xo:xo + N], qnf[:, 0:N], reads=[Bqnf], semof=Bqnf)
                        S.op("dve", lambda e: e.scalar_tensor_tensor(out=qrg[0:64, 0:N], in0=qro[0:64, 0:N], scalar=mlav[0:64, 8:9], in1=rh[0:64, 0:N],
                                                                      op0=ALU.mult, op1=ALU.mult), reads=[Bqro, Bmlav, Brh], writes=[Bqrg])
                        rope_apply(qrg, Bqrg, N, QR_v[:, h, xo:xo + N])
                for j in range(2):
                    p_, Bp_ = mm_chunk(1792 + j * 128, 128)
                    S.op("act", lambda e, p_=p_, j=j: e.activation(out=kvd[:, j, 0:N], in_=p_[:, 0:N], func=AF.Copy), reads=[Bp_], writes=[Bkvd])
                p_, Bp_ = mm_chunk(2048, 64)
                S.op("act", lambda e, p_=p_: e.activation(out=kro[0:64, 0:N], in_=p_[0:64, 0:N], func=AF.Copy), reads=[Bp_], writes=[Bkro])
                S.op("pool", lambda e: e.tensor_tensor(out=sqq[:, 0:2, 0:N], in0=kvd[:, :, 0:N], in1=kvd[:, :, 0:N], op=ALU.mult), reads=[Bkvd], writes=[Bsqq])

                def sskv(e):
                    for j in range(2):
                        ins = e.matmul(pst[:, 0:N], onesf[:, :], sqq[:, j, 0:N], start=(j == 0), stop=(j == 1))
                    return ins
                S.op("pe", sskv, reads=[Bsqq, Bonesf], writes=[Bpst])
                rstd_from(pst[:, 0:N], Bpst, rq, Brq, 128, N, 1.0 / 256)
                for j in range(2):
                    S.op("dve", lambda e, j=j: e.scalar_tensor_tensor(out=kvn[:, j, 0:N], in0=kvd[:, j, 0:N], scalar=mlav[:, 4 + j:5 + j], in1=rq[:, 0:N],
                                                                      op0=ALU.mult, op1=ALU.mult), reads=[Bkvd, Bmlav, Brq], writes=[Bkvn])
                S.op("pool", lambda e: e.tensor_tensor(out=sqh[0:64, 1, 0:N], in0=kro[0:64, 0:N], in1=kro[0:64, 0:N], op=ALU.mult), reads=[Bkro], writes=[Bsqh])
                for h in range(2):
                    p1, Bp1 = next_po()

                    def kup(e, h=h, p1=p1):
                        for kc in range(2):
                            ins = e.matmul(p1[:, 0:N], wukv[:, kc, h * 256:h * 256 + 128], kvn[:, kc, 0:N], start=(kc == 0), stop=(kc == 1))
                        return ins
                    S.op("pe", kup, reads=[Bwukv, Bkvn], writes=[Bp1])
                    S.op("act", lambda e, p1=p1: e.activation(out=qno[:, 0:N], in_=p1[:, 0:N], func=AF.Copy), reads=[Bp1], writes=[Bqno])
                    S.op("pool", lambda e: e.tensor_tensor(out=sqh[:, 0, 0:N], in0=qno[:, 0:N], in1=qno[:, 0:N], op=ALU.mult), reads=[Bqno], writes=[Bsqh])

                    def kss(e, h=h):
                        for t in range(ntile):
                            c = t * 2 + h
                            e.matmul(pks[:, c:c + 1], sqh[:, 0, t * 128:(t + 1) * 128], onesf[:, 0:1], start=True, stop=False)
                            ins = e.matmul(pks[:, c:c + 1], sqh[0:64, 1, t * 128:(t + 1) * 128], onesf[0:64, 0:1], start=False, stop=True)
                        return ins
                    S.op("pe", kss, reads=[Bsqh, Bonesf], writes=[Bpks])
                    S.op("dve", lambda e: e.tensor_scalar(out=qnf[:, 0:N], in0=qno[:, 0:N], scalar1=mlav[:, 7:8], scalar2=None, op0=ALU.mult),
                         reads=[Bqno, Bmlav], writes=[Bqnf])
                    S.dma("sp", KN_v[:, h, n0:n0 + N], qnf[:, 0:N], reads=[Bqnf], semof=Bqnf)

                    def vup(e, h=h):
                        for t in range(ntile):
                            for kc in range(2):
                                ins = e.matmul(pv[:, t * 128:(t + 1) * 128], kvn[:, kc, t * 128:(t + 1) * 128], wukv[:, kc, h * 256 + 128:h * 256 + 256],
                                               start=(kc == 0), stop=(kc == 1))
                        return ins
                    S.op("pe", vup, reads=[Bwukv, Bkvn], writes=[Bpv])
                    S.op("act", lambda e, h=h: e.activation(out=vts[:, h, 0:N], in_=pv[:, 0:N], func=AF.Copy), reads=[Bpv], writes=[Bvts])
                    S.dma("sp", VT[h, :, n0:n0 + N], vts[:, h, 0:N], reads=[Bvts], semof=Bvts)
                nk = ntile * 2
                t0 = (n0 // 128) * 2
                S.op("act", lambda e: e.activation(out=kst[:, 0:nk], in_=pks[:, 0:nk], func=AF.Sqrt, scale=1.0 / 192, bias=epsc[:, 0:1]),
                     reads=[Bpks, Bepsc], writes=[Bkst])
                S.op("dve", lambda e: e.reciprocal(out=kst[:, 0:nk], in_=kst[:, 0:nk]), reads=[Bkst], writes=[Bkst])
                S.op("dve", lambda e: e.tensor_scalar(out=KS[:, t0:t0 + nk], in0=kst[:, 0:nk], scalar1=float(192 ** -0.5), scalar2=None, op0=ALU.mult),
                     reads=[Bkst], writes=[BKS])
                S.op("dve", lambda e: e.tensor_scalar(out=qrg[0:64, 0:N], in0=kro[0:64, 0:N], scalar1=mlav[0:64, 9:10], scalar2=None, op0=ALU.mult),
                     reads=[Bkro, Bmlav], writes=[Bqrg])
                if isx:
                    rope_apply(qrg, Bqrg, N, KR[:, n0:n0 + N])
                else:
                    S.op("pool", lambda e: e.tensor_copy(out=qrf[0:64, 0:N], in_=qrg[0:64, 0:N]), reads=[Bqrg], writes=[Bqrf])
                    S.dma("sp", KR[:, n0:n0 + N], qrf[0:64, 0:N], reads=[Bqrf], semof=Bqrf)
                if isx:
                    for j in range(2):
                        p_, Bp_ = mm_chunk(2112 + j * 128, 128)
                        g_, Bg_ = gz[cnt["g"] % 2]
                        cnt["g"] += 1
                        S.op("act", lambda e, p_=p_, g_=g_: e.activation(out=g_[:, 0:N], in_=p_[:, 0:N], func=AF.Silu), reads=[Bp_], writes=[Bg_])
                        S.dma("sp", Gzm_v[:, j, xo:xo + N], g_[:, 0:N], reads=[Bg_], semof=Bg_)
                while inter:
                    inter.pop(0)()

            NGRP = 1 + NX // GS
            NGRP = int(os.environ.get("MK_NGRP", NGRP))
            for t in range(GS // 128):
                prep_tile(t * 128, 0, t)
            S.dma("sp", gain_bc[:], BC[0], writes=[Bgain], reads=[], semof=Bgain)
            S.dma("sp", shift_bc[:], BC[1], writes=[Bshift], reads=[], semof=Bshift)
            for gi in range(NGRP):
                inter = []
                if gi + 1 < NGRP:
                    r0 = 256 + gi * GS
                    inter = [(lambda t=t, r0=r0, hs=(gi + 1) % 2: prep_tile(r0 + t * 128, hs, t)) for t in range(GS // 128)]
                proj_group(gi, gi % 2, inter)
            if "KSD" in debug:
                S.dma("sp", KSD, KS[:], reads=[BKS], semof=BKS)
            S.barrier()
            S.emit()
            S.end_phase()
        if upto == "A":
            return nc

        GR = 256
        NCH = GR // 64
        NXG = NX // GR
        U_k = U_rkv.rearrange("(k c p) n -> p k c n", k=3, c=2, p=128)
        YD_v = [v3(YD[d]) for d in range(2)]
        BD_v = [v3(BD[d]) for d in range(2)]
        with ExitStack() as prs:
            P = Ctx(nc, prs); P.n = 300
            BWB = Buf("WB")
            wmg_v_ = I["w_in_mg"].rearrange("(kc p) n -> p kc n", p=128)
            wbr_r_v_ = I["w_br_r"].rearrange("(kc p) n -> p kc n", p=128)
            wbr_m_v_ = I["w_br_m"].rearrange("(kc p) n -> p kc n", p=128)
            wout_v_ = I["w_out"].rearrange("(kc p) n -> p kc n", p=128)
            for m in range(16):
                d_ = WMG_b[m].rearrange("p (k n) -> p k n", n=256)
                S.dma("pool", d_[:, :, 0:128], wmg_v_[:, :, m * 128:(m + 1) * 128], semof=BWB)
                S.dma("pool", d_[:, :, 128:256], wmg_v_[:, :, 2048 + m * 128:2048 + (m + 1) * 128], semof=BWB)
                d_ = WBR_b[m].rearrange("p (k n) -> p k n", n=256)
                S.dma("pool", d_[:, :, 0:128], wbr_r_v_[:, :, m * 128:(m + 1) * 128], semof=BWB)
                S.dma("pool", d_[:, :, 128:256], wbr_m_v_[:, :, m * 128:(m + 1) * 128], semof=BWB)
            for nb in range(4):
                S.dma("pool", WOUT_b[nb].rearrange("p (k n) -> p k n", n=512), wout_v_[:, :, nb * 512:(nb + 1) * 512], semof=BWB)
            omka, Bomka = P.sb([128, 2], F32, "omka")
            S.op("dve", lambda e: e.tensor_scalar(out=omka[:, :], in0=chanv[:, :, 5], scalar1=-1.0, scalar2=1.0, op0=ALU.mult, op1=ALU.add),
                 reads=[Bchanv], writes=[Bomka])

            class CP:
                pass
            cps = []
            for cc in range(2):
                for d in range(2):
                    c_ = CP()
                    c_.cc, c_.d = cc, d
                    for nm, shp, dt in (("ub", [128, 3, GR + 2], F32), ("cv", [128, 3, GR], F32), ("sgt", [128, GR], F32), ("aat", [128, GR], F32),
                                        ("sq", [128, GR], F32), ("rs", [128, GR], F32), ("kk", [128, GR], F32), ("ff", [128, GR], F32),
                                        ("kmod", [128, GR], F32), ("akk", [128, GR], F32), ("Pc", [128, GR], F32), ("Ei", [128, GR], F32),
                                        ("Ee", [128, GR], F32), ("g", [128, GR], F32), ("gp", [128, GR], F32), ("gi", [128, GR], F32),
                                        ("NA", [128, 256], BF16),
                                        ("KA", [128, 256], BF16), ("A0", [128, 128], BF16), ("PW0", [128, 256], BF16), ("PW1", [128, 256], BF16),
                                        ("Tm0", [128, 128], BF16), ("Tm1", [128, 128], BF16), ("TR", [128, 384], BF16), ("Xb", [128, 128], BF16),
                                        ("Ub", [128, 128], BF16), ("H", [128, 128], F32), ("Hb", [128, 128], BF16), ("S1", [128, 128], F32),
                                        ("pr", [128, GR], F32), ("bon", [128, GR], F32)):
                        t_, b_ = P.sb(shp, dt, nm)
                        setattr(c_, nm, t_)
                        setattr(c_, "B" + nm, b_)
                    for nm, shp, dt in (("gtot", [128, NCH], F32), ("AR", [128, NCH, 256], BF16), ("BE", [128, NCH, 128], BF16),
                                        ("KT", [128, NCH, 128], BF16), ("VB", [128, NCH, 128], BF16), ("Yg", [128, GR], F32)):
                        lst = [P.sb(shp, dt, nm) for _ in range(2)]
                        setattr(c_, nm, [x[0] for x in lst])
                        setattr(c_, "B" + nm, [x[1] for x in lst])
                    c_.bk1, c_.Bbk1 = P.ps([128, 512], F32, "bk1")
                    c_.bk2, c_.BpAD = P.ps([128, 512], F32, "bk2")
                    c_.BpTT = c_.BpAD
                    for nm in ("H", "Hb"):
                        t_ = getattr(c_, nm)
                        b_ = getattr(c_, "B" + nm)
                        S.op("pool", lambda e, t_=t_: e.memset(t_[:], 0.0), writes=[b_])
                    for nm in ("AR", "BE", "KT", "VB"):
                        for sl_ in range(2):
                            t_ = getattr(c_, nm)[sl_]
                            b_ = getattr(c_, "B" + nm)[sl_]
                            S.op("pool", lambda e, t_=t_: e.memset(t_[:], 0.0), writes=[b_])
                    cps.append(c_)

            c3 = lambda ap: ap.rearrange("p (c t) -> p c t", t=64)

            def prep_gen(c, sl, n0, N, s0, s1, xo):
                cc, d = c.cc, c.d
                AR, BAR = c.AR[sl], c.BAR[sl]
                BE, BBE = c.BE[sl], c.BBE[sl]
                KT, BKT = c.KT[sl], c.BKT[sl]
                VB, BVB = c.VB[sl], c.BVB[sl]
                gtot, Bgtot = c.gtot[sl], c.Bgtot[sl]
                lo, hi = n0 - 1, n0 + N + 1
                dl, dh = 0, N + 2
                if n0 == s0:
                    S.op("pool", lambda e: e.memset(c.ub[:, :, 0:1], 0.0), writes=[c.Bub])
                    lo, dl = n0, 1
                if n0 + N == s1:
                    S.op("pool", lambda e: e.memset(c.ub[:, :, N + 1:N + 2], 0.0), writes=[c.Bub])
                    hi, dh = n0 + N, N + 1
                S.dma("sp", c.ub[:, :, dl:dh], U_k[:, :, cc, lo:hi], writes=[c.Bub], semof=c.Bub)
                S.dma("sp", c.sgt[:, 0:N], SG_v[:, d * 2 + cc, n0:n0 + N], writes=[c.Bsgt], semof=c.Bsgt)
                S.dma("sp", c.aat[:, 0:N], AA_v[:, d * 2 + cc, n0:n0 + N], writes=[c.Baat], semof=c.Baat)
                yield
                for kind in range(3):
                    ch = kind * 2 + cc
                    S.op("act", lambda e, kind=kind, ch=ch: e.activation(out=c.cv[:, kind, 0:N], in_=c.ub[:, kind, 1:N + 1], func=AF.Copy,
                                                                         scale=convw[:, ch, 1:2]), reads=[c.Bub, Bconvw], writes=[c.Bcv])
                S.op("dve", lambda e: e.tensor_tensor_scan(out=c.Pc[:, 0:N], data0=resetm[:, 0:N], data1=c.sgt[:, 0:N], initial=0.0, op0=ALU.mult, op1=ALU.add),
                     reads=[Bresetm, c.Bsgt], writes=[c.BPc])
                yield
                for tap in (0, 2):
                    for kind in range(3):
                        ch = kind * 2 + cc
                        S.op("dve", lambda e, kind=kind, ch=ch, tap=tap: e.scalar_tensor_tensor(
                            out=c.cv[:, kind, 0:N], in0=c.ub[:, kind, tap:tap + N], scalar=convw[:, ch, tap:tap + 1],
                            in1=c.cv[:, kind, 0:N], op0=ALU.mult, op1=ALU.add), reads=[c.Bub, Bconvw, c.Bcv], writes=[c.Bcv])
                    yield
                nch = N // 64
                tot = c3(c.Pc[:, 0:N])[:, :, 63]
                if d == 0:
                    S.op("pool", lambda e: e.tensor_tensor(out=c.Ee[:, 0:N], in0=c.Pc[:, 0:N], in1=c.sgt[:, 0:N], op=ALU.subtract), reads=[c.BPc, c.Bsgt], writes=[c.BEe])
                    Ei, BEi = c.Pc, c.BPc
                else:
                    for k_ in range(nch):
                        S.op("dve", lambda e, k_=k_: e.tensor_scalar(out=c.Ee[:, k_ * 64:(k_ + 1) * 64], in0=c.Pc[:, k_ * 64:(k_ + 1) * 64], scalar1=-1.0,
                                                                     scalar2=c.Pc[:, k_ * 64 + 63:k_ * 64 + 64], op0=ALU.mult, op1=ALU.add),
                             reads=[c.BPc], writes=[c.BEe])
                    S.op("pool", lambda e: e.tensor_tensor(out=c.Ei[:, 0:N], in0=c.Ee[:, 0:N], in1=c.sgt[:, 0:N], op=ALU.add), reads=[c.BEe, c.Bsgt], writes=[c.BEi])
                    Ei, BEi = c.Ei, c.BEi
                S.op("act", lambda e: e.activation(out=c.sq[:, 0:N], in_=c.cv[:, 1, 0:N], func=AF.Square, scale=chanv[:, cc, 4:5]),
                     reads=[c.Bcv, Bchanv], writes=[c.Bsq])
                yield
                st_ = c.bk1[:, 0:N]
                Bst_ = c.Bbk1
                S.op("pe", lambda e: e.matmul(st_, bones[:, :], c.sq[:, 0:N], start=True, stop=True), reads=[c.Bsq, Bbones], writes=[Bst_])
                S.op("act", lambda e: e.activation(out=c.rs[:, 0:N], in_=st_, func=AF.Sqrt, bias=epsc[:, 1:2], scale=1.0), reads=[Bepsc], writes=[c.Brs, Bst_])
                yield
                S.op("act", lambda e: e.activation(out=c.g[:, 0:N], in_=Ei[:, 0:N], func=AF.Exp, scale=-C0), reads=[BEi], writes=[c.Bg])
                S.op("act", lambda e: e.activation(out=c.gp[:, 0:N], in_=c.Ee[:, 0:N], func=AF.Exp, scale=-C0), reads=[c.BEe], writes=[c.Bgp])
                S.op("act", lambda e: e.activation(out=c.gi[:, 0:N], in_=Ei[:, 0:N], func=AF.Exp, scale=C0), reads=[BEi], writes=[c.Bgi])
                S.op("act", lambda e: e.activation(out=gtot[:, 0:nch], in_=tot, func=AF.Exp, scale=-C0), reads=[c.BPc], writes=[Bgtot])
                S.op("dve", lambda e: e.tensor_scalar(out=c.ff[:, 0:N], in0=c.aat[:, 0:N], scalar1=chanv[:, cc, 5:6], scalar2=omka[:, cc:cc + 1],
                                                       op0=ALU.mult, op1=ALU.add), reads=[c.Baat, Bchanv, Bomka], writes=[c.Bff])
                S.op("pool", lambda e: e.tensor_tensor(out=c.kmod[:, 0:N], in0=c.cv[:, 1, 0:N], in1=c.ff[:, 0:N], op=ALU.mult), reads=[c.Bcv, c.Bff], writes=[c.Bkmod])
                S.op("dve", lambda e: e.reciprocal(out=c.rs[:, 0:N], in_=c.rs[:, 0:N]), reads=[c.Brs], writes=[c.Brs])
                yield
                S.op("dve", lambda e: e.scalar_tensor_tensor(out=c.pr[:, 0:N], in0=c.cv[:, 0, 0:N], scalar=chanv[:, cc, 8:9], in1=c.kmod[:, 0:N],
                                                              op0=ALU.mult, op1=ALU.mult), reads=[c.Bcv, Bchanv, c.Bkmod], writes=[c.Bpr])
                S.op("dve", lambda e: e.scalar_tensor_tensor(out=c.kk[:, 0:N], in0=c.cv[:, 1, 0:N], scalar=chanv[:, cc, 4:5], in1=c.rs[:, 0:N],
                                                              op0=ALU.mult, op1=ALU.mult), reads=[c.Bcv, Bchanv, c.Brs], writes=[c.Bkk])
                yield
                S.op("pe", lambda e: e.matmul(st_, bones[:, :], c.pr[:, 0:N], start=True, stop=True), reads=[c.Bpr, Bbones], writes=[Bst_])
                S.op("dve", lambda e: e.tensor_tensor(out=c.bon[:, 0:N], in0=st_, in1=c.cv[:, 2, 0:N], op=ALU.mult), reads=[c.Bcv], writes=[c.Bbon, Bst_])
                if xo is not None:
                    S.dma("sp", BD_v[d][:, cc, xo:xo + N], c.bon[:, 0:N], reads=[c.Bbon], semof=c.Bbon)
                yield
                S.op("pool", lambda e: e.tensor_tensor(out=c.akk[:, 0:N], in0=c.aat[:, 0:N], in1=c.kk[:, 0:N], op=ALU.mult), reads=[c.Baat, c.Bkk], writes=[c.Bakk])
                for hh in range(2):
                    ps_ = slice(64 * hh, 64 * hh + 64)
                    o1 = slice(64 * hh, 64 * hh + 64)
                    o2 = slice(128 + 64 * hh, 128 + 64 * hh + 64)
                    S.op("dve", lambda e, ps_=ps_, o2=o2: e.tensor_tensor(out=AR[ps_, 0:nch, o2], in0=c3(c.cv[ps_, 0, 0:N]), in1=c3(c.g[ps_, 0:N]), op=ALU.mult),
                         reads=[c.Bcv, c.Bg], writes=[BAR])
                    S.op("dve", lambda e, ps_=ps_, o1=o1: e.scalar_tensor_tensor(out=AR[ps_, 0:nch, o1], in0=c3(c.kk[ps_, 0:N]), scalar=-1.0, in1=c3(c.gp[ps_, 0:N]),
                                                                                 op0=ALU.mult, op1=ALU.mult), reads=[c.Bkk, c.Bgp], writes=[BAR])
                    S.op("pool", lambda e, ps_=ps_, o1=o1: e.tensor_tensor(out=KT[ps_, 0:nch, o1], in0=c3(c.kmod[ps_, 0:N]), in1=c3(c.gi[ps_, 0:N]), op=ALU.mult),
                         reads=[c.Bkmod, c.Bgi], writes=[BKT])
                    S.op("pool", lambda e, ps_=ps_, o1=o1: e.tensor_copy(out=VB[ps_, 0:nch, o1], in_=c3(c.cv[ps_, 2, 0:N])), reads=[c.Bcv], writes=[BVB])
                yield
                for hh in range(2):
                    ps_ = slice(64 * hh, 64 * hh + 64)
                    o1 = slice(64 * hh, 64 * hh + 64)
                    S.op("pool", lambda e, ps_=ps_, o1=o1: e.tensor_tensor(out=BE[ps_, 0:nch, o1], in0=c3(c.akk[ps_, 0:N]), in1=c3(c.gi[ps_, 0:N]), op=ALU.mult),
                         reads=[c.Bakk, c.Bgi], writes=[BBE])
                yield

            def chunk_gen(c, sl, k, want_y):
                AR, BAR = c.AR[sl], c.BAR[sl]
                BE, BBE = c.BE[sl], c.BBE[sl]
                KT, BKT = c.KT[sl], c.BKT[sl]
                VB, BVB = c.VB[sl], c.BVB[sl]
                gtot, Bgtot = c.gtot[sl], c.Bgtot[sl]
                Yg, BYg = c.Yg[sl], c.BYg[sl]
                bk1, Bbk1, bk2, BpAD, BpTT = c.bk1, c.Bbk1, c.bk2, c.BpAD, c.BpTT
                mN = (msk[:, 0:2, :] if c.d == 0 else msk[:, 2:4, :]).rearrange("p a b -> p (a b)")
                mA = msk[:, 2, :] if c.d == 0 else msk[:, 0, :]

                def f1(e):
                    e.matmul(bk1[:, 0:256], BE[:, k, :], AR[:, k, :], start=True, stop=True)
                    return e.matmul(bk1[:, 256:512], KT[:, k, :], AR[:, k, :], start=True, stop=True)
                S.op("pe", f1, reads=[BBE, BKT, BAR], writes=[Bbk1])
                S.op("pe", lambda e: e.matmul(bk2[:, 0:128], AR[:, k, 0:128], BE[:, k, :], start=True, stop=True), reads=[BAR, BBE], writes=[BpAD])
                S.op("dve", lambda e: e.tensor_tensor(out=c.NA[:, :], in0=bk1[:, 0:256], in1=mN, op=ALU.mult), reads=[Bmsk], writes=[c.BNA, Bbk1])
                S.op("dve", lambda e: e.tensor_tensor(out=c.KA[:, :], in0=bk1[:, 256:512], in1=mN, op=ALU.mult), reads=[Bmsk], writes=[c.BKA, Bbk1])
                S.op("dve", lambda e: e.tensor_tensor(out=c.A0[:, :], in0=bk2[:, 0:128], in1=mA, op=ALU.mult), reads=[Bmsk], writes=[c.BA0, BpAD])
                yield
                def f2(e):
                    e.matmul(bk1[:, 0:128], BE[:, k, :], identb[:, :], start=True, stop=True)
                    e.matmul(bk1[:, 128:256], KT[:, k, :], identb[:, :], start=True, stop=True)
                    return e.matmul(bk1[:, 256:384], VB[:, k, :], identb[:, :], start=True, stop=True)
                S.op("pe", f2, reads=[BBE, BKT, BVB, Bidentb], writes=[Bbk1])
                S.op("act", lambda e: e.activation(out=c.TR[:, :], in_=bk1[:, 0:384], func=AF.Copy), reads=[], writes=[c.BTR, Bbk1])
                yield
                S.op("pool", lambda e: e.tensor_tensor(out=c.Tm0[:, :], in0=c.NA[:, 0:128], in1=identb[:, :], op=ALU.add), reads=[c.BNA, Bidentb], writes=[c.BTm0])
                Nk, BNk, Ak, BAk = c.NA[:, 0:128], c.BNA, c.A0[:, :], c.BA0
                Tc, BTc = c.Tm0, c.BTm0
                for lvl in range(5):
                    pw, Bpw = (c.PW0, c.BPW0) if lvl % 2 == 0 else (c.PW1, c.BPW1)
                    if lvl < 4:
                        def f3(e, Nk=Nk, Ak=Ak):
                            e.matmul(bk2[:, 0:128], Ak, Nk, start=True, stop=True)
                            return e.matmul(bk2[:, 128:256], Nk, Ak, start=True, stop=True)
                        S.op("pe", f3, reads=[BNk, BAk], writes=[BpAD])
                        S.op("act", lambda e, pw=pw: e.activation(out=pw[:, :], in_=bk2[:, 0:256], func=AF.Copy), reads=[BpAD], writes=[Bpw])
                    else:
                        S.op("pe", lambda e, Nk=Nk, Ak=Ak: e.matmul(bk2[:, 128:256], Nk, Ak, start=True, stop=True), reads=[BNk, BAk], writes=[BpAD])
                        S.op("act", lambda e, pw=pw: e.activation(out=pw[:, 128:256], in_=bk2[:, 128:256], func=AF.Copy), reads=[BpAD], writes=[Bpw])
                    yield
                    Nk, BNk, Ak, BAk = pw[:, 0:128], Bpw, pw[:, 128:256], Bpw
                    Tn, BTn = (c.Tm1, c.BTm1) if lvl % 2 == 0 else (c.Tm0, c.BTm0)
                    S.op("pe", lambda e, Ak=Ak, Tc=Tc: e.matmul(bk1[:, 384:512], Ak, Tc[:, :], start=True, stop=True), reads=[BAk, BTc], writes=[Bbk1])
                    S.op("dve", lambda e, Tc=Tc, Tn=Tn: e.tensor_tensor(out=Tn[:, :], in0=bk1[:, 384:512], in1=Tc[:, :], op=ALU.add), reads=[BTc], writes=[BTn, Bbk1])
                    Tc, BTc = Tn, BTn
                    yield
                Tf, BTf = Tc, BTc
                def fx(e):
                    e.matmul(bk1[:, 0:128], c.KA[:, 0:128], c.TR[:, 256:384], start=True, stop=False)
                    return e.matmul(bk1[:, 0:128], AR[:, k, 0:128], c.Hb[:, :], start=False, stop=True)
                S.op("pe", fx, reads=[c.BKA, c.BTR, BAR, c.BHb], writes=[Bbk1])
                S.op("dve", lambda e: e.tensor_copy(out=c.Xb[:, :], in_=bk1[:, 0:128]), reads=[Bbk1], writes=[c.BXb])
                yield
                S.op("pe", lambda e: e.matmul(bk1[:, 128:256], Tf[:, :], c.Xb[:, :], start=True, stop=True), reads=[BTf, c.BXb], writes=[Bbk1])
                S.op("act", lambda e: e.activation(out=c.Ub[:, :], in_=bk1[:, 128:256], func=AF.Copy), reads=[Bbk1], writes=[c.BUb])
                yield

                def fh(e):
                    e.matmul(bk1[:, 256:384], c.TR[:, 128:256], c.TR[:, 256:384], start=True, stop=False)
                    ins = e.matmul(bk1[:, 256:384], c.TR[:, 0:128], c.Ub[:, :], start=False, stop=True)
                    if want_y:
                        e.matmul(bk1[:, 384:512], c.Hb[:, :], AR[:, k, 128:256], start=True, stop=False)
                        e.matmul(bk1[:, 384:512], c.Ub[:, :], c.NA[:, 128:256], start=False, stop=False)
                        ins = e.matmul(bk1[:, 384:512], c.TR[:, 256:384], c.KA[:, 128:256], start=False, stop=True)
                    return ins
                S.op("pe", fh, reads=[c.BTR, c.BUb, c.BHb, BAR, c.BNA, c.BKA], writes=[Bbk1])
                S.op("dve", lambda e: e.tensor_tensor(out=c.S1[:, :], in0=bk1[:, 256:384], in1=c.H[:, :], op=ALU.add), reads=[Bbk1, c.BH], writes=[c.BS1])
                if want_y:
                    for hh in range(2):
                        ps_ = slice(64 * hh, 64 * hh + 64)
                        S.op("act", lambda e, ps_=ps_, hh=hh: e.activation(out=Yg[ps_, k * 64:(k + 1) * 64], in_=bk1[ps_, 384 + 64 * hh:384 + 64 * hh + 64], func=AF.Copy),
                             reads=[], writes=[BYg, Bbk1])
                yield
                S.op("act", lambda e: e.activation(out=c.Hb[:, :], in_=c.S1[:, :], func=AF.Copy, scale=gtot[:, k:k + 1]), reads=[c.BS1, Bgtot], writes=[c.BHb])
                S.op("dve", lambda e: e.tensor_scalar(out=c.H[:, :], in0=c.S1[:, :], scalar1=gtot[:, k:k + 1], scalar2=None, op0=ALU.mult),
                     reads=[c.BS1, Bgtot], writes=[c.BH])
                yield

            def step_info(c, step):
                if step == 0:
                    return 0, 0, 256, None
                xg = (step - 1) if c.d == 0 else (NXG - step)
                return 256 + xg * GR, 256, NT, xg * GR

            def run_rr(gens):
                alive = list(gens)
                while alive:
                    nxt = []
                    for g_ in alive:
                        try:
                            next(g_)
                            nxt.append(g_)
                        except StopIteration:
                            pass
                    alive = nxt

            NSTEP = int(os.environ.get("MK_RSTEPS", 1 + NXG))

            def mk_prep(c, step):
                n0, s0, s1, xo = step_info(c, step)
                return prep_gen(c, step % 2, n0, GR, s0, s1, xo)

            run_rr([mk_prep(c, 0) for c in cps])
            for step in range(NSTEP):
                isx = step > 0
                sl = step % 2

                def seq(c):
                    for ci in range(NCH):
                        k = ci if c.d == 0 else NCH - 1 - ci
                        yield from chunk_gen(c, sl, k, isx)
                    if isx:
                        xo = step_info(c, step)[3]
                        S.dma("sp", YD_v[c.d][:, c.cc, xo:xo + GR], c.Yg[sl][:, :], reads=[c.BYg[sl]], semof=c.BYg[sl])

                gens = [seq(c) for c in cps]
                preps = [mk_prep(c, step + 1) for c in cps] if step + 1 < NSTEP else []
                rnd = 0
                alive = gens
                while alive or preps:
                    nxt = []
                    for g_ in alive:
                        try:
                            next(g_)
                            nxt.append(g_)
                        except StopIteration:
                            pass
                    alive = nxt
                    rnd += 1
                    if preps and (rnd % 4 == 0 or not alive):
                        np_ = []
                        for g_ in preps:
                            try:
                                next(g_)
                                np_.append(g_)
                            except StopIteration:
                                pass
                        preps = np_
            S.barrier()
            S.emit()
            S.end_phase()
        if upto == "R":
            return nc

        NF = 512
        BOX = Buf("OX")
        with ExitStack() as pfs:
            P = Ctx(nc, pfs); P.n = 400
            ld = [[P.sb([128, NF], F32, "fld") for _ in range(5)] for _ in range(2)]
            yy, Byy = P.sb([128, NF], F32, "yy")
            bs, Bbs = P.sb([128, NF], F32, "bs")
            ysq, Bysq = P.sb([128, NF], F32, "ysq")
            mm_, Bmm_ = P.sb([128, NF], F32, "mm")
            msq, Bmsq = P.sb([128, NF], F32, "msq")
            var, Bvar = P.sb([128, NF], F32, "var")
            yc, Byc = P.sb([128, NF], F32, "yc")
            ob = [P.sb([128, NF], BF16, "ob") for _ in range(2)]
            ps1 = [P.ps([128, 512], F32, "ps1") for _ in range(2)]
            ps2 = [P.ps([128, 512], F32, "ps2") for _ in range(2)]
            it = 0
            for cc in range(2):
                for ti in range(NX // NF):
                    xo = ti * NF
                    sl = it % 2
                    (y0, By0), (y1, By1), (b0, Bb0), (b1, Bb1), (gzt, Bgzt) = ld[sl]
                    S.dma("sp", y0[:], YD_v[0][:, cc, xo:xo + NF], writes=[By0], semof=By0)
                    S.dma("sp", y1[:], YD_v[1][:, cc, xo:xo + NF], writes=[By1], semof=By1)
                    S.dma("sp", b0[:], BD_v[0][:, cc, xo:xo + NF], writes=[Bb0], semof=Bb0)
                    S.dma("sp", b1[:], BD_v[1][:, cc, xo:xo + NF], writes=[Bb1], semof=Bb1)
                    S.dma("sp", gzt[:], Gzr_v[:, cc, xo:xo + NF], writes=[Bgzt], semof=Bgzt)
                    p1, Bp1 = ps1[sl]
                    p2, Bp2 = ps2[sl]
                    o_, Bo_ = ob[sl]
                    S.op("pool", lambda e, y0=y0, y1=y1: e.tensor_tensor(out=yy[:], in0=y0[:], in1=y1[:], op=ALU.add), reads=[By0, By1], writes=[Byy])
                    S.op("pool", lambda e, b0=b0, b1=b1: e.tensor_tensor(out=bs[:], in0=b0[:], in1=b1[:], op=ALU.add), reads=[Bb0, Bb1], writes=[Bbs])
                    S.op("pe", lambda e, p1=p1: e.matmul(p1[:, :], bones[:, :], yy[:], start=True, stop=True), reads=[Byy, Bbones], writes=[Bp1])
                    S.op("act", lambda e: e.activation(out=ysq[:], in_=yy[:], func=AF.Square), reads=[Byy], writes=[Bysq])
                    S.op("pe", lambda e, p2=p2: e.matmul(p2[:, :], bones[:, :], ysq[:], start=True, stop=True), reads=[Bysq, Bbones], writes=[Bp2])
                    S.op("dve", lambda e, p1=p1: e.tensor_scalar(out=mm_[:], in0=p1[:, :], scalar1=1.0 / 64, scalar2=None, op0=ALU.mult), reads=[Bp1], writes=[Bmm_])
                    S.op("pool", lambda e: e.tensor_tensor(out=msq[:], in0=mm_[:], in1=mm_[:], op=ALU.mult), reads=[Bmm_], writes=[Bmsq])
                    S.op("dve", lambda e, p2=p2: e.scalar_tensor_tensor(out=var[:], in0=p2[:, :], scalar=1.0 / 64, in1=msq[:], op0=ALU.mult, op1=ALU.subtract),
                         reads=[Bp2, Bmsq], writes=[Bvar])
                    S.op("act", lambda e: e.activation(out=var[:], in_=var[:], func=AF.Sqrt, bias=epsc[:, 2:3], scale=1.0), reads=[Bvar, Bepsc], writes=[Bvar])
                    S.op("dve", lambda e: e.reciprocal(out=var[:], in_=var[:]), reads=[Bvar], writes=[Bvar])
                    S.op("pool", lambda e: e.tensor_tensor(out=yc[:], in0=yy[:], in1=mm_[:], op=ALU.subtract), reads=[Byy, Bmm_], writes=[Byc])
                    S.op("pool", lambda e: e.tensor_tensor(out=yc[:], in0=yc[:], in1=var[:], op=ALU.mult), reads=[Byc, Bvar], writes=[Byc])
                    S.op("dve", lambda e, cc=cc: e.tensor_scalar(out=yc[:], in0=yc[:], scalar1=chanv[:, cc, 6:7], scalar2=chanv[:, cc, 7:8], op0=ALU.mult, op1=ALU.add),
                         reads=[Byc, Bchanv], writes=[Byc])
                    S.op("pool", lambda e: e.tensor_tensor(out=yc[:], in0=yc[:], in1=bs[:], op=ALU.add), reads=[Byc, Bbs], writes=[Byc])
                    S.op("dve", lambda e, o_=o_, gzt=gzt: e.tensor_tensor(out=o_[:], in0=yc[:], in1=gzt[:], op=ALU.mult), reads=[Byc, Bgzt], writes=[Bo_])
                    S.dma("sp", OXs[2 * cc][:, xo:xo + NF], o_[0:64, :], reads=[Bo_], semof=Bo_)
                    S.dma("sp", OXs[2 * cc + 1][:, xo:xo + NF], o_[64:128, :], reads=[Bo_], semof=Bo_)
                    it += 1
            S.barrier()
            S.emit()
            S.end_phase()
        if upto == "F":
            return nc

        QG = 512
        NKT = NT // 128
        with ExitStack() as pms:
            P = Ctx(nc, pms); P.n = 500
            Kn, BKn = P.sb([128, NT], BF16, "Kn")
            Kr, BKr = P.sb([128, NT], BF16, "Kr")
            Vt, BVt = P.sb([128, NKT, 128], BF16, "Vt")
            onesb, Bonesb = P.sb([128, 128], BF16, "onesb")
            Qn = [P.sb([128, QG], BF16, "Qn") for _ in range(2)]
            Qr = [P.sb([128, QG], BF16, "Qr") for _ in range(2)]
            Pacc = [[P.sb([128, QG], F32, "Pacc") for _ in range(2)] for _ in range(2)]
            gmt = [P.sb([128, QG], F32, "gmt") for _ in range(2)]
            Pt = [P.sb([128, QG], BF16, "Pt") for _ in range(3)]
            rl, Brl = P.sb([128, QG], F32, "rl")
            oo, Boo = P.sb([128, QG], F32, "oo")
            om = [P.sb([128, QG], BF16, "om") for _ in range(2)]
            pS = [P.ps([128, 512], F32, "pS") for _ in range(3)]
            pO = [P.ps([128, 512], F32, "pO") for _ in range(2)]
            pL = [P.ps([128, 512], F32, "pL") for _ in range(2)]
            S.op("pool", lambda e: e.memset(onesb[:], 1.0), writes=[Bonesb])
            S.op("pool", lambda e: e.memset(Kr[64:128, :], 0.0), writes=[BKr])
            for q_, Bq_ in Qr:
                S.op("pool", lambda e, q_=q_: e.memset(q_[64:128, :], 0.0), writes=[Bq_])
            S.dma("sp", Kr[0:64, :], KR, writes=[BKr], semof=BKr)
            NQG = int(os.environ.get("MK_NQG", NX // QG))
            def attn_group(h, qg, sl):
                qo = qg * QG
                qn_, Bqn_ = Qn[sl]
                qr_, Bqr_ = Qr[sl]
                gm_, Bgm_ = gmt[sl]
                po_, Bpo_ = pO[sl]
                pl_, Bpl_ = pL[sl]
                o_, Bo_ = om[sl]
                S.dma("sp", qn_[:], QN_v[:, h, qo:qo + QG], writes=[Bqn_], semof=Bqn_)
                S.dma("sp", qr_[0:64, :], QR_v[:, h, qo:qo + QG], writes=[Bqr_], semof=Bqr_)
                (pa0, Bpa0), (pa1, Bpa1) = Pacc[sl]
                S.dma("sp", gm_[:], Gzm_v[:, h, qo:qo + QG], writes=[Bgm_], semof=Bgm_)

                def qk(kt):
                    ps_, Bps_ = pS[kt % 3]

                    def f(e):
                        e.matmul(ps_[:, :], Kn[:, kt * 128:(kt + 1) * 128], qn_[:, :], start=True, stop=False)
                        return e.matmul(ps_[:, :], Kr[:, kt * 128:(kt + 1) * 128], qr_[:, :], start=False, stop=True)
                    S.op("pe", f, reads=[BKn, BKr, Bqn_, Bqr_], writes=[Bps_])

                def ex_pv(kt):
                    ps_, Bps_ = pS[kt % 3]
                    pt_, Bpt_ = Pt[kt % 3]
                    S.op("act", lambda e: e.activation(out=pt_[:, :], in_=ps_[:, :], func=AF.Exp, scale=KS[:, kt * 2 + h:kt * 2 + h + 1]),
                         reads=[Bps_, BKS], writes=[Bpt_])

                    S.op("pe", lambda e: e.matmul(po_[:, :], Vt[:, kt, :], pt_[:, :], start=(kt == 0), stop=(kt == NKT - 1)),
                         reads=[BVt, Bpt_], writes=[Bpo_])
                    pa_, Bpa_ = (pa0, Bpa0) if kt % 2 == 0 else (pa1, Bpa1)
                    eng_ = "pool" if kt % 2 == 0 else "dve"
                    if kt < 2:
                        S.op(eng_, lambda e: e.tensor_copy(out=pa_[:, :], in_=pt_[:, :]), reads=[Bpt_], writes=[Bpa_])
                    else:
                        S.op(eng_, lambda e: e.tensor_tensor(out=pa_[:, :], in0=pa_[:, :], in1=pt_[:, :], op=ALU.add), reads=[Bpt_, Bpa_], writes=[Bpa_])

                qk(0)
                for kt in range(NKT):
                    if kt + 1 < NKT:
                        qk(kt + 1)
                    ex_pv(kt)
                def lsum(e):
                    e.matmul(pl_[:, :], onesf[:, :], pa0[:, :], start=True, stop=False)
                    return e.matmul(pl_[:, :], onesf[:, :], pa1[:, :], start=False, stop=True)
                S.op("pe", lsum, reads=[Bonesf, Bpa0, Bpa1], writes=[Bpl_])
                S.op("dve", lambda e: e.reciprocal(out=rl[:, :], in_=pl_[:, :]), reads=[Bpl_], writes=[Brl])
                S.op("dve", lambda e: e.tensor_tensor(out=oo[:, :], in0=po_[:, :], in1=rl[:, :], op=ALU.mult), reads=[Bpo_, Brl], writes=[Boo])
                S.op("pool", lambda e: e.tensor_tensor(out=o_[:, :], in0=oo[:, :], in1=gm_[:, :], op=ALU.mult), reads=[Boo, Bgm_], writes=[Bo_])
                S.dma("sp", OXs[4 + 2 * h][:, qo:qo + QG], o_[0:64, :], reads=[Bo_], semof=Bo_)
                S.dma("sp", OXs[5 + 2 * h][:, qo:qo + QG], o_[64:128, :], reads=[Bo_], semof=Bo_)

            gcount = 0
            for h in range(2):
                S.dma("sp", Kn[:], KN_v[:, h, :], writes=[BKn], semof=BKn)
                S.dma("sp", Vt[:], VT[h].rearrange("p (t d) -> p t d", d=128), writes=[BVt], semof=BVt)
                for qg in range(NQG):
                    attn_group(h, qg, gcount % 2)
                    gcount += 1
            S.barrier()
            S.emit()
            S.end_phase()
        if upto == "M":
            return nc

        BOG = Buf("OG")
        for j in range(8):
            S.collective("AllGather", [[0, 1, 2, 3], [4, 5, 6, 7]], OXs[j], OGs[j], reads=[BOX], writes=[BOG], semof=BOG)
        S.barrier()
        S.emit()
        S.end_phase()

        CG = 512
        with ExitStack() as pcs:
            P = Ctx(nc, pcs); P.n = 600
            selq, Bselq = P.sb([128, 4], F32, "selq")
            gain_bc, Bgain = P.sb([128, D], F32, "gain_c")
            shift_bc, Bshift = P.sb([128, D], F32, "shift_c")
            gate_bc, Bgate = P.sb([128, D], F32, "gate_c")
            xt = [P.sb([128, D], F32, "xtc") for _ in range(2)]
            hm = [P.sb([128, D], BF16, "hmc") for _ in range(2)]
            ss = [P.sb([128, 4], F32, "ssc") for _ in range(2)]
            hT, BhT = P.sb([128, 16, CG], BF16, "hTc")
            ldq = [P.sb([128, 16, CG], BF16, "ldq") for _ in range(2)]
            osel, Bosel = P.sb([128, 16, CG], BF16, "osel")
            wmg = [P.sb([128, 16, 256], BF16, "wmg") for _ in range(2)]
            wbr = [P.sb([128, 8, 256], BF16, "wbr") for _ in range(2)]
            sgr, Bsgr = P.sb([128, CG], F32, "sgr")
            sgm, Bsgm = P.sb([128, CG], F32, "sgm")
            tr_, Btr_ = P.sb([128, CG], F32, "tr")
            tm_, Btm_ = P.sb([128, CG], F32, "tm")
            merged, Bmerged = P.sb([128, 16, CG], BF16, "merged")
            wout = [P.sb([128, 16, 512], BF16, "wout") for _ in range(2)]
            xr = [P.sb([128, 512], F32, "xr") for _ in range(2)]
            res = [P.sb([128, 512], F32, "res") for _ in range(2)]
            pT = [P.ps([128, 1024], BF16, "pTc") for _ in range(2)]
            pg = [P.ps([128, 512], F32, "pg") for _ in range(4)]
            pout = [P.ps([128, 512], F32, "pout") for _ in range(2)]
            S.dma("sp", selq[:], I["selq"], writes=[Bselq], semof=Bselq)
            S.dma("sp", gain_bc[:], BC[0], writes=[Bgain], semof=Bgain)
            S.dma("sp", shift_bc[:], BC[1], writes=[Bshift], semof=Bshift)
            S.dma("sp", gate_bc[:], BC[2], writes=[Bgate], semof=Bgate)
            wmg_v = I["w_in_mg"].rearrange("(kc p) n -> p kc n", p=128)
            wbr_r_v = I["w_br_r"].rearrange("(kc p) n -> p kc n", p=128)
            wbr_m_v = I["w_br_m"].rearrange("(kc p) n -> p kc n", p=128)
            wout_v = I["w_out"].rearrange("(kc p) n -> p kc n", p=128)
            cnt = {"tile": 0, "w": 0, "o": 0, "r": 0}
            NCG = int(os.environ.get("MK_NCG", 2048 // CG))
            for gj in range(NCG):
                go = gj * CG
                for t in range(CG // 128):
                    s = cnt["tile"] % 2
                    cnt["tile"] += 1
                    x_, Bx_ = xt[s]
                    h_, Bh_ = hm[s]
                    s_, Bs_ = ss[s]
                    S.dma("sp", x_[:], I["xm"][go + t * 128:go + (t + 1) * 128, :], writes=[Bx_], semof=Bx_)
                    S.op("act", lambda e, x_=x_, h_=h_, s_=s_: e.activation(out=h_[:], in_=x_[:], func=AF.Square, accum_out=s_[:, 0:1]), reads=[Bx_], writes=[Bh_, Bs_])
                    S.op("act", lambda e, s_=s_: e.activation(out=s_[:, 1:2], in_=s_[:, 0:1], func=AF.Sqrt, scale=1.0 / D, bias=epsc[:, 0:1]), reads=[Bs_, Bepsc], writes=[Bs_])
                    S.op("dve", lambda e, s_=s_: e.reciprocal(out=s_[:, 2:3], in_=s_[:, 1:2]), reads=[Bs_], writes=[Bs_])
                    S.op("dve", lambda e, x_=x_, s_=s_: e.scalar_tensor_tensor(out=x_[:], in0=x_[:], scalar=s_[:, 2:3], in1=gain_bc[:], op0=ALU.mult, op1=ALU.mult),
                         reads=[Bx_, Bs_, Bgain], writes=[Bx_])
                    S.op("pool", lambda e, x_=x_, h_=h_: e.tensor_tensor(out=h_[:], in0=x_[:], in1=shift_bc[:], op=ALU.add), reads=[Bx_, Bshift], writes=[Bh_])
                    for half in range(2):
                        p_, Bp_ = pT[half]

                        def trf(e, half=half, p_=p_, h_=h_):
                            for j in range(8):
                                kc = half * 8 + j
                                ins = e.transpose(p_[:, j * 128:(j + 1) * 128], h_[:, kc * 128:(kc + 1) * 128], identb[:])
                            return ins
                        S.op("pe", trf, reads=[Bh_, Bidentb], writes=[Bp_])
                        S.op("act" if half == 0 else "dve",
                             (lambda e, half=half, p_=p_, t=t: e.activation(out=hT[:, half * 8:(half + 1) * 8, t * 128:(t + 1) * 128],
                                                                            in_=p_[:, :].rearrange("p (j n) -> p j n", n=128), func=AF.Copy)) if half == 0 else
                             (lambda e, half=half, p_=p_, t=t: e.tensor_copy(out=hT[:, half * 8:(half + 1) * 8, t * 128:(t + 1) * 128],
                                                                             in_=p_[:, :].rearrange("p (j n) -> p j n", n=128))),
                             reads=[Bp_], writes=[BhT])
                for q in range(4):
                    l_, Bl_ = ldq[q % 2]
                    for j in range(8):
                        S.dma("sp", l_[(j % 2) * 64:(j % 2) * 64 + 64, :, :].rearrange("p (r c) n -> p r c n", c=4)[:, :, j // 2, :],
                              OGs[j].rearrange("(r p) n -> p r n", p=64)[:, :, q * 2048 + go:q * 2048 + go + CG],
                              reads=[BOG], writes=[Bl_], semof=Bl_)
                    if q == 0:
                        S.op("dve", lambda e, l_=l_: e.tensor_scalar(out=osel[:], in0=l_[:], scalar1=selq[:, 0:1], scalar2=None, op0=ALU.mult),
                             reads=[Bl_, Bselq], writes=[Bosel])
                    else:
                        S.op("dve", lambda e, l_=l_, q=q: e.scalar_tensor_tensor(out=osel[:], in0=l_[:], scalar=selq[:, q:q + 1], in1=osel[:], op0=ALU.mult, op1=ALU.add),
                             reads=[Bl_, Bselq, Bosel], writes=[Bosel])
                for m in range(16):
                    w_, Bw_ = wmg[cnt["w"] % 2]
                    b_, Bb_ = wbr[cnt["w"] % 2]
                    cnt["w"] += 1
                    S.dma("sp", w_[:, :, :], WMG_b[m].rearrange("p (k n) -> p k n", n=256), writes=[Bw_], semof=Bw_)
                    S.dma("sp", b_[:, :, :], WBR_b[m].rearrange("p (k n) -> p k n", n=256), writes=[Bb_], semof=Bb_)
                    (pgr, Bpgr), (pgm, Bpgm), (ppr, Bppr), (ppm, Bppm) = pg

                    def fg(e, w_=w_):
                        for kc in range(16):
                            e.matmul(pgr[:, 0:CG], w_[:, kc, 0:128], hT[:, kc, :], start=(kc == 0), stop=(kc == 15))
                        for kc in range(16):
                            ins = e.matmul(pgm[:, 0:CG], w_[:, kc, 128:256], hT[:, kc, :], start=(kc == 0), stop=(kc == 15))
                        return ins
                    S.op("pe", fg, reads=[Bw_, BhT], writes=[Bpgr, Bpgm])
                    S.op("act", lambda e: e.activation(out=sgr[:, :], in_=pgr[:, 0:CG], func=AF.Sigmoid), reads=[Bpgr], writes=[Bsgr])
                    S.op("act", lambda e: e.activation(out=sgm[:, :], in_=pgm[:, 0:CG], func=AF.Sigmoid), reads=[Bpgm], writes=[Bsgm])

                    def fb(e, b_=b_):
                        for j in range(8):
                            kc = (j // 2) * 4 + (j % 2)
                            e.matmul(ppr[:, 0:CG], b_[:, j, 0:128], osel[:, kc, :], start=(j == 0), stop=(j == 7))
                        for j in range(8):
                            kc = (j // 2) * 4 + 2 + (j % 2)
                            ins = e.matmul(ppm[:, 0:CG], b_[:, j, 128:256], osel[:, kc, :], start=(j == 0), stop=(j == 7))
                        return ins
                    S.op("pe", fb, reads=[Bb_, Bosel], writes=[Bppr, Bppm])
                    S.op("dve", lambda e: e.tensor_tensor(out=tr_[:, :], in0=ppr[:, 0:CG], in1=sgr[:, :], op=ALU.mult), reads=[Bppr, Bsgr], writes=[Btr_])
                    S.op("dve", lambda e: e.tensor_tensor(out=tm_[:, :], in0=ppm[:, 0:CG], in1=sgm[:, :], op=ALU.mult), reads=[Bppm, Bsgm], writes=[Btm_])
                    S.op("pool", lambda e, m=m: e.tensor_tensor(out=merged[:, m, :], in0=tr_[:, :], in1=tm_[:, :], op=ALU.add), reads=[Btr_, Btm_], writes=[Bmerged])
                for nb in range(4):
                    wo_, Bwo_ = wout[cnt["o"] % 2]
                    cnt["o"] += 1
                    S.dma("sp", wo_[:], WOUT_b[nb].rearrange("p (k n) -> p k n", n=512), writes=[Bwo_], semof=Bwo_)
                    for t in range(CG // 128):
                        r_ = cnt["r"] % 2
                        cnt["r"] += 1
                        po_, Bpo_ = pout[r_]
                        xr_, Bxr_ = xr[r_]
                        rs_, Brs_ = res[r_]
                        S.dma("sp", xr_[:], I["xm"][go + t * 128:go + (t + 1) * 128, nb * 512:(nb + 1) * 512], writes=[Bxr_], semof=Bxr_)

                        def fo(e, wo_=wo_, po_=po_, t=t):
                            for kc in range(16):
                                ins = e.matmul(po_[:, :], merged[:, kc, t * 128:(t + 1) * 128], wo_[:, kc, :], start=(kc == 0), stop=(kc == 15))
                            return ins
                        S.op("pe", fo, reads=[Bwo_, Bmerged], writes=[Bpo_])
                        S.op("dve", lambda e, po_=po_, rs_=rs_, nb=nb: e.tensor_tensor(out=rs_[:], in0=po_[:, :], in1=gate_bc[:, nb * 512:(nb + 1) * 512], op=ALU.mult),
                             reads=[Bpo_, Bgate], writes=[Brs_])
                        S.op("pool", lambda e, rs_=rs_, xr_=xr_: e.tensor_tensor(out=rs_[:], in0=rs_[:], in1=xr_[:], op=ALU.add), reads=[Brs_, Bxr_], writes=[Brs_])
                        S.dma("sp", out[go + t * 128:go + (t + 1) * 128, nb * 512:(nb + 1) * 512], rs_[:], reads=[Brs_], semof=Brs_)
            S.barrier()
            S.emit()
            S.end_phase()
    return nc


_NC_CACHE = {}


def kernel(**inputs):
    maps = _host_inputs(inputs)
    if "nc" not in _NC_CACHE:
        _NC_CACHE["nc"] = build()
    nc = _NC_CACHE["nc"]
    res = run_bass_kernel_spmd(nc, maps, core_ids=list(range(8)))
    outp = np.zeros((2, NX, D), np.float32)
    for c in range(8):
        b, g = c // 4, c % 4
        outp[b, 2048 * g:2048 * g + 2048] = res.results[c]["out"]
    return outp
```

```python
import os
from contextlib import ExitStack
import numpy as np
import ml_dtypes
import concourse.bass as bass
import concourse.mybir as mybir
from concourse.bass_utils import run_bass_kernel_spmd

F32 = mybir.dt.float32
BF16 = mybir.dt.bfloat16
ALU = mybir.AluOpType
AF = mybir.ActivationFunctionType

NT = 8448
NX = 8192
NCTX = 256
D = 2048
WC = 2368
GS = 256
C0 = float(np.exp(-0.5))
EPS = 1e-6
GN_EPS = 64e-5


class Tok:
    __slots__ = ("sem", "val", "key")

    def __init__(self, sem, val, key):
        self.sem = sem
        self.val = val
        self.key = key


class DSem:
    _n = 0

    def __init__(self, sem, kind):
        DSem._n += 1
        self.uid = DSem._n
        self.sem = sem
        self.cnt = 0
        self.kind = kind


class Buf:
    def __init__(self, name, const=False):
        self.name = name
        self.w = None
        self.r = []
        self.const = const
        self.dsem = None
        self.dcnt = 0


class Sched:
    ENG = ["pe", "act", "dve", "pool", "sp"]

    def __init__(self, nc, stack):
        self.nc = nc
        self.stack = stack
        self.plan = {e: [] for e in self.ENG}
        self.ecnt = {e: 0 for e in self.ENG}
        self.esem = {}
        self.waited = {e: {} for e in self.ENG}
        self.nsem = 0
        for e in ("pe", "act", "dve", "pool"):
            self.esem[e] = self._newsem("e_" + e)
        self.dbufs = []
        self.free_dsems = {}
        self.ninst = 0

    def _newsem(self, name):
        self.nsem += 1
        return self.stack.enter_context(self.nc.semaphore(f"{name}_{self.nsem}"))

    def _waits(self, eng, toks):
        best = {}
        for t in toks:
            if t is not None and (t.key not in best or best[t.key].val < t.val):
                best[t.key] = t
        for t in best.values():
            if self.waited[eng].get(t.key, 0) >= t.val:
                continue
            if eng == "pe" and t.key == "e_pe":
                continue
            self.waited[eng][t.key] = t.val
            self.plan[eng].append(lambda e, sem=t.sem, v=t.val: e.wait_ge(sem, v))

    def _deps(self, reads, writes):
        deps = []
        for b in reads:
            deps.append(b.w)
        for b in writes:
            deps.append(b.w)
            deps.extend(b.r)
        return deps

    def _mark(self, tok, reads, writes):
        for b in reads:
            if not b.const:
                b.r.append(tok)
        for b in writes:
            b.w = tok
            b.r = []

    def op(self, eng, fn, reads=(), writes=()):
        self._waits(eng, self._deps(reads, writes))
        self.ecnt[eng] += 1
        self.ninst += 1
        tok = Tok(self.esem[eng], self.ecnt[eng], "e_" + eng)
        self.plan[eng].append(lambda e, fn=fn, sem=tok.sem: fn(e).then_inc(sem, 1))
        self._mark(tok, reads, writes)
        return tok

    def _dsem(self, b, kind):
        if b.dsem is None:
            fl = self.free_dsems.setdefault(kind, [])
            if fl:
                b.dsem = fl.pop()
            else:
                b.dsem = DSem(self._newsem("d" + kind), kind)
            self.dbufs.append(b)
        assert b.dsem.kind == kind, (b.name, b.dsem.kind, kind)

    def end_phase(self, recycle_hw=True):
        for b in self.dbufs:
            b.dsem = None
            b.w = None
            b.r = []
        self.dbufs = []

    def dma(self, q, out_ap, in_ap, reads=(), writes=(), semof=None, **kw):
        self._waits(q, self._deps(reads, writes))
        b = semof
        self._dsem(b, "sw" if q == "pool" else "hw")
        ds = b.dsem
        ds.cnt += 16
        tok = Tok(ds.sem, ds.cnt, "d%d" % ds.uid)
        self.plan[q].append(
            lambda e, o=out_ap, i=in_ap, sem=ds.sem, kw=kw: e.dma_start(out=o, in_=i, **kw).then_inc(sem, 16)
        )
        self._mark(tok, reads, writes)
        return tok

    def dma_group(self, q, pairs, reads=(), writes=(), semof=None):
        if os.environ.get("MK_SEQGROUP"):
            for (o, i) in pairs:
                tok = self.dma(q, o, i, reads=reads, writes=writes, semof=semof)
            return tok
        self._waits(q, self._deps(reads, writes))
        b = semof
        self._dsem(b, "sw" if q == "pool" else "hw")
        ds = b.dsem
        for (o, i) in pairs:
            ds.cnt += 16
            self.plan[q].append(lambda e, o=o, i=i, sem=ds.sem: e.dma_start(out=o, in_=i).then_inc(sem, 16))
        tok = Tok(ds.sem, ds.cnt, "d%d" % ds.uid)
        self._mark(tok, reads, writes)
        return tok

    def collective(self, kind, groups, in_ap, out_ap, reads, writes, semof):
        q = "pool"
        self._waits(q, self._deps(reads, writes))
        b = semof
        self._dsem(b, "cc")
        ds = b.dsem
        ds.cnt += 1
        tok = Tok(ds.sem, ds.cnt, "d%d" % ds.uid)
        self.plan[q].append(
            lambda e, sem=ds.sem: e.collective_compute(
                kind, ALU.bypass, replica_groups=groups, ins=[in_ap], outs=[out_ap]
            ).then_inc(sem, 1)
        )
        self._mark(tok, reads, writes)
        return tok

    def barrier(self):
        toks = []
        for e in ("pe", "act", "dve", "pool"):
            if self.ecnt[e] > 0:
                toks.append(Tok(self.esem[e], self.ecnt[e], "e_" + e))
        for b in self.dbufs:
            toks.append(Tok(b.dsem.sem, b.dsem.cnt, "d%d" % b.dsem.uid))
        for e in self.ENG:
            self._waits(e, toks)

    def emit(self):
        plan = self.plan
        with self.nc.Block() as block:

            @block.tensor
            def _(e):
                for f in plan["pe"]:
                    f(e)

            @block.scalar
            def _(e):
                for f in plan["act"]:
                    f(e)

            @block.vector
            def _(e):
                for f in plan["dve"]:
                    f(e)

            @block.gpsimd
            def _(e):
                for f in plan["pool"]:
                    f(e)

            @block.sync
            def _(e):
                for f in plan["sp"]:
                    f(e)

        self.plan = {e: [] for e in self.ENG}


class Ctx:
    def __init__(self, nc, stack):
        self.nc = nc
        self.stack = stack
        self.n = 0

    def sb(self, shape, dt, name=None):
        self.n += 1
        name = (name or "t") + f"_{self.n}"
        t = self.stack.enter_context(self.nc.sbuf_tensor(name, list(shape), dt))
        return t, Buf(name)

    def ps(self, shape, dt, name=None):
        self.n += 1
        name = (name or "p") + f"_{self.n}"
        t = self.stack.enter_context(self.nc.psum_tensor(name, list(shape), dt))
        return t, Buf(name)

    def sub(self):
        c = Ctx(self.nc, ExitStack())
        c.n = self.n + 1000
        return c


def _host_consts():
    idx = np.arange(64)
    us = (idx[:, None] < idx[None, :]).astype(np.float32)
    ui = (idx[:, None] <= idx[None, :]).astype(np.float32)
    ls = (idx[:, None] > idx[None, :]).astype(np.float32)
    li = (idx[:, None] >= idx[None, :]).astype(np.float32)
    masks = np.zeros((128, 4, 128), np.float32)
    for i, m in enumerate((us, ui, ls, li)):
        masks[0:64, i, 0:64] = m
        masks[64:128, i, 64:128] = m
    bones = np.zeros((128, 128), np.float32)
    bones[0:64, 0:64] = 1
    bones[64:128, 64:128] = 1
    sel = np.zeros((2, 2, 128), np.float32)
    sel[0, 0, :] = 1
    sel[1, 1, :] = 1
    reset = np.ones((128, 512), np.float32)
    reset[:, ::64] = 0
    rows = np.repeat(np.arange(128), 64).astype(np.float32)
    cols = np.tile(np.arange(64), 128).astype(np.float32)
    inv = np.power(np.float32(10000.0), -np.arange(0, 32, 2, dtype=np.float32) / np.float32(32)).astype(np.float32)
    ang = np.zeros((64, NX), np.float32)
    for d in range(64):
        pos = rows if d < 32 else cols
        ang[d] = pos * inv[d % 16]
    cos = np.cos(ang.astype(np.float64)).astype(np.float32)
    sin = np.sin(ang.astype(np.float64)).astype(np.float32)
    P = np.zeros((64, 64), np.float32)
    for d in range(64):
        if d % 32 < 16:
            P[d, d + 16] = -1
        else:
            P[d, d - 16] = 1
    return dict(
        ident=np.eye(128, dtype=np.float32), masks=masks, bones=bones, onesf=np.ones((128, 128), np.float32),
        sel=sel, reset=reset, rope_cos=cos, rope_sin=sin, ropePT=np.ascontiguousarray(P.T),
    )


def _host_inputs(inp):
    f = lambda a: np.ascontiguousarray(a, dtype=np.float32)
    consts = _host_consts()
    w_in = inp["w_in"][0]
    offs = np.cumsum([0, 3072, 1024, 128, 128, 512, 256, 64, 1024, 4096])
    o_rkv, o_zr, o_wd, o_ad, o_qd, o_kvd, o_kr, o_zm, o_mg = offs[:9]
    maps = []
    for c in range(8):
        b, g = c // 4, c % 4
        ch = slice(256 * g, 256 * g + 256)
        cols = np.concatenate([
            o_rkv + np.arange(256 * g, 256 * g + 256),
            o_rkv + 1024 + np.arange(256 * g, 256 * g + 256),
            o_rkv + 2048 + np.arange(256 * g, 256 * g + 256),
            o_zr + np.arange(256 * g, 256 * g + 256),
            o_wd + np.arange(128), o_ad + np.arange(128),
            o_qd + np.arange(512), o_kvd + np.arange(256), o_kr + np.arange(64),
            o_zm + np.arange(256 * g, 256 * g + 256),
        ])
        assert cols.size == WC
        conv = inp["conv_rkv"][0]
        conv_fm = np.zeros((128, 6, 3), np.float32)
        for kind in range(3):
            for cc in range(2):
                cidx = kind * 1024 + 256 * g + cc * 128 + np.arange(128)
                conv_fm[:, kind * 2 + cc, :] = conv[:, cidx].T
        chanv = np.zeros((128, 2, 10), np.float32)
        for cc in range(2):
            cidx = 256 * g + cc * 128 + np.arange(128)
            chanv[:, cc, 0] = inp["w0"][0, 0, cidx]
            chanv[:, cc, 1] = inp["w0"][0, 1, cidx]
            chanv[:, cc, 2] = inp["a0"][0, 0, cidx]
            chanv[:, cc, 3] = inp["a0"][0, 1, cidx]
            chanv[:, cc, 4] = inp["k_k"][0, cidx]
            chanv[:, cc, 5] = inp["k_a"][0, cidx]
            chanv[:, cc, 6] = inp["ln_x_w"][0, cidx]
            chanv[:, cc, 7] = inp["ln_x_b"][0, cidx]
            chanv[:, cc, 8] = inp["r_k"][0].reshape(-1)[cidx]
        wlora = np.zeros((128, 2, 256), np.float32)
        for d in range(2):
            wlora[64 * d:64 * d + 64, 0, :] = inp["w_decay_up"][0, d][:, ch]
            wlora[64 * d:64 * d + 64, 1, :] = inp["w_a_up"][0, d][:, ch]
        mlav = np.zeros((128, 10), np.float32)
        mlav[:, 0:4] = inp["q_norm_w"][0].reshape(4, 128).T
        mlav[:, 4:6] = inp["kv_norm_w"][0].reshape(2, 128).T
        mlav[:, 6] = inp["q_gain"][0][:128]
        mlav[:, 7] = inp["k_gain"][0][:128]
        mlav[0:64, 8] = inp["q_gain"][0][128:]
        mlav[0:64, 9] = inp["k_gain"][0][128:]
        hq = [2 * g, 2 * g + 1]
        w_uq_c = np.concatenate([inp["w_uq"][0][:, h * 192:(h + 1) * 192] for h in hq], axis=1)
        w_ukv_c = np.concatenate([inp["w_ukv"][0][:, h * 256:(h + 1) * 256] for h in hq], axis=1)
        cfm = np.stack([inp["c"][b].reshape(16, 128).T, inp["c_ctx"].reshape(16, 128).T], axis=-1)
        selq = np.zeros((128, 4), np.float32)
        selq[:, g] = 1
        m = dict(
            xs=f(np.concatenate([inp["ctx"][b], inp["x"][b]], axis=0)),
            xm=f(inp["x"][b, 2048 * g:2048 * g + 2048]),
            cfm=f(cfm), norm_w_row=f(np.stack([inp["norm_w"][0]] * 2)), b_mod_row=f(np.stack([inp["b_mod"][0]] * 2)),
            w_mod=f(inp["w_mod"][0]), w_in_core=f(w_in[:, cols]), w_in_mg=f(w_in[:, o_mg:o_mg + 4096]),
            conv_fm=f(conv_fm), chanv=f(chanv), wlora=f(wlora), mlav=f(mlav), w_uq_c=f(w_uq_c), w_ukv_c=f(w_ukv_c),
            w_br_r=f(inp["w_branch_rwkv"][0]), w_br_m=f(inp["w_branch_mla"][0]), w_out=f(inp["w_out"][0]),
            selq=f(selq),
        )
        m.update({k: f(v) for k, v in consts.items()})
        maps.append(m)
    return maps


INPUT_SHAPES = dict(
    xs=[NT, D], xm=[2048, D], cfm=[128, 16, 2], norm_w_row=[2, D], b_mod_row=[2, 3 * D], w_mod=[D, 3 * D],
    w_in_core=[D, WC], w_in_mg=[D, 4096], conv_fm=[128, 6, 3], chanv=[128, 2, 10], wlora=[128, 2, 256],
    mlav=[128, 10], w_uq_c=[512, 384], w_ukv_c=[256, 512], w_br_r=[1024, D], w_br_m=[1024, D], w_out=[D, D],
    selq=[128, 4], ident=[128, 128], masks=[128, 4, 128], bones=[128, 128], onesf=[128, 128], sel=[2, 2, 128],
    reset=[128, 512], rope_cos=[64, NX], rope_sin=[64, NX], ropePT=[64, 64],
)


def build(debug=(), upto="C"):
    nc = bass.Bass("TRN2", target_bir_lowering=False)
    I = {k: nc.dram_tensor(k, s, F32, kind="ExternalInput").ap() for k, s in INPUT_SHAPES.items()}
    out = nc.dram_tensor("out", [2048, D], F32, kind="ExternalOutput").ap()

    def scratch(name, shape, dt):
        if name in debug:
            return nc.dram_tensor(name, shape, dt, kind="ExternalOutput").ap()
        return nc.dram_tensor(name, shape, dt).ap()

    BC = scratch("BC", [5, 128, D], F32)
    U_rkv = scratch("U_rkv", [768, NT], F32)
    G_zr = scratch("G_zr", [256, NX], F32)
    SG = scratch("SG", [512, NT], F32)
    AA = scratch("AA", [512, NT], F32)
    QN = scratch("QN", [256, NX], BF16)
    QR = scratch("QR", [128, NX], BF16)
    KN = scratch("KN", [256, NT], BF16)
    KR = scratch("KR", [64, NT], BF16)
    VT = scratch("VT", [2, 128, NT], BF16)
    G_zm = scratch("G_zm", [256, NX], F32)
    KSD = scratch("KSD", [128, 132], F32)
    YD = [scratch("YD0", [256, NX], F32), scratch("YD1", [256, NX], F32)]
    BD = [scratch("BD0", [256, NX], F32), scratch("BD1", [256, NX], F32)]
    DBG = [scratch(f"DBG{i}", [128, 512], BF16 if i < 2 else F32) for i in range(4)]
    OXs = [scratch(f"OX{j}", [64, NX], BF16) for j in range(8)]
    OGs = [scratch(f"OG{j}", [256, NX], BF16) for j in range(8)]

    WMG_b = scratch("WMG_b", [16, 128, 16 * 256], BF16)
    WBR_b = scratch("WBR_b", [16, 128, 8 * 256], BF16)
    WOUT_b = scratch("WOUT_b", [4, 128, 16 * 512], BF16)
    v3 = lambda ap, p=128: ap.rearrange("(c p) n -> p c n", p=p)
    U_v, Gzr_v, SG_v, AA_v = v3(U_rkv), v3(G_zr), v3(SG), v3(AA)
    QN_v, QR_v, KN_v, Gzm_v = v3(QN), v3(QR, 64), v3(KN), v3(G_zm)

    with ExitStack() as top:
        S = Sched(nc, top)
        T = Ctx(nc, top)

        identb, Bidentb = T.sb([128, 128], BF16, "identb")
        msk, Bmsk = T.sb([128, 4, 128], BF16, "msk")
        bones, Bbones = T.sb([128, 128], F32, "bones")
        onesf, Bonesf = T.sb([128, 128], F32, "onesf")
        selt, Bselt = T.sb([2, 2, 128], F32, "selt")
        resetm, Bresetm = T.sb([128, 512], F32, "resetm")
        ropePT, BropePT = T.sb([64, 64], F32, "ropePT")
        epsc, Bepsc = T.sb([128, 4], F32, "epsc")
        convw, Bconvw = T.sb([128, 6, 3], F32, "convw")
        chanv, Bchanv = T.sb([128, 2, 10], F32, "chanv")
        mlav, Bmlav = T.sb([128, 10], F32, "mlav")
        KS, BKS = T.sb([128, 132], F32, "KS")
        S.dma("pool", identb[:], I["ident"], writes=[Bidentb], semof=Bidentb)
        S.dma("pool", msk[:], I["masks"], writes=[Bmsk], semof=Bmsk)
        for t_, b_, k_ in ((bones, Bbones, "bones"), (onesf, Bonesf, "onesf"), (selt, Bselt, "sel"), (resetm, Bresetm, "reset"),
                           (ropePT, BropePT, "ropePT"), (convw, Bconvw, "conv_fm"), (chanv, Bchanv, "chanv"), (mlav, Bmlav, "mlav")):
            S.dma("sp", t_[:], I[k_], writes=[b_], semof=b_)
        S.op("pool", lambda e: e.memset(epsc[:, 0:1], EPS), writes=[Bepsc])
        S.op("pool", lambda e: e.memset(epsc[:, 1:2], 1e-12), writes=[Bepsc])
        S.op("pool", lambda e: e.memset(epsc[:, 2:3], GN_EPS), writes=[Bepsc])
        S.op("pool", lambda e: e.memset(epsc[:, 3:4], 0.0), writes=[Bepsc])
        for b_ in (Bidentb, Bmsk, Bbones, Bonesf, Bselt, Bresetm, BropePT, Bepsc, Bconvw, Bchanv, Bmlav):
            b_.const = True

        with ExitStack() as p0s:
            P = Ctx(nc, p0s); P.n = 100
            cf, Bcf = P.sb([128, 16, 2], F32, "cf")
            sc, Bsc = P.sb([128, 16, 2], BF16, "sc")
            b2, Bb2 = P.sb([2, 3 * D], F32, "b2")
            nw2, Bnw2 = P.sb([2, D], F32, "nw2")
            mrow, Bmrow = P.sb([2, 3 * D], F32, "mrow")
            grow, Bgrow = P.sb([2, D], F32, "grow")
            wm = [P.sb([128, 16, 512], BF16, "wm") for _ in range(2)]
            bct = [P.sb([128, D], F32, "bct") for _ in range(2)]
            pm, Bpm = P.ps([128, 512], F32, "pm")
            pb = [P.ps([128, 512], F32, "pb") for _ in range(2)]
            S.dma("sp", cf[:], I["cfm"], writes=[Bcf], semof=Bcf)
            S.dma("sp", b2[:], I["b_mod_row"], writes=[Bb2], semof=Bb2)
            S.dma("sp", nw2[:], I["norm_w_row"], writes=[Bnw2], semof=Bnw2)
            S.op("act", lambda e: e.activation(out=sc[:], in_=cf[:], func=AF.Silu), reads=[Bcf], writes=[Bsc])
            wmod_v = I["w_mod"].rearrange("(kc p) n -> p kc n", p=128)
            for cb in range(12):
                wt, Bwt = wm[cb % 2]
                S.dma("pool", wt[:], wmod_v[:, :, cb * 512:(cb + 1) * 512], writes=[Bwt], semof=Bwt)

                def mmf(e, wt=wt):
                    for kc in range(16):
                        ins = e.matmul(pm[0:2, :], sc[:, kc, :], wt[:, kc, :], start=(kc == 0), stop=(kc == 15))
                    return ins
                S.op("pe", mmf, reads=[Bsc, Bwt], writes=[Bpm])
                S.op("act", lambda e, cb=cb: e.activation(out=mrow[0:2, cb * 512:(cb + 1) * 512], in_=pm[0:2, :], func=AF.Copy),
                     reads=[Bpm], writes=[Bmrow])
            S.op("pool", lambda e: e.tensor_tensor(out=mrow[:], in0=mrow[:], in1=b2[:], op=ALU.add), reads=[Bmrow, Bb2], writes=[Bmrow])
            S.op("dve", lambda e: e.scalar_tensor_tensor(out=grow[:], in0=mrow[:, D:2 * D], scalar=1.0, in1=nw2[:],
                                                          op0=ALU.add, op1=ALU.mult), reads=[Bmrow, Bnw2], writes=[Bgrow])
            plan0 = [(0, grow, Bgrow, 0, 0), (1, mrow, Bmrow, 0, 0), (2, mrow, Bmrow, 2 * D, 0), (3, grow, Bgrow, 0, 1), (4, mrow, Bmrow, 0, 1)]
            k = 0
            for (idx, src, Bsrc, off, si) in plan0:
                st, Bst = bct[idx % 2]
                for blk in range(4):
                    pt_, Bpt_ = pb[k % 2]
                    k += 1
                    S.op("pe", lambda e, pt_=pt_, src=src, off=off, blk=blk, si=si: e.matmul(
                        pt_[:, :], selt[0:2, si, :], src[0:2, off + blk * 512: off + (blk + 1) * 512], start=True, stop=True),
                        reads=[Bsrc, Bselt], writes=[Bpt_])
                    S.op("act", lambda e, pt_=pt_, st=st, blk=blk: e.activation(out=st[:, blk * 512:(blk + 1) * 512], in_=pt_[:, :], func=AF.Copy),
                         reads=[Bpt_], writes=[Bst])
                S.dma("sp", BC[idx], st[:], reads=[Bst], semof=Bst)
            S.barrier()
            S.emit()
            S.end_phase()
        if upto == "0":
            return nc

        with ExitStack() as pas:
            P = Ctx(nc, pas); P.n = 200
            W, BW = P.sb([128, 16, WC], BF16, "W")
            wlora, Bwlora = P.sb([128, 2, 256], BF16, "wlora")
            wuq, Bwuq = P.sb([128, 4, 384], BF16, "wuq")
            wukv, Bwukv = P.sb([128, 2, 512], BF16, "wukv")
            gain_bc, Bgain = P.sb([128, D], F32, "gain_bc")
            shift_bc, Bshift = P.sb([128, D], F32, "shift_bc")
            xt = [P.sb([128, D], F32, "xt") for _ in range(2)]
            hm = [P.sb([128, D], BF16, "hm") for _ in range(2)]
            ss = [P.sb([128, 4], F32, "ss") for _ in range(2)]
            hmT = [P.sb([128, 16, GS], BF16, "hmT") for _ in range(2)]
            urkv = [P.sb([128, GS], F32, "urkv") for _ in range(2)]
            gz = [P.sb([128, GS], F32, "gz") for _ in range(2)]
            sga = [P.sb([128, GS], F32, "sga") for _ in range(2)]
            twd2 = [P.sb([128, GS], BF16, "twd") for _ in range(2)]
            adb2 = [P.sb([128, GS], BF16, "adb") for _ in range(2)]
            qd2 = [P.sb([128, 4, GS], F32, "qd") for _ in range(2)]
            sqq, Bsqq = P.sb([128, 4, GS], F32, "sqq")
            qn, Bqn = P.sb([128, 4, GS], BF16, "qn")
            rq, Brq = P.sb([128, GS], F32, "rq")
            qno, Bqno = P.sb([128, GS], F32, "qno")
            qro, Bqro = P.sb([64, GS], F32, "qro")
            sqh, Bsqh = P.sb([128, 2, GS], F32, "sqh")
            rh, Brh = P.sb([128, GS], F32, "rh")
            qnf, Bqnf = P.sb([128, GS], BF16, "qnf")
            qrg, Bqrg = P.sb([64, GS], F32, "qrg")
            t1, Bt1 = P.sb([64, GS], F32, "t1")
            t2, Bt2 = P.sb([64, GS], F32, "t2")
            qrf, Bqrf = P.sb([64, GS], BF16, "qrf")
            cost, Bcost = P.sb([64, GS], F32, "cost")
            sint, Bsint = P.sb([64, GS], F32, "sint")
            kvd2 = [P.sb([128, 2, GS], F32, "kvd") for _ in range(2)]
            kvn, Bkvn = P.sb([128, 2, GS], BF16, "kvn")
            kro2 = [P.sb([64, GS], F32, "kro") for _ in range(2)]
            vts, Bvts = P.sb([128, 2, GS], BF16, "vts")
            kst, Bkst = P.sb([128, 8], F32, "kst")
            pT = [P.ps([128, 1024], BF16, "pT") for _ in range(2)]
            po = [P.ps([128, 512], F32, "po") for _ in range(3)]
            pst, Bpst = P.ps([128, 512], F32, "pst")
            pv, Bpv = P.ps([128, 512], F32, "pv")
            pks, Bpks = P.ps([128, 512], F32, "pks")
            cnt = {"po": 0, "tile": 0, "u": 0, "g": 0, "s": 0}

            def next_po():
                cnt["po"] += 1
                return po[cnt["po"] % 3]

            win_v = I["w_in_core"].rearrange("(kc p) n -> p kc n", p=128)
            S.dma("pool", W[:, :, :], win_v[:, :, :], writes=[BW], semof=BW)
            S.dma("pool", wlora[:], I["wlora"], writes=[Bwlora], semof=Bwlora)
            S.dma("pool", wuq[:], I["w_uq_c"].rearrange("(kc p) n -> p kc n", p=128), writes=[Bwuq], semof=Bwuq)
            S.dma("pool", wukv[:], I["w_ukv_c"].rearrange("(kc p) n -> p kc n", p=128), writes=[Bwukv], semof=Bwukv)
            S.dma("sp", gain_bc[:], BC[3], writes=[Bgain], semof=Bgain)
            S.dma("sp", shift_bc[:], BC[4], writes=[Bshift], semof=Bshift)

            def prep_tile(row0, hslot, t):
                s = cnt["tile"] % 2
                cnt["tile"] += 1
                x_, Bx_ = xt[s]
                h_, Bh_ = hm[s]
                s_, Bs_ = ss[s]
                hT, BhT = hmT[hslot]
                S.dma("sp", x_[:], I["xs"][row0:row0 + 128, :], writes=[Bx_], semof=Bx_)
                S.op("act", lambda e: e.activation(out=h_[:], in_=x_[:], func=AF.Square, accum_out=s_[:, 0:1]), reads=[Bx_], writes=[Bh_, Bs_])
                S.op("act", lambda e: e.activation(out=s_[:, 1:2], in_=s_[:, 0:1], func=AF.Sqrt, scale=1.0 / D, bias=epsc[:, 0:1]),
                     reads=[Bs_, Bepsc], writes=[Bs_])
                S.op("dve", lambda e: e.reciprocal(out=s_[:, 2:3], in_=s_[:, 1:2]), reads=[Bs_], writes=[Bs_])
                S.op("dve", lambda e: e.scalar_tensor_tensor(out=x_[:], in0=x_[:], scalar=s_[:, 2:3], in1=gain_bc[:], op0=ALU.mult, op1=ALU.mult),
                     reads=[Bx_, Bs_, Bgain], writes=[Bx_])
                S.op("pool", lambda e: e.tensor_tensor(out=h_[:], in0=x_[:], in1=shift_bc[:], op=ALU.add), reads=[Bx_, Bshift], writes=[Bh_])
                for half in range(2):
                    p_, Bp_ = pT[half]

                    def trf(e, half=half, p_=p_):
                        for j in range(8):
                            kc = half * 8 + j
                            ins = e.transpose(p_[:, j * 128:(j + 1) * 128], h_[:, kc * 128:(kc + 1) * 128], identb[:])
                        return ins
                    S.op("pe", trf, reads=[Bh_, Bidentb], writes=[Bp_])
                    cp = (lambda e, half=half, p_=p_: e.activation(out=hT[:, half * 8:(half + 1) * 8, t * 128:(t + 1) * 128],
                                                                   in_=p_[:, :].rearrange("p (j n) -> p j n", n=128), func=AF.Copy)) if half == 0 else \
                         (lambda e, half=half, p_=p_: e.tensor_copy(out=hT[:, half * 8:(half + 1) * 8, t * 128:(t + 1) * 128],
                                                                    in_=p_[:, :].rearrange("p (j n) -> p j n", n=128)))
                    S.op("act" if half == 0 else "dve", cp, reads=[Bp_], writes=[BhT])

            def rstd_from(psum_ap, Bps, out_t, Bout, npart, N, inv_n):
                S.op("act", lambda e: e.activation(out=out_t[0:npart, 0:N], in_=psum_ap, func=AF.Sqrt, scale=inv_n, bias=epsc[0:npart, 0:1]),
                     reads=[Bps, Bepsc], writes=[Bout])
                S.op("dve", lambda e: e.reciprocal(out=out_t[0:npart, 0:N], in_=out_t[0:npart, 0:N]), reads=[Bout], writes=[Bout])

            def rope_apply(src, Bsrc, N, dst_dram):
                pr, Bpr = next_po()
                S.op("pe", lambda e: e.matmul(pr[0:64, 0:N], ropePT[0:64, 0:64], src[0:64, 0:N], start=True, stop=True),
                     reads=[Bsrc, BropePT], writes=[Bpr])
                S.op("dve", lambda e: e.tensor_tensor(out=t1[0:64, 0:N], in0=src[0:64, 0:N], in1=cost[0:64, 0:N], op=ALU.mult),
                     reads=[Bsrc, Bcost], writes=[Bt1])
                S.op("dve", lambda e: e.tensor_tensor(out=t2[0:64, 0:N], in0=pr[0:64, 0:N], in1=sint[0:64, 0:N], op=ALU.mult),
                     reads=[Bpr, Bsint], writes=[Bt2])
                S.op("pool", lambda e: e.tensor_tensor(out=qrf[0:64, 0:N], in0=t1[0:64, 0:N], in1=t2[0:64, 0:N], op=ALU.add),
                     reads=[Bt1, Bt2], writes=[Bqrf])
                S.dma("sp", dst_dram, qrf[0:64, 0:N], reads=[Bqrf], semof=Bqrf)

            def grp_info(gi):
                isx = gi > 0
                N = GS
                n0 = 256 + (gi - 1) * GS if isx else 0
                xo = (gi - 1) * GS
                return isx, N, n0, xo

            def dense_gen(gi, hslot):
                isx, N, n0, xo = grp_info(gi)
                hT, BhT = hmT[hslot]
                s2 = gi % 2
                twd, Btwd = twd2[s2]
                adb, Badb = adb2[s2]
                qd, Bqd = qd2[s2]
                kvd, Bkvd = kvd2[s2]
                kro, Bkro = kro2[s2]

                def mm_chunk(col0, M):
                    p_, Bp_ = next_po()

                    def f(e):
                        for kc in range(16):
                            ins = e.matmul(p_[0:M, 0:N], W[:, kc, col0:col0 + M], hT[:, kc, 0:N], start=(kc == 0), stop=(kc == 15))
                        return ins
                    S.op("pe", f, reads=[BW, BhT], writes=[Bp_])
                    return p_, Bp_

                for j in range(6):
                    p_, Bp_ = mm_chunk(j * 128, 128)
                    u_, Bu_ = urkv[cnt["u"] % 2]
                    cnt["u"] += 1
                    S.op("act", lambda e, p_=p_, u_=u_: e.activation(out=u_[:, 0:N], in_=p_[:, 0:N], func=AF.Copy), reads=[Bp_], writes=[Bu_])
                    S.dma("sp", U_v[:, j, n0:n0 + N], u_[:, 0:N], reads=[Bu_], semof=Bu_)
                    yield
                p_, Bp_ = mm_chunk(1024, 128)
                S.op("act", lambda e, p_=p_: e.activation(out=twd[:, 0:N], in_=p_[:, 0:N], func=AF.Tanh), reads=[Bp_], writes=[Btwd])
                yield
                p_, Bp_ = mm_chunk(1152, 128)
                S.op("act", lambda e, p_=p_: e.activation(out=adb[:, 0:N], in_=p_[:, 0:N], func=AF.Copy), reads=[Bp_], writes=[Badb])
                yield
                if isx:
                    for j in range(4):
                        p_, Bp_ = mm_chunk(1280 + j * 128, 128)
                        S.op("act", lambda e, p_=p_, j=j: e.activation(out=qd[:, j, 0:N], in_=p_[:, 0:N], func=AF.Copy), reads=[Bp_], writes=[Bqd])
                        yield
                for j in range(2):
                    p_, Bp_ = mm_chunk(1792 + j * 128, 128)
                    S.op("act", lambda e, p_=p_, j=j: e.activation(out=kvd[:, j, 0:N], in_=p_[:, 0:N], func=AF.Copy), reads=[Bp_], writes=[Bkvd])
                    yield
                p_, Bp_ = mm_chunk(2048, 64)
                S.op("act", lambda e, p_=p_: e.activation(out=kro[0:64, 0:N], in_=p_[0:64, 0:N], func=AF.Copy), reads=[Bp_], writes=[Bkro])
                yield
                if isx:
                    for j in range(2):
                        p_, Bp_ = mm_chunk(768 + j * 128, 128)
                        g_, Bg_ = gz[cnt["g"] % 2]
                        cnt["g"] += 1
                        S.op("act", lambda e, p_=p_, g_=g_: e.activation(out=g_[:, 0:N], in_=p_[:, 0:N], func=AF.Silu), reads=[Bp_], writes=[Bg_])
                        S.dma("sp", Gzr_v[:, j, xo:xo + N], g_[:, 0:N], reads=[Bg_], semof=Bg_)
                        yield
                    for j in range(2):
                        p_, Bp_ = mm_chunk(2112 + j * 128, 128)
                        g_, Bg_ = gz[cnt["g"] % 2]
                        cnt["g"] += 1
                        S.op("act", lambda e, p_=p_, g_=g_: e.activation(out=g_[:, 0:N], in_=p_[:, 0:N], func=AF.Silu), reads=[Bp_], writes=[Bg_])
                        S.dma("sp", Gzm_v[:, j, xo:xo + N], g_[:, 0:N], reads=[Bg_], semof=Bg_)
                        yield

            def chain_gen(gi):
                isx, N, n0, xo = grp_info(gi)
                ntile = N // 128
                s2 = gi % 2
                twd, Btwd = twd2[s2]
                adb, Badb = adb2[s2]
                qd, Bqd = qd2[s2]
                kvd, Bkvd = kvd2[s2]
                kro, Bkro = kro2[s2]
                if isx:
                    S.dma("sp", cost[:, 0:N], I["rope_cos"][:, xo:xo + N], writes=[Bcost], semof=Bcost)
                    S.dma("sp", sint[:, 0:N], I["rope_sin"][:, xo:xo + N], writes=[Bsint], semof=Bsint)
                for which, (src, Bsrc, dst_v, cbase) in enumerate(((twd, Btwd, SG_v, 0), (adb, Badb, AA_v, 2))):
                    for d in range(2):
                        for cc in range(2):
                            p_, Bp_ = next_po()
                            S.op("pe", lambda e, p_=p_, src=src, d=d, cc=cc, which=which: e.matmul(
                                p_[:, 0:N], wlora[64 * d:64 * d + 64, which, cc * 128:(cc + 1) * 128], src[64 * d:64 * d + 64, 0:N],
                                start=True, stop=True), reads=[Bwlora, Bsrc], writes=[Bp_])
                            s_, Bs_ = sga[cnt["s"] % 2]
                            cnt["s"] += 1
                            S.op("act", lambda e, p_=p_, s_=s_, d=d, cc=cc, cbase=cbase: e.activation(
                                out=s_[:, 0:N], in_=p_[:, 0:N], func=AF.Sigmoid, bias=chanv[:, cc, cbase + d:cbase + d + 1]),
                                reads=[Bp_, Bchanv], writes=[Bs_])
                            S.dma("sp", dst_v[:, d * 2 + cc, n0:n0 + N], s_[:, 0:N], reads=[Bs_], semof=Bs_)
                            yield
                if isx:
                    S.op("pool", lambda e: e.tensor_tensor(out=sqq[:, :, 0:N], in0=qd[:, :, 0:N], in1=qd[:, :, 0:N], op=ALU.mult), reads=[Bqd], writes=[Bsqq])
                    yield

                    def ssq(e):
                        for j in range(4):
                            ins = e.matmul(pst[:, 0:N], onesf[:, :], sqq[:, j, 0:N], start=(j == 0), stop=(j == 3))
                        return ins
                    S.op("pe", ssq, reads=[Bsqq, Bonesf], writes=[Bpst])
                    S.op("act", lambda e: e.activation(out=rq[:, 0:N], in_=pst[:, 0:N], func=AF.Sqrt, scale=1.0 / 512, bias=epsc[:, 0:1]),
                         reads=[Bpst, Bepsc], writes=[Brq])
                    yield
                    S.op("dve", lambda e: e.reciprocal(out=rq[:, 0:N], in_=rq[:, 0:N]), reads=[Brq], writes=[Brq])
                    yield
                    for j in range(4):
                        S.op("dve", lambda e, j=j: e.scalar_tensor_tensor(out=qn[:, j, 0:N], in0=qd[:, j, 0:N], scalar=mlav[:, j:j + 1], in1=rq[:, 0:N],
                                                                          op0=ALU.mult, op1=ALU.mult), reads=[Bqd, Bmlav, Brq], writes=[Bqn])
                    yield
                    for h in range(2):
                        p1, Bp1 = next_po()
                        p2, Bp2 = next_po()

                        def qup(e, h=h, p1=p1, p2=p2):
                            for kc in range(4):
                                e.matmul(p1[:, 0:N], wuq[:, kc, h * 192:h * 192 + 128], qn[:, kc, 0:N], start=(kc == 0), stop=(kc == 3))
                            for kc in range(4):
                                ins = e.matmul(p2[0:64, 0:N], wuq[:, kc, h * 192 + 128:h * 192 + 192], qn[:, kc, 0:N], start=(kc == 0), stop=(kc == 3))
                            return ins
                        S.op("pe", qup, reads=[Bwuq, Bqn], writes=[Bp1, Bp2])
                        S.op("act", lambda e, p1=p1: e.activation(out=qno[:, 0:N], in_=p1[:, 0:N], func=AF.Copy), reads=[Bp1], writes=[Bqno])
                        S.op("act", lambda e, p2=p2: e.activation(out=qro[0:64, 0:N], in_=p2[0:64, 0:N], func=AF.Copy), reads=[Bp2], writes=[Bqro])
                        yield
                        S.op("pool", lambda e: e.tensor_tensor(out=sqh[:, 0, 0:N], in0=qno[:, 0:N], in1=qno[:, 0:N], op=ALU.mult), reads=[Bqno], writes=[Bsqh])
                        S.op("pool", lambda e: e.tensor_tensor(out=sqh[0:64, 1, 0:N], in0=qro[0:64, 0:N], in1=qro[0:64, 0:N], op=ALU.mult), reads=[Bqro], writes=[Bsqh])
                        yield

                        def ssh(e):
                            e.matmul(pst[:, 0:N], onesf[:, :], sqh[:, 0, 0:N], start=True, stop=False)
                            return e.matmul(pst[:, 0:N], onesf[0:64, :], sqh[0:64, 1, 0:N], start=False, stop=True)
                        S.op("pe", ssh, reads=[Bsqh, Bonesf], writes=[Bpst])
                        S.op("act", lambda e: e.activation(out=rh[:, 0:N], in_=pst[:, 0:N], func=AF.Sqrt, scale=1.0 / 192, bias=epsc[:, 0:1]),
                             reads=[Bpst, Bepsc], writes=[Brh])
                        yield
                        S.op("dve", lambda e: e.reciprocal(out=rh[:, 0:N], in_=rh[:, 0:N]), reads=[Brh], writes=[Brh])
                        yield
                        S.op("dve", lambda e: e.scalar_tensor_tensor(out=qnf[:, 0:N], in0=qno[:, 0:N], scalar=mlav[:, 6:7], in1=rh[:, 0:N],
                                                                      op0=ALU.mult, op1=ALU.mult), reads=[Bqno, Bmlav, Brh], writes=[Bqnf])
                        S.dma("sp", QN_v[:, h, xo:xo + N], qnf[:, 0:N], reads=[Bqnf], semof=Bqnf)
                        S.op("dve", lambda e: e.scalar_tensor_tensor(out=qrg[0:64, 0:N], in0=qro[0:64, 0:N], scalar=mlav[0:64, 8:9], in1=rh[0:64, 0:N],
                                                                      op0=ALU.mult, op1=ALU.mult), reads=[Bqro, Bmlav, Brh], writes=[Bqrg])
                        yield
                        rope_apply(qrg, Bqrg, N, QR_v[:, h, xo:xo + N])
                        yield
                S.op("pool", lambda e: e.tensor_tensor(out=sqq[:, 0:2, 0:N], in0=kvd[:, :, 0:N], in1=kvd[:, :, 0:N], op=ALU.mult), reads=[Bkvd], writes=[Bsqq])
                yield

                def sskv(e):
                    for j in range(2):
                        ins = e.matmul(pst[:, 0:N], onesf[:, :], sqq[:, j, 0:N], start=(j == 0), stop=(j == 1))
                    return ins
                S.op("pe", sskv, reads=[Bsqq, Bonesf], writes=[Bpst])
                S.op("act", lambda e: e.activation(out=rq[:, 0:N], in_=pst[:, 0:N], func=AF.Sqrt, scale=1.0 / 256, bias=epsc[:, 0:1]),
                     reads=[Bpst, Bepsc], writes=[Brq])
                yield
                S.op("dve", lambda e: e.reciprocal(out=rq[:, 0:N], in_=rq[:, 0:N]), reads=[Brq], writes=[Brq])
                yield
                for j in range(2):
                    S.op("dve", lambda e, j=j: e.scalar_tensor_tensor(out=kvn[:, j, 0:N], in0=kvd[:, j, 0:N], scalar=mlav[:, 4 + j:5 + j], in1=rq[:, 0:N],
                                                                      op0=ALU.mult, op1=ALU.mult), reads=[Bkvd, Bmlav, Brq], writes=[Bkvn])
                S.op("pool", lambda e: e.tensor_tensor(out=sqh[0:64, 1, 0:N], in0=kro[0:64, 0:N], in1=kro[0:64, 0:N], op=ALU.mult), reads=[Bkro], writes=[Bsqh])
                yield
                for h in range(2):
                    p1, Bp1 = next_po()

                    def kup(e, h=h, p1=p1):
                        for kc in range(2):
                            ins = e.matmul(p1[:, 0:N], wukv[:, kc, h * 256:h * 256 + 128], kvn[:, kc, 0:N], start=(kc == 0), stop=(kc == 1))
                        return ins
                    S.op("pe", kup, reads=[Bwukv, Bkvn], writes=[Bp1])
                    S.op("act", lambda e, p1=p1: e.activation(out=qno[:, 0:N], in_=p1[:, 0:N], func=AF.Copy), reads=[Bp1], writes=[Bqno])

                    def vup(e, h=h):
                        for t in range(ntile):
                            for kc in range(2):
                                ins = e.matmul(pv[:, t * 128:(t + 1) * 128], kvn[:, kc, t * 128:(t + 1) * 128], wukv[:, kc, h * 256 + 128:h * 256 + 256],
                                               start=(kc == 0), stop=(kc == 1))
                        return ins
                    S.op("pe", vup, reads=[Bwukv, Bkvn], writes=[Bpv])
                    S.op("act", lambda e, h=h: e.activation(out=vts[:, h, 0:N], in_=pv[:, 0:N], func=AF.Copy), reads=[Bpv], writes=[Bvts])
                    S.dma("sp", VT[h, :, n0:n0 + N], vts[:, h, 0:N], reads=[Bvts], semof=Bvts)
                    yield
                    S.op("pool", lambda e: e.tensor_tensor(out=sqh[:, 0, 0:N], in0=qno[:, 0:N], in1=qno[:, 0:N], op=ALU.mult), reads=[Bqno], writes=[Bsqh])
                    S.op("dve", lambda e: e.tensor_scalar(out=qnf[:, 0:N], in0=qno[:, 0:N], scalar1=mlav[:, 7:8], scalar2=None, op0=ALU.mult),
                         reads=[Bqno, Bmlav], writes=[Bqnf])
                    S.dma("sp", KN_v[:, h, n0:n0 + N], qnf[:, 0:N], reads=[Bqnf], semof=Bqnf)
                    yield

                    def kss(e, h=h):
                        for t in range(ntile):
                            c = t * 2 + h
                            e.matmul(pks[:, c:c + 1], sqh[:, 0, t * 128:(t + 1) * 128], onesf[:, 0:1], start=True, stop=False)
                            ins = e.matmul(pks[:, c:c + 1], sqh[0:64, 1, t * 128:(t + 1) * 128], onesf[0:64, 0:1], start=False, stop=True)
                        return ins
                    S.op("pe", kss, reads=[Bsqh, Bonesf], writes=[Bpks])
                    yield
                nk = ntile * 2
                t0 = (n0 // 128) * 2
                S.op("act", lambda e: e.activation(out=kst[:, 0:nk], in_=pks[:, 0:nk], func=AF.Sqrt, scale=1.0 / 192, bias=epsc[:, 0:1]),
                     reads=[Bpks, Bepsc], writes=[Bkst])
                S.op("dve", lambda e: e.tensor_scalar(out=qrg[0:64, 0:N], in0=kro[0:64, 0:N], scalar1=mlav[0:64, 9:10], scalar2=None, op0=ALU.mult),
                     reads=[Bkro, Bmlav], writes=[Bqrg])
                yield
                S.op("dve", lambda e: e.reciprocal(out=kst[:, 0:nk], in_=kst[:, 0:nk]), reads=[Bkst], writes=[Bkst])
                yield
                S.op("dve", lambda e: e.tensor_scalar(out=KS[:, t0:t0 + nk], in0=kst[:, 0:nk], scalar1=float(192 ** -0.5), scalar2=None, op0=ALU.mult),
                     reads=[Bkst], writes=[BKS])
                if isx:
                    rope_apply(qrg, Bqrg, N, KR[:, n0:n0 + N])
                else:
                    S.op("pool", lambda e: e.tensor_copy(out=qrf[0:64, 0:N], in_=qrg[0:64, 0:N]), reads=[Bqrg], writes=[Bqrf])
                    S.dma("sp", KR[:, n0:n0 + N], qrf[0:64, 0:N], reads=[Bqrf], semof=Bqrf)
                yield

            BWB = Buf("WB")
            wmg_v_ = I["w_in_mg"].rearrange("(kc p) n -> p kc n", p=128)
            wbr_r_v_ = I["w_br_r"].rearrange("(kc p) n -> p kc n", p=128)
            wbr_m_v_ = I["w_br_m"].rearrange("(kc p) n -> p kc n", p=128)
            wout_v_ = I["w_out"].rearrange("(kc p) n -> p kc n", p=128)
            castq = []
            for m in range(16):
                d_ = WMG_b[m].rearrange("p (k n) -> p k n", n=256)
                castq.append((d_[:, :, 0:128], wmg_v_[:, :, m * 128:(m + 1) * 128]))
                castq.append((d_[:, :, 128:256], wmg_v_[:, :, 2048 + m * 128:2048 + (m + 1) * 128]))
                d_ = WBR_b[m].rearrange("p (k n) -> p k n", n=256)
                castq.append((d_[:, :, 0:128], wbr_r_v_[:, :, m * 128:(m + 1) * 128]))
                castq.append((d_[:, :, 128:256], wbr_m_v_[:, :, m * 128:(m + 1) * 128]))
            for nb in range(4):
                castq.append((WOUT_b[nb].rearrange("p (k n) -> p k n", n=512), wout_v_[:, :, nb * 512:(nb + 1) * 512]))
            NGRP = 1 + NX // GS
            NGRP = int(os.environ.get("MK_NGRP", NGRP))
            for t in range(GS // 128):
                prep_tile(t * 128, 0, t)
            S.dma("sp", gain_bc[:], BC[0], writes=[Bgain], reads=[], semof=Bgain)
            S.dma("sp", shift_bc[:], BC[1], writes=[Bshift], reads=[], semof=Bshift)
            def gnext(g_):
                try:
                    next(g_)
                    return True
                except StopIteration:
                    return False

            chain = None
            for gi in range(NGRP):
                dense = dense_gen(gi, gi % 2)
                preps = []
                if gi + 1 < NGRP:
                    r0 = 256 + gi * GS
                    preps = [(lambda t=t, r0=r0, hs=(gi + 1) % 2: prep_tile(r0 + t * 128, hs, t)) for t in range(GS // 128)]
                rnd = 0
                dalive = True
                while dalive or chain is not None:
                    if dalive:
                        dalive = gnext(dense)
                    for _ in range((3 if dalive else 1000) if not os.environ.get('MK_NOINTER') else (0 if dalive else 1000)):
                        if chain is None:
                            break
                        if not gnext(chain):
                            chain = None
                    rnd += 1
                    if preps and (rnd % 6 == 0 or not dalive):
                        preps.pop(0)()
                while preps:
                    preps.pop(0)()
                chain = chain_gen(gi)
                for _ in range(3):
                    if castq:
                        o_, i_ = castq.pop(0)
                        S.dma("pool", o_, i_, semof=BWB)
            while chain is not None:
                if not gnext(chain):
                    chain = None
            while castq:
                o_, i_ = castq.pop(0)
                S.dma("pool", o_, i_, semof=BWB)
            if "KSD" in debug:
                S.dma("sp", KSD, KS[:], reads=[BKS], semof=BKS)
            S.barrier()
            S.emit()
            S.end_phase(recycle_hw=False)
        if upto == "A":
            return nc

        GR = 256
        NCH = GR // 64
        NXG = NX // GR
        U_k = U_rkv.rearrange("(k c p) n -> p k c n", k=3, c=2, p=128)
        YD_v = [v3(YD[d]) for d in range(2)]
        BD_v = [v3(BD[d]) for d in range(2)]
        with ExitStack() as prs:
            P = Ctx(nc, prs); P.n = 300
            omka, Bomka = P.sb([128, 2], F32, "omka")
            S.op("dve", lambda e: e.tensor_scalar(out=omka[:, :], in0=chanv[:, :, 5], scalar1=-1.0, scalar2=1.0, op0=ALU.mult, op1=ALU.add),
                 reads=[Bchanv], writes=[Bomka])

            class CP:
                pass
            cps = []
            for cc in range(2):
                for d in range(2):
                    c_ = CP()
                    c_.cc, c_.d = cc, d
                    for nm, shp, dt in (("ub", [128, 3, GR + 2], F32), ("cv", [128, 3, GR], F32), ("sgt", [128, GR], F32), ("aat", [128, GR], F32),
                                        ("sq", [128, GR], F32), ("rs", [128, GR], F32), ("kk", [128, GR], F32), ("ff", [128, GR], F32),
                                        ("kmod", [128, GR], F32), ("akk", [128, GR], F32), ("Pc", [128, GR], F32), ("Ei", [128, GR], F32),
                                        ("Ee", [128, GR], F32), ("g", [128, GR], F32), ("gp", [128, GR], F32), ("gi", [128, GR], F32),
                                        ("NA", [128, 256], BF16),
                                        ("KA", [128, 256], BF16), ("A0", [128, 128], BF16), ("PW0", [128, 256], BF16), ("PW1", [128, 256], BF16),
                                        ("Tm0", [128, 128], BF16), ("Tm1", [128, 128], BF16), ("TR", [128, 384], BF16), ("Xb", [128, 128], BF16),
                                        ("Ub", [128, 128], BF16), ("H", [128, 128], F32), ("Hb", [128, 128], BF16), ("S1", [128, 128], F32),
                                        ("pr", [128, GR], F32), ("bon", [128, GR], F32)):
                        t_, b_ = P.sb(shp, dt, nm)
                        setattr(c_, nm, t_)
                        setattr(c_, "B" + nm, b_)
                    for nm, shp, dt in (("gtot", [128, NCH], F32), ("AR", [128, NCH, 256], BF16), ("BE", [128, NCH, 128], BF16),
                                        ("KT", [128, NCH, 128], BF16), ("VB", [128, NCH, 128], BF16), ("Yg", [128, GR], F32)):
                        lst = [P.sb(shp, dt, nm) for _ in range(2)]
                        setattr(c_, nm, [x[0] for x in lst])
                        setattr(c_, "B" + nm, [x[1] for x in lst])
                    c_.Bsgt = c_.Bub
                    c_.Baat = c_.Bub
                    c_.bk1, c_.Bbk1 = P.ps([128, 512], F32, "bk1")
                    c_.bk2, c_.BpAD = P.ps([128, 512], F32, "bk2")
                    c_.BpTT = c_.BpAD
                    for nm in ("H", "Hb"):
                        t_ = getattr(c_, nm)
                        b_ = getattr(c_, "B" + nm)
                        S.op("pool", lambda e, t_=t_: e.memset(t_[:], 0.0), writes=[b_])
                    for nm in ("AR", "BE", "KT", "VB"):
                        for sl_ in range(2):
                            t_ = getattr(c_, nm)[sl_]
                            b_ = getattr(c_, "B" + nm)[sl_]
                            S.op("pool", lambda e, t_=t_: e.memset(t_[:], 0.0), writes=[b_])
                    cps.append(c_)

            if os.environ.get("MK_WARM"):
                def warm(e):
                    for _ in range(400):
                        ins = e.matmul(cps[0].bk1[:, :], msk[:, 0, :], resetb[:, :], start=True, stop=True)
                    return ins
                resetb, Bresetb = P.sb([128, 512], BF16, "resetb")
                S.op("pool", lambda e: e.memset(resetb[:], 1.0), writes=[Bresetb])
                S.op("pe", warm, reads=[Bresetb, Bmsk], writes=[cps[0].Bbk1])
            c3 = lambda ap: ap.rearrange("p (c t) -> p c t", t=64)

            def prep_gen(c, sl, n0, N, s0, s1, xo):
                cc, d = c.cc, c.d
                AR, BAR = c.AR[sl], c.BAR[sl]
                BE, BBE = c.BE[sl], c.BBE[sl]
                KT, BKT = c.KT[sl], c.BKT[sl]
                VB, BVB = c.VB[sl], c.BVB[sl]
                gtot, Bgtot = c.gtot[sl], c.Bgtot[sl]
                lo, hi = n0 - 1, n0 + N + 1
                dl, dh = 0, N + 2
                if n0 == s0:
                    S.op("pool", lambda e: e.memset(c.ub[:, :, 0:1], 0.0), writes=[c.Bub])
                    lo, dl = n0, 1
                if n0 + N == s1:
                    S.op("pool", lambda e: e.memset(c.ub[:, :, N + 1:N + 2], 0.0), writes=[c.Bub])
                    hi, dh = n0 + N, N + 1
                S.dma_group("sp", [(c.ub[:, :, dl:dh], U_k[:, :, cc, lo:hi]),
                                   (c.sgt[:, 0:N], SG_v[:, d * 2 + cc, n0:n0 + N]),
                                   (c.aat[:, 0:N], AA_v[:, d * 2 + cc, n0:n0 + N])], writes=[c.Bub], semof=c.Bub)
                yield
                for kind in range(3):
                    ch = kind * 2 + cc
                    S.op("act", lambda e, kind=kind, ch=ch: e.activation(out=c.cv[:, kind, 0:N], in_=c.ub[:, kind, 1:N + 1], func=AF.Copy,
                                                                         scale=convw[:, ch, 1:2]), reads=[c.Bub, Bconvw], writes=[c.Bcv])
                S.op("dve", lambda e: e.tensor_tensor_scan(out=c.Pc[:, 0:N], data0=resetm[:, 0:N], data1=c.sgt[:, 0:N], initial=0.0, op0=ALU.mult, op1=ALU.add),
                     reads=[Bresetm, c.Bsgt], writes=[c.BPc])
                yield
                for tap in (0, 2):
                    for kind in range(3):
                        ch = kind * 2 + cc
                        S.op("dve", lambda e, kind=kind, ch=ch, tap=tap: e.scalar_tensor_tensor(
                            out=c.cv[:, kind, 0:N], in0=c.ub[:, kind, tap:tap + N], scalar=convw[:, ch, tap:tap + 1],
                            in1=c.cv[:, kind, 0:N], op0=ALU.mult, op1=ALU.add), reads=[c.Bub, Bconvw, c.Bcv], writes=[c.Bcv])
                    yield
                nch = N // 64
                tot = c3(c.Pc[:, 0:N])[:, :, 63]
                if d == 0:
                    S.op("pool", lambda e: e.tensor_tensor(out=c.Ee[:, 0:N], in0=c.Pc[:, 0:N], in1=c.sgt[:, 0:N], op=ALU.subtract), reads=[c.BPc, c.Bsgt], writes=[c.BEe])
                    Ei, BEi = c.Pc, c.BPc
                else:
                    for k_ in range(nch):
                        S.op("pool", lambda e, k_=k_: e.tensor_scalar(out=c.Ee[:, k_ * 64:(k_ + 1) * 64], in0=c.Pc[:, k_ * 64:(k_ + 1) * 64], scalar1=-1.0,
                                                                      scalar2=c.Pc[:, k_ * 64 + 63:k_ * 64 + 64], op0=ALU.mult, op1=ALU.add),
                             reads=[c.BPc], writes=[c.BEe])
                    S.op("pool", lambda e: e.tensor_tensor(out=c.Ei[:, 0:N], in0=c.Ee[:, 0:N], in1=c.sgt[:, 0:N], op=ALU.add), reads=[c.BEe, c.Bsgt], writes=[c.BEi])
                    Ei, BEi = c.Ei, c.BEi
                S.op("act", lambda e: e.activation(out=c.sq[:, 0:N], in_=c.cv[:, 1, 0:N], func=AF.Square, scale=chanv[:, cc, 4:5]),
                     reads=[c.Bcv, Bchanv], writes=[c.Bsq])
                yield
                st_ = c.bk1[:, 0:N]
                Bst_ = c.Bbk1
                S.op("pe", lambda e: e.matmul(st_, bones[:, :], c.sq[:, 0:N], start=True, stop=True), reads=[c.Bsq, Bbones], writes=[Bst_])
                S.op("act", lambda e: e.activation(out=c.rs[:, 0:N], in_=st_, func=AF.Sqrt, bias=epsc[:, 1:2], scale=1.0), reads=[Bepsc], writes=[c.Brs, Bst_])
                yield
                S.op("act", lambda e: e.activation(out=c.g[:, 0:N], in_=Ei[:, 0:N], func=AF.Exp, scale=-C0), reads=[BEi], writes=[c.Bg])
                S.op("act", lambda e: e.activation(out=c.gp[:, 0:N], in_=c.Ee[:, 0:N], func=AF.Exp, scale=-C0), reads=[c.BEe], writes=[c.Bgp])
                S.op("act", lambda e: e.activation(out=c.gi[:, 0:N], in_=Ei[:, 0:N], func=AF.Exp, scale=C0), reads=[BEi], writes=[c.Bgi])
                S.op("act", lambda e: e.activation(out=gtot[:, 0:nch], in_=tot, func=AF.Exp, scale=-C0), reads=[c.BPc], writes=[Bgtot])
                S.op("pool", lambda e: e.tensor_scalar(out=c.ff[:, 0:N], in0=c.aat[:, 0:N], scalar1=chanv[:, cc, 5:6], scalar2=omka[:, cc:cc + 1],
                                                        op0=ALU.mult, op1=ALU.add), reads=[c.Baat, Bchanv, Bomka], writes=[c.Bff])
                S.op("pool", lambda e: e.tensor_tensor(out=c.kmod[:, 0:N], in0=c.cv[:, 1, 0:N], in1=c.ff[:, 0:N], op=ALU.mult), reads=[c.Bcv, c.Bff], writes=[c.Bkmod])
                S.op("dve", lambda e: e.reciprocal(out=c.rs[:, 0:N], in_=c.rs[:, 0:N]), reads=[c.Brs], writes=[c.Brs])
                yield
                S.op("dve", lambda e: e.scalar_tensor_tensor(out=c.pr[:, 0:N], in0=c.cv[:, 0, 0:N], scalar=chanv[:, cc, 8:9], in1=c.kmod[:, 0:N],
                                                              op0=ALU.mult, op1=ALU.mult), reads=[c.Bcv, Bchanv, c.Bkmod], writes=[c.Bpr])
                S.op("dve", lambda e: e.scalar_tensor_tensor(out=c.kk[:, 0:N], in0=c.cv[:, 1, 0:N], scalar=chanv[:, cc, 4:5], in1=c.rs[:, 0:N],
                                                              op0=ALU.mult, op1=ALU.mult), reads=[c.Bcv, Bchanv, c.Brs], writes=[c.Bkk])
                yield
                S.op("pe", lambda e: e.matmul(st_, bones[:, :], c.pr[:, 0:N], start=True, stop=True), reads=[c.Bpr, Bbones], writes=[Bst_])
                S.op("dve", lambda e: e.tensor_tensor(out=c.bon[:, 0:N], in0=st_, in1=c.cv[:, 2, 0:N], op=ALU.mult), reads=[c.Bcv], writes=[c.Bbon, Bst_])
                if xo is not None:
                    S.dma("sp", BD_v[d][:, cc, xo:xo + N], c.bon[:, 0:N], reads=[c.Bbon], semof=c.Bbon)
                yield
                S.op("pool", lambda e: e.tensor_tensor(out=c.akk[:, 0:N], in0=c.aat[:, 0:N], in1=c.kk[:, 0:N], op=ALU.mult), reads=[c.Baat, c.Bkk], writes=[c.Bakk])
                for hh in range(2):
                    ps_ = slice(64 * hh, 64 * hh + 64)
                    o1 = slice(64 * hh, 64 * hh + 64)
                    o2 = slice(128 + 64 * hh, 128 + 64 * hh + 64)
                    S.op("dve", lambda e, ps_=ps_, o2=o2: e.tensor_tensor(out=AR[ps_, 0:nch, o2], in0=c3(c.cv[ps_, 0, 0:N]), in1=c3(c.g[ps_, 0:N]), op=ALU.mult),
                         reads=[c.Bcv, c.Bg], writes=[BAR])
                    S.op("dve", lambda e, ps_=ps_, o1=o1: e.scalar_tensor_tensor(out=AR[ps_, 0:nch, o1], in0=c3(c.kk[ps_, 0:N]), scalar=-1.0, in1=c3(c.gp[ps_, 0:N]),
                                                                                 op0=ALU.mult, op1=ALU.mult), reads=[c.Bkk, c.Bgp], writes=[BAR])
                    S.op("pool", lambda e, ps_=ps_, o1=o1: e.tensor_tensor(out=KT[ps_, 0:nch, o1], in0=c3(c.kmod[ps_, 0:N]), in1=c3(c.gi[ps_, 0:N]), op=ALU.mult),
                         reads=[c.Bkmod, c.Bgi], writes=[BKT])
                    S.op("pool", lambda e, ps_=ps_, o1=o1: e.tensor_copy(out=VB[ps_, 0:nch, o1], in_=c3(c.cv[ps_, 2, 0:N])), reads=[c.Bcv], writes=[BVB])
                yield
                for hh in range(2):
                    ps_ = slice(64 * hh, 64 * hh + 64)
                    o1 = slice(64 * hh, 64 * hh + 64)
                    S.op("pool", lambda e, ps_=ps_, o1=o1: e.tensor_tensor(out=BE[ps_, 0:nch, o1], in0=c3(c.akk[ps_, 0:N]), in1=c3(c.gi[ps_, 0:N]), op=ALU.mult),
                         reads=[c.Bakk, c.Bgi], writes=[BBE])
                yield

            def chunk_gen(c, sl, k, want_y):
                AR, BAR = c.AR[sl], c.BAR[sl]
                BE, BBE = c.BE[sl], c.BBE[sl]
                KT, BKT = c.KT[sl], c.BKT[sl]
                VB, BVB = c.VB[sl], c.BVB[sl]
                gtot, Bgtot = c.gtot[sl], c.Bgtot[sl]
                Yg, BYg = c.Yg[sl], c.BYg[sl]
                bk1, Bbk1, bk2, BpAD, BpTT = c.bk1, c.Bbk1, c.bk2, c.BpAD, c.BpTT
                mN = (msk[:, 0:2, :] if c.d == 0 else msk[:, 2:4, :]).rearrange("p a b -> p (a b)")
                mA = msk[:, 2, :] if c.d == 0 else msk[:, 0, :]

                def f1(e):
                    e.matmul(bk1[:, 0:256], BE[:, k, :], AR[:, k, :], start=True, stop=True)
                    return e.matmul(bk1[:, 256:512], KT[:, k, :], AR[:, k, :], start=True, stop=True)
                S.op("pe", f1, reads=[BBE, BKT, BAR], writes=[Bbk1])
                S.op("pe", lambda e: e.matmul(bk2[:, 0:128], AR[:, k, 0:128], BE[:, k, :], start=True, stop=True), reads=[BAR, BBE], writes=[BpAD])
                S.op("dve", lambda e: e.tensor_tensor(out=c.NA[:, :], in0=bk1[:, 0:256], in1=mN, op=ALU.mult), reads=[Bmsk], writes=[c.BNA, Bbk1])
                S.op("dve", lambda e: e.tensor_tensor(out=c.KA[:, :], in0=bk1[:, 256:512], in1=mN, op=ALU.mult), reads=[Bmsk], writes=[c.BKA, Bbk1])
                S.op("dve", lambda e: e.tensor_tensor(out=c.A0[:, :], in0=bk2[:, 0:128], in1=mA, op=ALU.mult), reads=[Bmsk], writes=[c.BA0, BpAD])
                yield
                def f2(e):
                    e.matmul(bk1[:, 0:128], BE[:, k, :], identb[:, :], start=True, stop=True)
                    e.matmul(bk1[:, 128:256], KT[:, k, :], identb[:, :], start=True, stop=True)
                    return e.matmul(bk1[:, 256:384], VB[:, k, :], identb[:, :], start=True, stop=True)
                S.op("pe", f2, reads=[BBE, BKT, BVB, Bidentb], writes=[Bbk1])
                S.op("act", lambda e: e.activation(out=c.TR[:, :], in_=bk1[:, 0:384], func=AF.Copy), reads=[], writes=[c.BTR, Bbk1])
                yield
                S.op("pool", lambda e: e.tensor_tensor(out=c.Tm0[:, :], in0=c.NA[:, 0:128], in1=identb[:, :], op=ALU.add), reads=[c.BNA, Bidentb], writes=[c.BTm0])
                Nk, BNk, Ak, BAk = c.NA[:, 0:128], c.BNA, c.A0[:, :], c.BA0
                Tc, BTc = c.Tm0, c.BTm0
                for lvl in range(5):
                    pw, Bpw = (c.PW0, c.BPW0) if lvl % 2 == 0 else (c.PW1, c.BPW1)
                    if lvl < 4:
                        def f3(e, Nk=Nk, Ak=Ak):
                            e.matmul(bk2[:, 0:128], Ak, Nk, start=True, stop=True)
                            return e.matmul(bk2[:, 128:256], Nk, Ak, start=True, stop=True)
                        S.op("pe", f3, reads=[BNk, BAk], writes=[BpAD])
                        S.op("act", lambda e, pw=pw: e.activation(out=pw[:, :], in_=bk2[:, 0:256], func=AF.Copy), reads=[BpAD], writes=[Bpw])
                    else:
                        S.op("pe", lambda e, Nk=Nk, Ak=Ak: e.matmul(bk2[:, 128:256], Nk, Ak, start=True, stop=True), reads=[BNk, BAk], writes=[BpAD])
                        S.op("act", lambda e, pw=pw: e.activation(out=pw[:, 128:256], in_=bk2[:, 128:256], func=AF.Copy), reads=[BpAD], writes=[Bpw])
                    yield
                    Nk, BNk, Ak, BAk = pw[:, 0:128], Bpw, pw[:, 128:256], Bpw
                    Tn, BTn = (c.Tm1, c.BTm1) if lvl % 2 == 0 else (c.Tm0, c.BTm0)
                    S.op("pe", lambda e, Ak=Ak, Tc=Tc: e.matmul(bk1[:, 384:512], Ak, Tc[:, :], start=True, stop=True), reads=[BAk, BTc], writes=[Bbk1])
                    S.op("dve", lambda e, Tc=Tc, Tn=Tn: e.tensor_tensor(out=Tn[:, :], in0=bk1[:, 384:512], in1=Tc[:, :], op=ALU.add), reads=[BTc], writes=[BTn, Bbk1])
                    Tc, BTc = Tn, BTn
                    yield
                Tf, BTf = Tc, BTc
                def fx(e):
                    e.matmul(bk1[:, 0:128], c.KA[:, 0:128], c.TR[:, 256:384], start=True, stop=False)
                    return e.matmul(bk1[:, 0:128], AR[:, k, 0:128], c.Hb[:, :], start=False, stop=True)
                S.op("pe", fx, reads=[c.BKA, c.BTR, BAR, c.BHb], writes=[Bbk1])
                S.op("act", lambda e: e.activation(out=c.Xb[:, :], in_=bk1[:, 0:128], func=AF.Copy), reads=[Bbk1], writes=[c.BXb])
                yield
                S.op("pe", lambda e: e.matmul(bk1[:, 128:256], Tf[:, :], c.Xb[:, :], start=True, stop=True), reads=[BTf, c.BXb], writes=[Bbk1])
                S.op("act", lambda e: e.activation(out=c.Ub[:, :], in_=bk1[:, 128:256], func=AF.Copy), reads=[Bbk1], writes=[c.BUb])
                yield

                def fh(e):
                    e.matmul(bk1[:, 256:384], c.TR[:, 128:256], c.TR[:, 256:384], start=True, stop=False)
                    ins = e.matmul(bk1[:, 256:384], c.TR[:, 0:128], c.Ub[:, :], start=False, stop=True)
                    if want_y:
                        e.matmul(bk1[:, 384:512], c.Hb[:, :], AR[:, k, 128:256], start=True, stop=False)
                        e.matmul(bk1[:, 384:512], c.Ub[:, :], c.NA[:, 128:256], start=False, stop=False)
                        ins = e.matmul(bk1[:, 384:512], c.TR[:, 256:384], c.KA[:, 128:256], start=False, stop=True)
                    return ins
                S.op("pe", fh, reads=[c.BTR, c.BUb, c.BHb, BAR, c.BNA, c.BKA], writes=[Bbk1])
                S.op("dve", lambda e: e.tensor_tensor(out=c.S1[:, :], in0=bk1[:, 256:384], in1=c.H[:, :], op=ALU.add), reads=[Bbk1, c.BH], writes=[c.BS1])
                if want_y:
                    for hh in range(2):
                        ps_ = slice(64 * hh, 64 * hh + 64)
                        S.op("act", lambda e, ps_=ps_, hh=hh: e.activation(out=Yg[ps_, k * 64:(k + 1) * 64], in_=bk1[ps_, 384 + 64 * hh:384 + 64 * hh + 64], func=AF.Copy),
                             reads=[], writes=[BYg, Bbk1])
                yield
                S.op("act", lambda e: e.activation(out=c.Hb[:, :], in_=c.S1[:, :], func=AF.Copy, scale=gtot[:, k:k + 1]), reads=[c.BS1, Bgtot], writes=[c.BHb])
                S.op("pool", lambda e: e.tensor_scalar(out=c.H[:, :], in0=c.S1[:, :], scalar1=gtot[:, k:k + 1], scalar2=None, op0=ALU.mult),
                     reads=[c.BS1, Bgtot], writes=[c.BH])
                yield

            def step_info(c, step):
                if step == 0:
                    return 0, 0, 256, None
                xg = (step - 1) if c.d == 0 else (NXG - step)
                return 256 + xg * GR, 256, NT, xg * GR

            def run_rr(gens):
                alive = list(gens)
                while alive:
                    nxt = []
                    for g_ in alive:
                        try:
                            next(g_)
                            nxt.append(g_)
                        except StopIteration:
                            pass
                    alive = nxt

            NSTEP = int(os.environ.get("MK_RSTEPS", 1 + NXG))

            def mk_prep(c, step):
                n0, s0, s1, xo = step_info(c, step)
                return prep_gen(c, step % 2, n0, GR, s0, s1, xo)

            run_rr([mk_prep(c, 0) for c in cps])
            for step in range(NSTEP):
                isx = step > 0
                sl = step % 2

                def seq(c):
                    for ci in range(NCH):
                        k = ci if c.d == 0 else NCH - 1 - ci
                        yield from chunk_gen(c, sl, k, isx)
                    if isx:
                        xo = step_info(c, step)[3]
                        S.dma("sp", YD_v[c.d][:, c.cc, xo:xo + GR], c.Yg[sl][:, :], reads=[c.BYg[sl]], semof=c.BYg[sl])

                gens = [seq(c) for c in cps]
                preps = [mk_prep(c, step + 1) for c in cps] if step + 1 < NSTEP else []
                rnd = 0
                alive = gens
                while alive or preps:
                    nxt = []
                    for g_ in alive:
                        try:
                            next(g_)
                            nxt.append(g_)
                        except StopIteration:
                            pass
                    alive = nxt
                    rnd += 1
                    if preps and (rnd % 4 == 0 or not alive):
                        np_ = []
                        for g_ in preps:
                            try:
                                next(g_)
                                np_.append(g_)
                            except StopIteration:
                                pass
                        preps = np_
            S.barrier()
            S.emit()
            S.end_phase()
        if upto == "R":
            return nc

        NF = 512
        BOX = Buf("OX")
        with ExitStack() as pfs:
            P = Ctx(nc, pfs); P.n = 400
            ld = [[P.sb([128, NF], F32, "fld") for _ in range(5)] for _ in range(2)]
            ld = [[(t_, grp[0][1]) for (t_, _) in grp] for grp in ld]
            yy, Byy = P.sb([128, NF], F32, "yy")
            bs, Bbs = P.sb([128, NF], F32, "bs")
            ysq, Bysq = P.sb([128, NF], F32, "ysq")
            mm_, Bmm_ = P.sb([128, NF], F32, "mm")
            msq, Bmsq = P.sb([128, NF], F32, "msq")
            var, Bvar = P.sb([128, NF], F32, "var")
            yc, Byc = P.sb([128, NF], F32, "yc")
            ob = [P.sb([128, NF], BF16, "ob") for _ in range(2)]
            ps1 = [P.ps([128, 512], F32, "ps1") for _ in range(2)]
            ps2 = [P.ps([128, 512], F32, "ps2") for _ in range(2)]
            it = 0
            for cc in range(2):
                for ti in range(NX // NF):
                    xo = ti * NF
                    sl = it % 2
                    (y0, By0), (y1, By1), (b0, Bb0), (b1, Bb1), (gzt, Bgzt) = ld[sl]
                    S.dma_group("sp", [(y0[:], YD_v[0][:, cc, xo:xo + NF]), (y1[:], YD_v[1][:, cc, xo:xo + NF]),
                                       (b0[:], BD_v[0][:, cc, xo:xo + NF]), (b1[:], BD_v[1][:, cc, xo:xo + NF]),
                                       (gzt[:], Gzr_v[:, cc, xo:xo + NF])], writes=[By0], semof=By0)
                    p1, Bp1 = ps1[sl]
                    p2, Bp2 = ps2[sl]
                    o_, Bo_ = ob[sl]
                    S.op("pool", lambda e, y0=y0, y1=y1: e.tensor_tensor(out=yy[:], in0=y0[:], in1=y1[:], op=ALU.add), reads=[By0, By1], writes=[Byy])
                    S.op("pool", lambda e, b0=b0, b1=b1: e.tensor_tensor(out=bs[:], in0=b0[:], in1=b1[:], op=ALU.add), reads=[Bb0, Bb1], writes=[Bbs])
                    S.op("pe", lambda e, p1=p1: e.matmul(p1[:, :], bones[:, :], yy[:], start=True, stop=True), reads=[Byy, Bbones], writes=[Bp1])
                    S.op("act", lambda e: e.activation(out=ysq[:], in_=yy[:], func=AF.Square), reads=[Byy], writes=[Bysq])
                    S.op("pe", lambda e, p2=p2: e.matmul(p2[:, :], bones[:, :], ysq[:], start=True, stop=True), reads=[Bysq, Bbones], writes=[Bp2])
                    S.op("dve", lambda e, p1=p1: e.tensor_scalar(out=mm_[:], in0=p1[:, :], scalar1=1.0 / 64, scalar2=None, op0=ALU.mult), reads=[Bp1], writes=[Bmm_])
                    S.op("pool", lambda e: e.tensor_tensor(out=msq[:], in0=mm_[:], in1=mm_[:], op=ALU.mult), reads=[Bmm_], writes=[Bmsq])
                    S.op("dve", lambda e, p2=p2: e.scalar_tensor_tensor(out=var[:], in0=p2[:, :], scalar=1.0 / 64, in1=msq[:], op0=ALU.mult, op1=ALU.subtract),
                         reads=[Bp2, Bmsq], writes=[Bvar])
                    S.op("act", lambda e: e.activation(out=var[:], in_=var[:], func=AF.Sqrt, bias=epsc[:, 2:3], scale=1.0), reads=[Bvar, Bepsc], writes=[Bvar])
                    S.op("dve", lambda e: e.reciprocal(out=var[:], in_=var[:]), reads=[Bvar], writes=[Bvar])
                    S.op("pool", lambda e: e.tensor_tensor(out=yc[:], in0=yy[:], in1=mm_[:], op=ALU.subtract), reads=[Byy, Bmm_], writes=[Byc])
                    S.op("pool", lambda e: e.tensor_tensor(out=yc[:], in0=yc[:], in1=var[:], op=ALU.mult), reads=[Byc, Bvar], writes=[Byc])
                    S.op("dve", lambda e, cc=cc: e.tensor_scalar(out=yc[:], in0=yc[:], scalar1=chanv[:, cc, 6:7], scalar2=chanv[:, cc, 7:8], op0=ALU.mult, op1=ALU.add),
                         reads=[Byc, Bchanv], writes=[Byc])
                    S.op("pool", lambda e: e.tensor_tensor(out=yc[:], in0=yc[:], in1=bs[:], op=ALU.add), reads=[Byc, Bbs], writes=[Byc])
                    S.op("dve", lambda e, o_=o_, gzt=gzt: e.tensor_tensor(out=o_[:], in0=yc[:], in1=gzt[:], op=ALU.mult), reads=[Byc, Bgzt], writes=[Bo_])
                    S.dma("sp", OXs[2 * cc][:, xo:xo + NF], o_[0:64, :], reads=[Bo_], semof=Bo_)
                    S.dma("sp", OXs[2 * cc + 1][:, xo:xo + NF], o_[64:128, :], reads=[Bo_], semof=Bo_)
                    it += 1
            S.barrier()
            S.emit()
            S.end_phase()
        if upto == "F":
            return nc

        QG = 512
        NKT = NT // 128
        with ExitStack() as pms:
            P = Ctx(nc, pms); P.n = 500
            Kn, BKn = P.sb([128, NT], BF16, "Kn")
            Kr, BKr = P.sb([128, NT], BF16, "Kr")
            Vt, BVt = P.sb([128, NKT, 128], BF16, "Vt")
            onesb, Bonesb = P.sb([128, 128], BF16, "onesb")
            Qn = [P.sb([128, QG], BF16, "Qn") for _ in range(2)]
            Qr = [P.sb([128, QG], BF16, "Qr") for _ in range(2)]
            Pacc = [[P.sb([128, QG], F32, "Pacc") for _ in range(2)] for _ in range(2)]
            gmt = [P.sb([128, QG], F32, "gmt") for _ in range(2)]
            Pt = [P.sb([128, QG], BF16, "Pt") for _ in range(4)]
            rl, Brl = P.sb([128, QG], F32, "rl")
            oo, Boo = P.sb([128, QG], F32, "oo")
            om = [P.sb([128, QG], BF16, "om") for _ in range(2)]
            pS = [P.ps([128, 512], F32, "pS") for _ in range(4)]
            pO = [P.ps([128, 512], F32, "pO") for _ in range(2)]
            pL = [P.ps([128, 512], F32, "pL") for _ in range(2)]
            S.op("pool", lambda e: e.memset(onesb[:], 1.0), writes=[Bonesb])
            S.op("pool", lambda e: e.memset(Kr[64:128, :], 0.0), writes=[BKr])
            for q_, Bq_ in Qr:
                S.op("pool", lambda e, q_=q_: e.memset(q_[64:128, :], 0.0), writes=[Bq_])
            S.dma("sp", Kr[0:64, :], KR, writes=[BKr], semof=BKr)
            NQG = int(os.environ.get("MK_NQG", NX // QG))
            def attn_group(h, qg, sl):
                qo = qg * QG
                qn_, Bqn_ = Qn[sl]
                qr_, Bqr_ = Qr[sl]
                gm_, Bgm_ = gmt[sl]
                po_, Bpo_ = pO[sl]
                pl_, Bpl_ = pL[sl]
                o_, Bo_ = om[sl]
                S.dma("sp", qn_[:], QN_v[:, h, qo:qo + QG], writes=[Bqn_], semof=Bqn_)
                S.dma("sp", qr_[0:64, :], QR_v[:, h, qo:qo + QG], writes=[Bqr_], semof=Bqr_)
                (pa0, Bpa0), (pa1, Bpa1) = Pacc[sl]
                S.dma("sp", gm_[:], Gzm_v[:, h, qo:qo + QG], writes=[Bgm_], semof=Bgm_)

                def qk(kt):
                    ps_, Bps_ = pS[kt % 4]

                    def f(e):
                        e.matmul(ps_[:, :], Kn[:, kt * 128:(kt + 1) * 128], qn_[:, :], start=True, stop=False)
                        return e.matmul(ps_[:, :], Kr[:, kt * 128:(kt + 1) * 128], qr_[:, :], start=False, stop=True)
                    S.op("pe", f, reads=[BKn, BKr, Bqn_, Bqr_], writes=[Bps_])

                def ex_pv(kt):
                    ps_, Bps_ = pS[kt % 4]
                    pt_, Bpt_ = Pt[kt % 4]
                    S.op("act", lambda e: e.activation(out=pt_[:, :], in_=ps_[:, :], func=AF.Exp, scale=KS[:, kt * 2 + h:kt * 2 + h + 1]),
                         reads=[Bps_, BKS], writes=[Bpt_])

                    S.op("pe", lambda e: e.matmul(po_[:, :], Vt[:, kt, :], pt_[:, :], start=(kt == 0), stop=(kt == NKT - 1)),
                         reads=[BVt, Bpt_], writes=[Bpo_])
                    pa_, Bpa_ = (pa0, Bpa0) if kt % 2 == 0 else (pa1, Bpa1)
                    eng_ = "pool" if kt % 2 == 0 else "dve"
                    if kt < 2:
                        S.op(eng_, lambda e: e.tensor_copy(out=pa_[:, :], in_=pt_[:, :]), reads=[Bpt_], writes=[Bpa_])
                    else:
                        S.op(eng_, lambda e: e.tensor_tensor(out=pa_[:, :], in0=pa_[:, :], in1=pt_[:, :], op=ALU.add), reads=[Bpt_, Bpa_], writes=[Bpa_])

                qk(0)
                qk(1)
                for kt in range(NKT):
                    if kt + 2 < NKT:
                        qk(kt + 2)
                    ex_pv(kt)
                def lsum(e):
                    e.matmul(pl_[:, :], onesf[:, :], pa0[:, :], start=True, stop=False)
                    return e.matmul(pl_[:, :], onesf[:, :], pa1[:, :], start=False, stop=True)
                S.op("pe", lsum, reads=[Bonesf, Bpa0, Bpa1], writes=[Bpl_])
                S.op("dve", lambda e: e.reciprocal(out=rl[:, :], in_=pl_[:, :]), reads=[Bpl_], writes=[Brl])
                S.op("dve", lambda e: e.tensor_tensor(out=oo[:, :], in0=po_[:, :], in1=rl[:, :], op=ALU.mult), reads=[Bpo_, Brl], writes=[Boo])
                S.op("pool", lambda e: e.tensor_tensor(out=o_[:, :], in0=oo[:, :], in1=gm_[:, :], op=ALU.mult), reads=[Boo, Bgm_], writes=[Bo_])
                S.dma("sp", OXs[4 + 2 * h][:, qo:qo + QG], o_[0:64, :], reads=[Bo_], semof=Bo_)
                S.dma("sp", OXs[5 + 2 * h][:, qo:qo + QG], o_[64:128, :], reads=[Bo_], semof=Bo_)

            gcount = 0
            for h in range(2):
                S.dma("sp", Kn[:], KN_v[:, h, :], writes=[BKn], semof=BKn)
                S.dma("sp", Vt[:], VT[h].rearrange("p (t d) -> p t d", d=128), writes=[BVt], semof=BVt)
                for qg in range(NQG):
                    attn_group(h, qg, gcount % 2)
                    gcount += 1
            S.barrier()
            S.emit()
            S.end_phase()
        if upto == "M":
            return nc

        BOG = Buf("OG")
        for j in range(8):
            S.collective("AllGather", [[0, 1, 2, 3], [4, 5, 6, 7]], OXs[j], OGs[j], reads=[BOX], writes=[BOG], semof=BOG)
        S.barrier()
        S.emit()
        S.end_phase()

        CG = 512
        with ExitStack() as pcs:
            P = Ctx(nc, pcs); P.n = 600
            selq, Bselq = P.sb([128, 4], F32, "selq")
            gain_bc, Bgain = P.sb([128, D], F32, "gain_c")
            shift_bc, Bshift = P.sb([128, D], F32, "shift_c")
            gate_bc, Bgate = P.sb([128, D], F32, "gate_c")
            Bselq = Bshift = Bgate = Bgain
            xt = [P.sb([128, D], F32, "xtc") for _ in range(2)]
            hm = [P.sb([128, D], BF16, "hmc") for _ in range(2)]
            ss = [P.sb([128, 4], F32, "ssc") for _ in range(2)]
            hT, BhT = P.sb([128, 16, CG], BF16, "hTc")
            ldq = [P.sb([128, 16, CG], BF16, "ldq") for _ in range(2)]
            osel, Bosel = P.sb([128, 16, CG], BF16, "osel")
            wmg = [P.sb([128, 16, 256], BF16, "wmg") for _ in range(2)]
            wbr = [P.sb([128, 8, 256], BF16, "wbr") for _ in range(2)]
            sgr, Bsgr = P.sb([128, CG], F32, "sgr")
            sgm, Bsgm = P.sb([128, CG], F32, "sgm")
            tr_, Btr_ = P.sb([128, CG], F32, "tr")
            tm_, Btm_ = P.sb([128, CG], F32, "tm")
            merged, Bmerged = P.sb([128, 16, CG], BF16, "merged")
            wout = [P.sb([128, 16, 512], BF16, "wout") for _ in range(2)]
            xr = [P.sb([128, 512], F32, "xr") for _ in range(2)]
            res = [P.sb([128, 512], F32, "res") for _ in range(2)]
            pT = [P.ps([128, 1024], BF16, "pTc") for _ in range(2)]
            pg = [P.ps([128, 512], F32, "pg") for _ in range(4)]
            pout = [P.ps([128, 512], F32, "pout") for _ in range(2)]
            S.dma_group("sp", [(selq[:], I["selq"]), (gain_bc[:], BC[0]), (shift_bc[:], BC[1]), (gate_bc[:], BC[2])], writes=[Bgain], semof=Bgain)
            wmg_v = I["w_in_mg"].rearrange("(kc p) n -> p kc n", p=128)
            wbr_r_v = I["w_br_r"].rearrange("(kc p) n -> p kc n", p=128)
            wbr_m_v = I["w_br_m"].rearrange("(kc p) n -> p kc n", p=128)
            wout_v = I["w_out"].rearrange("(kc p) n -> p kc n", p=128)
            cnt = {"tile": 0, "w": 0, "o": 0, "r": 0}
            NCG = int(os.environ.get("MK_NCG", 2048 // CG))
            for gj in range(NCG):
                go = gj * CG
                for t in range(CG // 128):
                    s = cnt["tile"] % 2
                    cnt["tile"] += 1
                    x_, Bx_ = xt[s]
                    h_, Bh_ = hm[s]
                    s_, Bs_ = ss[s]
                    S.dma("sp", x_[:], I["xm"][go + t * 128:go + (t + 1) * 128, :], writes=[Bx_], semof=Bx_)
                    S.op("act", lambda e, x_=x_, h_=h_, s_=s_: e.activation(out=h_[:], in_=x_[:], func=AF.Square, accum_out=s_[:, 0:1]), reads=[Bx_], writes=[Bh_, Bs_])
                    S.op("act", lambda e, s_=s_: e.activation(out=s_[:, 1:2], in_=s_[:, 0:1], func=AF.Sqrt, scale=1.0 / D, bias=epsc[:, 0:1]), reads=[Bs_, Bepsc], writes=[Bs_])
                    S.op("dve", lambda e, s_=s_: e.reciprocal(out=s_[:, 2:3], in_=s_[:, 1:2]), reads=[Bs_], writes=[Bs_])
                    S.op("dve", lambda e, x_=x_, s_=s_: e.scalar_tensor_tensor(out=x_[:], in0=x_[:], scalar=s_[:, 2:3], in1=gain_bc[:], op0=ALU.mult, op1=ALU.mult),
                         reads=[Bx_, Bs_, Bgain], writes=[Bx_])
                    S.op("pool", lambda e, x_=x_, h_=h_: e.tensor_tensor(out=h_[:], in0=x_[:], in1=shift_bc[:], op=ALU.add), reads=[Bx_, Bshift], writes=[Bh_])
                    for half in range(2):
                        p_, Bp_ = pT[half]

                        def trf(e, half=half, p_=p_, h_=h_):
                            for j in range(8):
                                kc = half * 8 + j
                                ins = e.transpose(p_[:, j * 128:(j + 1) * 128], h_[:, kc * 128:(kc + 1) * 128], identb[:])
                            return ins
                        S.op("pe", trf, reads=[Bh_, Bidentb], writes=[Bp_])
                        S.op("act" if half == 0 else "dve",
                             (lambda e, half=half, p_=p_, t=t: e.activation(out=hT[:, half * 8:(half + 1) * 8, t * 128:(t + 1) * 128],
                                                                            in_=p_[:, :].rearrange("p (j n) -> p j n", n=128), func=AF.Copy)) if half == 0 else
                             (lambda e, half=half, p_=p_, t=t: e.tensor_copy(out=hT[:, half * 8:(half + 1) * 8, t * 128:(t + 1) * 128],
                                                                             in_=p_[:, :].rearrange("p (j n) -> p j n", n=128))),
                             reads=[Bp_], writes=[BhT])
                for q in range(4):
                    l_, Bl_ = ldq[q % 2]
                    S.dma_group("sp", [(l_[(j % 2) * 64:(j % 2) * 64 + 64, :, :].rearrange("p (r c) n -> p r c n", c=4)[:, :, j // 2, :],
                                        OGs[j].rearrange("(r p) n -> p r n", p=64)[:, :, q * 2048 + go:q * 2048 + go + CG]) for j in range(8)],
                                reads=[BOG], writes=[Bl_], semof=Bl_)
                    if q == 0:
                        S.op("dve", lambda e, l_=l_: e.tensor_scalar(out=osel[:], in0=l_[:], scalar1=selq[:, 0:1], scalar2=None, op0=ALU.mult),
                             reads=[Bl_, Bselq], writes=[Bosel])
                    else:
                        S.op("dve", lambda e, l_=l_, q=q: e.scalar_tensor_tensor(out=osel[:], in0=l_[:], scalar=selq[:, q:q + 1], in1=osel[:], op0=ALU.mult, op1=ALU.add),
                             reads=[Bl_, Bselq, Bosel], writes=[Bosel])
                for m in range(16):
                    w_, Bw_ = wmg[cnt["w"] % 2]
                    b_, Bb_ = wbr[cnt["w"] % 2]
                    cnt["w"] += 1
                    S.dma("sp", w_[:, :, :], WMG_b[m].rearrange("p (k n) -> p k n", n=256), writes=[Bw_], semof=Bw_)
                    S.dma("sp", b_[:, :, :], WBR_b[m].rearrange("p (k n) -> p k n", n=256), writes=[Bb_], semof=Bb_)
                    (pgr, Bpgr), (pgm, Bpgm), (ppr, Bppr), (ppm, Bppm) = pg

                    def fg(e, w_=w_):
                        for kc in range(16):
                            e.matmul(pgr[:, 0:CG], w_[:, kc, 0:128], hT[:, kc, :], start=(kc == 0), stop=(kc == 15))
                        for kc in range(16):
                            ins = e.matmul(pgm[:, 0:CG], w_[:, kc, 128:256], hT[:, kc, :], start=(kc == 0), stop=(kc == 15))
                        return ins
                    S.op("pe", fg, reads=[Bw_, BhT], writes=[Bpgr, Bpgm])
                    S.op("act", lambda e: e.activation(out=sgr[:, :], in_=pgr[:, 0:CG], func=AF.Sigmoid), reads=[Bpgr], writes=[Bsgr])
                    S.op("act", lambda e: e.activation(out=sgm[:, :], in_=pgm[:, 0:CG], func=AF.Sigmoid), reads=[Bpgm], writes=[Bsgm])

                    def fb(e, b_=b_):
                        for j in range(8):
                            kc = (j // 2) * 4 + (j % 2)
                            e.matmul(ppr[:, 0:CG], b_[:, j, 0:128], osel[:, kc, :], start=(j == 0), stop=(j == 7))
                        for j in range(8):
                            kc = (j // 2) * 4 + 2 + (j % 2)
                            ins = e.matmul(ppm[:, 0:CG], b_[:, j, 128:256], osel[:, kc, :], start=(j == 0), stop=(j == 7))
                        return ins
                    S.op("pe", fb, reads=[Bb_, Bosel], writes=[Bppr, Bppm])
                    S.op("dve", lambda e: e.tensor_tensor(out=tr_[:, :], in0=ppr[:, 0:CG], in1=sgr[:, :], op=ALU.mult), reads=[Bppr, Bsgr], writes=[Btr_])
                    S.op("dve", lambda e: e.tensor_tensor(out=tm_[:, :], in0=ppm[:, 0:CG], in1=sgm[:, :], op=ALU.mult), reads=[Bppm, Bsgm], writes=[Btm_])
                    S.op("pool", lambda e, m=m: e.tensor_tensor(out=merged[:, m, :], in0=tr_[:, :], in1=tm_[:, :], op=ALU.add), reads=[Btr_, Btm_], writes=[Bmerged])
                for nb in range(4):
                    wo_, Bwo_ = wout[cnt["o"] % 2]
                    cnt["o"] += 1
                    S.dma("sp", wo_[:], WOUT_b[nb].rearrange("p (k n) -> p k n", n=512), writes=[Bwo_], semof=Bwo_)
                    for t in range(CG // 128):
                        r_ = cnt["r"] % 2
                        cnt["r"] += 1
                        po_, Bpo_ = pout[r_]
                        xr_, Bxr_ = xr[r_]
                        rs_, Brs_ = res[r_]
                        S.dma("sp", xr_[:], I["xm"][go + t * 128:go + (t + 1) * 128, nb * 512:(nb + 1) * 512], writes=[Bxr_], semof=Bxr_)

                        def fo(e, wo_=wo_, po_=po_, t=t):
                            for kc in range(16):
                                ins = e.matmul(po_[:, :], merged[:, kc, t * 128:(t + 1) * 128], wo_[:, kc, :], start=(kc == 0), stop=(kc == 15))
                            return ins
                        S.op("pe", fo, reads=[Bwo_, Bmerged], writes=[Bpo_])
                        S.op("dve", lambda e, po_=po_, rs_=rs_, nb=nb: e.tensor_tensor(out=rs_[:], in0=po_[:, :], in1=gate_bc[:, nb * 512:(nb + 1) * 512], op=ALU.mult),
                             reads=[Bpo_, Bgate], writes=[Brs_])
                        S.op("pool", lambda e, rs_=rs_, xr_=xr_: e.tensor_tensor(out=rs_[:], in0=rs_[:], in1=xr_[:], op=ALU.add), reads=[Brs_, Bxr_], writes=[Brs_])
                        S.dma("sp", out[go + t * 128:go + (t + 1) * 128, nb * 512:(nb + 1) * 512], rs_[:], reads=[Brs_], semof=Brs_)
            S.barrier()
            S.emit()
            S.end_phase()
    return nc


_NC_CACHE = {}


def kernel(**inputs):
    maps = _host_inputs(inputs)
    if "nc" not in _NC_CACHE:
        _NC_CACHE["nc"] = build()
    nc = _NC_CACHE["nc"]
    res = run_bass_kernel_spmd(nc, maps, core_ids=list(range(8)))
    outp = np.zeros((2, NX, D), np.float32)
    for c in range(8):
        b, g = c // 4, c % 4
        outp[b, 2048 * g:2048 * g + 2048] = res.results[c]["out"]
    return outp
```

```python
import os
from contextlib import ExitStack
import numpy as np
import ml_dtypes
import concourse.bass as bass
import concourse.mybir as mybir
from concourse.bass_utils import run_bass_kernel_spmd

F32 = mybir.dt.float32
BF16 = mybir.dt.bfloat16
ALU = mybir.AluOpType
AF = mybir.ActivationFunctionType

NT = 8448
NX = 8192
NCTX = 256
D = 2048
WC = 2368
GS = 256
C0 = float(np.exp(-0.5))
EPS = 1e-6
GN_EPS = 64e-5


class Tok:
    __slots__ = ("sem", "val", "key")

    def __init__(self, sem, val, key):
        self.sem = sem
        self.val = val
        self.key = key


class DSem:
    _n = 0

    def __init__(self, sem, kind):
        DSem._n += 1
        self.uid = DSem._n
        self.sem = sem
        self.cnt = 0
        self.kind = kind


class Buf:
    def __init__(self, name, const=False):
        self.name = name
        self.w = None
        self.r = []
        self.const = const
        self.dsem = None
        self.dcnt = 0


class Sched:
    ENG = ["pe", "act", "dve", "pool", "sp"]

    def __init__(self, nc, stack):
        self.nc = nc
        self.stack = stack
        self.plan = {e: [] for e in self.ENG}
        self.ecnt = {e: 0 for e in self.ENG}
        self.esem = {}
        self.waited = {e: {} for e in self.ENG}
        self.nsem = 0
        for e in ("pe", "act", "dve", "pool"):
            self.esem[e] = self._newsem("e_" + e)
        self.dbufs = []
        self.free_dsems = {}
        self.ninst = 0

    def _newsem(self, name):
        self.nsem += 1
        return self.stack.enter_context(self.nc.semaphore(f"{name}_{self.nsem}"))

    def _waits(self, eng, toks):
        best = {}
        for t in toks:
            if t is not None and (t.key not in best or best[t.key].val < t.val):
                best[t.key] = t
        for t in best.values():
            if self.waited[eng].get(t.key, 0) >= t.val:
                continue
            if eng == "pe" and t.key == "e_pe":
                continue
            self.waited[eng][t.key] = t.val
            self.plan[eng].append(lambda e, sem=t.sem, v=t.val: e.wait_ge(sem, v))

    def _deps(self, reads, writes):
        deps = []
        for b in reads:
            deps.append(b.w)
        for b in writes:
            deps.append(b.w)
            deps.extend(b.r)
        return deps

    def _mark(self, tok, reads, writes):
        for b in reads:
            if not b.const:
                b.r.append(tok)
        for b in writes:
            b.w = tok
            b.r = []

    def op(self, eng, fn, reads=(), writes=()):
        self._waits(eng, self._deps(reads, writes))
        self.ecnt[eng] += 1
        self.ninst += 1
        tok = Tok(self.esem[eng], self.ecnt[eng], "e_" + eng)
        self.plan[eng].append(lambda e, fn=fn, sem=tok.sem: fn(e).then_inc(sem, 1))
        self._mark(tok, reads, writes)
        return tok

    def _dsem(self, b, kind):
        if b.dsem is None:
            fl = self.free_dsems.setdefault(kind, [])
            if fl:
                b.dsem = fl.pop()
            else:
                b.dsem = DSem(self._newsem("d" + kind), kind)
            self.dbufs.append(b)
        assert b.dsem.kind == kind, (b.name, b.dsem.kind, kind)

    def end_phase(self, recycle_hw=True):
        for b in self.dbufs:
            b.dsem = None
            b.w = None
            b.r = []
        self.dbufs = []

    def dma(self, q, out_ap, in_ap, reads=(), writes=(), semof=None, **kw):
        self._waits(q, self._deps(reads, writes))
        b = semof
        self._dsem(b, "sw" if q == "pool" else "hw")
        ds = b.dsem
        ds.cnt += 16
        tok = Tok(ds.sem, ds.cnt, "d%d" % ds.uid)
        self.plan[q].append(
            lambda e, o=out_ap, i=in_ap, sem=ds.sem, kw=kw: e.dma_start(out=o, in_=i, **kw).then_inc(sem, 16)
        )
        self._mark(tok, reads, writes)
        return tok

    def dma_group(self, q, pairs, reads=(), writes=(), semof=None):
        if os.environ.get("MK_SEQGROUP"):
            for (o, i) in pairs:
                tok = self.dma(q, o, i, reads=reads, writes=writes, semof=semof)
            return tok
        self._waits(q, self._deps(reads, writes))
        b = semof
        self._dsem(b, "sw" if q == "pool" else "hw")
        ds = b.dsem
        for (o, i) in pairs:
            ds.cnt += 16
            self.plan[q].append(lambda e, o=o, i=i, sem=ds.sem: e.dma_start(out=o, in_=i).then_inc(sem, 16))
        tok = Tok(ds.sem, ds.cnt, "d%d" % ds.uid)
        self._mark(tok, reads, writes)
        return tok

    def collective(self, kind, groups, in_ap, out_ap, reads, writes, semof):
        q = "pool"
        self._waits(q, self._deps(reads, writes))
        b = semof
        self._dsem(b, "cc")
        ds = b.dsem
        ds.cnt += 1
        tok = Tok(ds.sem, ds.cnt, "d%d" % ds.uid)
        self.plan[q].append(
            lambda e, sem=ds.sem: e.collective_compute(
                kind, ALU.bypass, replica_groups=groups, ins=[in_ap], outs=[out_ap]
            ).then_inc(sem, 1)
        )
        self._mark(tok, reads, writes)
        return tok

    def barrier(self):
        toks = []
        for e in ("pe", "act", "dve", "pool"):
            if self.ecnt[e] > 0:
                toks.append(Tok(self.esem[e], self.ecnt[e], "e_" + e))
        for b in self.dbufs:
            toks.append(Tok(b.dsem.sem, b.dsem.cnt, "d%d" % b.dsem.uid))
        for e in self.ENG:
            self._waits(e, toks)

    def emit(self):
        plan = self.plan
        with self.nc.Block() as block:

            @block.tensor
            def _(e):
                for f in plan["pe"]:
                    f(e)

            @block.scalar
            def _(e):
                for f in plan["act"]:
                    f(e)

            @block.vector
            def _(e):
                for f in plan["dve"]:
                    f(e)

            @block.gpsimd
            def _(e):
                for f in plan["pool"]:
                    f(e)

            @block.sync
            def _(e):
                for f in plan["sp"]:
                    f(e)

        self.plan = {e: [] for e in self.ENG}


class Ctx:
    def __init__(self, nc, stack):
        self.nc = nc
        self.stack = stack
        self.n = 0

    def sb(self, shape, dt, name=None):
        self.n += 1
        name = (name or "t") + f"_{self.n}"
        t = self.stack.enter_context(self.nc.sbuf_tensor(name, list(shape), dt))
        return t, Buf(name)

    def ps(self, shape, dt, name=None):
        self.n += 1
        name = (name or "p") + f"_{self.n}"
        t = self.stack.enter_context(self.nc.psum_tensor(name, list(shape), dt))
        return t, Buf(name)

    def sub(self):
        c = Ctx(self.nc, ExitStack())
        c.n = self.n + 1000
        return c


def _host_consts():
    idx = np.arange(64)
    us = (idx[:, None] < idx[None, :]).astype(np.float32)
    ui = (idx[:, None] <= idx[None, :]).astype(np.float32)
    ls = (idx[:, None] > idx[None, :]).astype(np.float32)
    li = (idx[:, None] >= idx[None, :]).astype(np.float32)
    masks = np.zeros((128, 4, 128), np.float32)
    for i, m in enumerate((us, ui, ls, li)):
        masks[0:64, i, 0:64] = m
        masks[64:128, i, 64:128] = m
    bones = np.zeros((128, 128), np.float32)
    bones[0:64, 0:64] = 1
    bones[64:128, 64:128] = 1
    sel = np.zeros((2, 2, 128), np.float32)
    sel[0, 0, :] = 1
    sel[1, 1, :] = 1
    reset = np.ones((128, 512), np.float32)
    reset[:, ::64] = 0
    rows = np.repeat(np.arange(128), 64).astype(np.float32)
    cols = np.tile(np.arange(64), 128).astype(np.float32)
    inv = np.power(np.float32(10000.0), -np.arange(0, 32, 2, dtype=np.float32) / np.float32(32)).astype(np.float32)
    ang = np.zeros((64, NX), np.float32)
    for d in range(64):
        pos = rows if d < 32 else cols
        ang[d] = pos * inv[d % 16]
    cos = np.cos(ang.astype(np.float64)).astype(np.float32)
    sin = np.sin(ang.astype(np.float64)).astype(np.float32)
    P = np.zeros((64, 64), np.float32)
    for d in range(64):
        if d % 32 < 16:
            P[d, d + 16] = -1
        else:
            P[d, d - 16] = 1
    return dict(
        ident=np.eye(128, dtype=np.float32), masks=masks, bones=bones, onesf=np.ones((128, 128), np.float32),
        sel=sel, reset=reset, rope_cos=cos, rope_sin=sin, ropePT=np.ascontiguousarray(P.T),
    )


def _host_inputs(inp):
    f = lambda a: np.ascontiguousarray(a, dtype=np.float32)
    consts = _host_consts()
    w_in = inp["w_in"][0]
    offs = np.cumsum([0, 3072, 1024, 128, 128, 512, 256, 64, 1024, 4096])
    o_rkv, o_zr, o_wd, o_ad, o_qd, o_kvd, o_kr, o_zm, o_mg = offs[:9]
    maps = []
    for c in range(8):
        b, g = c // 4, c % 4
        ch = slice(256 * g, 256 * g + 256)
        cols = np.concatenate([
            o_rkv + np.arange(256 * g, 256 * g + 256),
            o_rkv + 1024 + np.arange(256 * g, 256 * g + 256),
            o_rkv + 2048 + np.arange(256 * g, 256 * g + 256),
            o_zr + np.arange(256 * g, 256 * g + 256),
            o_wd + np.arange(128), o_ad + np.arange(128),
            o_qd + np.arange(512), o_kvd + np.arange(256), o_kr + np.arange(64),
            o_zm + np.arange(256 * g, 256 * g + 256),
        ])
        assert cols.size == WC
        conv = inp["conv_rkv"][0]
        conv_fm = np.zeros((128, 6, 3), np.float32)
        for kind in range(3):
            for cc in range(2):
                cidx = kind * 1024 + 256 * g + cc * 128 + np.arange(128)
                conv_fm[:, kind * 2 + cc, :] = conv[:, cidx].T
        chanv = np.zeros((128, 2, 10), np.float32)
        for cc in range(2):
            cidx = 256 * g + cc * 128 + np.arange(128)
            chanv[:, cc, 0] = inp["w0"][0, 0, cidx]
            chanv[:, cc, 1] = inp["w0"][0, 1, cidx]
            chanv[:, cc, 2] = inp["a0"][0, 0, cidx]
            chanv[:, cc, 3] = inp["a0"][0, 1, cidx]
            chanv[:, cc, 4] = inp["k_k"][0, cidx]
            chanv[:, cc, 5] = inp["k_a"][0, cidx]
            chanv[:, cc, 6] = inp["ln_x_w"][0, cidx]
            chanv[:, cc, 7] = inp["ln_x_b"][0, cidx]
            chanv[:, cc, 8] = inp["r_k"][0].reshape(-1)[cidx]
        wlora = np.zeros((128, 2, 256), np.float32)
        for d in range(2):
            wlora[64 * d:64 * d + 64, 0, :] = inp["w_decay_up"][0, d][:, ch]
            wlora[64 * d:64 * d + 64, 1, :] = inp["w_a_up"][0, d][:, ch]
        mlav = np.zeros((128, 10), np.float32)
        mlav[:, 0:4] = inp["q_norm_w"][0].reshape(4, 128).T
        mlav[:, 4:6] = inp["kv_norm_w"][0].reshape(2, 128).T
        mlav[:, 6] = inp["q_gain"][0][:128]
        mlav[:, 7] = inp["k_gain"][0][:128]
        mlav[0:64, 8] = inp["q_gain"][0][128:]
        mlav[0:64, 9] = inp["k_gain"][0][128:]
        hq = [2 * g, 2 * g + 1]
        w_uq_c = np.concatenate([inp["w_uq"][0][:, h * 192:(h + 1) * 192] for h in hq], axis=1)
        w_ukv_c = np.concatenate([inp["w_ukv"][0][:, h * 256:(h + 1) * 256] for h in hq], axis=1)
        cfm = np.stack([inp["c"][b].reshape(16, 128).T, inp["c_ctx"].reshape(16, 128).T], axis=-1)
        selq = np.zeros((128, 4), np.float32)
        selq[:, g] = 1
        m = dict(
            xs=f(np.concatenate([inp["ctx"][b], inp["x"][b]], axis=0)),
            xm=f(inp["x"][b, 2048 * g:2048 * g + 2048]),
            cfm=f(cfm), norm_w_row=f(np.stack([inp["norm_w"][0]] * 2)), b_mod_row=f(np.stack([inp["b_mod"][0]] * 2)),
            w_mod=f(inp["w_mod"][0]), w_in_core=f(w_in[:, cols]), w_in_mg=f(w_in[:, o_mg:o_mg + 4096]),
            conv_fm=f(conv_fm), chanv=f(chanv), wlora=f(wlora), mlav=f(mlav), w_uq_c=f(w_uq_c), w_ukv_c=f(w_ukv_c),
            w_br_r=f(inp["w_branch_rwkv"][0]), w_br_m=f(inp["w_branch_mla"][0]), w_out=f(inp["w_out"][0]),
            selq=f(selq),
        )
        m.update({k: f(v) for k, v in consts.items()})
        maps.append(m)
    return maps


INPUT_SHAPES = dict(
    xs=[NT, D], xm=[2048, D], cfm=[128, 16, 2], norm_w_row=[2, D], b_mod_row=[2, 3 * D], w_mod=[D, 3 * D],
    w_in_core=[D, WC], w_in_mg=[D, 4096], conv_fm=[128, 6, 3], chanv=[128, 2, 10], wlora=[128, 2, 256],
    mlav=[128, 10], w_uq_c=[512, 384], w_ukv_c=[256, 512], w_br_r=[1024, D], w_br_m=[1024, D], w_out=[D, D],
    selq=[128, 4], ident=[128, 128], masks=[128, 4, 128], bones=[128, 128], onesf=[128, 128], sel=[2, 2, 128],
    reset=[128, 512], rope_cos=[64, NX], rope_sin=[64, NX], ropePT=[64, 64],
)


def build(debug=(), upto="C"):
    nc = bass.Bass("TRN2", target_bir_lowering=False)
    I = {k: nc.dram_tensor(k, s, F32, kind="ExternalInput").ap() for k, s in INPUT_SHAPES.items()}
    out = nc.dram_tensor("out", [2048, D], F32, kind="ExternalOutput").ap()

    def scratch(name, shape, dt):
        if name in debug:
            return nc.dram_tensor(name, shape, dt, kind="ExternalOutput").ap()
        return nc.dram_tensor(name, shape, dt).ap()

    BC = scratch("BC", [5, 128, D], F32)
    U_rkv = scratch("U_rkv", [768, NT], F32)
    G_zr = scratch("G_zr", [256, NX], F32)
    SG = scratch("SG", [512, NT], F32)
    AA = scratch("AA", [512, NT], F32)
    QN = scratch("QN", [256, NX], BF16)
    QR = scratch("QR", [128, NX], BF16)
    KN = scratch("KN", [256, NT], BF16)
    KR = scratch("KR", [64, NT], BF16)
    VT = scratch("VT", [2, 128, NT], BF16)
    G_zm = scratch("G_zm", [256, NX], F32)
    KSD = scratch("KSD", [128, 132], F32)
    YD = [scratch("YD0", [256, NX], F32), scratch("YD1", [256, NX], F32)]
    BD = [scratch("BD0", [256, NX], F32), scratch("BD1", [256, NX], F32)]
    DBG = [scratch(f"DBG{i}", [128, 512], BF16 if i < 2 else F32) for i in range(4)]
    OXs = [scratch(f"OX{j}", [64, NX], BF16) for j in range(8)]
    OGs = [scratch(f"OG{j}", [256, NX], BF16) for j in range(8)]

    WMG_b = scratch("WMG_b", [16, 128, 16 * 256], BF16)
    WBR_b = scratch("WBR_b", [16, 128, 8 * 256], BF16)
    WOUT_b = scratch("WOUT_b", [4, 128, 16 * 512], BF16)
    v3 = lambda ap, p=128: ap.rearrange("(c p) n -> p c n", p=p)
    U_v, Gzr_v, SG_v, AA_v = v3(U_rkv), v3(G_zr), v3(SG), v3(AA)
    QN_v, QR_v, KN_v, Gzm_v = v3(QN), v3(QR, 64), v3(KN), v3(G_zm)

    with ExitStack() as top:
        S = Sched(nc, top)
        T = Ctx(nc, top)

        identb, Bidentb = T.sb([128, 128], BF16, "identb")
        msk, Bmsk = T.sb([128, 4, 128], BF16, "msk")
        bones, Bbones = T.sb([128, 128], F32, "bones")
        onesf, Bonesf = T.sb([128, 128], F32, "onesf")
        selt, Bselt = T.sb([2, 2, 128], F32, "selt")
        resetm, Bresetm = T.sb([128, 512], F32, "resetm")
        ropePT, BropePT = T.sb([64, 64], F32, "ropePT")
        epsc, Bepsc = T.sb([128, 4], F32, "epsc")
        convw, Bconvw = T.sb([128, 6, 3], F32, "convw")
        chanv, Bchanv = T.sb([128, 2, 10], F32, "chanv")
        mlav, Bmlav = T.sb([128, 10], F32, "mlav")
        KS, BKS = T.sb([128, 132], F32, "KS")
        S.dma("pool", identb[:], I["ident"], writes=[Bidentb], semof=Bidentb)
        S.dma("pool", msk[:], I["masks"], writes=[Bmsk], semof=Bmsk)
        for t_, b_, k_ in ((bones, Bbones, "bones"), (onesf, Bonesf, "onesf"), (selt, Bselt, "sel"), (resetm, Bresetm, "reset"),
                           (ropePT, BropePT, "ropePT"), (convw, Bconvw, "conv_fm"), (chanv, Bchanv, "chanv"), (mlav, Bmlav, "mlav")):
            S.dma("sp", t_[:], I[k_], writes=[b_], semof=b_)
        S.op("pool", lambda e: e.memset(epsc[:, 0:1], EPS), writes=[Bepsc])
        S.op("pool", lambda e: e.memset(epsc[:, 1:2], 1e-12), writes=[Bepsc])
        S.op("pool", lambda e: e.memset(epsc[:, 2:3], GN_EPS), writes=[Bepsc])
        S.op("pool", lambda e: e.memset(epsc[:, 3:4], 0.0), writes=[Bepsc])
        for b_ in (Bidentb, Bmsk, Bbones, Bonesf, Bselt, Bresetm, BropePT, Bepsc, Bconvw, Bchanv, Bmlav):
            b_.const = True

        with ExitStack() as p0s:
            P = Ctx(nc, p0s); P.n = 100
            cf, Bcf = P.sb([128, 16, 2], F32, "cf")
            sc, Bsc = P.sb([128, 16, 2], BF16, "sc")
            b2, Bb2 = P.sb([2, 3 * D], F32, "b2")
            nw2, Bnw2 = P.sb([2, D], F32, "nw2")
            mrow, Bmrow = P.sb([2, 3 * D], F32, "mrow")
            grow, Bgrow = P.sb([2, D], F32, "grow")
            wm = [P.sb([128, 16, 512], BF16, "wm") for _ in range(2)]
            bct = [P.sb([128, D], F32, "bct") for _ in range(2)]
            pm, Bpm = P.ps([128, 512], F32, "pm")
            pb = [P.ps([128, 512], F32, "pb") for _ in range(2)]
            S.dma("sp", cf[:], I["cfm"], writes=[Bcf], semof=Bcf)
            S.dma("sp", b2[:], I["b_mod_row"], writes=[Bb2], semof=Bb2)
            S.dma("sp", nw2[:], I["norm_w_row"], writes=[Bnw2], semof=Bnw2)
            S.op("act", lambda e: e.activation(out=sc[:], in_=cf[:], func=AF.Silu), reads=[Bcf], writes=[Bsc])
            wmod_v = I["w_mod"].rearrange("(kc p) n -> p kc n", p=128)
            for cb in range(12):
                wt, Bwt = wm[cb % 2]
                S.dma("pool", wt[:], wmod_v[:, :, cb * 512:(cb + 1) * 512], writes=[Bwt], semof=Bwt)

                def mmf(e, wt=wt):
                    for kc in range(16):
                        ins = e.matmul(pm[0:2, :], sc[:, kc, :], wt[:, kc, :], start=(kc == 0), stop=(kc == 15))
                    return ins
                S.op("pe", mmf, reads=[Bsc, Bwt], writes=[Bpm])
                S.op("act", lambda e, cb=cb: e.activation(out=mrow[0:2, cb * 512:(cb + 1) * 512], in_=pm[0:2, :], func=AF.Copy),
                     reads=[Bpm], writes=[Bmrow])
            S.op("pool", lambda e: e.tensor_tensor(out=mrow[:], in0=mrow[:], in1=b2[:], op=ALU.add), reads=[Bmrow, Bb2], writes=[Bmrow])
            S.op("dve", lambda e: e.scalar_tensor_tensor(out=grow[:], in0=mrow[:, D:2 * D], scalar=1.0, in1=nw2[:],
                                                          op0=ALU.add, op1=ALU.mult), reads=[Bmrow, Bnw2], writes=[Bgrow])
            plan0 = [(0, grow, Bgrow, 0, 0), (1, mrow, Bmrow, 0, 0), (2, mrow, Bmrow, 2 * D, 0), (3, grow, Bgrow, 0, 1), (4, mrow, Bmrow, 0, 1)]
            k = 0
            for (idx, src, Bsrc, off, si) in plan0:
                st, Bst = bct[idx % 2]
                for blk in range(4):
                    pt_, Bpt_ = pb[k % 2]
                    k += 1
                    S.op("pe", lambda e, pt_=pt_, src=src, off=off, blk=blk, si=si: e.matmul(
                        pt_[:, :], selt[0:2, si, :], src[0:2, off + blk * 512: off + (blk + 1) * 512], start=True, stop=True),
                        reads=[Bsrc, Bselt], writes=[Bpt_])
                    S.op("act", lambda e, pt_=pt_, st=st, blk=blk: e.activation(out=st[:, blk * 512:(blk + 1) * 512], in_=pt_[:, :], func=AF.Copy),
                         reads=[Bpt_], writes=[Bst])
                S.dma("sp", BC[idx], st[:], reads=[Bst], semof=Bst)
            S.barrier()
            S.emit()
            S.end_phase()
        if upto == "0":
            return nc

        with ExitStack() as pas:
            P = Ctx(nc, pas); P.n = 200
            W, BW = P.sb([128, 16, WC], BF16, "W")
            wlora, Bwlora = P.sb([128, 2, 256], BF16, "wlora")
            wuq, Bwuq = P.sb([128, 4, 384], BF16, "wuq")
            wukv, Bwukv = P.sb([128, 2, 512], BF16, "wukv")
            gain_bc, Bgain = P.sb([128, D], F32, "gain_bc")
            shift_bc, Bshift = P.sb([128, D], F32, "shift_bc")
            xt = [P.sb([128, D], F32, "xt") for _ in range(2)]
            hm = [P.sb([128, D], BF16, "hm") for _ in range(2)]
            ss = [P.sb([128, 4], F32, "ss") for _ in range(2)]
            hmT = [P.sb([128, 16, GS], BF16, "hmT") for _ in range(2)]
            urkv = [P.sb([128, GS], F32, "urkv") for _ in range(2)]
            gz = [P.sb([128, GS], F32, "gz") for _ in range(2)]
            sga = [P.sb([128, GS], F32, "sga") for _ in range(2)]
            twd2 = [P.sb([128, GS], BF16, "twd") for _ in range(2)]
            adb2 = [P.sb([128, GS], BF16, "adb") for _ in range(2)]
            qd2 = [P.sb([128, 4, GS], F32, "qd") for _ in range(2)]
            sqq, Bsqq = P.sb([128, 4, GS], F32, "sqq")
            qn, Bqn = P.sb([128, 4, GS], BF16, "qn")
            rq, Brq = P.sb([128, GS], F32, "rq")
            qno, Bqno = P.sb([128, GS], F32, "qno")
            qro, Bqro = P.sb([64, GS], F32, "qro")
            sqh, Bsqh = P.sb([128, 2, GS], F32, "sqh")
            rh, Brh = P.sb([128, GS], F32, "rh")
            qnf, Bqnf = P.sb([128, GS], BF16, "qnf")
            qrg, Bqrg = P.sb([64, GS], F32, "qrg")
            t1, Bt1 = P.sb([64, GS], F32, "t1")
            t2, Bt2 = P.sb([64, GS], F32, "t2")
            qrf, Bqrf = P.sb([64, GS], BF16, "qrf")
            cost, Bcost = P.sb([64, GS], F32, "cost")
            sint, Bsint = P.sb([64, GS], F32, "sint")
            kvd2 = [P.sb([128, 2, GS], F32, "kvd") for _ in range(2)]
            kvn, Bkvn = P.sb([128, 2, GS], BF16, "kvn")
            kro2 = [P.sb([64, GS], F32, "kro") for _ in range(2)]
            vts, Bvts = P.sb([128, 2, GS], BF16, "vts")
            kst, Bkst = P.sb([128, 8], F32, "kst")
            pT = [P.ps([128, 1024], BF16, "pT") for _ in range(2)]
            po = [P.ps([128, 512], F32, "po") for _ in range(3)]
            pst, Bpst = P.ps([128, 512], F32, "pst")
            pv, Bpv = P.ps([128, 512], F32, "pv")
            pks, Bpks = P.ps([128, 512], F32, "pks")
            cnt = {"po": 0, "tile": 0, "u": 0, "g": 0, "s": 0}

            def next_po():
                cnt["po"] += 1
                return po[cnt["po"] % 3]

            win_v = I["w_in_core"].rearrange("(kc p) n -> p kc n", p=128)
            S.dma("pool", W[:, :, :], win_v[:, :, :], writes=[BW], semof=BW)
            S.dma("pool", wlora[:], I["wlora"], writes=[Bwlora], semof=Bwlora)
            S.dma("pool", wuq[:], I["w_uq_c"].rearrange("(kc p) n -> p kc n", p=128), writes=[Bwuq], semof=Bwuq)
            S.dma("pool", wukv[:], I["w_ukv_c"].rearrange("(kc p) n -> p kc n", p=128), writes=[Bwukv], semof=Bwukv)
            S.dma("sp", gain_bc[:], BC[3], writes=[Bgain], semof=Bgain)
            S.dma("sp", shift_bc[:], BC[4], writes=[Bshift], semof=Bshift)

            def prep_a(row0):
                s = cnt["tile"] % 2
                cnt["tile"] += 1
                x_, Bx_ = xt[s]
                h_, Bh_ = hm[s]
                s_, Bs_ = ss[s]
                S.dma("sp", x_[:], I["xs"][row0:row0 + 128, :], writes=[Bx_], semof=Bx_)
                S.op("act", lambda e: e.activation(out=h_[:], in_=x_[:], func=AF.Square, accum_out=s_[:, 0:1]), reads=[Bx_], writes=[Bh_, Bs_])
                S.op("act", lambda e: e.activation(out=s_[:, 1:2], in_=s_[:, 0:1], func=AF.Sqrt, scale=1.0 / D, bias=epsc[:, 0:1]),
                     reads=[Bs_, Bepsc], writes=[Bs_])
                S.op("dve", lambda e: e.reciprocal(out=s_[:, 2:3], in_=s_[:, 1:2]), reads=[Bs_], writes=[Bs_])
                S.op("dve", lambda e: e.scalar_tensor_tensor(out=x_[:], in0=x_[:], scalar=s_[:, 2:3], in1=gain_bc[:], op0=ALU.mult, op1=ALU.mult),
                     reads=[Bx_, Bs_, Bgain], writes=[Bx_])
                S.op("pool", lambda e: e.tensor_tensor(out=h_[:], in0=x_[:], in1=shift_bc[:], op=ALU.add), reads=[Bx_, Bshift], writes=[Bh_])
                return s

            def prep_b(s, hslot, t):
                h_, Bh_ = hm[s]
                hT, BhT = hmT[hslot]
                for half in range(2):
                    p_, Bp_ = pT[half]

                    def trf(e, half=half, p_=p_):
                        for j in range(8):
                            kc = half * 8 + j
                            ins = e.transpose(p_[:, j * 128:(j + 1) * 128], h_[:, kc * 128:(kc + 1) * 128], identb[:])
                        return ins
                    S.op("pe", trf, reads=[Bh_, Bidentb], writes=[Bp_])
                    cp = (lambda e, half=half, p_=p_: e.activation(out=hT[:, half * 8:(half + 1) * 8, t * 128:(t + 1) * 128],
                                                                   in_=p_[:, :].rearrange("p (j n) -> p j n", n=128), func=AF.Copy)) if half == 0 else \
                         (lambda e, half=half, p_=p_: e.tensor_copy(out=hT[:, half * 8:(half + 1) * 8, t * 128:(t + 1) * 128],
                                                                    in_=p_[:, :].rearrange("p (j n) -> p j n", n=128)))
                    S.op("act" if half == 0 else "dve", cp, reads=[Bp_], writes=[BhT])

            def prep_tile(row0, hslot, t):
                prep_b(prep_a(row0), hslot, t)

            def rstd_from(psum_ap, Bps, out_t, Bout, npart, N, inv_n):
                S.op("act", lambda e: e.activation(out=out_t[0:npart, 0:N], in_=psum_ap, func=AF.Sqrt, scale=inv_n, bias=epsc[0:npart, 0:1]),
                     reads=[Bps, Bepsc], writes=[Bout])
                S.op("dve", lambda e: e.reciprocal(out=out_t[0:npart, 0:N], in_=out_t[0:npart, 0:N]), reads=[Bout], writes=[Bout])

            def rope_apply(src, Bsrc, N, dst_dram):
                pr, Bpr = next_po()
                S.op("pe", lambda e: e.matmul(pr[0:64, 0:N], ropePT[0:64, 0:64], src[0:64, 0:N], start=True, stop=True),
                     reads=[Bsrc, BropePT], writes=[Bpr])
                S.op("dve", lambda e: e.tensor_tensor(out=t1[0:64, 0:N], in0=src[0:64, 0:N], in1=cost[0:64, 0:N], op=ALU.mult),
                     reads=[Bsrc, Bcost], writes=[Bt1])
                S.op("dve", lambda e: e.tensor_tensor(out=t2[0:64, 0:N], in0=pr[0:64, 0:N], in1=sint[0:64, 0:N], op=ALU.mult),
                     reads=[Bpr, Bsint], writes=[Bt2])
                S.op("pool", lambda e: e.tensor_tensor(out=qrf[0:64, 0:N], in0=t1[0:64, 0:N], in1=t2[0:64, 0:N], op=ALU.add),
                     reads=[Bt1, Bt2], writes=[Bqrf])
                S.dma("sp", dst_dram, qrf[0:64, 0:N], reads=[Bqrf], semof=Bqrf)

            def grp_info(gi):
                isx = gi > 0
                N = GS
                n0 = 256 + (gi - 1) * GS if isx else 0
                xo = (gi - 1) * GS
                return isx, N, n0, xo

            def dense_gen(gi, hslot):
                isx, N, n0, xo = grp_info(gi)
                hT, BhT = hmT[hslot]
                s2 = gi % 2
                twd, Btwd = twd2[s2]
                adb, Badb = adb2[s2]
                qd, Bqd = qd2[s2]
                kvd, Bkvd = kvd2[s2]
                kro, Bkro = kro2[s2]

                def mm_chunk(col0, M):
                    p_, Bp_ = next_po()

                    def f(e):
                        for kc in range(16):
                            ins = e.matmul(p_[0:M, 0:N], W[:, kc, col0:col0 + M], hT[:, kc, 0:N], start=(kc == 0), stop=(kc == 15))
                        return ins
                    S.op("pe", f, reads=[BW, BhT], writes=[Bp_])
                    return p_, Bp_

                for j in range(6):
                    p_, Bp_ = mm_chunk(j * 128, 128)
                    u_, Bu_ = urkv[cnt["u"] % 2]
                    cnt["u"] += 1
                    S.op("act", lambda e, p_=p_, u_=u_: e.activation(out=u_[:, 0:N], in_=p_[:, 0:N], func=AF.Copy), reads=[Bp_], writes=[Bu_])
                    S.dma("sp", U_v[:, j, n0:n0 + N], u_[:, 0:N], reads=[Bu_], semof=Bu_)
                    yield
                p_, Bp_ = mm_chunk(1024, 128)
                S.op("act", lambda e, p_=p_: e.activation(out=twd[:, 0:N], in_=p_[:, 0:N], func=AF.Tanh), reads=[Bp_], writes=[Btwd])
                yield
                p_, Bp_ = mm_chunk(1152, 128)
                S.op("act", lambda e, p_=p_: e.activation(out=adb[:, 0:N], in_=p_[:, 0:N], func=AF.Copy), reads=[Bp_], writes=[Badb])
                yield
                if isx:
                    for j in range(4):
                        p_, Bp_ = mm_chunk(1280 + j * 128, 128)
                        S.op("act", lambda e, p_=p_, j=j: e.activation(out=qd[:, j, 0:N], in_=p_[:, 0:N], func=AF.Copy), reads=[Bp_], writes=[Bqd])
                        yield
                for j in range(2):
                    p_, Bp_ = mm_chunk(1792 + j * 128, 128)
                    S.op("act", lambda e, p_=p_, j=j: e.activation(out=kvd[:, j, 0:N], in_=p_[:, 0:N], func=AF.Copy), reads=[Bp_], writes=[Bkvd])
                    yield
                p_, Bp_ = mm_chunk(2048, 64)
                S.op("act", lambda e, p_=p_: e.activation(out=kro[0:64, 0:N], in_=p_[0:64, 0:N], func=AF.Copy), reads=[Bp_], writes=[Bkro])
                yield
                if isx:
                    for j in range(2):
                        p_, Bp_ = mm_chunk(768 + j * 128, 128)
                        g_, Bg_ = gz[cnt["g"] % 2]
                        cnt["g"] += 1
                        S.op("act", lambda e, p_=p_, g_=g_: e.activation(out=g_[:, 0:N], in_=p_[:, 0:N], func=AF.Silu), reads=[Bp_], writes=[Bg_])
                        S.dma("sp", Gzr_v[:, j, xo:xo + N], g_[:, 0:N], reads=[Bg_], semof=Bg_)
                        yield
                    for j in range(2):
                        p_, Bp_ = mm_chunk(2112 + j * 128, 128)
                        g_, Bg_ = gz[cnt["g"] % 2]
                        cnt["g"] += 1
                        S.op("act", lambda e, p_=p_, g_=g_: e.activation(out=g_[:, 0:N], in_=p_[:, 0:N], func=AF.Silu), reads=[Bp_], writes=[Bg_])
                        S.dma("sp", Gzm_v[:, j, xo:xo + N], g_[:, 0:N], reads=[Bg_], semof=Bg_)
                        yield

            def chain_gen(gi):
                isx, N, n0, xo = grp_info(gi)
                ntile = N // 128
                s2 = gi % 2
                twd, Btwd = twd2[s2]
                adb, Badb = adb2[s2]
                qd, Bqd = qd2[s2]
                kvd, Bkvd = kvd2[s2]
                kro, Bkro = kro2[s2]
                if isx:
                    S.dma("sp", cost[:, 0:N], I["rope_cos"][:, xo:xo + N], writes=[Bcost], semof=Bcost)
                    S.dma("sp", sint[:, 0:N], I["rope_sin"][:, xo:xo + N], writes=[Bsint], semof=Bsint)
                for which, (src, Bsrc, dst_v, cbase) in enumerate(((twd, Btwd, SG_v, 0), (adb, Badb, AA_v, 2))):
                    for d in range(2):
                        for cc in range(2):
                            p_, Bp_ = next_po()
                            S.op("pe", lambda e, p_=p_, src=src, d=d, cc=cc, which=which: e.matmul(
                                p_[:, 0:N], wlora[64 * d:64 * d + 64, which, cc * 128:(cc + 1) * 128], src[64 * d:64 * d + 64, 0:N],
                                start=True, stop=True), reads=[Bwlora, Bsrc], writes=[Bp_])
                            s_, Bs_ = sga[cnt["s"] % 2]
                            cnt["s"] += 1
                            S.op("act", lambda e, p_=p_, s_=s_, d=d, cc=cc, cbase=cbase: e.activation(
                                out=s_[:, 0:N], in_=p_[:, 0:N], func=AF.Sigmoid, bias=chanv[:, cc, cbase + d:cbase + d + 1]),
                                reads=[Bp_, Bchanv], writes=[Bs_])
                            S.dma("sp", dst_v[:, d * 2 + cc, n0:n0 + N], s_[:, 0:N], reads=[Bs_], semof=Bs_)
                            yield
                if isx:
                    S.op("pool", lambda e: e.tensor_tensor(out=sqq[:, :, 0:N], in0=qd[:, :, 0:N], in1=qd[:, :, 0:N], op=ALU.mult), reads=[Bqd], writes=[Bsqq])
                    yield

                    def ssq(e):
                        for j in range(4):
                            ins = e.matmul(pst[:, 0:N], onesf[:, :], sqq[:, j, 0:N], start=(j == 0), stop=(j == 3))
                        return ins
                    S.op("pe", ssq, reads=[Bsqq, Bonesf], writes=[Bpst])
                    S.op("act", lambda e: e.activation(out=rq[:, 0:N], in_=pst[:, 0:N], func=AF.Sqrt, scale=1.0 / 512, bias=epsc[:, 0:1]),
                         reads=[Bpst, Bepsc], writes=[Brq])
                    yield
                    S.op("dve", lambda e: e.reciprocal(out=rq[:, 0:N], in_=rq[:, 0:N]), reads=[Brq], writes=[Brq])
                    yield
                    for j in range(4):
                        S.op("dve", lambda e, j=j: e.scalar_tensor_tensor(out=qn[:, j, 0:N], in0=qd[:, j, 0:N], scalar=mlav[:, j:j + 1], in1=rq[:, 0:N],
                                                                          op0=ALU.mult, op1=ALU.mult), reads=[Bqd, Bmlav, Brq], writes=[Bqn])
                    yield
                    for h in range(2):
                        p1, Bp1 = next_po()
                        p2, Bp2 = next_po()

                        def qup(e, h=h, p1=p1, p2=p2):
                            for kc in range(4):
                                e.matmul(p1[:, 0:N], wuq[:, kc, h * 192:h * 192 + 128], qn[:, kc, 0:N], start=(kc == 0), stop=(kc == 3))
                            for kc in range(4):
                                ins = e.matmul(p2[0:64, 0:N], wuq[:, kc, h * 192 + 128:h * 192 + 192], qn[:, kc, 0:N], start=(kc == 0), stop=(kc == 3))
                            return ins
                        S.op("pe", qup, reads=[Bwuq, Bqn], writes=[Bp1, Bp2])
                        S.op("act", lambda e, p1=p1: e.activation(out=qno[:, 0:N], in_=p1[:, 0:N], func=AF.Copy), reads=[Bp1], writes=[Bqno])
                        S.op("act", lambda e, p2=p2: e.activation(out=qro[0:64, 0:N], in_=p2[0:64, 0:N], func=AF.Copy), reads=[Bp2], writes=[Bqro])
                        yield
                        S.op("pool", lambda e: e.tensor_tensor(out=sqh[:, 0, 0:N], in0=qno[:, 0:N], in1=qno[:, 0:N], op=ALU.mult), reads=[Bqno], writes=[Bsqh])
                        S.op("pool", lambda e: e.tensor_tensor(out=sqh[0:64, 1, 0:N], in0=qro[0:64, 0:N], in1=qro[0:64, 0:N], op=ALU.mult), reads=[Bqro], writes=[Bsqh])
                        yield

                        def ssh(e):
                            e.matmul(pst[:, 0:N], onesf[:, :], sqh[:, 0, 0:N], start=True, stop=False)
                            return e.matmul(pst[:, 0:N], onesf[0:64, :], sqh[0:64, 1, 0:N], start=False, stop=True)
                        S.op("pe", ssh, reads=[Bsqh, Bonesf], writes=[Bpst])
                        S.op("act", lambda e: e.activation(out=rh[:, 0:N], in_=pst[:, 0:N], func=AF.Sqrt, scale=1.0 / 192, bias=epsc[:, 0:1]),
                             reads=[Bpst, Bepsc], writes=[Brh])
                        yield
                        S.op("dve", lambda e: e.reciprocal(out=rh[:, 0:N], in_=rh[:, 0:N]), reads=[Brh], writes=[Brh])
                        yield
                        S.op("dve", lambda e: e.scalar_tensor_tensor(out=qnf[:, 0:N], in0=qno[:, 0:N], scalar=mlav[:, 6:7], in1=rh[:, 0:N],
                                                                      op0=ALU.mult, op1=ALU.mult), reads=[Bqno, Bmlav, Brh], writes=[Bqnf])
                        S.dma("sp", QN_v[:, h, xo:xo + N], qnf[:, 0:N], reads=[Bqnf], semof=Bqnf)
                        S.op("dve", lambda e: e.scalar_tensor_tensor(out=qrg[0:64, 0:N], in0=qro[0:64, 0:N], scalar=mlav[0:64, 8:9], in1=rh[0:64, 0:N],
                                                                      op0=ALU.mult, op1=ALU.mult), reads=[Bqro, Bmlav, Brh], writes=[Bqrg])
                        yield
                        rope_apply(qrg, Bqrg, N, QR_v[:, h, xo:xo + N])
                        yield
                S.op("pool", lambda e: e.tensor_tensor(out=sqq[:, 0:2, 0:N], in0=kvd[:, :, 0:N], in1=kvd[:, :, 0:N], op=ALU.mult), reads=[Bkvd], writes=[Bsqq])
                yield

                def sskv(e):
                    for j in range(2):
                        ins = e.matmul(pst[:, 0:N], onesf[:, :], sqq[:, j, 0:N], start=(j == 0), stop=(j == 1))
                    return ins
                S.op("pe", sskv, reads=[Bsqq, Bonesf], writes=[Bpst])
                S.op("act", lambda e: e.activation(out=rq[:, 0:N], in_=pst[:, 0:N], func=AF.Sqrt, scale=1.0 / 256, bias=epsc[:, 0:1]),
                     reads=[Bpst, Bepsc], writes=[Brq])
                yield
                S.op("dve", lambda e: e.reciprocal(out=rq[:, 0:N], in_=rq[:, 0:N]), reads=[Brq], writes=[Brq])
                yield
                for j in range(2):
                    S.op("dve", lambda e, j=j: e.scalar_tensor_tensor(out=kvn[:, j, 0:N], in0=kvd[:, j, 0:N], scalar=mlav[:, 4 + j:5 + j], in1=rq[:, 0:N],
                                                                      op0=ALU.mult, op1=ALU.mult), reads=[Bkvd, Bmlav, Brq], writes=[Bkvn])
                S.op("pool", lambda e: e.tensor_tensor(out=sqh[0:64, 1, 0:N], in0=kro[0:64, 0:N], in1=kro[0:64, 0:N], op=ALU.mult), reads=[Bkro], writes=[Bsqh])
                yield
                for h in range(2):
                    p1, Bp1 = next_po()

                    def kup(e, h=h, p1=p1):
                        for kc in range(2):
                            ins = e.matmul(p1[:, 0:N], wukv[:, kc, h * 256:h * 256 + 128], kvn[:, kc, 0:N], start=(kc == 0), stop=(kc == 1))
                        return ins
                    S.op("pe", kup, reads=[Bwukv, Bkvn], writes=[Bp1])
                    S.op("act", lambda e, p1=p1: e.activation(out=qno[:, 0:N], in_=p1[:, 0:N], func=AF.Copy), reads=[Bp1], writes=[Bqno])

                    def vup(e, h=h):
                        for t in range(ntile):
                            for kc in range(2):
                                ins = e.matmul(pv[:, t * 128:(t + 1) * 128], kvn[:, kc, t * 128:(t + 1) * 128], wukv[:, kc, h * 256 + 128:h * 256 + 256],
                                               start=(kc == 0), stop=(kc == 1))
                        return ins
                    S.op("pe", vup, reads=[Bwukv, Bkvn], writes=[Bpv])
                    S.op("act", lambda e, h=h: e.activation(out=vts[:, h, 0:N], in_=pv[:, 0:N], func=AF.Copy), reads=[Bpv], writes=[Bvts])
                    S.dma("sp", VT[h, :, n0:n0 + N], vts[:, h, 0:N], reads=[Bvts], semof=Bvts)
                    yield
                    S.op("pool", lambda e: e.tensor_tensor(out=sqh[:, 0, 0:N], in0=qno[:, 0:N], in1=qno[:, 0:N], op=ALU.mult), reads=[Bqno], writes=[Bsqh])
                    S.op("dve", lambda e: e.tensor_scalar(out=qnf[:, 0:N], in0=qno[:, 0:N], scalar1=mlav[:, 7:8], scalar2=None, op0=ALU.mult),
                         reads=[Bqno, Bmlav], writes=[Bqnf])
                    S.dma("sp", KN_v[:, h, n0:n0 + N], qnf[:, 0:N], reads=[Bqnf], semof=Bqnf)
                    yield

                    def kss(e, h=h):
                        for t in range(ntile):
                            c = t * 2 + h
                            e.matmul(pks[:, c:c + 1], sqh[:, 0, t * 128:(t + 1) * 128], onesf[:, 0:1], start=True, stop=False)
                            ins = e.matmul(pks[:, c:c + 1], sqh[0:64, 1, t * 128:(t + 1) * 128], onesf[0:64, 0:1], start=False, stop=True)
                        return ins
                    S.op("pe", kss, reads=[Bsqh, Bonesf], writes=[Bpks])
                    yield
                nk = ntile * 2
                t0 = (n0 // 128) * 2
                S.op("act", lambda e: e.activation(out=kst[:, 0:nk], in_=pks[:, 0:nk], func=AF.Sqrt, scale=1.0 / 192, bias=epsc[:, 0:1]),
                     reads=[Bpks, Bepsc], writes=[Bkst])
                S.op("dve", lambda e: e.tensor_scalar(out=qrg[0:64, 0:N], in0=kro[0:64, 0:N], scalar1=mlav[0:64, 9:10], scalar2=None, op0=ALU.mult),
                     reads=[Bkro, Bmlav], writes=[Bqrg])
                yield
                S.op("dve", lambda e: e.reciprocal(out=kst[:, 0:nk], in_=kst[:, 0:nk]), reads=[Bkst], writes=[Bkst])
                yield
                S.op("dve", lambda e: e.tensor_scalar(out=KS[:, t0:t0 + nk], in0=kst[:, 0:nk], scalar1=float(192 ** -0.5), scalar2=None, op0=ALU.mult),
                     reads=[Bkst], writes=[BKS])
                if isx:
                    rope_apply(qrg, Bqrg, N, KR[:, n0:n0 + N])
                else:
                    S.op("pool", lambda e: e.tensor_copy(out=qrf[0:64, 0:N], in_=qrg[0:64, 0:N]), reads=[Bqrg], writes=[Bqrf])
                    S.dma("sp", KR[:, n0:n0 + N], qrf[0:64, 0:N], reads=[Bqrf], semof=Bqrf)
                yield

            BWB = Buf("WB")
            wmg_v_ = I["w_in_mg"].rearrange("(kc p) n -> p kc n", p=128)
            wbr_r_v_ = I["w_br_r"].rearrange("(kc p) n -> p kc n", p=128)
            wbr_m_v_ = I["w_br_m"].rearrange("(kc p) n -> p kc n", p=128)
            wout_v_ = I["w_out"].rearrange("(kc p) n -> p kc n", p=128)
            castq = []
            for m in range(16):
                d_ = WMG_b[m].rearrange("p (k n) -> p k n", n=256)
                castq.append((d_[:, :, 0:128], wmg_v_[:, :, m * 128:(m + 1) * 128]))
                castq.append((d_[:, :, 128:256], wmg_v_[:, :, 2048 + m * 128:2048 + (m + 1) * 128]))
                d_ = WBR_b[m].rearrange("p (k n) -> p k n", n=256)
                castq.append((d_[:, :, 0:128], wbr_r_v_[:, :, m * 128:(m + 1) * 128]))
                castq.append((d_[:, :, 128:256], wbr_m_v_[:, :, m * 128:(m + 1) * 128]))
            for nb in range(4):
                castq.append((WOUT_b[nb].rearrange("p (k n) -> p k n", n=512), wout_v_[:, :, nb * 512:(nb + 1) * 512]))
            NGRP = 1 + NX // GS
            NGRP = int(os.environ.get("MK_NGRP", NGRP))
            for t in range(GS // 128):
                prep_tile(t * 128, 0, t)
            S.dma("sp", gain_bc[:], BC[0], writes=[Bgain], reads=[], semof=Bgain)
            S.dma("sp", shift_bc[:], BC[1], writes=[Bshift], reads=[], semof=Bshift)
            def gnext(g_):
                try:
                    next(g_)
                    return True
                except StopIteration:
                    return False

            chain = None
            for gi in range(NGRP):
                dense = dense_gen(gi, gi % 2)
                preps = []
                if gi + 1 < NGRP:
                    r0 = 256 + gi * GS
                    slots_ = {}
                    preps = [(lambda t=t, r0=r0: slots_.__setitem__(t, prep_a(r0 + t * 128))) for t in range(GS // 128)]
                    late = [(lambda t=t, hs=(gi + 1) % 2: prep_b(slots_[t], hs, t)) for t in range(GS // 128)]
                else:
                    late = []
                rnd = 0
                dalive = True
                while dalive or chain is not None:
                    if dalive:
                        dalive = gnext(dense)
                    for _ in range((3 if dalive else 1000) if not os.environ.get('MK_NOINTER') else (0 if dalive else 1000)):
                        if chain is None:
                            break
                        if not gnext(chain):
                            chain = None
                    rnd += 1
                    if preps and (rnd % 3 == 1 or not dalive):
                        preps.pop(0)()
                while preps:
                    preps.pop(0)()
                while late:
                    late.pop(0)()
                chain = chain_gen(gi)
                for _ in range(3):
                    if castq:
                        o_, i_ = castq.pop(0)
                        S.dma("pool", o_, i_, semof=BWB)
            while chain is not None:
                if not gnext(chain):
                    chain = None
            while castq:
                o_, i_ = castq.pop(0)
                S.dma("pool", o_, i_, semof=BWB)
            if "KSD" in debug:
                S.dma("sp", KSD, KS[:], reads=[BKS], semof=BKS)
            S.barrier()
            S.emit()
            S.end_phase(recycle_hw=False)
        if upto == "A":
            return nc

        GR = 256
        NCH = GR // 64
        NXG = NX // GR
        U_k = U_rkv.rearrange("(k c p) n -> p k c n", k=3, c=2, p=128)
        YD_v = [v3(YD[d]) for d in range(2)]
        BD_v = [v3(BD[d]) for d in range(2)]
        with ExitStack() as prs:
            P = Ctx(nc, prs); P.n = 300
            omka, Bomka = P.sb([128, 2], F32, "omka")
            S.op("dve", lambda e: e.tensor_scalar(out=omka[:, :], in0=chanv[:, :, 5], scalar1=-1.0, scalar2=1.0, op0=ALU.mult, op1=ALU.add),
                 reads=[Bchanv], writes=[Bomka])

            class CP:
                pass
            cps = []
            for cc in range(2):
                for d in range(2):
                    c_ = CP()
                    c_.cc, c_.d = cc, d
                    for nm, shp, dt in (("ub", [128, 3, GR + 2], F32), ("cv", [128, 3, GR], F32), ("sgt", [128, GR], F32), ("aat", [128, GR], F32),
                                        ("sq", [128, GR], F32), ("rs", [128, GR], F32), ("kk", [128, GR], F32), ("ff", [128, GR], F32),
                                        ("kmod", [128, GR], F32), ("akk", [128, GR], F32), ("Pc", [128, GR], F32), ("Ei", [128, GR], F32),
                                        ("Ee", [128, GR], F32), ("g", [128, GR], F32), ("gp", [128, GR], F32), ("gi", [128, GR], F32),
                                        ("NA", [128, 256], BF16),
                                        ("KA", [128, 256], BF16), ("A0", [128, 128], BF16), ("PW0", [128, 256], BF16), ("PW1", [128, 256], BF16),
                                        ("Tm0", [128, 128], BF16), ("Tm1", [128, 128], BF16), ("TR", [128, 384], BF16), ("Xb", [128, 128], BF16),
                                        ("Ub", [128, 128], BF16), ("H", [128, 128], F32), ("Hb", [128, 128], BF16), ("S1", [128, 128], F32),
                                        ("pr", [128, GR], F32), ("bon", [128, GR], F32)):
                        t_, b_ = P.sb(shp, dt, nm)
                        setattr(c_, nm, t_)
                        setattr(c_, "B" + nm, b_)
                    for nm, shp, dt in (("gtot", [128, NCH], F32), ("AR", [128, NCH, 256], BF16), ("BE", [128, NCH, 128], BF16),
                                        ("KT", [128, NCH, 128], BF16), ("VB", [128, NCH, 128], BF16), ("Yg", [128, GR], F32)):
                        lst = [P.sb(shp, dt, nm) for _ in range(2)]
                        setattr(c_, nm, [x[0] for x in lst])
                        setattr(c_, "B" + nm, [x[1] for x in lst])
                    c_.Bsgt = c_.Bub
                    c_.Baat = c_.Bub
                    c_.bk1, c_.Bbk1 = P.ps([128, 512], F32, "bk1")
                    c_.bk2, c_.BpAD = P.ps([128, 512], F32, "bk2")
                    c_.BpTT = c_.BpAD
                    for nm in ("H", "Hb"):
                        t_ = getattr(c_, nm)
                        b_ = getattr(c_, "B" + nm)
                        S.op("pool", lambda e, t_=t_: e.memset(t_[:], 0.0), writes=[b_])
                    for nm in ("AR", "BE", "KT", "VB"):
                        for sl_ in range(2):
                            t_ = getattr(c_, nm)[sl_]
                            b_ = getattr(c_, "B" + nm)[sl_]
                            S.op("pool", lambda e, t_=t_: e.memset(t_[:], 0.0), writes=[b_])
                    cps.append(c_)

            if os.environ.get("MK_WARM"):
                def warm(e):
                    for _ in range(400):
                        ins = e.matmul(cps[0].bk1[:, :], msk[:, 0, :], resetb[:, :], start=True, stop=True)
                    return ins
                resetb, Bresetb = P.sb([128, 512], BF16, "resetb")
                S.op("pool", lambda e: e.memset(resetb[:], 1.0), writes=[Bresetb])
                S.op("pe", warm, reads=[Bresetb, Bmsk], writes=[cps[0].Bbk1])
            c3 = lambda ap: ap.rearrange("p (c t) -> p c t", t=64)

            def prep_gen(c, sl, n0, N, s0, s1, xo):
                cc, d = c.cc, c.d
                AR, BAR = c.AR[sl], c.BAR[sl]
                BE, BBE = c.BE[sl], c.BBE[sl]
                KT, BKT = c.KT[sl], c.BKT[sl]
                VB, BVB = c.VB[sl], c.BVB[sl]
                gtot, Bgtot = c.gtot[sl], c.Bgtot[sl]
                lo, hi = n0 - 1, n0 + N + 1
                dl, dh = 0, N + 2
                if n0 == s0:
                    S.op("pool", lambda e: e.memset(c.ub[:, :, 0:1], 0.0), writes=[c.Bub])
                    lo, dl = n0, 1
                if n0 + N == s1:
                    S.op("pool", lambda e: e.memset(c.ub[:, :, N + 1:N + 2], 0.0), writes=[c.Bub])
                    hi, dh = n0 + N, N + 1
                S.dma_group("sp", [(c.ub[:, :, dl:dh], U_k[:, :, cc, lo:hi]),
                                   (c.sgt[:, 0:N], SG_v[:, d * 2 + cc, n0:n0 + N]),
                                   (c.aat[:, 0:N], AA_v[:, d * 2 + cc, n0:n0 + N])], writes=[c.Bub], semof=c.Bub)
                yield
                for kind in range(3):
                    ch = kind * 2 + cc
                    S.op("act", lambda e, kind=kind, ch=ch: e.activation(out=c.cv[:, kind, 0:N], in_=c.ub[:, kind, 1:N + 1], func=AF.Copy,
                                                                         scale=convw[:, ch, 1:2]), reads=[c.Bub, Bconvw], writes=[c.Bcv])
                S.op("dve", lambda e: e.tensor_tensor_scan(out=c.Pc[:, 0:N], data0=resetm[:, 0:N], data1=c.sgt[:, 0:N], initial=0.0, op0=ALU.mult, op1=ALU.add),
                     reads=[Bresetm, c.Bsgt], writes=[c.BPc])
                yield
                for tap in (0, 2):
                    for kind in range(3):
                        ch = kind * 2 + cc
                        S.op("dve", lambda e, kind=kind, ch=ch, tap=tap: e.scalar_tensor_tensor(
                            out=c.cv[:, kind, 0:N], in0=c.ub[:, kind, tap:tap + N], scalar=convw[:, ch, tap:tap + 1],
                            in1=c.cv[:, kind, 0:N], op0=ALU.mult, op1=ALU.add), reads=[c.Bub, Bconvw, c.Bcv], writes=[c.Bcv])
                    yield
                nch = N // 64
                tot = c3(c.Pc[:, 0:N])[:, :, 63]
                if d == 0:
                    S.op("pool", lambda e: e.tensor_tensor(out=c.Ee[:, 0:N], in0=c.Pc[:, 0:N], in1=c.sgt[:, 0:N], op=ALU.subtract), reads=[c.BPc, c.Bsgt], writes=[c.BEe])
                    Ei, BEi = c.Pc, c.BPc
                else:
                    for k_ in range(nch):
                        S.op("pool", lambda e, k_=k_: e.tensor_scalar(out=c.Ee[:, k_ * 64:(k_ + 1) * 64], in0=c.Pc[:, k_ * 64:(k_ + 1) * 64], scalar1=-1.0,
                                                                      scalar2=c.Pc[:, k_ * 64 + 63:k_ * 64 + 64], op0=ALU.mult, op1=ALU.add),
                             reads=[c.BPc], writes=[c.BEe])
                    S.op("pool", lambda e: e.tensor_tensor(out=c.Ei[:, 0:N], in0=c.Ee[:, 0:N], in1=c.sgt[:, 0:N], op=ALU.add), reads=[c.BEe, c.Bsgt], writes=[c.BEi])
                    Ei, BEi = c.Ei, c.BEi
                S.op("act", lambda e: e.activation(out=c.sq[:, 0:N], in_=c.cv[:, 1, 0:N], func=AF.Square, scale=chanv[:, cc, 4:5]),
                     reads=[c.Bcv, Bchanv], writes=[c.Bsq])
                yield
                st_ = c.bk1[:, 0:N]
                Bst_ = c.Bbk1
                S.op("pe", lambda e: e.matmul(st_, bones[:, :], c.sq[:, 0:N], start=True, stop=True), reads=[c.Bsq, Bbones], writes=[Bst_])
                S.op("act", lambda e: e.activation(out=c.rs[:, 0:N], in_=st_, func=AF.Sqrt, bias=epsc[:, 1:2], scale=1.0), reads=[Bepsc], writes=[c.Brs, Bst_])
                yield
                S.op("act", lambda e: e.activation(out=c.g[:, 0:N], in_=Ei[:, 0:N], func=AF.Exp, scale=-C0), reads=[BEi], writes=[c.Bg])
                S.op("act", lambda e: e.activation(out=c.gp[:, 0:N], in_=c.Ee[:, 0:N], func=AF.Exp, scale=-C0), reads=[c.BEe], writes=[c.Bgp])
                S.op("act", lambda e: e.activation(out=c.gi[:, 0:N], in_=Ei[:, 0:N], func=AF.Exp, scale=C0), reads=[BEi], writes=[c.Bgi])
                S.op("act", lambda e: e.activation(out=gtot[:, 0:nch], in_=tot, func=AF.Exp, scale=-C0), reads=[c.BPc], writes=[Bgtot])
                S.op("pool", lambda e: e.tensor_scalar(out=c.ff[:, 0:N], in0=c.aat[:, 0:N], scalar1=chanv[:, cc, 5:6], scalar2=omka[:, cc:cc + 1],
                                                        op0=ALU.mult, op1=ALU.add), reads=[c.Baat, Bchanv, Bomka], writes=[c.Bff])
                S.op("pool", lambda e: e.tensor_tensor(out=c.kmod[:, 0:N], in0=c.cv[:, 1, 0:N], in1=c.ff[:, 0:N], op=ALU.mult), reads=[c.Bcv, c.Bff], writes=[c.Bkmod])
                S.op("dve", lambda e: e.reciprocal(out=c.rs[:, 0:N], in_=c.rs[:, 0:N]), reads=[c.Brs], writes=[c.Brs])
                yield
                S.op("dve", lambda e: e.scalar_tensor_tensor(out=c.pr[:, 0:N], in0=c.cv[:, 0, 0:N], scalar=chanv[:, cc, 8:9], in1=c.kmod[:, 0:N],
                                                              op0=ALU.mult, op1=ALU.mult), reads=[c.Bcv, Bchanv, c.Bkmod], writes=[c.Bpr])
                S.op("dve", lambda e: e.scalar_tensor_tensor(out=c.kk[:, 0:N], in0=c.cv[:, 1, 0:N], scalar=chanv[:, cc, 4:5], in1=c.rs[:, 0:N],
                                                              op0=ALU.mult, op1=ALU.mult), reads=[c.Bcv, Bchanv, c.Brs], writes=[c.Bkk])
                yield
                S.op("pe", lambda e: e.matmul(st_, bones[:, :], c.pr[:, 0:N], start=True, stop=True), reads=[c.Bpr, Bbones], writes=[Bst_])
                S.op("dve", lambda e: e.tensor_tensor(out=c.bon[:, 0:N], in0=st_, in1=c.cv[:, 2, 0:N], op=ALU.mult), reads=[c.Bcv], writes=[c.Bbon, Bst_])
                if xo is not None:
                    S.dma("sp", BD_v[d][:, cc, xo:xo + N], c.bon[:, 0:N], reads=[c.Bbon], semof=c.Bbon)
                yield
                S.op("pool", lambda e: e.tensor_tensor(out=c.akk[:, 0:N], in0=c.aat[:, 0:N], in1=c.kk[:, 0:N], op=ALU.mult), reads=[c.Baat, c.Bkk], writes=[c.Bakk])
                for hh in range(2):
                    ps_ = slice(64 * hh, 64 * hh + 64)
                    o1 = slice(64 * hh, 64 * hh + 64)
                    o2 = slice(128 + 64 * hh, 128 + 64 * hh + 64)
                    S.op("dve", lambda e, ps_=ps_, o2=o2: e.tensor_tensor(out=AR[ps_, 0:nch, o2], in0=c3(c.cv[ps_, 0, 0:N]), in1=c3(c.g[ps_, 0:N]), op=ALU.mult),
                         reads=[c.Bcv, c.Bg], writes=[BAR])
                    S.op("dve", lambda e, ps_=ps_, o1=o1: e.scalar_tensor_tensor(out=AR[ps_, 0:nch, o1], in0=c3(c.kk[ps_, 0:N]), scalar=-1.0, in1=c3(c.gp[ps_, 0:N]),
                                                                                 op0=ALU.mult, op1=ALU.mult), reads=[c.Bkk, c.Bgp], writes=[BAR])
                    S.op("pool", lambda e, ps_=ps_, o1=o1: e.tensor_tensor(out=KT[ps_, 0:nch, o1], in0=c3(c.kmod[ps_, 0:N]), in1=c3(c.gi[ps_, 0:N]), op=ALU.mult),
                         reads=[c.Bkmod, c.Bgi], writes=[BKT])
                    S.op("pool", lambda e, ps_=ps_, o1=o1: e.tensor_copy(out=VB[ps_, 0:nch, o1], in_=c3(c.cv[ps_, 2, 0:N])), reads=[c.Bcv], writes=[BVB])
                yield
                for hh in range(2):
                    ps_ = slice(64 * hh, 64 * hh + 64)
                    o1 = slice(64 * hh, 64 * hh + 64)
                    S.op("pool", lambda e, ps_=ps_, o1=o1: e.tensor_tensor(out=BE[ps_, 0:nch, o1], in0=c3(c.akk[ps_, 0:N]), in1=c3(c.gi[ps_, 0:N]), op=ALU.mult),
                         reads=[c.Bakk, c.Bgi], writes=[BBE])
                yield

            def chunk_gen(c, sl, k, want_y):
                AR, BAR = c.AR[sl], c.BAR[sl]
                BE, BBE = c.BE[sl], c.BBE[sl]
                KT, BKT = c.KT[sl], c.BKT[sl]
                VB, BVB = c.VB[sl], c.BVB[sl]
                gtot, Bgtot = c.gtot[sl], c.Bgtot[sl]
                Yg, BYg = c.Yg[sl], c.BYg[sl]
                bk1, Bbk1, bk2, BpAD, BpTT = c.bk1, c.Bbk1, c.bk2, c.BpAD, c.BpTT
                mN = (msk[:, 0:2, :] if c.d == 0 else msk[:, 2:4, :]).rearrange("p a b -> p (a b)")
                mA = msk[:, 2, :] if c.d == 0 else msk[:, 0, :]

                def f1(e):
                    e.matmul(bk1[:, 0:256], BE[:, k, :], AR[:, k, :], start=True, stop=True)
                    return e.matmul(bk1[:, 256:512], KT[:, k, :], AR[:, k, :], start=True, stop=True)
                S.op("pe", f1, reads=[BBE, BKT, BAR], writes=[Bbk1])
                S.op("pe", lambda e: e.matmul(bk2[:, 0:128], AR[:, k, 0:128], BE[:, k, :], start=True, stop=True), reads=[BAR, BBE], writes=[BpAD])
                S.op("dve", lambda e: e.tensor_tensor(out=c.NA[:, :], in0=bk1[:, 0:256], in1=mN, op=ALU.mult), reads=[Bmsk], writes=[c.BNA, Bbk1])
                S.op("dve", lambda e: e.tensor_tensor(out=c.KA[:, :], in0=bk1[:, 256:512], in1=mN, op=ALU.mult), reads=[Bmsk], writes=[c.BKA, Bbk1])
                S.op("dve", lambda e: e.tensor_tensor(out=c.A0[:, :], in0=bk2[:, 0:128], in1=mA, op=ALU.mult), reads=[Bmsk], writes=[c.BA0, BpAD])
                yield
                def f2(e):
                    e.matmul(bk1[:, 0:128], BE[:, k, :], identb[:, :], start=True, stop=True)
                    e.matmul(bk1[:, 128:256], KT[:, k, :], identb[:, :], start=True, stop=True)
                    return e.matmul(bk1[:, 256:384], VB[:, k, :], identb[:, :], start=True, stop=True)
                S.op("pe", f2, reads=[BBE, BKT, BVB, Bidentb], writes=[Bbk1])
                S.op("act", lambda e: e.activation(out=c.TR[:, :], in_=bk1[:, 0:384], func=AF.Copy), reads=[], writes=[c.BTR, Bbk1])
                yield
                S.op("pool", lambda e: e.tensor_tensor(out=c.Tm0[:, :], in0=c.NA[:, 0:128], in1=identb[:, :], op=ALU.add), reads=[c.BNA, Bidentb], writes=[c.BTm0])
                Nk, BNk, Ak, BAk = c.NA[:, 0:128], c.BNA, c.A0[:, :], c.BA0
                Tc, BTc = c.Tm0, c.BTm0
                for lvl in range(5):
                    pw, Bpw = (c.PW0, c.BPW0) if lvl % 2 == 0 else (c.PW1, c.BPW1)
                    if lvl < 4:
                        def f3(e, Nk=Nk, Ak=Ak):
                            e.matmul(bk2[:, 0:128], Ak, Nk, start=True, stop=True)
                            return e.matmul(bk2[:, 128:256], Nk, Ak, start=True, stop=True)
                        S.op("pe", f3, reads=[BNk, BAk], writes=[BpAD])
                        S.op("act", lambda e, pw=pw: e.activation(out=pw[:, :], in_=bk2[:, 0:256], func=AF.Copy), reads=[BpAD], writes=[Bpw])
                    else:
                        S.op("pe", lambda e, Nk=Nk, Ak=Ak: e.matmul(bk2[:, 128:256], Nk, Ak, start=True, stop=True), reads=[BNk, BAk], writes=[BpAD])
                        S.op("act", lambda e, pw=pw: e.activation(out=pw[:, 128:256], in_=bk2[:, 128:256], func=AF.Copy), reads=[BpAD], writes=[Bpw])
                    yield
                    Nk, BNk, Ak, BAk = pw[:, 0:128], Bpw, pw[:, 128:256], Bpw
                    Tn, BTn = (c.Tm1, c.BTm1) if lvl % 2 == 0 else (c.Tm0, c.BTm0)
                    S.op("pe", lambda e, Ak=Ak, Tc=Tc: e.matmul(bk1[:, 384:512], Ak, Tc[:, :], start=True, stop=True), reads=[BAk, BTc], writes=[Bbk1])
                    S.op("dve", lambda e, Tc=Tc, Tn=Tn: e.tensor_tensor(out=Tn[:, :], in0=bk1[:, 384:512], in1=Tc[:, :], op=ALU.add), reads=[BTc], writes=[BTn, Bbk1])
                    Tc, BTc = Tn, BTn
                    yield
                Tf, BTf = Tc, BTc
                def fx(e):
                    e.matmul(bk1[:, 0:128], c.KA[:, 0:128], c.TR[:, 256:384], start=True, stop=False)
                    return e.matmul(bk1[:, 0:128], AR[:, k, 0:128], c.Hb[:, :], start=False, stop=True)
                S.op("pe", fx, reads=[c.BKA, c.BTR, BAR, c.BHb], writes=[Bbk1])
                S.op("act", lambda e: e.activation(out=c.Xb[:, :], in_=bk1[:, 0:128], func=AF.Copy), reads=[Bbk1], writes=[c.BXb])
                yield
                S.op("pe", lambda e: e.matmul(bk1[:, 128:256], Tf[:, :], c.Xb[:, :], start=True, stop=True), reads=[BTf, c.BXb], writes=[Bbk1])
                S.op("act", lambda e: e.activation(out=c.Ub[:, :], in_=bk1[:, 128:256], func=AF.Copy), reads=[Bbk1], writes=[c.BUb])
                yield

                def fh(e):
                    e.matmul(bk1[:, 256:384], c.TR[:, 128:256], c.TR[:, 256:384], start=True, stop=False)
                    ins = e.matmul(bk1[:, 256:384], c.TR[:, 0:128], c.Ub[:, :], start=False, stop=True)
                    if want_y:
                        e.matmul(bk1[:, 384:512], c.Hb[:, :], AR[:, k, 128:256], start=True, stop=False)
                        e.matmul(bk1[:, 384:512], c.Ub[:, :], c.NA[:, 128:256], start=False, stop=False)
                        ins = e.matmul(bk1[:, 384:512], c.TR[:, 256:384], c.KA[:, 128:256], start=False, stop=True)
                    return ins
                S.op("pe", fh, reads=[c.BTR, c.BUb, c.BHb, BAR, c.BNA, c.BKA], writes=[Bbk1])
                S.op("dve", lambda e: e.tensor_tensor(out=c.S1[:, :], in0=bk1[:, 256:384], in1=c.H[:, :], op=ALU.add), reads=[Bbk1, c.BH], writes=[c.BS1])
                if want_y:
                    for hh in range(2):
                        ps_ = slice(64 * hh, 64 * hh + 64)
                        S.op("act", lambda e, ps_=ps_, hh=hh: e.activation(out=Yg[ps_, k * 64:(k + 1) * 64], in_=bk1[ps_, 384 + 64 * hh:384 + 64 * hh + 64], func=AF.Copy),
                             reads=[], writes=[BYg, Bbk1])
                yield
                S.op("act", lambda e: e.activation(out=c.Hb[:, :], in_=c.S1[:, :], func=AF.Copy, scale=gtot[:, k:k + 1]), reads=[c.BS1, Bgtot], writes=[c.BHb])
                S.op("pool", lambda e: e.tensor_scalar(out=c.H[:, :], in0=c.S1[:, :], scalar1=gtot[:, k:k + 1], scalar2=None, op0=ALU.mult),
                     reads=[c.BS1, Bgtot], writes=[c.BH])
                yield

            def step_info(c, step):
                if step == 0:
                    return 0, 0, 256, None
                xg = (step - 1) if c.d == 0 else (NXG - step)
                return 256 + xg * GR, 256, NT, xg * GR

            def run_rr(gens):
                alive = list(gens)
                while alive:
                    nxt = []
                    for g_ in alive:
                        try:
                            next(g_)
                            nxt.append(g_)
                        except StopIteration:
                            pass
                    alive = nxt

            NSTEP = int(os.environ.get("MK_RSTEPS", 1 + NXG))

            def mk_prep(c, step):
                n0, s0, s1, xo = step_info(c, step)
                return prep_gen(c, step % 2, n0, GR, s0, s1, xo)

            run_rr([mk_prep(c, 0) for c in cps])
            for step in range(NSTEP):
                isx = step > 0
                sl = step % 2

                def seq(c):
                    for ci in range(NCH):
                        k = ci if c.d == 0 else NCH - 1 - ci
                        yield from chunk_gen(c, sl, k, isx)
                    if isx:
                        xo = step_info(c, step)[3]
                        S.dma("sp", YD_v[c.d][:, c.cc, xo:xo + GR], c.Yg[sl][:, :], reads=[c.BYg[sl]], semof=c.BYg[sl])

                gens = [seq(c) for c in cps]
                preps = [mk_prep(c, step + 1) for c in cps] if step + 1 < NSTEP else []
                rnd = 0
                alive = gens
                while alive or preps:
                    nxt = []
                    for g_ in alive:
                        try:
                            next(g_)
                            nxt.append(g_)
                        except StopIteration:
                            pass
                    alive = nxt
                    rnd += 1
                    if preps and (rnd % 4 == 0 or not alive):
                        np_ = []
                        for g_ in preps:
                            try:
                                next(g_)
                                np_.append(g_)
                            except StopIteration:
                                pass
                        preps = np_
            S.barrier()
            S.emit()
            S.end_phase()
        if upto == "R":
            return nc

        NF = 512
        BOX = Buf("OX")
        with ExitStack() as pfs:
            P = Ctx(nc, pfs); P.n = 400
            ld = [[P.sb([128, NF], F32, "fld") for _ in range(5)] for _ in range(2)]
            ld = [[(t_, grp[0][1]) for (t_, _) in grp] for grp in ld]
            yy, Byy = P.sb([128, NF], F32, "yy")
            bs, Bbs = P.sb([128, NF], F32, "bs")
            ysq, Bysq = P.sb([128, NF], F32, "ysq")
            mm_, Bmm_ = P.sb([128, NF], F32, "mm")
            msq, Bmsq = P.sb([128, NF], F32, "msq")
            var, Bvar = P.sb([128, NF], F32, "var")
            yc, Byc = P.sb([128, NF], F32, "yc")
            ob = [P.sb([128, NF], BF16, "ob") for _ in range(2)]
            ps1 = [P.ps([128, 512], F32, "ps1") for _ in range(2)]
            ps2 = [P.ps([128, 512], F32, "ps2") for _ in range(2)]
            it = 0
            for cc in range(2):
                for ti in range(NX // NF):
                    xo = ti * NF
                    sl = it % 2
                    (y0, By0), (y1, By1), (b0, Bb0), (b1, Bb1), (gzt, Bgzt) = ld[sl]
                    S.dma_group("sp", [(y0[:], YD_v[0][:, cc, xo:xo + NF]), (y1[:], YD_v[1][:, cc, xo:xo + NF]),
                                       (b0[:], BD_v[0][:, cc, xo:xo + NF]), (b1[:], BD_v[1][:, cc, xo:xo + NF]),
                                       (gzt[:], Gzr_v[:, cc, xo:xo + NF])], writes=[By0], semof=By0)
                    p1, Bp1 = ps1[sl]
                    p2, Bp2 = ps2[sl]
                    o_, Bo_ = ob[sl]
                    S.op("pool", lambda e, y0=y0, y1=y1: e.tensor_tensor(out=yy[:], in0=y0[:], in1=y1[:], op=ALU.add), reads=[By0, By1], writes=[Byy])
                    S.op("pool", lambda e, b0=b0, b1=b1: e.tensor_tensor(out=bs[:], in0=b0[:], in1=b1[:], op=ALU.add), reads=[Bb0, Bb1], writes=[Bbs])
                    S.op("pe", lambda e, p1=p1: e.matmul(p1[:, :], bones[:, :], yy[:], start=True, stop=True), reads=[Byy, Bbones], writes=[Bp1])
                    S.op("act", lambda e: e.activation(out=ysq[:], in_=yy[:], func=AF.Square), reads=[Byy], writes=[Bysq])
                    S.op("pe", lambda e, p2=p2: e.matmul(p2[:, :], bones[:, :], ysq[:], start=True, stop=True), reads=[Bysq, Bbones], writes=[Bp2])
                    S.op("dve", lambda e, p1=p1: e.tensor_scalar(out=mm_[:], in0=p1[:, :], scalar1=1.0 / 64, scalar2=None, op0=ALU.mult), reads=[Bp1], writes=[Bmm_])
                    S.op("pool", lambda e: e.tensor_tensor(out=msq[:], in0=mm_[:], in1=mm_[:], op=ALU.mult), reads=[Bmm_], writes=[Bmsq])
                    S.op("dve", lambda e, p2=p2: e.scalar_tensor_tensor(out=var[:], in0=p2[:, :], scalar=1.0 / 64, in1=msq[:], op0=ALU.mult, op1=ALU.subtract),
                         reads=[Bp2, Bmsq], writes=[Bvar])
                    S.op("act", lambda e: e.activation(out=var[:], in_=var[:], func=AF.Sqrt, bias=epsc[:, 2:3], scale=1.0), reads=[Bvar, Bepsc], writes=[Bvar])
                    S.op("dve", lambda e: e.reciprocal(out=var[:], in_=var[:]), reads=[Bvar], writes=[Bvar])
                    S.op("pool", lambda e: e.tensor_tensor(out=yc[:], in0=yy[:], in1=mm_[:], op=ALU.subtract), reads=[Byy, Bmm_], writes=[Byc])
                    S.op("pool", lambda e: e.tensor_tensor(out=yc[:], in0=yc[:], in1=var[:], op=ALU.mult), reads=[Byc, Bvar], writes=[Byc])
                    S.op("dve", lambda e, cc=cc: e.tensor_scalar(out=yc[:], in0=yc[:], scalar1=chanv[:, cc, 6:7], scalar2=chanv[:, cc, 7:8], op0=ALU.mult, op1=ALU.add),
                         reads=[Byc, Bchanv], writes=[Byc])
                    S.op("pool", lambda e: e.tensor_tensor(out=yc[:], in0=yc[:], in1=bs[:], op=ALU.add), reads=[Byc, Bbs], writes=[Byc])
                    S.op("dve", lambda e, o_=o_, gzt=gzt: e.tensor_tensor(out=o_[:], in0=yc[:], in1=gzt[:], op=ALU.mult), reads=[Byc, Bgzt], writes=[Bo_])
                    S.dma("sp", OXs[2 * cc][:, xo:xo + NF], o_[0:64, :], reads=[Bo_], semof=Bo_)
                    S.dma("sp", OXs[2 * cc + 1][:, xo:xo + NF], o_[64:128, :], reads=[Bo_], semof=Bo_)
                    it += 1
            S.barrier()
            S.emit()
            S.end_phase()
        if upto == "F":
            return nc

        QG = 512
        NKT = NT // 128
        with ExitStack() as pms:
            P = Ctx(nc, pms); P.n = 500
            Kn, BKn = P.sb([128, NT], BF16, "Kn")
            Kr, BKr = P.sb([128, NT], BF16, "Kr")
            Vt, BVt = P.sb([128, NKT, 128], BF16, "Vt")
            onesb, Bonesb = P.sb([128, 128], BF16, "onesb")
            Qn = [P.sb([128, QG], BF16, "Qn") for _ in range(2)]
            Qr = [P.sb([128, QG], BF16, "Qr") for _ in range(2)]
            Pacc = [[P.sb([128, QG], F32, "Pacc") for _ in range(2)] for _ in range(2)]
            gmt = [P.sb([128, QG], F32, "gmt") for _ in range(2)]
            Pt = [P.sb([128, QG], BF16, "Pt") for _ in range(4)]
            rl, Brl = P.sb([128, QG], F32, "rl")
            oo, Boo = P.sb([128, QG], F32, "oo")
            om = [P.sb([128, QG], BF16, "om") for _ in range(2)]
            pS = [P.ps([128, 512], F32, "pS") for _ in range(4)]
            pO = [P.ps([128, 512], F32, "pO") for _ in range(2)]
            pL = [P.ps([128, 512], F32, "pL") for _ in range(2)]
            S.op("pool", lambda e: e.memset(onesb[:], 1.0), writes=[Bonesb])
            S.op("pool", lambda e: e.memset(Kr[64:128, :], 0.0), writes=[BKr])
            for q_, Bq_ in Qr:
                S.op("pool", lambda e, q_=q_: e.memset(q_[64:128, :], 0.0), writes=[Bq_])
            S.dma("sp", Kr[0:64, :], KR, writes=[BKr], semof=BKr)
            NQG = int(os.environ.get("MK_NQG", NX // QG))
            def attn_group(h, qg, sl):
                qo = qg * QG
                qn_, Bqn_ = Qn[sl]
                qr_, Bqr_ = Qr[sl]
                gm_, Bgm_ = gmt[sl]
                po_, Bpo_ = pO[sl]
                pl_, Bpl_ = pL[sl]
                o_, Bo_ = om[sl]
                S.dma("sp", qn_[:], QN_v[:, h, qo:qo + QG], writes=[Bqn_], semof=Bqn_)
                S.dma("sp", qr_[0:64, :], QR_v[:, h, qo:qo + QG], writes=[Bqr_], semof=Bqr_)
                (pa0, Bpa0), (pa1, Bpa1) = Pacc[sl]
                S.dma("sp", gm_[:], Gzm_v[:, h, qo:qo + QG], writes=[Bgm_], semof=Bgm_)

                def qk(kt):
                    ps_, Bps_ = pS[kt % 4]

                    def f(e):
                        e.matmul(ps_[:, :], Kn[:, kt * 128:(kt + 1) * 128], qn_[:, :], start=True, stop=False)
                        return e.matmul(ps_[:, :], Kr[:, kt * 128:(kt + 1) * 128], qr_[:, :], start=False, stop=True)
                    S.op("pe", f, reads=[BKn, BKr, Bqn_, Bqr_], writes=[Bps_])

                def ex_pv(kt):
                    ps_, Bps_ = pS[kt % 4]
                    pt_, Bpt_ = Pt[kt % 4]
                    S.op("act", lambda e: e.activation(out=pt_[:, :], in_=ps_[:, :], func=AF.Exp, scale=KS[:, kt * 2 + h:kt * 2 + h + 1]),
                         reads=[Bps_, BKS], writes=[Bpt_])

                    S.op("pe", lambda e: e.matmul(po_[:, :], Vt[:, kt, :], pt_[:, :], start=(kt == 0), stop=(kt == NKT - 1)),
                         reads=[BVt, Bpt_], writes=[Bpo_])
                    pa_, Bpa_ = (pa0, Bpa0) if kt % 2 == 0 else (pa1, Bpa1)
                    eng_ = "pool" if kt % 2 == 0 else "dve"
                    if kt < 2:
                        S.op(eng_, lambda e: e.tensor_copy(out=pa_[:, :], in_=pt_[:, :]), reads=[Bpt_], writes=[Bpa_])
                    else:
                        S.op(eng_, lambda e: e.tensor_tensor(out=pa_[:, :], in0=pa_[:, :], in1=pt_[:, :], op=ALU.add), reads=[Bpt_, Bpa_], writes=[Bpa_])

                qk(0)
                qk(1)
                for kt in range(NKT):
                    if kt + 2 < NKT:
                        qk(kt + 2)
                    ex_pv(kt)
                def lsum(e):
                    e.matmul(pl_[:, :], onesf[:, :], pa0[:, :], start=True, stop=False)
                    return e.matmul(pl_[:, :], onesf[:, :], pa1[:, :], start=False, stop=True)
                S.op("pe", lsum, reads=[Bonesf, Bpa0, Bpa1], writes=[Bpl_])
                S.op("dve", lambda e: e.reciprocal(out=rl[:, :], in_=pl_[:, :]), reads=[Bpl_], writes=[Brl])
                S.op("dve", lambda e: e.tensor_tensor(out=oo[:, :], in0=po_[:, :], in1=rl[:, :], op=ALU.mult), reads=[Bpo_, Brl], writes=[Boo])
                S.op("pool", lambda e: e.tensor_tensor(out=o_[:, :], in0=oo[:, :], in1=gm_[:, :], op=ALU.mult), reads=[Boo, Bgm_], writes=[Bo_])
                S.dma("sp", OXs[4 + 2 * h][:, qo:qo + QG], o_[0:64, :], reads=[Bo_], semof=Bo_)
                S.dma("sp", OXs[5 + 2 * h][:, qo:qo + QG], o_[64:128, :], reads=[Bo_], semof=Bo_)

            gcount = 0
            for h in range(2):
                S.dma("sp", Kn[:], KN_v[:, h, :], writes=[BKn], semof=BKn)
                S.dma("sp", Vt[:], VT[h].rearrange("p (t d) -> p t d", d=128), writes=[BVt], semof=BVt)
                for qg in range(NQG):
                    attn_group(h, qg, gcount % 2)
                    gcount += 1
            S.barrier()
            S.emit()
            S.end_phase()
        if upto == "M":
            return nc

        BOG = Buf("OG")
        for j in range(8):
            S.collective("AllGather", [[0, 1, 2, 3], [4, 5, 6, 7]], OXs[j], OGs[j], reads=[BOX], writes=[BOG], semof=BOG)
        S.barrier()
        S.emit()
        S.end_phase()

        CG = 512
        with ExitStack() as pcs:
            P = Ctx(nc, pcs); P.n = 600
            selq, Bselq = P.sb([128, 4], F32, "selq")
            gain_bc, Bgain = P.sb([128, D], F32, "gain_c")
            shift_bc, Bshift = P.sb([128, D], F32, "shift_c")
            gate_bc, Bgate = P.sb([128, D], F32, "gate_c")
            Bselq = Bshift = Bgate = Bgain
            xt = [P.sb([128, D], F32, "xtc") for _ in range(2)]
            hm = [P.sb([128, D], BF16, "hmc") for _ in range(2)]
            ss = [P.sb([128, 4], F32, "ssc") for _ in range(2)]
            hT, BhT = P.sb([128, 16, CG], BF16, "hTc")
            ldq = [P.sb([128, 16, CG], BF16, "ldq") for _ in range(2)]
            osel, Bosel = P.sb([128, 16, CG], BF16, "osel")
            wmg = [P.sb([128, 16, 256], BF16, "wmg") for _ in range(2)]
            wbr = [P.sb([128, 8, 256], BF16, "wbr") for _ in range(2)]
            sgr, Bsgr = P.sb([128, CG], F32, "sgr")
            sgm, Bsgm = P.sb([128, CG], F32, "sgm")
            tr_, Btr_ = P.sb([128, CG], F32, "tr")
            tm_, Btm_ = P.sb([128, CG], F32, "tm")
            merged, Bmerged = P.sb([128, 16, CG], BF16, "merged")
            wout = [P.sb([128, 16, 512], BF16, "wout") for _ in range(2)]
            xr = [P.sb([128, 512], F32, "xr") for _ in range(2)]
            res = [P.sb([128, 512], F32, "res") for _ in range(2)]
            pT = [P.ps([128, 1024], BF16, "pTc") for _ in range(2)]
            pg = [P.ps([128, 512], F32, "pg") for _ in range(4)]
            pout = [P.ps([128, 512], F32, "pout") for _ in range(2)]
            S.dma_group("sp", [(selq[:], I["selq"]), (gain_bc[:], BC[0]), (shift_bc[:], BC[1]), (gate_bc[:], BC[2])], writes=[Bgain], semof=Bgain)
            wmg_v = I["w_in_mg"].rearrange("(kc p) n -> p kc n", p=128)
            wbr_r_v = I["w_br_r"].rearrange("(kc p) n -> p kc n", p=128)
            wbr_m_v = I["w_br_m"].rearrange("(kc p) n -> p kc n", p=128)
            wout_v = I["w_out"].rearrange("(kc p) n -> p kc n", p=128)
            cnt = {"tile": 0, "w": 0, "o": 0, "r": 0}
            NCG = int(os.environ.get("MK_NCG", 2048 // CG))
            for gj in range(NCG):
                go = gj * CG
                for t in range(CG // 128):
                    s = cnt["tile"] % 2
                    cnt["tile"] += 1
                    x_, Bx_ = xt[s]
                    h_, Bh_ = hm[s]
                    s_, Bs_ = ss[s]
                    S.dma("sp", x_[:], I["xm"][go + t * 128:go + (t + 1) * 128, :], writes=[Bx_], semof=Bx_)
                    S.op("act", lambda e, x_=x_, h_=h_, s_=s_: e.activation(out=h_[:], in_=x_[:], func=AF.Square, accum_out=s_[:, 0:1]), reads=[Bx_], writes=[Bh_, Bs_])
                    S.op("act", lambda e, s_=s_: e.activation(out=s_[:, 1:2], in_=s_[:, 0:1], func=AF.Sqrt, scale=1.0 / D, bias=epsc[:, 0:1]), reads=[Bs_, Bepsc], writes=[Bs_])
                    S.op("dve", lambda e, s_=s_: e.reciprocal(out=s_[:, 2:3], in_=s_[:, 1:2]), reads=[Bs_], writes=[Bs_])
                    S.op("dve", lambda e, x_=x_, s_=s_: e.scalar_tensor_tensor(out=x_[:], in0=x_[:], scalar=s_[:, 2:3], in1=gain_bc[:], op0=ALU.mult, op1=ALU.mult),
                         reads=[Bx_, Bs_, Bgain], writes=[Bx_])
                    S.op("pool", lambda e, x_=x_, h_=h_: e.tensor_tensor(out=h_[:], in0=x_[:], in1=shift_bc[:], op=ALU.add), reads=[Bx_, Bshift], writes=[Bh_])
                    for half in range(2):
                        p_, Bp_ = pT[half]

                        def trf(e, half=half, p_=p_, h_=h_):
                            for j in range(8):
                                kc = half * 8 + j
                                ins = e.transpose(p_[:, j * 128:(j + 1) * 128], h_[:, kc * 128:(kc + 1) * 128], identb[:])
                            return ins
                        S.op("pe", trf, reads=[Bh_, Bidentb], writes=[Bp_])
                        S.op("act" if half == 0 else "dve",
                             (lambda e, half=half, p_=p_, t=t: e.activation(out=hT[:, half * 8:(half + 1) * 8, t * 128:(t + 1) * 128],
                                                                            in_=p_[:, :].rearrange("p (j n) -> p j n", n=128), func=AF.Copy)) if half == 0 else
                             (lambda e, half=half, p_=p_, t=t: e.tensor_copy(out=hT[:, half * 8:(half + 1) * 8, t * 128:(t + 1) * 128],
                                                                             in_=p_[:, :].rearrange("p (j n) -> p j n", n=128))),
                             reads=[Bp_], writes=[BhT])
                for q in range(4):
                    l_, Bl_ = ldq[q % 2]
                    S.dma_group("sp", [(l_[(j % 2) * 64:(j % 2) * 64 + 64, :, :].rearrange("p (r c) n -> p r c n", c=4)[:, :, j // 2, :],
                                        OGs[j].rearrange("(r p) n -> p r n", p=64)[:, :, q * 2048 + go:q * 2048 + go + CG]) for j in range(8)],
                                reads=[BOG], writes=[Bl_], semof=Bl_)
                    if q == 0:
                        S.op("dve", lambda e, l_=l_: e.tensor_scalar(out=osel[:], in0=l_[:], scalar1=selq[:, 0:1], scalar2=None, op0=ALU.mult),
                             reads=[Bl_, Bselq], writes=[Bosel])
                    else:
                        S.op("dve", lambda e, l_=l_, q=q: e.scalar_tensor_tensor(out=osel[:], in0=l_[:], scalar=selq[:, q:q + 1], in1=osel[:], op0=ALU.mult, op1=ALU.add),
                             reads=[Bl_, Bselq, Bosel], writes=[Bosel])
                for m in range(16):
                    w_, Bw_ = wmg[cnt["w"] % 2]
                    b_, Bb_ = wbr[cnt["w"] % 2]
                    cnt["w"] += 1
                    S.dma("sp", w_[:, :, :], WMG_b[m].rearrange("p (k n) -> p k n", n=256), writes=[Bw_], semof=Bw_)
                    S.dma("sp", b_[:, :, :], WBR_b[m].rearrange("p (k n) -> p k n", n=256), writes=[Bb_], semof=Bb_)
                    (pgr, Bpgr), (pgm, Bpgm), (ppr, Bppr), (ppm, Bppm) = pg

                    def fg(e, w_=w_):
                        for kc in range(16):
                            e.matmul(pgr[:, 0:CG], w_[:, kc, 0:128], hT[:, kc, :], start=(kc == 0), stop=(kc == 15))
                        for kc in range(16):
                            ins = e.matmul(pgm[:, 0:CG], w_[:, kc, 128:256], hT[:, kc, :], start=(kc == 0), stop=(kc == 15))
                        return ins
                    S.op("pe", fg, reads=[Bw_, BhT], writes=[Bpgr, Bpgm])
                    S.op("act", lambda e: e.activation(out=sgr[:, :], in_=pgr[:, 0:CG], func=AF.Sigmoid), reads=[Bpgr], writes=[Bsgr])
                    S.op("act", lambda e: e.activation(out=sgm[:, :], in_=pgm[:, 0:CG], func=AF.Sigmoid), reads=[Bpgm], writes=[Bsgm])

                    def fb(e, b_=b_):
                        for j in range(8):
                            kc = (j // 2) * 4 + (j % 2)
                            e.matmul(ppr[:, 0:CG], b_[:, j, 0:128], osel[:, kc, :], start=(j == 0), stop=(j == 7))
                        for j in range(8):
                            kc = (j // 2) * 4 + 2 + (j % 2)
                            ins = e.matmul(ppm[:, 0:CG], b_[:, j, 128:256], osel[:, kc, :], start=(j == 0), stop=(j == 7))
                        return ins
                    S.op("pe", fb, reads=[Bb_, Bosel], writes=[Bppr, Bppm])
                    S.op("dve", lambda e: e.tensor_tensor(out=tr_[:, :], in0=ppr[:, 0:CG], in1=sgr[:, :], op=ALU.mult), reads=[Bppr, Bsgr], writes=[Btr_])
                    S.op("dve", lambda e: e.tensor_tensor(out=tm_[:, :], in0=ppm[:, 0:CG], in1=sgm[:, :], op=ALU.mult), reads=[Bppm, Bsgm], writes=[Btm_])
                    S.op("pool", lambda e, m=m: e.tensor_tensor(out=merged[:, m, :], in0=tr_[:, :], in1=tm_[:, :], op=ALU.add), reads=[Btr_, Btm_], writes=[Bmerged])
                for nb in range(4):
                    wo_, Bwo_ = wout[cnt["o"] % 2]
                    cnt["o"] += 1
                    S.dma("sp", wo_[:], WOUT_b[nb].rearrange("p (k n) -> p k n", n=512), writes=[Bwo_], semof=Bwo_)
                    for t in range(CG // 128):
                        r_ = cnt["r"] % 2
                        cnt["r"] += 1
                        po_, Bpo_ = pout[r_]
                        xr_, Bxr_ = xr[r_]
                        rs_, Brs_ = res[r_]
                        S.dma("sp", xr_[:], I["xm"][go + t * 128:go + (t + 1) * 128, nb * 512:(nb + 1) * 512], writes=[Bxr_], semof=Bxr_)

                        def fo(e, wo_=wo_, po_=po_, t=t):
                            for kc in range(16):
                                ins = e.matmul(po_[:, :], merged[:, kc, t * 128:(t + 1) * 128], wo_[:, kc, :], start=(kc == 0), stop=(kc == 15))
                            return ins
                        S.op("pe", fo, reads=[Bwo_, Bmerged], writes=[Bpo_])
                        S.op("dve", lambda e, po_=po_, rs_=rs_, nb=nb: e.tensor_tensor(out=rs_[:], in0=po_[:, :], in1=gate_bc[:, nb * 512:(nb + 1) * 512], op=ALU.mult),
                             reads=[Bpo_, Bgate], writes=[Brs_])
                        S.op("pool", lambda e, rs_=rs_, xr_=xr_: e.tensor_tensor(out=rs_[:], in0=rs_[:], in1=xr_[:], op=ALU.add), reads=[Brs_, Bxr_], writes=[Brs_])
                        S.dma("sp", out[go + t * 128:go + (t + 1) * 128, nb * 512:(nb + 1) * 512], rs_[:], reads=[Brs_], semof=Brs_)
            S.barrier()
            S.emit()
            S.end_phase()
    return nc


_NC_CACHE = {}


def kernel(**inputs):
    maps = _host_inputs(inputs)
    if "nc" not in _NC_CACHE:
        _NC_CACHE["nc"] = build()
    nc = _NC_CACHE["nc"]
    res = run_bass_kernel_spmd(nc, maps, core_ids=list(range(8)))
    outp = np.zeros((2, NX, D), np.float32)
    for c in range(8):
        b, g = c // 4, c % 4
        outp[b, 2048 * g:2048 * g + 2048] = res.results[c]["out"]
    return outp
```

```python
import os
from contextlib import ExitStack
import numpy as np
import ml_dtypes
import concourse.bass as bass
import concourse.mybir as mybir
from concourse.bass_utils import run_bass_kernel_spmd

F32 = mybir.dt.float32
BF16 = mybir.dt.bfloat16
ALU = mybir.AluOpType
AF = mybir.ActivationFunctionType

NT = 8448
NX = 8192
NCTX = 256
D = 2048
WC = 2368
GS = 256
C0 = float(np.exp(-0.5))
EPS = 1e-6
GN_EPS = 64e-5


class Tok:
    __slots__ = ("sem", "val", "key")

    def __init__(self, sem, val, key):
        self.sem = sem
        self.val = val
        self.key = key


class DSem:
    _n = 0

    def __init__(self, sem, kind):
        DSem._n += 1
        self.uid = DSem._n
        self.sem = sem
        self.cnt = 0
        self.kind = kind


class Buf:
    def __init__(self, name, const=False):
        self.name = name
        self.w = None
        self.r = []
        self.const = const
        self.dsem = None
        self.dcnt = 0


class Sched:
    ENG = ["pe", "act", "dve", "pool", "sp"]

    def __init__(self, nc, stack):
        self.nc = nc
        self.stack = stack
        self.plan = {e: [] for e in self.ENG}
        self.ecnt = {e: 0 for e in self.ENG}
        self.esem = {}
        self.waited = {e: {} for e in self.ENG}
        self.nsem = 0
        for e in ("pe", "act", "dve", "pool"):
            self.esem[e] = self._newsem("e_" + e)
        self.dbufs = []
        self.free_dsems = {}
        self.ninst = 0

    def _newsem(self, name):
        self.nsem += 1
        return self.stack.enter_context(self.nc.semaphore(f"{name}_{self.nsem}"))

    def _waits(self, eng, toks):
        best = {}
        for t in toks:
            if t is not None and (t.key not in best or best[t.key].val < t.val):
                best[t.key] = t
        for t in best.values():
            if self.waited[eng].get(t.key, 0) >= t.val:
                continue
            if eng == "pe" and t.key == "e_pe":
                continue
            self.waited[eng][t.key] = t.val
            self.plan[eng].append(lambda e, sem=t.sem, v=t.val: e.wait_ge(sem, v))

    def _deps(self, reads, writes):
        deps = []
        for b in reads:
            deps.append(b.w)
        for b in writes:
            deps.append(b.w)
            deps.extend(b.r)
        return deps

    def _mark(self, tok, reads, writes):
        for b in reads:
            if not b.const:
                b.r.append(tok)
        for b in writes:
            b.w = tok
            b.r = []

    def op(self, eng, fn, reads=(), writes=()):
        self._waits(eng, self._deps(reads, writes))
        self.ecnt[eng] += 1
        self.ninst += 1
        tok = Tok(self.esem[eng], self.ecnt[eng], "e_" + eng)
        self.plan[eng].append(lambda e, fn=fn, sem=tok.sem: fn(e).then_inc(sem, 1))
        self._mark(tok, reads, writes)
        return tok

    def _dsem(self, b, kind):
        if b.dsem is None:
            fl = self.free_dsems.setdefault(kind, [])
            if fl:
                b.dsem = fl.pop()
            else:
                b.dsem = DSem(self._newsem("d" + kind), kind)
            self.dbufs.append(b)
        assert b.dsem.kind == kind, (b.name, b.dsem.kind, kind)

    def end_phase(self, recycle_hw=True):
        keep = []
        for b in self.dbufs:
            if b.dsem.kind == "cc":
                keep.append(b)
                continue
            b.dsem = None
            b.w = None
            b.r = []
        self.dbufs = keep

    def dma(self, q, out_ap, in_ap, reads=(), writes=(), semof=None, **kw):
        self._waits(q, self._deps(reads, writes))
        b = semof
        self._dsem(b, "sw" if q == "pool" else "hw")
        ds = b.dsem
        ds.cnt += 16
        tok = Tok(ds.sem, ds.cnt, "d%d" % ds.uid)
        self.plan[q].append(
            lambda e, o=out_ap, i=in_ap, sem=ds.sem, kw=kw: e.dma_start(out=o, in_=i, **kw).then_inc(sem, 16)
        )
        self._mark(tok, reads, writes)
        return tok

    def dma_group(self, q, pairs, reads=(), writes=(), semof=None):
        if os.environ.get("MK_SEQGROUP"):
            for (o, i) in pairs:
                tok = self.dma(q, o, i, reads=reads, writes=writes, semof=semof)
            return tok
        self._waits(q, self._deps(reads, writes))
        b = semof
        self._dsem(b, "sw" if q == "pool" else "hw")
        ds = b.dsem
        for (o, i) in pairs:
            ds.cnt += 16
            self.plan[q].append(lambda e, o=o, i=i, sem=ds.sem: e.dma_start(out=o, in_=i).then_inc(sem, 16))
        tok = Tok(ds.sem, ds.cnt, "d%d" % ds.uid)
        self._mark(tok, reads, writes)
        return tok

    def collective(self, kind, groups, in_ap, out_ap, reads, writes, semof):
        q = "pool"
        self._waits(q, self._deps(reads, writes))
        b = semof
        self._dsem(b, "cc")
        ds = b.dsem
        ds.cnt += 1
        tok = Tok(ds.sem, ds.cnt, "d%d" % ds.uid)
        self.plan[q].append(
            lambda e, sem=ds.sem: e.collective_compute(
                kind, ALU.bypass, replica_groups=groups, ins=[in_ap], outs=[out_ap]
            ).then_inc(sem, 1)
        )
        self._mark(tok, reads, writes)
        return tok

    def barrier(self):
        toks = []
        for e in ("pe", "act", "dve", "pool"):
            if self.ecnt[e] > 0:
                toks.append(Tok(self.esem[e], self.ecnt[e], "e_" + e))
        for b in self.dbufs:
            toks.append(Tok(b.dsem.sem, b.dsem.cnt, "d%d" % b.dsem.uid))
        for e in self.ENG:
            self._waits(e, toks)

    def emit(self):
        plan = self.plan
        with self.nc.Block() as block:

            @block.tensor
            def _(e):
                for f in plan["pe"]:
                    f(e)

            @block.scalar
            def _(e):
                for f in plan["act"]:
                    f(e)

            @block.vector
            def _(e):
                for f in plan["dve"]:
                    f(e)

            @block.gpsimd
            def _(e):
                for f in plan["pool"]:
                    f(e)

            @block.sync
            def _(e):
                for f in plan["sp"]:
                    f(e)

        self.plan = {e: [] for e in self.ENG}


class Ctx:
    def __init__(self, nc, stack):
        self.nc = nc
        self.stack = stack
        self.n = 0

    def sb(self, shape, dt, name=None):
        self.n += 1
        name = (name or "t") + f"_{self.n}"
        t = self.stack.enter_context(self.nc.sbuf_tensor(name, list(shape), dt))
        return t, Buf(name)

    def ps(self, shape, dt, name=None):
        self.n += 1
        name = (name or "p") + f"_{self.n}"
        t = self.stack.enter_context(self.nc.psum_tensor(name, list(shape), dt))
        return t, Buf(name)

    def sub(self):
        c = Ctx(self.nc, ExitStack())
        c.n = self.n + 1000
        return c


def _host_consts():
    idx = np.arange(64)
    us = (idx[:, None] < idx[None, :]).astype(np.float32)
    ui = (idx[:, None] <= idx[None, :]).astype(np.float32)
    ls = (idx[:, None] > idx[None, :]).astype(np.float32)
    li = (idx[:, None] >= idx[None, :]).astype(np.float32)
    masks = np.zeros((128, 4, 128), np.float32)
    for i, m in enumerate((us, ui, ls, li)):
        masks[0:64, i, 0:64] = m
        masks[64:128, i, 64:128] = m
    bones = np.zeros((128, 128), np.float32)
    bones[0:64, 0:64] = 1
    bones[64:128, 64:128] = 1
    sel = np.zeros((2, 2, 128), np.float32)
    sel[0, 0, :] = 1
    sel[1, 1, :] = 1
    reset = np.ones((128, 512), np.float32)
    reset[:, ::64] = 0
    rows = np.repeat(np.arange(128), 64).astype(np.float32)
    cols = np.tile(np.arange(64), 128).astype(np.float32)
    inv = np.power(np.float32(10000.0), -np.arange(0, 32, 2, dtype=np.float32) / np.float32(32)).astype(np.float32)
    ang = np.zeros((64, NX), np.float32)
    for d in range(64):
        pos = rows if d < 32 else cols
        ang[d] = pos * inv[d % 16]
    cos = np.cos(ang.astype(np.float64)).astype(np.float32)
    sin = np.sin(ang.astype(np.float64)).astype(np.float32)
    P = np.zeros((64, 64), np.float32)
    for d in range(64):
        if d % 32 < 16:
            P[d, d + 16] = -1
        else:
            P[d, d - 16] = 1
    return dict(
        ident=np.eye(128, dtype=np.float32), masks=masks, bones=bones, onesf=np.ones((128, 128), np.float32),
        sel=sel, reset=reset, rope_cos=cos, rope_sin=sin, ropePT=np.ascontiguousarray(P.T),
    )


def _host_inputs(inp):
    f = lambda a: np.ascontiguousarray(a, dtype=np.float32)
    consts = _host_consts()
    w_in = inp["w_in"][0]
    offs = np.cumsum([0, 3072, 1024, 128, 128, 512, 256, 64, 1024, 4096])
    o_rkv, o_zr, o_wd, o_ad, o_qd, o_kvd, o_kr, o_zm, o_mg = offs[:9]
    maps = []
    for c in range(8):
        b, g = c // 4, c % 4
        ch = slice(256 * g, 256 * g + 256)
        cols = np.concatenate([
            o_rkv + np.arange(256 * g, 256 * g + 256),
            o_rkv + 1024 + np.arange(256 * g, 256 * g + 256),
            o_rkv + 2048 + np.arange(256 * g, 256 * g + 256),
            o_zr + np.arange(256 * g, 256 * g + 256),
            o_wd + np.arange(128), o_ad + np.arange(128),
            o_qd + np.arange(512), o_kvd + np.arange(256), o_kr + np.arange(64),
            o_zm + np.arange(256 * g, 256 * g + 256),
        ])
        assert cols.size == WC
        conv = inp["conv_rkv"][0]
        conv_fm = np.zeros((128, 6, 3), np.float32)
        for kind in range(3):
            for cc in range(2):
                cidx = kind * 1024 + 256 * g + cc * 128 + np.arange(128)
                conv_fm[:, kind * 2 + cc, :] = conv[:, cidx].T
        chanv = np.zeros((128, 2, 10), np.float32)
        for cc in range(2):
            cidx = 256 * g + cc * 128 + np.arange(128)
            chanv[:, cc, 0] = inp["w0"][0, 0, cidx]
            chanv[:, cc, 1] = inp["w0"][0, 1, cidx]
            chanv[:, cc, 2] = inp["a0"][0, 0, cidx]
            chanv[:, cc, 3] = inp["a0"][0, 1, cidx]
            chanv[:, cc, 4] = inp["k_k"][0, cidx]
            chanv[:, cc, 5] = inp["k_a"][0, cidx]
            chanv[:, cc, 6] = inp["ln_x_w"][0, cidx]
            chanv[:, cc, 7] = inp["ln_x_b"][0, cidx]
            chanv[:, cc, 8] = inp["r_k"][0].reshape(-1)[cidx]
        wlora = np.zeros((128, 2, 256), np.float32)
        for d in range(2):
            wlora[64 * d:64 * d + 64, 0, :] = inp["w_decay_up"][0, d][:, ch]
            wlora[64 * d:64 * d + 64, 1, :] = inp["w_a_up"][0, d][:, ch]
        mlav = np.zeros((128, 10), np.float32)
        mlav[:, 0:4] = inp["q_norm_w"][0].reshape(4, 128).T
        mlav[:, 4:6] = inp["kv_norm_w"][0].reshape(2, 128).T
        mlav[:, 6] = inp["q_gain"][0][:128]
        mlav[:, 7] = inp["k_gain"][0][:128]
        mlav[0:64, 8] = inp["q_gain"][0][128:]
        mlav[0:64, 9] = inp["k_gain"][0][128:]
        hq = [2 * g, 2 * g + 1]
        w_uq_c = np.concatenate([inp["w_uq"][0][:, h * 192:(h + 1) * 192] for h in hq], axis=1)
        w_ukv_c = np.concatenate([inp["w_ukv"][0][:, h * 256:(h + 1) * 256] for h in hq], axis=1)
        cfm = np.stack([inp["c"][b].reshape(16, 128).T, inp["c_ctx"].reshape(16, 128).T], axis=-1)
        selq = np.zeros((128, 4), np.float32)
        selq[:, g] = 1
        m = dict(
            xs=f(np.concatenate([inp["ctx"][b], inp["x"][b]], axis=0)),
            xm=f(inp["x"][b, 2048 * g:2048 * g + 2048]),
            cfm=f(cfm), norm_w_row=f(np.stack([inp["norm_w"][0]] * 2)), b_mod_row=f(np.stack([inp["b_mod"][0]] * 2)),
            w_mod=f(inp["w_mod"][0]), w_in_core=f(w_in[:, cols]), w_in_mg=f(w_in[:, o_mg:o_mg + 4096]),
            conv_fm=f(conv_fm), chanv=f(chanv), wlora=f(wlora), mlav=f(mlav), w_uq_c=f(w_uq_c), w_ukv_c=f(w_ukv_c),
            w_br_r=f(inp["w_branch_rwkv"][0]), w_br_m=f(inp["w_branch_mla"][0]), w_out=f(inp["w_out"][0]),
            selq=f(selq),
        )
        m.update({k: f(v) for k, v in consts.items()})
        maps.append(m)
    return maps


INPUT_SHAPES = dict(
    xs=[NT, D], xm=[2048, D], cfm=[128, 16, 2], norm_w_row=[2, D], b_mod_row=[2, 3 * D], w_mod=[D, 3 * D],
    w_in_core=[D, WC], w_in_mg=[D, 4096], conv_fm=[128, 6, 3], chanv=[128, 2, 10], wlora=[128, 2, 256],
    mlav=[128, 10], w_uq_c=[512, 384], w_ukv_c=[256, 512], w_br_r=[1024, D], w_br_m=[1024, D], w_out=[D, D],
    selq=[128, 4], ident=[128, 128], masks=[128, 4, 128], bones=[128, 128], onesf=[128, 128], sel=[2, 2, 128],
    reset=[128, 512], rope_cos=[64, NX], rope_sin=[64, NX], ropePT=[64, 64],
)


def build(debug=(), upto="C"):
    nc = bass.Bass("TRN2", target_bir_lowering=False)
    I = {k: nc.dram_tensor(k, s, F32, kind="ExternalInput").ap() for k, s in INPUT_SHAPES.items()}
    out = nc.dram_tensor("out", [2048, D], F32, kind="ExternalOutput").ap()

    def scratch(name, shape, dt):
        if name in debug:
            return nc.dram_tensor(name, shape, dt, kind="ExternalOutput").ap()
        return nc.dram_tensor(name, shape, dt).ap()

    BC = scratch("BC", [5, 128, D], F32)
    U_rkv = scratch("U_rkv", [768, NT], F32)
    G_zr = scratch("G_zr", [256, NX], F32)
    SG = scratch("SG", [512, NT], F32)
    AA = scratch("AA", [512, NT], F32)
    QN = scratch("QN", [256, NX], BF16)
    QR = scratch("QR", [128, NX], BF16)
    KN = scratch("KN", [256, NT], BF16)
    KR = scratch("KR", [64, NT], BF16)
    VT = scratch("VT", [2, 128, NT], BF16)
    G_zm = scratch("G_zm", [256, NX], F32)
    KSD = scratch("KSD", [128, 132], F32)
    YD = [scratch("YD0", [256, NX], F32), scratch("YD1", [256, NX], F32)]
    BD = [scratch("BD0", [256, NX], F32), scratch("BD1", [256, NX], F32)]
    DBG = [scratch(f"DBG{i}", [128, 512], BF16 if i < 2 else F32) for i in range(4)]
    OXs = [scratch(f"OX{j}", [64, NX], BF16) for j in range(8)]
    OGs = [scratch(f"OG{j}", [256, NX], BF16) for j in range(8)]

    WMG_b = scratch("WMG_b", [16, 128, 16 * 256], BF16)
    WBR_b = scratch("WBR_b", [16, 128, 8 * 256], BF16)
    WOUT_b = scratch("WOUT_b", [4, 128, 16 * 512], BF16)
    v3 = lambda ap, p=128: ap.rearrange("(c p) n -> p c n", p=p)
    U_v, Gzr_v, SG_v, AA_v = v3(U_rkv), v3(G_zr), v3(SG), v3(AA)
    QN_v, QR_v, KN_v, Gzm_v = v3(QN), v3(QR, 64), v3(KN), v3(G_zm)

    with ExitStack() as top:
        S = Sched(nc, top)
        T = Ctx(nc, top)

        identb, Bidentb = T.sb([128, 128], BF16, "identb")
        msk, Bmsk = T.sb([128, 4, 128], BF16, "msk")
        bones, Bbones = T.sb([128, 128], F32, "bones")
        onesf, Bonesf = T.sb([128, 128], F32, "onesf")
        selt, Bselt = T.sb([2, 2, 128], F32, "selt")
        resetm, Bresetm = T.sb([128, 512], F32, "resetm")
        ropePT, BropePT = T.sb([64, 64], F32, "ropePT")
        epsc, Bepsc = T.sb([128, 4], F32, "epsc")
        convw, Bconvw = T.sb([128, 6, 3], F32, "convw")
        chanv, Bchanv = T.sb([128, 2, 10], F32, "chanv")
        mlav, Bmlav = T.sb([128, 10], F32, "mlav")
        KS, BKS = T.sb([128, 132], F32, "KS")
        S.dma("pool", identb[:], I["ident"], writes=[Bidentb], semof=Bidentb)
        S.dma("pool", msk[:], I["masks"], writes=[Bmsk], semof=Bmsk)
        for t_, b_, k_ in ((bones, Bbones, "bones"), (onesf, Bonesf, "onesf"), (selt, Bselt, "sel"), (resetm, Bresetm, "reset"),
                           (ropePT, BropePT, "ropePT"), (convw, Bconvw, "conv_fm"), (chanv, Bchanv, "chanv"), (mlav, Bmlav, "mlav")):
            S.dma("sp", t_[:], I[k_], writes=[b_], semof=b_)
        S.op("pool", lambda e: e.memset(epsc[:, 0:1], EPS), writes=[Bepsc])
        S.op("pool", lambda e: e.memset(epsc[:, 1:2], 1e-12), writes=[Bepsc])
        S.op("pool", lambda e: e.memset(epsc[:, 2:3], GN_EPS), writes=[Bepsc])
        S.op("pool", lambda e: e.memset(epsc[:, 3:4], 0.0), writes=[Bepsc])
        for b_ in (Bidentb, Bmsk, Bbones, Bonesf, Bselt, Bresetm, BropePT, Bepsc, Bconvw, Bchanv, Bmlav):
            b_.const = True

        with ExitStack() as p0s:
            P = Ctx(nc, p0s); P.n = 100
            cf, Bcf = P.sb([128, 16, 2], F32, "cf")
            sc, Bsc = P.sb([128, 16, 2], BF16, "sc")
            b2, Bb2 = P.sb([2, 3 * D], F32, "b2")
            nw2, Bnw2 = P.sb([2, D], F32, "nw2")
            mrow, Bmrow = P.sb([2, 3 * D], F32, "mrow")
            grow, Bgrow = P.sb([2, D], F32, "grow")
            wm = [P.sb([128, 16, 512], BF16, "wm") for _ in range(2)]
            bct = [P.sb([128, D], F32, "bct") for _ in range(2)]
            pm, Bpm = P.ps([128, 512], F32, "pm")
            pb = [P.ps([128, 512], F32, "pb") for _ in range(2)]
            S.dma("sp", cf[:], I["cfm"], writes=[Bcf], semof=Bcf)
            S.dma("sp", b2[:], I["b_mod_row"], writes=[Bb2], semof=Bb2)
            S.dma("sp", nw2[:], I["norm_w_row"], writes=[Bnw2], semof=Bnw2)
            S.op("act", lambda e: e.activation(out=sc[:], in_=cf[:], func=AF.Silu), reads=[Bcf], writes=[Bsc])
            wmod_v = I["w_mod"].rearrange("(kc p) n -> p kc n", p=128)
            for cb in range(12):
                wt, Bwt = wm[cb % 2]
                S.dma("pool", wt[:], wmod_v[:, :, cb * 512:(cb + 1) * 512], writes=[Bwt], semof=Bwt)

                def mmf(e, wt=wt):
                    for kc in range(16):
                        ins = e.matmul(pm[0:2, :], sc[:, kc, :], wt[:, kc, :], start=(kc == 0), stop=(kc == 15))
                    return ins
                S.op("pe", mmf, reads=[Bsc, Bwt], writes=[Bpm])
                S.op("act", lambda e, cb=cb: e.activation(out=mrow[0:2, cb * 512:(cb + 1) * 512], in_=pm[0:2, :], func=AF.Copy),
                     reads=[Bpm], writes=[Bmrow])
            S.op("pool", lambda e: e.tensor_tensor(out=mrow[:], in0=mrow[:], in1=b2[:], op=ALU.add), reads=[Bmrow, Bb2], writes=[Bmrow])
            S.op("dve", lambda e: e.scalar_tensor_tensor(out=grow[:], in0=mrow[:, D:2 * D], scalar=1.0, in1=nw2[:],
                                                          op0=ALU.add, op1=ALU.mult), reads=[Bmrow, Bnw2], writes=[Bgrow])
            plan0 = [(0, grow, Bgrow, 0, 0), (1, mrow, Bmrow, 0, 0), (2, mrow, Bmrow, 2 * D, 0), (3, grow, Bgrow, 0, 1), (4, mrow, Bmrow, 0, 1)]
            k = 0
            for (idx, src, Bsrc, off, si) in plan0:
                st, Bst = bct[idx % 2]
                for blk in range(4):
                    pt_, Bpt_ = pb[k % 2]
                    k += 1
                    S.op("pe", lambda e, pt_=pt_, src=src, off=off, blk=blk, si=si: e.matmul(
                        pt_[:, :], selt[0:2, si, :], src[0:2, off + blk * 512: off + (blk + 1) * 512], start=True, stop=True),
                        reads=[Bsrc, Bselt], writes=[Bpt_])
                    S.op("act", lambda e, pt_=pt_, st=st, blk=blk: e.activation(out=st[:, blk * 512:(blk + 1) * 512], in_=pt_[:, :], func=AF.Copy),
                         reads=[Bpt_], writes=[Bst])
                S.dma("sp", BC[idx], st[:], reads=[Bst], semof=Bst)
            S.barrier()
            S.emit()
            S.end_phase()
        if upto == "0":
            return nc

        with ExitStack() as pas:
            P = Ctx(nc, pas); P.n = 200
            W, BW = P.sb([128, 16, WC], BF16, "W")
            wlora, Bwlora = P.sb([128, 2, 256], BF16, "wlora")
            wuq, Bwuq = P.sb([128, 4, 384], BF16, "wuq")
            wukv, Bwukv = P.sb([128, 2, 512], BF16, "wukv")
            gain_bc, Bgain = P.sb([128, D], F32, "gain_bc")
            shift_bc, Bshift = P.sb([128, D], F32, "shift_bc")
            xt = [P.sb([128, D], F32, "xt") for _ in range(2)]
            hm = [P.sb([128, D], BF16, "hm") for _ in range(2)]
            ss = [P.sb([128, 4], F32, "ss") for _ in range(2)]
            hmT = [P.sb([128, 16, GS], BF16, "hmT") for _ in range(2)]
            urkv = [P.sb([128, GS], F32, "urkv") for _ in range(2)]
            gz = [P.sb([128, GS], F32, "gz") for _ in range(2)]
            sga = [P.sb([128, GS], F32, "sga") for _ in range(2)]
            twd2 = [P.sb([128, GS], BF16, "twd") for _ in range(2)]
            adb2 = [P.sb([128, GS], BF16, "adb") for _ in range(2)]
            qd2 = [P.sb([128, 4, GS], F32, "qd") for _ in range(2)]
            sqq, Bsqq = P.sb([128, 4, GS], F32, "sqq")
            qn, Bqn = P.sb([128, 4, GS], BF16, "qn")
            rq, Brq = P.sb([128, GS], F32, "rq")
            qno, Bqno = P.sb([128, GS], F32, "qno")
            qro, Bqro = P.sb([64, GS], F32, "qro")
            sqh, Bsqh = P.sb([128, 2, GS], F32, "sqh")
            rh, Brh = P.sb([128, GS], F32, "rh")
            qnf, Bqnf = P.sb([128, GS], BF16, "qnf")
            qrg, Bqrg = P.sb([64, GS], F32, "qrg")
            t1, Bt1 = P.sb([64, GS], F32, "t1")
            t2, Bt2 = P.sb([64, GS], F32, "t2")
            qrf, Bqrf = P.sb([64, GS], BF16, "qrf")
            cost, Bcost = P.sb([64, GS], F32, "cost")
            sint, Bsint = P.sb([64, GS], F32, "sint")
            kvd2 = [P.sb([128, 2, GS], F32, "kvd") for _ in range(2)]
            kvn, Bkvn = P.sb([128, 2, GS], BF16, "kvn")
            kro2 = [P.sb([64, GS], F32, "kro") for _ in range(2)]
            vts, Bvts = P.sb([128, 2, GS], BF16, "vts")
            kst, Bkst = P.sb([128, 8], F32, "kst")
            pT = [P.ps([128, 1024], BF16, "pT") for _ in range(2)]
            po = [P.ps([128, 512], F32, "po") for _ in range(3)]
            pst, Bpst = P.ps([128, 512], F32, "pst")
            pv, Bpv = P.ps([128, 512], F32, "pv")
            pks, Bpks = P.ps([128, 512], F32, "pks")
            cnt = {"po": 0, "tile": 0, "u": 0, "g": 0, "s": 0}

            def next_po():
                cnt["po"] += 1
                return po[cnt["po"] % 3]

            win_v = I["w_in_core"].rearrange("(kc p) n -> p kc n", p=128)
            S.dma("pool", W[:, :, :], win_v[:, :, :], writes=[BW], semof=BW)
            S.dma("pool", wlora[:], I["wlora"], writes=[Bwlora], semof=Bwlora)
            S.dma("pool", wuq[:], I["w_uq_c"].rearrange("(kc p) n -> p kc n", p=128), writes=[Bwuq], semof=Bwuq)
            S.dma("pool", wukv[:], I["w_ukv_c"].rearrange("(kc p) n -> p kc n", p=128), writes=[Bwukv], semof=Bwukv)
            S.dma("sp", gain_bc[:], BC[3], writes=[Bgain], semof=Bgain)
            S.dma("sp", shift_bc[:], BC[4], writes=[Bshift], semof=Bshift)

            def prep_a(row0):
                s = cnt["tile"] % 2
                cnt["tile"] += 1
                x_, Bx_ = xt[s]
                h_, Bh_ = hm[s]
                s_, Bs_ = ss[s]
                S.dma("sp", x_[:], I["xs"][row0:row0 + 128, :], writes=[Bx_], semof=Bx_)
                S.op("act", lambda e: e.activation(out=h_[:], in_=x_[:], func=AF.Square, accum_out=s_[:, 0:1]), reads=[Bx_], writes=[Bh_, Bs_])
                S.op("act", lambda e: e.activation(out=s_[:, 1:2], in_=s_[:, 0:1], func=AF.Sqrt, scale=1.0 / D, bias=epsc[:, 0:1]),
                     reads=[Bs_, Bepsc], writes=[Bs_])
                S.op("dve", lambda e: e.reciprocal(out=s_[:, 2:3], in_=s_[:, 1:2]), reads=[Bs_], writes=[Bs_])
                S.op("dve", lambda e: e.scalar_tensor_tensor(out=x_[:], in0=x_[:], scalar=s_[:, 2:3], in1=gain_bc[:], op0=ALU.mult, op1=ALU.mult),
                     reads=[Bx_, Bs_, Bgain], writes=[Bx_])
                S.op("pool", lambda e: e.tensor_tensor(out=h_[:], in0=x_[:], in1=shift_bc[:], op=ALU.add), reads=[Bx_, Bshift], writes=[Bh_])
                return s

            def prep_b(s, hslot, t):
                h_, Bh_ = hm[s]
                hT, BhT = hmT[hslot]
                for half in range(2):
                    p_, Bp_ = pT[half]

                    def trf(e, half=half, p_=p_):
                        for j in range(8):
                            kc = half * 8 + j
                            ins = e.transpose(p_[:, j * 128:(j + 1) * 128], h_[:, kc * 128:(kc + 1) * 128], identb[:])
                        return ins
                    S.op("pe", trf, reads=[Bh_, Bidentb], writes=[Bp_])
                    cp = (lambda e, half=half, p_=p_: e.activation(out=hT[:, half * 8:(half + 1) * 8, t * 128:(t + 1) * 128],
                                                                   in_=p_[:, :].rearrange("p (j n) -> p j n", n=128), func=AF.Copy)) if half == 0 else \
                         (lambda e, half=half, p_=p_: e.tensor_copy(out=hT[:, half * 8:(half + 1) * 8, t * 128:(t + 1) * 128],
                                                                    in_=p_[:, :].rearrange("p (j n) -> p j n", n=128)))
                    S.op("act" if half == 0 else "dve", cp, reads=[Bp_], writes=[BhT])

            def prep_tile(row0, hslot, t):
                prep_b(prep_a(row0), hslot, t)

            def rstd_from(psum_ap, Bps, out_t, Bout, npart, N, inv_n):
                S.op("act", lambda e: e.activation(out=out_t[0:npart, 0:N], in_=psum_ap, func=AF.Sqrt, scale=inv_n, bias=epsc[0:npart, 0:1]),
                     reads=[Bps, Bepsc], writes=[Bout])
                S.op("dve", lambda e: e.reciprocal(out=out_t[0:npart, 0:N], in_=out_t[0:npart, 0:N]), reads=[Bout], writes=[Bout])

            def rope_apply(src, Bsrc, N, dst_dram):
                pr, Bpr = next_po()
                S.op("pe", lambda e: e.matmul(pr[0:64, 0:N], ropePT[0:64, 0:64], src[0:64, 0:N], start=True, stop=True),
                     reads=[Bsrc, BropePT], writes=[Bpr])
                S.op("dve", lambda e: e.tensor_tensor(out=t1[0:64, 0:N], in0=src[0:64, 0:N], in1=cost[0:64, 0:N], op=ALU.mult),
                     reads=[Bsrc, Bcost], writes=[Bt1])
                S.op("dve", lambda e: e.tensor_tensor(out=t2[0:64, 0:N], in0=pr[0:64, 0:N], in1=sint[0:64, 0:N], op=ALU.mult),
                     reads=[Bpr, Bsint], writes=[Bt2])
                S.op("pool", lambda e: e.tensor_tensor(out=qrf[0:64, 0:N], in0=t1[0:64, 0:N], in1=t2[0:64, 0:N], op=ALU.add),
                     reads=[Bt1, Bt2], writes=[Bqrf])
                S.dma("sp", dst_dram, qrf[0:64, 0:N], reads=[Bqrf], semof=Bqrf)

            def grp_info(gi):
                isx = gi > 0
                N = GS
                n0 = 256 + (gi - 1) * GS if isx else 0
                xo = (gi - 1) * GS
                return isx, N, n0, xo

            def dense_gen(gi, hslot):
                isx, N, n0, xo = grp_info(gi)
                hT, BhT = hmT[hslot]
                s2 = gi % 2
                twd, Btwd = twd2[s2]
                adb, Badb = adb2[s2]
                qd, Bqd = qd2[s2]
                kvd, Bkvd = kvd2[s2]
                kro, Bkro = kro2[s2]

                def mm_chunk(col0, M):
                    p_, Bp_ = next_po()

                    def f(e):
                        for kc in range(16):
                            ins = e.matmul(p_[0:M, 0:N], W[:, kc, col0:col0 + M], hT[:, kc, 0:N], start=(kc == 0), stop=(kc == 15))
                        return ins
                    S.op("pe", f, reads=[BW, BhT], writes=[Bp_])
                    return p_, Bp_

                for j in range(6):
                    p_, Bp_ = mm_chunk(j * 128, 128)
                    u_, Bu_ = urkv[cnt["u"] % 2]
                    cnt["u"] += 1
                    S.op("act", lambda e, p_=p_, u_=u_: e.activation(out=u_[:, 0:N], in_=p_[:, 0:N], func=AF.Copy), reads=[Bp_], writes=[Bu_])
                    S.dma("sp", U_v[:, j, n0:n0 + N], u_[:, 0:N], reads=[Bu_], semof=Bu_)
                    yield
                p_, Bp_ = mm_chunk(1024, 128)
                S.op("act", lambda e, p_=p_: e.activation(out=twd[:, 0:N], in_=p_[:, 0:N], func=AF.Tanh), reads=[Bp_], writes=[Btwd])
                yield
                p_, Bp_ = mm_chunk(1152, 128)
                S.op("act", lambda e, p_=p_: e.activation(out=adb[:, 0:N], in_=p_[:, 0:N], func=AF.Copy), reads=[Bp_], writes=[Badb])
                yield
                if isx:
                    for j in range(4):
                        p_, Bp_ = mm_chunk(1280 + j * 128, 128)
                        S.op("act", lambda e, p_=p_, j=j: e.activation(out=qd[:, j, 0:N], in_=p_[:, 0:N], func=AF.Copy), reads=[Bp_], writes=[Bqd])
                        yield
                for j in range(2):
                    p_, Bp_ = mm_chunk(1792 + j * 128, 128)
                    S.op("act", lambda e, p_=p_, j=j: e.activation(out=kvd[:, j, 0:N], in_=p_[:, 0:N], func=AF.Copy), reads=[Bp_], writes=[Bkvd])
                    yield
                p_, Bp_ = mm_chunk(2048, 64)
                S.op("act", lambda e, p_=p_: e.activation(out=kro[0:64, 0:N], in_=p_[0:64, 0:N], func=AF.Copy), reads=[Bp_], writes=[Bkro])
                yield
                if isx:
                    for j in range(2):
                        p_, Bp_ = mm_chunk(768 + j * 128, 128)
                        g_, Bg_ = gz[cnt["g"] % 2]
                        cnt["g"] += 1
                        S.op("act", lambda e, p_=p_, g_=g_: e.activation(out=g_[:, 0:N], in_=p_[:, 0:N], func=AF.Silu), reads=[Bp_], writes=[Bg_])
                        S.dma("sp", Gzr_v[:, j, xo:xo + N], g_[:, 0:N], reads=[Bg_], semof=Bg_)
                        yield
                    for j in range(2):
                        p_, Bp_ = mm_chunk(2112 + j * 128, 128)
                        g_, Bg_ = gz[cnt["g"] % 2]
                        cnt["g"] += 1
                        S.op("act", lambda e, p_=p_, g_=g_: e.activation(out=g_[:, 0:N], in_=p_[:, 0:N], func=AF.Silu), reads=[Bp_], writes=[Bg_])
                        S.dma("sp", Gzm_v[:, j, xo:xo + N], g_[:, 0:N], reads=[Bg_], semof=Bg_)
                        yield

            def chain_gen(gi):
                isx, N, n0, xo = grp_info(gi)
                ntile = N // 128
                s2 = gi % 2
                twd, Btwd = twd2[s2]
                adb, Badb = adb2[s2]
                qd, Bqd = qd2[s2]
                kvd, Bkvd = kvd2[s2]
                kro, Bkro = kro2[s2]
                if isx:
                    S.dma("sp", cost[:, 0:N], I["rope_cos"][:, xo:xo + N], writes=[Bcost], semof=Bcost)
                    S.dma("sp", sint[:, 0:N], I["rope_sin"][:, xo:xo + N], writes=[Bsint], semof=Bsint)
                for which, (src, Bsrc, dst_v, cbase) in enumerate(((twd, Btwd, SG_v, 0), (adb, Badb, AA_v, 2))):
                    for d in range(2):
                        for cc in range(2):
                            p_, Bp_ = next_po()
                            S.op("pe", lambda e, p_=p_, src=src, d=d, cc=cc, which=which: e.matmul(
                                p_[:, 0:N], wlora[64 * d:64 * d + 64, which, cc * 128:(cc + 1) * 128], src[64 * d:64 * d + 64, 0:N],
                                start=True, stop=True), reads=[Bwlora, Bsrc], writes=[Bp_])
                            s_, Bs_ = sga[cnt["s"] % 2]
                            cnt["s"] += 1
                            S.op("act", lambda e, p_=p_, s_=s_, d=d, cc=cc, cbase=cbase: e.activation(
                                out=s_[:, 0:N], in_=p_[:, 0:N], func=AF.Sigmoid, bias=chanv[:, cc, cbase + d:cbase + d + 1]),
                                reads=[Bp_, Bchanv], writes=[Bs_])
                            S.dma("sp", dst_v[:, d * 2 + cc, n0:n0 + N], s_[:, 0:N], reads=[Bs_], semof=Bs_)
                            yield
                if isx:
                    S.op("pool", lambda e: e.tensor_tensor(out=sqq[:, :, 0:N], in0=qd[:, :, 0:N], in1=qd[:, :, 0:N], op=ALU.mult), reads=[Bqd], writes=[Bsqq])
                    yield

                    def ssq(e):
                        for j in range(4):
                            ins = e.matmul(pst[:, 0:N], onesf[:, :], sqq[:, j, 0:N], start=(j == 0), stop=(j == 3))
                        return ins
                    S.op("pe", ssq, reads=[Bsqq, Bonesf], writes=[Bpst])
                    S.op("act", lambda e: e.activation(out=rq[:, 0:N], in_=pst[:, 0:N], func=AF.Sqrt, scale=1.0 / 512, bias=epsc[:, 0:1]),
                         reads=[Bpst, Bepsc], writes=[Brq])
                    yield
                    S.op("dve", lambda e: e.reciprocal(out=rq[:, 0:N], in_=rq[:, 0:N]), reads=[Brq], writes=[Brq])
                    yield
                    for j in range(4):
                        S.op("dve", lambda e, j=j: e.scalar_tensor_tensor(out=qn[:, j, 0:N], in0=qd[:, j, 0:N], scalar=mlav[:, j:j + 1], in1=rq[:, 0:N],
                                                                          op0=ALU.mult, op1=ALU.mult), reads=[Bqd, Bmlav, Brq], writes=[Bqn])
                    yield
                    for h in range(2):
                        p1, Bp1 = next_po()
                        p2, Bp2 = next_po()

                        def qup(e, h=h, p1=p1, p2=p2):
                            for kc in range(4):
                                e.matmul(p1[:, 0:N], wuq[:, kc, h * 192:h * 192 + 128], qn[:, kc, 0:N], start=(kc == 0), stop=(kc == 3))
                            for kc in range(4):
                                ins = e.matmul(p2[0:64, 0:N], wuq[:, kc, h * 192 + 128:h * 192 + 192], qn[:, kc, 0:N], start=(kc == 0), stop=(kc == 3))
                            return ins
                        S.op("pe", qup, reads=[Bwuq, Bqn], writes=[Bp1, Bp2])
                        S.op("act", lambda e, p1=p1: e.activation(out=qno[:, 0:N], in_=p1[:, 0:N], func=AF.Copy), reads=[Bp1], writes=[Bqno])
                        S.op("act", lambda e, p2=p2: e.activation(out=qro[0:64, 0:N], in_=p2[0:64, 0:N], func=AF.Copy), reads=[Bp2], writes=[Bqro])
                        yield
                        S.op("pool", lambda e: e.tensor_tensor(out=sqh[:, 0, 0:N], in0=qno[:, 0:N], in1=qno[:, 0:N], op=ALU.mult), reads=[Bqno], writes=[Bsqh])
                        S.op("pool", lambda e: e.tensor_tensor(out=sqh[0:64, 1, 0:N], in0=qro[0:64, 0:N], in1=qro[0:64, 0:N], op=ALU.mult), reads=[Bqro], writes=[Bsqh])
                        yield

                        def ssh(e):
                            e.matmul(pst[:, 0:N], onesf[:, :], sqh[:, 0, 0:N], start=True, stop=False)
                            return e.matmul(pst[:, 0:N], onesf[0:64, :], sqh[0:64, 1, 0:N], start=False, stop=True)
                        S.op("pe", ssh, reads=[Bsqh, Bonesf], writes=[Bpst])
                        S.op("act", lambda e: e.activation(out=rh[:, 0:N], in_=pst[:, 0:N], func=AF.Sqrt, scale=1.0 / 192, bias=epsc[:, 0:1]),
                             reads=[Bpst, Bepsc], writes=[Brh])
                        yield
                        S.op("dve", lambda e: e.reciprocal(out=rh[:, 0:N], in_=rh[:, 0:N]), reads=[Brh], writes=[Brh])
                        yield
                        S.op("dve", lambda e: e.scalar_tensor_tensor(out=qnf[:, 0:N], in0=qno[:, 0:N], scalar=mlav[:, 6:7], in1=rh[:, 0:N],
                                                                      op0=ALU.mult, op1=ALU.mult), reads=[Bqno, Bmlav, Brh], writes=[Bqnf])
                        S.dma("sp", QN_v[:, h, xo:xo + N], qnf[:, 0:N], reads=[Bqnf], semof=Bqnf)
                        S.op("dve", lambda e: e.scalar_tensor_tensor(out=qrg[0:64, 0:N], in0=qro[0:64, 0:N], scalar=mlav[0:64, 8:9], in1=rh[0:64, 0:N],
                                                                      op0=ALU.mult, op1=ALU.mult), reads=[Bqro, Bmlav, Brh], writes=[Bqrg])
                        yield
                        rope_apply(qrg, Bqrg, N, QR_v[:, h, xo:xo + N])
                        yield
                S.op("pool", lambda e: e.tensor_tensor(out=sqq[:, 0:2, 0:N], in0=kvd[:, :, 0:N], in1=kvd[:, :, 0:N], op=ALU.mult), reads=[Bkvd], writes=[Bsqq])
                yield

                def sskv(e):
                    for j in range(2):
                        ins = e.matmul(pst[:, 0:N], onesf[:, :], sqq[:, j, 0:N], start=(j == 0), stop=(j == 1))
                    return ins
                S.op("pe", sskv, reads=[Bsqq, Bonesf], writes=[Bpst])
                S.op("act", lambda e: e.activation(out=rq[:, 0:N], in_=pst[:, 0:N], func=AF.Sqrt, scale=1.0 / 256, bias=epsc[:, 0:1]),
                     reads=[Bpst, Bepsc], writes=[Brq])
                yield
                S.op("dve", lambda e: e.reciprocal(out=rq[:, 0:N], in_=rq[:, 0:N]), reads=[Brq], writes=[Brq])
                yield
                for j in range(2):
                    S.op("dve", lambda e, j=j: e.scalar_tensor_tensor(out=kvn[:, j, 0:N], in0=kvd[:, j, 0:N], scalar=mlav[:, 4 + j:5 + j], in1=rq[:, 0:N],
                                                                      op0=ALU.mult, op1=ALU.mult), reads=[Bkvd, Bmlav, Brq], writes=[Bkvn])
                S.op("pool", lambda e: e.tensor_tensor(out=sqh[0:64, 1, 0:N], in0=kro[0:64, 0:N], in1=kro[0:64, 0:N], op=ALU.mult), reads=[Bkro], writes=[Bsqh])
                yield
                for h in range(2):
                    p1, Bp1 = next_po()

                    def kup(e, h=h, p1=p1):
                        for kc in range(2):
                            ins = e.matmul(p1[:, 0:N], wukv[:, kc, h * 256:h * 256 + 128], kvn[:, kc, 0:N], start=(kc == 0), stop=(kc == 1))
                        return ins
                    S.op("pe", kup, reads=[Bwukv, Bkvn], writes=[Bp1])
                    S.op("act", lambda e, p1=p1: e.activation(out=qno[:, 0:N], in_=p1[:, 0:N], func=AF.Copy), reads=[Bp1], writes=[Bqno])

                    def vup(e, h=h):
                        for t in range(ntile):
                            for kc in range(2):
                                ins = e.matmul(pv[:, t * 128:(t + 1) * 128], kvn[:, kc, t * 128:(t + 1) * 128], wukv[:, kc, h * 256 + 128:h * 256 + 256],
                                               start=(kc == 0), stop=(kc == 1))
                        return ins
                    S.op("pe", vup, reads=[Bwukv, Bkvn], writes=[Bpv])
                    S.op("act", lambda e, h=h: e.activation(out=vts[:, h, 0:N], in_=pv[:, 0:N], func=AF.Copy), reads=[Bpv], writes=[Bvts])
                    S.dma("sp", VT[h, :, n0:n0 + N], vts[:, h, 0:N], reads=[Bvts], semof=Bvts)
                    yield
                    S.op("pool", lambda e: e.tensor_tensor(out=sqh[:, 0, 0:N], in0=qno[:, 0:N], in1=qno[:, 0:N], op=ALU.mult), reads=[Bqno], writes=[Bsqh])
                    S.op("dve", lambda e: e.tensor_scalar(out=qnf[:, 0:N], in0=qno[:, 0:N], scalar1=mlav[:, 7:8], scalar2=None, op0=ALU.mult),
                         reads=[Bqno, Bmlav], writes=[Bqnf])
                    S.dma("sp", KN_v[:, h, n0:n0 + N], qnf[:, 0:N], reads=[Bqnf], semof=Bqnf)
                    yield

                    def kss(e, h=h):
                        for t in range(ntile):
                            c = t * 2 + h
                            e.matmul(pks[:, c:c + 1], sqh[:, 0, t * 128:(t + 1) * 128], onesf[:, 0:1], start=True, stop=False)
                            ins = e.matmul(pks[:, c:c + 1], sqh[0:64, 1, t * 128:(t + 1) * 128], onesf[0:64, 0:1], start=False, stop=True)
                        return ins
                    S.op("pe", kss, reads=[Bsqh, Bonesf], writes=[Bpks])
                    yield
                nk = ntile * 2
                t0 = (n0 // 128) * 2
                S.op("act", lambda e: e.activation(out=kst[:, 0:nk], in_=pks[:, 0:nk], func=AF.Sqrt, scale=1.0 / 192, bias=epsc[:, 0:1]),
                     reads=[Bpks, Bepsc], writes=[Bkst])
                S.op("dve", lambda e: e.tensor_scalar(out=qrg[0:64, 0:N], in0=kro[0:64, 0:N], scalar1=mlav[0:64, 9:10], scalar2=None, op0=ALU.mult),
                     reads=[Bkro, Bmlav], writes=[Bqrg])
                yield
                S.op("dve", lambda e: e.reciprocal(out=kst[:, 0:nk], in_=kst[:, 0:nk]), reads=[Bkst], writes=[Bkst])
                yield
                S.op("dve", lambda e: e.tensor_scalar(out=KS[:, t0:t0 + nk], in0=kst[:, 0:nk], scalar1=float(192 ** -0.5), scalar2=None, op0=ALU.mult),
                     reads=[Bkst], writes=[BKS])
                if isx:
                    rope_apply(qrg, Bqrg, N, KR[:, n0:n0 + N])
                else:
                    S.op("pool", lambda e: e.tensor_copy(out=qrf[0:64, 0:N], in_=qrg[0:64, 0:N]), reads=[Bqrg], writes=[Bqrf])
                    S.dma("sp", KR[:, n0:n0 + N], qrf[0:64, 0:N], reads=[Bqrf], semof=Bqrf)
                yield

            BWB = Buf("WB")
            wmg_v_ = I["w_in_mg"].rearrange("(kc p) n -> p kc n", p=128)
            wbr_r_v_ = I["w_br_r"].rearrange("(kc p) n -> p kc n", p=128)
            wbr_m_v_ = I["w_br_m"].rearrange("(kc p) n -> p kc n", p=128)
            wout_v_ = I["w_out"].rearrange("(kc p) n -> p kc n", p=128)
            castq = []
            for m in range(16):
                d_ = WMG_b[m].rearrange("p (k n) -> p k n", n=256)
                castq.append((d_[:, :, 0:128], wmg_v_[:, :, m * 128:(m + 1) * 128]))
                castq.append((d_[:, :, 128:256], wmg_v_[:, :, 2048 + m * 128:2048 + (m + 1) * 128]))
                d_ = WBR_b[m].rearrange("p (k n) -> p k n", n=256)
                castq.append((d_[:, :, 0:128], wbr_r_v_[:, :, m * 128:(m + 1) * 128]))
                castq.append((d_[:, :, 128:256], wbr_m_v_[:, :, m * 128:(m + 1) * 128]))
            for nb in range(4):
                castq.append((WOUT_b[nb].rearrange("p (k n) -> p k n", n=512), wout_v_[:, :, nb * 512:(nb + 1) * 512]))
            NGRP = 1 + NX // GS
            NGRP = int(os.environ.get("MK_NGRP", NGRP))
            for t in range(GS // 128):
                prep_tile(t * 128, 0, t)
            S.dma("sp", gain_bc[:], BC[0], writes=[Bgain], reads=[], semof=Bgain)
            S.dma("sp", shift_bc[:], BC[1], writes=[Bshift], reads=[], semof=Bshift)
            def gnext(g_):
                try:
                    next(g_)
                    return True
                except StopIteration:
                    return False

            chain = None
            for gi in range(NGRP):
                dense = dense_gen(gi, gi % 2)
                preps = []
                if gi + 1 < NGRP:
                    r0 = 256 + gi * GS
                    slots_ = {}
                    preps = [(lambda t=t, r0=r0: slots_.__setitem__(t, prep_a(r0 + t * 128))) for t in range(GS // 128)]
                    late = [(lambda t=t, hs=(gi + 1) % 2: prep_b(slots_[t], hs, t)) for t in range(GS // 128)]
                else:
                    late = []
                rnd = 0
                dalive = True
                while dalive or chain is not None:
                    if dalive:
                        dalive = gnext(dense)
                    for _ in range((3 if dalive else 1000) if not os.environ.get('MK_NOINTER') else (0 if dalive else 1000)):
                        if chain is None:
                            break
                        if not gnext(chain):
                            chain = None
                    rnd += 1
                    if preps and (rnd % 3 == 1 or not dalive):
                        preps.pop(0)()
                while preps:
                    preps.pop(0)()
                while late:
                    late.pop(0)()
                chain = chain_gen(gi)
                for _ in range(3):
                    if castq:
                        o_, i_ = castq.pop(0)
                        S.dma("pool", o_, i_, semof=BWB)
            while chain is not None:
                if not gnext(chain):
                    chain = None
            while castq:
                o_, i_ = castq.pop(0)
                S.dma("pool", o_, i_, semof=BWB)
            if "KSD" in debug:
                S.dma("sp", KSD, KS[:], reads=[BKS], semof=BKS)
            S.barrier()
            S.emit()
            S.end_phase(recycle_hw=False)
        if upto == "A":
            return nc

        GR = 256
        NCH = GR // 64
        NXG = NX // GR
        U_k = U_rkv.rearrange("(k c p) n -> p k c n", k=3, c=2, p=128)
        YD_v = [v3(YD[d]) for d in range(2)]
        BD_v = [v3(BD[d]) for d in range(2)]
        with ExitStack() as prs:
            P = Ctx(nc, prs); P.n = 300
            omka, Bomka = P.sb([128, 2], F32, "omka")
            S.op("dve", lambda e: e.tensor_scalar(out=omka[:, :], in0=chanv[:, :, 5], scalar1=-1.0, scalar2=1.0, op0=ALU.mult, op1=ALU.add),
                 reads=[Bchanv], writes=[Bomka])

            class CP:
                pass
            cps = []
            for cc in range(2):
                for d in range(2):
                    c_ = CP()
                    c_.cc, c_.d = cc, d
                    for nm, shp, dt in (("ub", [128, 3, GR + 2], F32), ("cv", [128, 3, GR], F32), ("sgt", [128, GR], F32), ("aat", [128, GR], F32),
                                        ("sq", [128, GR], F32), ("rs", [128, GR], F32), ("kk", [128, GR], F32), ("ff", [128, GR], F32),
                                        ("kmod", [128, GR], F32), ("akk", [128, GR], F32), ("Pc", [128, GR], F32), ("Ei", [128, GR], F32),
                                        ("Ee", [128, GR], F32), ("g", [128, GR], F32), ("gp", [128, GR], F32), ("gi", [128, GR], F32),
                                        ("NA", [128, 256], BF16),
                                        ("KA", [128, 256], BF16), ("A0", [128, 128], BF16), ("PW0", [128, 256], BF16), ("PW1", [128, 256], BF16),
                                        ("Tm0", [128, 128], BF16), ("Tm1", [128, 128], BF16), ("TR", [128, 384], BF16), ("Xb", [128, 128], BF16),
                                        ("Ub", [128, 128], BF16), ("H", [128, 128], F32), ("Hb", [128, 128], BF16), ("S1", [128, 128], F32),
                                        ("pr", [128, GR], F32), ("bon", [128, GR], F32)):
                        t_, b_ = P.sb(shp, dt, nm)
                        setattr(c_, nm, t_)
                        setattr(c_, "B" + nm, b_)
                    for nm, shp, dt in (("gtot", [128, NCH], F32), ("AR", [128, NCH, 256], BF16), ("BE", [128, NCH, 128], BF16),
                                        ("KT", [128, NCH, 128], BF16), ("VB", [128, NCH, 128], BF16), ("Yg", [128, GR], F32)):
                        lst = [P.sb(shp, dt, nm) for _ in range(2)]
                        setattr(c_, nm, [x[0] for x in lst])
                        setattr(c_, "B" + nm, [x[1] for x in lst])
                    c_.Bsgt = c_.Bub
                    c_.Baat = c_.Bub
                    c_.bk1, c_.Bbk1 = P.ps([128, 512], F32, "bk1")
                    c_.bk2, c_.BpAD = P.ps([128, 512], F32, "bk2")
                    c_.BpTT = c_.BpAD
                    for nm in ("H", "Hb"):
                        t_ = getattr(c_, nm)
                        b_ = getattr(c_, "B" + nm)
                        S.op("pool", lambda e, t_=t_: e.memset(t_[:], 0.0), writes=[b_])
                    for nm in ("AR", "BE", "KT", "VB"):
                        for sl_ in range(2):
                            t_ = getattr(c_, nm)[sl_]
                            b_ = getattr(c_, "B" + nm)[sl_]
                            S.op("pool", lambda e, t_=t_: e.memset(t_[:], 0.0), writes=[b_])
                    cps.append(c_)

            if os.environ.get("MK_WARM"):
                def warm(e):
                    for _ in range(400):
                        ins = e.matmul(cps[0].bk1[:, :], msk[:, 0, :], resetb[:, :], start=True, stop=True)
                    return ins
                resetb, Bresetb = P.sb([128, 512], BF16, "resetb")
                S.op("pool", lambda e: e.memset(resetb[:], 1.0), writes=[Bresetb])
                S.op("pe", warm, reads=[Bresetb, Bmsk], writes=[cps[0].Bbk1])
            c3 = lambda ap: ap.rearrange("p (c t) -> p c t", t=64)

            def prep_gen(c, sl, n0, N, s0, s1, xo):
                cc, d = c.cc, c.d
                AR, BAR = c.AR[sl], c.BAR[sl]
                BE, BBE = c.BE[sl], c.BBE[sl]
                KT, BKT = c.KT[sl], c.BKT[sl]
                VB, BVB = c.VB[sl], c.BVB[sl]
                gtot, Bgtot = c.gtot[sl], c.Bgtot[sl]
                lo, hi = n0 - 1, n0 + N + 1
                dl, dh = 0, N + 2
                if n0 == s0:
                    S.op("pool", lambda e: e.memset(c.ub[:, :, 0:1], 0.0), writes=[c.Bub])
                    lo, dl = n0, 1
                if n0 + N == s1:
                    S.op("pool", lambda e: e.memset(c.ub[:, :, N + 1:N + 2], 0.0), writes=[c.Bub])
                    hi, dh = n0 + N, N + 1
                S.dma_group("sp", [(c.ub[:, :, dl:dh], U_k[:, :, cc, lo:hi]),
                                   (c.sgt[:, 0:N], SG_v[:, d * 2 + cc, n0:n0 + N]),
                                   (c.aat[:, 0:N], AA_v[:, d * 2 + cc, n0:n0 + N])], writes=[c.Bub], semof=c.Bub)
                yield
                for kind in range(3):
                    ch = kind * 2 + cc
                    S.op("act", lambda e, kind=kind, ch=ch: e.activation(out=c.cv[:, kind, 0:N], in_=c.ub[:, kind, 1:N + 1], func=AF.Copy,
                                                                         scale=convw[:, ch, 1:2]), reads=[c.Bub, Bconvw], writes=[c.Bcv])
                S.op("dve", lambda e: e.tensor_tensor_scan(out=c.Pc[:, 0:N], data0=resetm[:, 0:N], data1=c.sgt[:, 0:N], initial=0.0, op0=ALU.mult, op1=ALU.add),
                     reads=[Bresetm, c.Bsgt], writes=[c.BPc])
                yield
                for tap in (0, 2):
                    for kind in range(3):
                        ch = kind * 2 + cc
                        S.op("dve", lambda e, kind=kind, ch=ch, tap=tap: e.scalar_tensor_tensor(
                            out=c.cv[:, kind, 0:N], in0=c.ub[:, kind, tap:tap + N], scalar=convw[:, ch, tap:tap + 1],
                            in1=c.cv[:, kind, 0:N], op0=ALU.mult, op1=ALU.add), reads=[c.Bub, Bconvw, c.Bcv], writes=[c.Bcv])
                    yield
                nch = N // 64
                tot = c3(c.Pc[:, 0:N])[:, :, 63]
                if d == 0:
                    S.op("pool", lambda e: e.tensor_tensor(out=c.Ee[:, 0:N], in0=c.Pc[:, 0:N], in1=c.sgt[:, 0:N], op=ALU.subtract), reads=[c.BPc, c.Bsgt], writes=[c.BEe])
                    Ei, BEi = c.Pc, c.BPc
                else:
                    for k_ in range(nch):
                        S.op("pool", lambda e, k_=k_: e.tensor_scalar(out=c.Ee[:, k_ * 64:(k_ + 1) * 64], in0=c.Pc[:, k_ * 64:(k_ + 1) * 64], scalar1=-1.0,
                                                                      scalar2=c.Pc[:, k_ * 64 + 63:k_ * 64 + 64], op0=ALU.mult, op1=ALU.add),
                             reads=[c.BPc], writes=[c.BEe])
                    S.op("pool", lambda e: e.tensor_tensor(out=c.Ei[:, 0:N], in0=c.Ee[:, 0:N], in1=c.sgt[:, 0:N], op=ALU.add), reads=[c.BEe, c.Bsgt], writes=[c.BEi])
                    Ei, BEi = c.Ei, c.BEi
                S.op("act", lambda e: e.activation(out=c.sq[:, 0:N], in_=c.cv[:, 1, 0:N], func=AF.Square, scale=chanv[:, cc, 4:5]),
                     reads=[c.Bcv, Bchanv], writes=[c.Bsq])
                yield
                st_ = c.bk1[:, 0:N]
                Bst_ = c.Bbk1
                S.op("pe", lambda e: e.matmul(st_, bones[:, :], c.sq[:, 0:N], start=True, stop=True), reads=[c.Bsq, Bbones], writes=[Bst_])
                S.op("act", lambda e: e.activation(out=c.rs[:, 0:N], in_=st_, func=AF.Sqrt, bias=epsc[:, 1:2], scale=1.0), reads=[Bepsc], writes=[c.Brs, Bst_])
                yield
                S.op("act", lambda e: e.activation(out=c.g[:, 0:N], in_=Ei[:, 0:N], func=AF.Exp, scale=-C0), reads=[BEi], writes=[c.Bg])
                S.op("act", lambda e: e.activation(out=c.gp[:, 0:N], in_=c.Ee[:, 0:N], func=AF.Exp, scale=-C0), reads=[c.BEe], writes=[c.Bgp])
                S.op("act", lambda e: e.activation(out=c.gi[:, 0:N], in_=Ei[:, 0:N], func=AF.Exp, scale=C0), reads=[BEi], writes=[c.Bgi])
                S.op("act", lambda e: e.activation(out=gtot[:, 0:nch], in_=tot, func=AF.Exp, scale=-C0), reads=[c.BPc], writes=[Bgtot])
                S.op("pool", lambda e: e.tensor_scalar(out=c.ff[:, 0:N], in0=c.aat[:, 0:N], scalar1=chanv[:, cc, 5:6], scalar2=omka[:, cc:cc + 1],
                                                        op0=ALU.mult, op1=ALU.add), reads=[c.Baat, Bchanv, Bomka], writes=[c.Bff])
                S.op("pool", lambda e: e.tensor_tensor(out=c.kmod[:, 0:N], in0=c.cv[:, 1, 0:N], in1=c.ff[:, 0:N], op=ALU.mult), reads=[c.Bcv, c.Bff], writes=[c.Bkmod])
                S.op("dve", lambda e: e.reciprocal(out=c.rs[:, 0:N], in_=c.rs[:, 0:N]), reads=[c.Brs], writes=[c.Brs])
                yield
                S.op("dve", lambda e: e.scalar_tensor_tensor(out=c.pr[:, 0:N], in0=c.cv[:, 0, 0:N], scalar=chanv[:, cc, 8:9], in1=c.kmod[:, 0:N],
                                                              op0=ALU.mult, op1=ALU.mult), reads=[c.Bcv, Bchanv, c.Bkmod], writes=[c.Bpr])
                S.op("dve", lambda e: e.scalar_tensor_tensor(out=c.kk[:, 0:N], in0=c.cv[:, 1, 0:N], scalar=chanv[:, cc, 4:5], in1=c.rs[:, 0:N],
                                                              op0=ALU.mult, op1=ALU.mult), reads=[c.Bcv, Bchanv, c.Brs], writes=[c.Bkk])
                yield
                S.op("pe", lambda e: e.matmul(st_, bones[:, :], c.pr[:, 0:N], start=True, stop=True), reads=[c.Bpr, Bbones], writes=[Bst_])
                S.op("dve", lambda e: e.tensor_tensor(out=c.bon[:, 0:N], in0=st_, in1=c.cv[:, 2, 0:N], op=ALU.mult), reads=[c.Bcv], writes=[c.Bbon, Bst_])
                if xo is not None:
                    S.dma("sp", BD_v[d][:, cc, xo:xo + N], c.bon[:, 0:N], reads=[c.Bbon], semof=c.Bbon)
                yield
                S.op("pool", lambda e: e.tensor_tensor(out=c.akk[:, 0:N], in0=c.aat[:, 0:N], in1=c.kk[:, 0:N], op=ALU.mult), reads=[c.Baat, c.Bkk], writes=[c.Bakk])
                for hh in range(2):
                    ps_ = slice(64 * hh, 64 * hh + 64)
                    o1 = slice(64 * hh, 64 * hh + 64)
                    o2 = slice(128 + 64 * hh, 128 + 64 * hh + 64)
                    S.op("dve", lambda e, ps_=ps_, o2=o2: e.tensor_tensor(out=AR[ps_, 0:nch, o2], in0=c3(c.cv[ps_, 0, 0:N]), in1=c3(c.g[ps_, 0:N]), op=ALU.mult),
                         reads=[c.Bcv, c.Bg], writes=[BAR])
                    S.op("dve", lambda e, ps_=ps_, o1=o1: e.scalar_tensor_tensor(out=AR[ps_, 0:nch, o1], in0=c3(c.kk[ps_, 0:N]), scalar=-1.0, in1=c3(c.gp[ps_, 0:N]),
                                                                                 op0=ALU.mult, op1=ALU.mult), reads=[c.Bkk, c.Bgp], writes=[BAR])
                    S.op("pool", lambda e, ps_=ps_, o1=o1: e.tensor_tensor(out=KT[ps_, 0:nch, o1], in0=c3(c.kmod[ps_, 0:N]), in1=c3(c.gi[ps_, 0:N]), op=ALU.mult),
                         reads=[c.Bkmod, c.Bgi], writes=[BKT])
                    S.op("pool", lambda e, ps_=ps_, o1=o1: e.tensor_copy(out=VB[ps_, 0:nch, o1], in_=c3(c.cv[ps_, 2, 0:N])), reads=[c.Bcv], writes=[BVB])
                yield
                for hh in range(2):
                    ps_ = slice(64 * hh, 64 * hh + 64)
                    o1 = slice(64 * hh, 64 * hh + 64)
                    S.op("pool", lambda e, ps_=ps_, o1=o1: e.tensor_tensor(out=BE[ps_, 0:nch, o1], in0=c3(c.akk[ps_, 0:N]), in1=c3(c.gi[ps_, 0:N]), op=ALU.mult),
                         reads=[c.Bakk, c.Bgi], writes=[BBE])
                yield

            def chunk_gen(c, sl, k, want_y):
                AR, BAR = c.AR[sl], c.BAR[sl]
                BE, BBE = c.BE[sl], c.BBE[sl]
                KT, BKT = c.KT[sl], c.BKT[sl]
                VB, BVB = c.VB[sl], c.BVB[sl]
                gtot, Bgtot = c.gtot[sl], c.Bgtot[sl]
                Yg, BYg = c.Yg[sl], c.BYg[sl]
                bk1, Bbk1, bk2, BpAD, BpTT = c.bk1, c.Bbk1, c.bk2, c.BpAD, c.BpTT
                mN = (msk[:, 0:2, :] if c.d == 0 else msk[:, 2:4, :]).rearrange("p a b -> p (a b)")
                mA = msk[:, 2, :] if c.d == 0 else msk[:, 0, :]

                def f1(e):
                    e.matmul(bk1[:, 0:256], BE[:, k, :], AR[:, k, :], start=True, stop=True)
                    return e.matmul(bk1[:, 256:512], KT[:, k, :], AR[:, k, :], start=True, stop=True)
                S.op("pe", f1, reads=[BBE, BKT, BAR], writes=[Bbk1])
                S.op("pe", lambda e: e.matmul(bk2[:, 0:128], AR[:, k, 0:128], BE[:, k, :], start=True, stop=True), reads=[BAR, BBE], writes=[BpAD])
                S.op("dve", lambda e: e.tensor_tensor(out=c.NA[:, :], in0=bk1[:, 0:256], in1=mN, op=ALU.mult), reads=[Bmsk], writes=[c.BNA, Bbk1])
                S.op("dve", lambda e: e.tensor_tensor(out=c.KA[:, :], in0=bk1[:, 256:512], in1=mN, op=ALU.mult), reads=[Bmsk], writes=[c.BKA, Bbk1])
                S.op("dve", lambda e: e.tensor_tensor(out=c.A0[:, :], in0=bk2[:, 0:128], in1=mA, op=ALU.mult), reads=[Bmsk], writes=[c.BA0, BpAD])
                yield
                def f2(e):
                    e.matmul(bk1[:, 0:128], BE[:, k, :], identb[:, :], start=True, stop=True)
                    e.matmul(bk1[:, 128:256], KT[:, k, :], identb[:, :], start=True, stop=True)
                    return e.matmul(bk1[:, 256:384], VB[:, k, :], identb[:, :], start=True, stop=True)
                S.op("pe", f2, reads=[BBE, BKT, BVB, Bidentb], writes=[Bbk1])
                S.op("act", lambda e: e.activation(out=c.TR[:, :], in_=bk1[:, 0:384], func=AF.Copy), reads=[], writes=[c.BTR, Bbk1])
                yield
                S.op("pool", lambda e: e.tensor_tensor(out=c.Tm0[:, :], in0=c.NA[:, 0:128], in1=identb[:, :], op=ALU.add), reads=[c.BNA, Bidentb], writes=[c.BTm0])
                Nk, BNk, Ak, BAk = c.NA[:, 0:128], c.BNA, c.A0[:, :], c.BA0
                Tc, BTc = c.Tm0, c.BTm0
                for lvl in range(5):
                    pw, Bpw = (c.PW0, c.BPW0) if lvl % 2 == 0 else (c.PW1, c.BPW1)
                    if lvl < 4:
                        def f3(e, Nk=Nk, Ak=Ak):
                            e.matmul(bk2[:, 0:128], Ak, Nk, start=True, stop=True)
                            return e.matmul(bk2[:, 128:256], Nk, Ak, start=True, stop=True)
                        S.op("pe", f3, reads=[BNk, BAk], writes=[BpAD])
                        S.op("act", lambda e, pw=pw: e.activation(out=pw[:, :], in_=bk2[:, 0:256], func=AF.Copy), reads=[BpAD], writes=[Bpw])
                    else:
                        S.op("pe", lambda e, Nk=Nk, Ak=Ak: e.matmul(bk2[:, 128:256], Nk, Ak, start=True, stop=True), reads=[BNk, BAk], writes=[BpAD])
                        S.op("act", lambda e, pw=pw: e.activation(out=pw[:, 128:256], in_=bk2[:, 128:256], func=AF.Copy), reads=[BpAD], writes=[Bpw])
                    yield
                    Nk, BNk, Ak, BAk = pw[:, 0:128], Bpw, pw[:, 128:256], Bpw
                    Tn, BTn = (c.Tm1, c.BTm1) if lvl % 2 == 0 else (c.Tm0, c.BTm0)
                    S.op("pe", lambda e, Ak=Ak, Tc=Tc: e.matmul(bk1[:, 384:512], Ak, Tc[:, :], start=True, stop=True), reads=[BAk, BTc], writes=[Bbk1])
                    S.op("dve", lambda e, Tc=Tc, Tn=Tn: e.tensor_tensor(out=Tn[:, :], in0=bk1[:, 384:512], in1=Tc[:, :], op=ALU.add), reads=[BTc], writes=[BTn, Bbk1])
                    Tc, BTc = Tn, BTn
                    yield
                Tf, BTf = Tc, BTc
                def fx(e):
                    e.matmul(bk1[:, 0:128], c.KA[:, 0:128], c.TR[:, 256:384], start=True, stop=False)
                    return e.matmul(bk1[:, 0:128], AR[:, k, 0:128], c.Hb[:, :], start=False, stop=True)
                S.op("pe", fx, reads=[c.BKA, c.BTR, BAR, c.BHb], writes=[Bbk1])
                S.op("act", lambda e: e.activation(out=c.Xb[:, :], in_=bk1[:, 0:128], func=AF.Copy), reads=[Bbk1], writes=[c.BXb])
                yield
                S.op("pe", lambda e: e.matmul(bk1[:, 128:256], Tf[:, :], c.Xb[:, :], start=True, stop=True), reads=[BTf, c.BXb], writes=[Bbk1])
                S.op("act", lambda e: e.activation(out=c.Ub[:, :], in_=bk1[:, 128:256], func=AF.Copy), reads=[Bbk1], writes=[c.BUb])
                yield

                def fh(e):
                    e.matmul(bk1[:, 256:384], c.TR[:, 128:256], c.TR[:, 256:384], start=True, stop=False)
                    ins = e.matmul(bk1[:, 256:384], c.TR[:, 0:128], c.Ub[:, :], start=False, stop=True)
                    if want_y:
                        e.matmul(bk1[:, 384:512], c.Hb[:, :], AR[:, k, 128:256], start=True, stop=False)
                        e.matmul(bk1[:, 384:512], c.Ub[:, :], c.NA[:, 128:256], start=False, stop=False)
                        ins = e.matmul(bk1[:, 384:512], c.TR[:, 256:384], c.KA[:, 128:256], start=False, stop=True)
                    return ins
                S.op("pe", fh, reads=[c.BTR, c.BUb, c.BHb, BAR, c.BNA, c.BKA], writes=[Bbk1])
                S.op("dve", lambda e: e.tensor_tensor(out=c.S1[:, :], in0=bk1[:, 256:384], in1=c.H[:, :], op=ALU.add), reads=[Bbk1, c.BH], writes=[c.BS1])
                if want_y:
                    for hh in range(2):
                        ps_ = slice(64 * hh, 64 * hh + 64)
                        S.op("act", lambda e, ps_=ps_, hh=hh: e.activation(out=Yg[ps_, k * 64:(k + 1) * 64], in_=bk1[ps_, 384 + 64 * hh:384 + 64 * hh + 64], func=AF.Copy),
                             reads=[], writes=[BYg, Bbk1])
                yield
                S.op("act", lambda e: e.activation(out=c.Hb[:, :], in_=c.S1[:, :], func=AF.Copy, scale=gtot[:, k:k + 1]), reads=[c.BS1, Bgtot], writes=[c.BHb])
                S.op("pool", lambda e: e.tensor_scalar(out=c.H[:, :], in0=c.S1[:, :], scalar1=gtot[:, k:k + 1], scalar2=None, op0=ALU.mult),
                     reads=[c.BS1, Bgtot], writes=[c.BH])
                yield

            def step_info(c, step):
                if step == 0:
                    return 0, 0, 256, None
                xg = (step - 1) if c.d == 0 else (NXG - step)
                return 256 + xg * GR, 256, NT, xg * GR

            def run_rr(gens):
                alive = list(gens)
                while alive:
                    nxt = []
                    for g_ in alive:
                        try:
                            next(g_)
                            nxt.append(g_)
                        except StopIteration:
                            pass
                    alive = nxt

            NSTEP = int(os.environ.get("MK_RSTEPS", 1 + NXG))

            def mk_prep(c, step):
                n0, s0, s1, xo = step_info(c, step)
                return prep_gen(c, step % 2, n0, GR, s0, s1, xo)

            run_rr([mk_prep(c, 0) for c in cps])
            for step in range(NSTEP):
                isx = step > 0
                sl = step % 2

                def seq(c):
                    for ci in range(NCH):
                        k = ci if c.d == 0 else NCH - 1 - ci
                        yield from chunk_gen(c, sl, k, isx)
                    if isx:
                        xo = step_info(c, step)[3]
                        S.dma("sp", YD_v[c.d][:, c.cc, xo:xo + GR], c.Yg[sl][:, :], reads=[c.BYg[sl]], semof=c.BYg[sl])

                gens = [seq(c) for c in cps]
                preps = [mk_prep(c, step + 1) for c in cps] if step + 1 < NSTEP else []
                rnd = 0
                alive = gens
                while alive or preps:
                    nxt = []
                    for g_ in alive:
                        try:
                            next(g_)
                            nxt.append(g_)
                        except StopIteration:
                            pass
                    alive = nxt
                    rnd += 1
                    if preps and (rnd % 4 == 0 or not alive):
                        np_ = []
                        for g_ in preps:
                            try:
                                next(g_)
                                np_.append(g_)
                            except StopIteration:
                                pass
                        preps = np_
            S.barrier()
            S.emit()
            S.end_phase()
        if upto == "R":
            return nc

        NF = 512
        BOX = Buf("OX")
        with ExitStack() as pfs:
            P = Ctx(nc, pfs); P.n = 400
            ld = [[P.sb([128, NF], F32, "fld") for _ in range(5)] for _ in range(2)]
            ld = [[(t_, grp[0][1]) for (t_, _) in grp] for grp in ld]
            tmp = [{nm: P.sb([128, NF], F32, nm) for nm in ("yy", "bs", "ysq", "mm", "msq", "var", "yc")} for _ in range(2)]
            ob = [P.sb([128, NF], BF16, "ob") for _ in range(2)]
            ps1 = [P.ps([128, 512], F32, "ps1") for _ in range(2)]
            ps2 = [P.ps([128, 512], F32, "ps2") for _ in range(2)]

            def ftile_gen(cc, ti, sl):
                xo = ti * NF
                (y0, By0), (y1, By1), (b0, Bb0), (b1, Bb1), (gzt, Bgzt) = ld[sl]
                T_ = tmp[sl]
                (yy, Byy), (bs, Bbs), (ysq, Bysq), (mm_, Bmm_), (msq, Bmsq), (var, Bvar), (yc, Byc) = (T_[k] for k in ("yy", "bs", "ysq", "mm", "msq", "var", "yc"))
                p1, Bp1 = ps1[sl]
                p2, Bp2 = ps2[sl]
                o_, Bo_ = ob[sl]
                S.dma_group("sp", [(y0[:], YD_v[0][:, cc, xo:xo + NF]), (y1[:], YD_v[1][:, cc, xo:xo + NF]),
                                   (b0[:], BD_v[0][:, cc, xo:xo + NF]), (b1[:], BD_v[1][:, cc, xo:xo + NF]),
                                   (gzt[:], Gzr_v[:, cc, xo:xo + NF])], writes=[By0], semof=By0)
                yield
                S.op("pool", lambda e: e.tensor_tensor(out=yy[:], in0=y0[:], in1=y1[:], op=ALU.add), reads=[By0, By1], writes=[Byy])
                S.op("pool", lambda e: e.tensor_tensor(out=bs[:], in0=b0[:], in1=b1[:], op=ALU.add), reads=[Bb0, Bb1], writes=[Bbs])
                yield
                S.op("pe", lambda e: e.matmul(p1[:, :], bones[:, :], yy[:], start=True, stop=True), reads=[Byy, Bbones], writes=[Bp1])
                S.op("act", lambda e: e.activation(out=ysq[:], in_=yy[:], func=AF.Square), reads=[Byy], writes=[Bysq])
                yield
                S.op("pe", lambda e: e.matmul(p2[:, :], bones[:, :], ysq[:], start=True, stop=True), reads=[Bysq, Bbones], writes=[Bp2])
                S.op("dve", lambda e: e.tensor_scalar(out=mm_[:], in0=p1[:, :], scalar1=1.0 / 64, scalar2=None, op0=ALU.mult), reads=[Bp1], writes=[Bmm_])
                yield
                S.op("pool", lambda e: e.tensor_tensor(out=msq[:], in0=mm_[:], in1=mm_[:], op=ALU.mult), reads=[Bmm_], writes=[Bmsq])
                S.op("pool", lambda e: e.tensor_tensor(out=yc[:], in0=yy[:], in1=mm_[:], op=ALU.subtract), reads=[Byy, Bmm_], writes=[Byc])
                yield
                S.op("dve", lambda e: e.scalar_tensor_tensor(out=var[:], in0=p2[:, :], scalar=1.0 / 64, in1=msq[:], op0=ALU.mult, op1=ALU.subtract),
                     reads=[Bp2, Bmsq], writes=[Bvar])
                yield
                S.op("act", lambda e: e.activation(out=var[:], in_=var[:], func=AF.Sqrt, bias=epsc[:, 2:3], scale=1.0), reads=[Bvar, Bepsc], writes=[Bvar])
                yield
                S.op("dve", lambda e: e.reciprocal(out=var[:], in_=var[:]), reads=[Bvar], writes=[Bvar])
                yield
                S.op("pool", lambda e: e.tensor_tensor(out=yc[:], in0=yc[:], in1=var[:], op=ALU.mult), reads=[Byc, Bvar], writes=[Byc])
                yield
                S.op("dve", lambda e: e.tensor_scalar(out=yc[:], in0=yc[:], scalar1=chanv[:, cc, 6:7], scalar2=chanv[:, cc, 7:8], op0=ALU.mult, op1=ALU.add),
                     reads=[Byc, Bchanv], writes=[Byc])
                yield
                S.op("pool", lambda e: e.tensor_tensor(out=yc[:], in0=yc[:], in1=bs[:], op=ALU.add), reads=[Byc, Bbs], writes=[Byc])
                yield
                S.op("dve", lambda e: e.tensor_tensor(out=o_[:], in0=yc[:], in1=gzt[:], op=ALU.mult), reads=[Byc, Bgzt], writes=[Bo_])
                S.dma("sp", OXs[2 * cc][:, xo:xo + NF], o_[0:64, :], reads=[Bo_], semof=Bo_)
                S.dma("sp", OXs[2 * cc + 1][:, xo:xo + NF], o_[64:128, :], reads=[Bo_], semof=Bo_)
                yield

            tiles = [(cc, ti) for cc in range(2) for ti in range(NX // NF)]
            for i in range(0, len(tiles), 2):
                gens = [ftile_gen(tiles[i + k][0], tiles[i + k][1], k) for k in range(2) if i + k < len(tiles)]
                while gens:
                    nxt = []
                    for g_ in gens:
                        try:
                            next(g_)
                            nxt.append(g_)
                        except StopIteration:
                            pass
                    gens = nxt
            BOG = Buf("OG")
            S.barrier()
            for j in range(4):
                if not os.environ.get("MK_NOCC"):
                    S.collective("AllGather", [[0, 1, 2, 3], [4, 5, 6, 7]], OXs[j], OGs[j], reads=[], writes=[BOG], semof=BOG)
            S.emit()
            S.end_phase()
        if upto == "F":
            return nc

        QG = 512
        NKT = NT // 128
        with ExitStack() as pms:
            P = Ctx(nc, pms); P.n = 500
            Kn, BKn = P.sb([128, NT], BF16, "Kn")
            Kr, BKr = P.sb([128, NT], BF16, "Kr")
            Vt, BVt = P.sb([128, NKT, 128], BF16, "Vt")
            onesb, Bonesb = P.sb([128, 128], BF16, "onesb")
            Qn = [P.sb([128, QG], BF16, "Qn") for _ in range(2)]
            Qr = [P.sb([128, QG], BF16, "Qr") for _ in range(2)]
            Pacc = [[P.sb([128, QG], F32, "Pacc") for _ in range(2)] for _ in range(2)]
            gmt = [P.sb([128, QG], F32, "gmt") for _ in range(2)]
            Pt = [P.sb([128, QG], BF16, "Pt") for _ in range(4)]
            rl, Brl = P.sb([128, QG], F32, "rl")
            oo, Boo = P.sb([128, QG], F32, "oo")
            om = [P.sb([128, QG], BF16, "om") for _ in range(2)]
            pS = [P.ps([128, 512], F32, "pS") for _ in range(4)]
            pO = [P.ps([128, 512], F32, "pO") for _ in range(2)]
            pL = [P.ps([128, 512], F32, "pL") for _ in range(2)]
            S.op("pool", lambda e: e.memset(onesb[:], 1.0), writes=[Bonesb])
            S.op("pool", lambda e: e.memset(Kr[64:128, :], 0.0), writes=[BKr])
            for q_, Bq_ in Qr:
                S.op("pool", lambda e, q_=q_: e.memset(q_[64:128, :], 0.0), writes=[Bq_])
            S.dma("sp", Kr[0:64, :], KR, writes=[BKr], semof=BKr)
            NQG = int(os.environ.get("MK_NQG", NX // QG))
            def attn_group(h, qg, sl):
                qo = qg * QG
                qn_, Bqn_ = Qn[sl]
                qr_, Bqr_ = Qr[sl]
                gm_, Bgm_ = gmt[sl]
                po_, Bpo_ = pO[sl]
                pl_, Bpl_ = pL[sl]
                o_, Bo_ = om[sl]
                S.dma("sp", qn_[:], QN_v[:, h, qo:qo + QG], writes=[Bqn_], semof=Bqn_)
                S.dma("sp", qr_[0:64, :], QR_v[:, h, qo:qo + QG], writes=[Bqr_], semof=Bqr_)
                (pa0, Bpa0), (pa1, Bpa1) = Pacc[sl]
                S.dma("sp", gm_[:], Gzm_v[:, h, qo:qo + QG], writes=[Bgm_], semof=Bgm_)

                def qk(kt):
                    ps_, Bps_ = pS[kt % 4]

                    def f(e):
                        e.matmul(ps_[:, :], Kn[:, kt * 128:(kt + 1) * 128], qn_[:, :], start=True, stop=False)
                        return e.matmul(ps_[:, :], Kr[:, kt * 128:(kt + 1) * 128], qr_[:, :], start=False, stop=True)
                    S.op("pe", f, reads=[BKn, BKr, Bqn_, Bqr_], writes=[Bps_])

                def ex_pv(kt):
                    ps_, Bps_ = pS[kt % 4]
                    pt_, Bpt_ = Pt[kt % 4]
                    S.op("act", lambda e: e.activation(out=pt_[:, :], in_=ps_[:, :], func=AF.Exp, scale=KS[:, kt * 2 + h:kt * 2 + h + 1]),
                         reads=[Bps_, BKS], writes=[Bpt_])

                    S.op("pe", lambda e: e.matmul(po_[:, :], Vt[:, kt, :], pt_[:, :], start=(kt == 0), stop=(kt == NKT - 1)),
                         reads=[BVt, Bpt_], writes=[Bpo_])
                    pa_, Bpa_ = (pa0, Bpa0) if kt % 2 == 0 else (pa1, Bpa1)
                    eng_ = "pool" if kt % 2 == 0 else "dve"
                    if kt < 2:
                        S.op(eng_, lambda e: e.tensor_copy(out=pa_[:, :], in_=pt_[:, :]), reads=[Bpt_], writes=[Bpa_])
                    else:
                        S.op(eng_, lambda e: e.tensor_tensor(out=pa_[:, :], in0=pa_[:, :], in1=pt_[:, :], op=ALU.add), reads=[Bpt_, Bpa_], writes=[Bpa_])

                qk(0)
                qk(1)
                for kt in range(NKT):
                    if kt + 2 < NKT:
                        qk(kt + 2)
                    ex_pv(kt)
                def lsum(e):
                    e.matmul(pl_[:, :], onesf[:, :], pa0[:, :], start=True, stop=False)
                    return e.matmul(pl_[:, :], onesf[:, :], pa1[:, :], start=False, stop=True)
                S.op("pe", lsum, reads=[Bonesf, Bpa0, Bpa1], writes=[Bpl_])
                S.op("dve", lambda e: e.reciprocal(out=rl[:, :], in_=pl_[:, :]), reads=[Bpl_], writes=[Brl])
                S.op("dve", lambda e: e.tensor_tensor(out=oo[:, :], in0=po_[:, :], in1=rl[:, :], op=ALU.mult), reads=[Bpo_, Brl], writes=[Boo])
                S.op("pool", lambda e: e.tensor_tensor(out=o_[:, :], in0=oo[:, :], in1=gm_[:, :], op=ALU.mult), reads=[Boo, Bgm_], writes=[Bo_])
                S.dma("sp", OXs[4 + 2 * h][:, qo:qo + QG], o_[0:64, :], reads=[Bo_], semof=Bo_)
                S.dma("sp", OXs[5 + 2 * h][:, qo:qo + QG], o_[64:128, :], reads=[Bo_], semof=Bo_)

            gcount = 0
            for h in range(2):
                S.dma("sp", Kn[:], KN_v[:, h, :], writes=[BKn], semof=BKn)
                S.dma("sp", Vt[:], VT[h].rearrange("p (t d) -> p t d", d=128), writes=[BVt], semof=BVt)
                for qg in range(NQG):
                    attn_group(h, qg, gcount % 2)
                    gcount += 1
            S.barrier()
            S.emit()
            S.end_phase()
        if upto == "M":
            return nc

        for j in range(4, 8):
            S.collective("AllGather", [[0, 1, 2, 3], [4, 5, 6, 7]], OXs[j], OGs[j], reads=[BOX], writes=[BOG], semof=BOG)
        S.barrier()
        S.emit()
        S.end_phase()

        CG = 512
        with ExitStack() as pcs:
            P = Ctx(nc, pcs); P.n = 600
            selq, Bselq = P.sb([128, 4], F32, "selq")
            gain_bc, Bgain = P.sb([128, D], F32, "gain_c")
            shift_bc, Bshift = P.sb([128, D], F32, "shift_c")
            gate_bc, Bgate = P.sb([128, D], F32, "gate_c")
            Bselq = Bshift = Bgate = Bgain
            xt = [P.sb([128, D], F32, "xtc") for _ in range(2)]
            hm = [P.sb([128, D], BF16, "hmc") for _ in range(2)]
            ss = [P.sb([128, 4], F32, "ssc") for _ in range(2)]
            hT, BhT = P.sb([128, 16, CG], BF16, "hTc")
            ldq = [P.sb([128, 16, CG], BF16, "ldq") for _ in range(2)]
            osel, Bosel = P.sb([128, 16, CG], BF16, "osel")
            wmg = [P.sb([128, 16, 256], BF16, "wmg") for _ in range(2)]
            wbr = [P.sb([128, 8, 256], BF16, "wbr") for _ in range(2)]
            sgr, Bsgr = P.sb([128, CG], F32, "sgr")
            sgm, Bsgm = P.sb([128, CG], F32, "sgm")
            tr_, Btr_ = P.sb([128, CG], F32, "tr")
            tm_, Btm_ = P.sb([128, CG], F32, "tm")
            merged, Bmerged = P.sb([128, 16, CG], BF16, "merged")
            wout = [P.sb([128, 16, 512], BF16, "wout") for _ in range(2)]
            xr = [P.sb([128, 512], F32, "xr") for _ in range(2)]
            res = [P.sb([128, 512], F32, "res") for _ in range(2)]
            pT = [P.ps([128, 1024], BF16, "pTc") for _ in range(2)]
            pg = [P.ps([128, 512], F32, "pg") for _ in range(4)]
            pout = [P.ps([128, 512], F32, "pout") for _ in range(2)]
            S.dma_group("sp", [(selq[:], I["selq"]), (gain_bc[:], BC[0]), (shift_bc[:], BC[1]), (gate_bc[:], BC[2])], writes=[Bgain], semof=Bgain)
            wmg_v = I["w_in_mg"].rearrange("(kc p) n -> p kc n", p=128)
            wbr_r_v = I["w_br_r"].rearrange("(kc p) n -> p kc n", p=128)
            wbr_m_v = I["w_br_m"].rearrange("(kc p) n -> p kc n", p=128)
            wout_v = I["w_out"].rearrange("(kc p) n -> p kc n", p=128)
            cnt = {"tile": 0, "w": 0, "o": 0, "r": 0}
            NCG = int(os.environ.get("MK_NCG", 2048 // CG))
            for gj in range(NCG):
                go = gj * CG
                for t in range(CG // 128):
                    s = cnt["tile"] % 2
                    cnt["tile"] += 1
                    x_, Bx_ = xt[s]
                    h_, Bh_ = hm[s]
                    s_, Bs_ = ss[s]
                    S.dma("sp", x_[:], I["xm"][go + t * 128:go + (t + 1) * 128, :], writes=[Bx_], semof=Bx_)
                    S.op("act", lambda e, x_=x_, h_=h_, s_=s_: e.activation(out=h_[:], in_=x_[:], func=AF.Square, accum_out=s_[:, 0:1]), reads=[Bx_], writes=[Bh_, Bs_])
                    S.op("act", lambda e, s_=s_: e.activation(out=s_[:, 1:2], in_=s_[:, 0:1], func=AF.Sqrt, scale=1.0 / D, bias=epsc[:, 0:1]), reads=[Bs_, Bepsc], writes=[Bs_])
                    S.op("dve", lambda e, s_=s_: e.reciprocal(out=s_[:, 2:3], in_=s_[:, 1:2]), reads=[Bs_], writes=[Bs_])
                    S.op("dve", lambda e, x_=x_, s_=s_: e.scalar_tensor_tensor(out=x_[:], in0=x_[:], scalar=s_[:, 2:3], in1=gain_bc[:], op0=ALU.mult, op1=ALU.mult),
                         reads=[Bx_, Bs_, Bgain], writes=[Bx_])
                    S.op("pool", lambda e, x_=x_, h_=h_: e.tensor_tensor(out=h_[:], in0=x_[:], in1=shift_bc[:], op=ALU.add), reads=[Bx_, Bshift], writes=[Bh_])
                    for half in range(2):
                        p_, Bp_ = pT[half]

                        def trf(e, half=half, p_=p_, h_=h_):
                            for j in range(8):
                                kc = half * 8 + j
                                ins = e.transpose(p_[:, j * 128:(j + 1) * 128], h_[:, kc * 128:(kc + 1) * 128], identb[:])
                            return ins
                        S.op("pe", trf, reads=[Bh_, Bidentb], writes=[Bp_])
                        S.op("act" if half == 0 else "dve",
                             (lambda e, half=half, p_=p_, t=t: e.activation(out=hT[:, half * 8:(half + 1) * 8, t * 128:(t + 1) * 128],
                                                                            in_=p_[:, :].rearrange("p (j n) -> p j n", n=128), func=AF.Copy)) if half == 0 else
                             (lambda e, half=half, p_=p_, t=t: e.tensor_copy(out=hT[:, half * 8:(half + 1) * 8, t * 128:(t + 1) * 128],
                                                                             in_=p_[:, :].rearrange("p (j n) -> p j n", n=128))),
                             reads=[Bp_], writes=[BhT])
                for q in range(4):
                    l_, Bl_ = ldq[q % 2]
                    S.dma_group("sp", [(l_[(j % 2) * 64:(j % 2) * 64 + 64, :, :].rearrange("p (r c) n -> p r c n", c=4)[:, :, j // 2, :],
                                        OGs[j].rearrange("(r p) n -> p r n", p=64)[:, :, q * 2048 + go:q * 2048 + go + CG]) for j in range(8)],
                                reads=[BOG], writes=[Bl_], semof=Bl_)
                    if q == 0:
                        S.op("dve", lambda e, l_=l_: e.tensor_scalar(out=osel[:], in0=l_[:], scalar1=selq[:, 0:1], scalar2=None, op0=ALU.mult),
                             reads=[Bl_, Bselq], writes=[Bosel])
                    else:
                        S.op("dve", lambda e, l_=l_, q=q: e.scalar_tensor_tensor(out=osel[:], in0=l_[:], scalar=selq[:, q:q + 1], in1=osel[:], op0=ALU.mult, op1=ALU.add),
                             reads=[Bl_, Bselq, Bosel], writes=[Bosel])
                for m in range(16):
                    w_, Bw_ = wmg[cnt["w"] % 2]
                    b_, Bb_ = wbr[cnt["w"] % 2]
                    cnt["w"] += 1
                    S.dma("sp", w_[:, :, :], WMG_b[m].rearrange("p (k n) -> p k n", n=256), writes=[Bw_], semof=Bw_)
                    S.dma("sp", b_[:, :, :], WBR_b[m].rearrange("p (k n) -> p k n", n=256), writes=[Bb_], semof=Bb_)
                    (pgr, Bpgr), (pgm, Bpgm), (ppr, Bppr), (ppm, Bppm) = pg

                    def fg(e, w_=w_):
                        for kc in range(16):
                            e.matmul(pgr[:, 0:CG], w_[:, kc, 0:128], hT[:, kc, :], start=(kc == 0), stop=(kc == 15))
                        for kc in range(16):
                            ins = e.matmul(pgm[:, 0:CG], w_[:, kc, 128:256], hT[:, kc, :], start=(kc == 0), stop=(kc == 15))
                        return ins
                    S.op("pe", fg, reads=[Bw_, BhT], writes=[Bpgr, Bpgm])
                    S.op("act", lambda e: e.activation(out=sgr[:, :], in_=pgr[:, 0:CG], func=AF.Sigmoid), reads=[Bpgr], writes=[Bsgr])
                    S.op("act", lambda e: e.activation(out=sgm[:, :], in_=pgm[:, 0:CG], func=AF.Sigmoid), reads=[Bpgm], writes=[Bsgm])

                    def fb(e, b_=b_):
                        for j in range(8):
                            kc = (j // 2) * 4 + (j % 2)
                            e.matmul(ppr[:, 0:CG], b_[:, j, 0:128], osel[:, kc, :], start=(j == 0), stop=(j == 7))
                        for j in range(8):
                            kc = (j // 2) * 4 + 2 + (j % 2)
                            ins = e.matmul(ppm[:, 0:CG], b_[:, j, 128:256], osel[:, kc, :], start=(j == 0), stop=(j == 7))
                        return ins
                    S.op("pe", fb, reads=[Bb_, Bosel], writes=[Bppr, Bppm])
                    S.op("dve", lambda e: e.tensor_tensor(out=tr_[:, :], in0=ppr[:, 0:CG], in1=sgr[:, :], op=ALU.mult), reads=[Bppr, Bsgr], writes=[Btr_])
                    S.op("dve", lambda e: e.tensor_tensor(out=tm_[:, :], in0=ppm[:, 0:CG], in1=sgm[:, :], op=ALU.mult), reads=[Bppm, Bsgm], writes=[Btm_])
                    S.op("pool", lambda e, m=m: e.tensor_tensor(out=merged[:, m, :], in0=tr_[:, :], in1=tm_[:, :], op=ALU.add), reads=[Btr_, Btm_], writes=[Bmerged])
                for nb in range(4):
                    wo_, Bwo_ = wout[cnt["o"] % 2]
                    cnt["o"] += 1
                    S.dma("sp", wo_[:], WOUT_b[nb].rearrange("p (k n) -> p k n", n=512), writes=[Bwo_], semof=Bwo_)
                    for t in range(CG // 128):
                        r_ = cnt["r"] % 2
                        cnt["r"] += 1
                        po_, Bpo_ = pout[r_]
                        xr_, Bxr_ = xr[r_]
                        rs_, Brs_ = res[r_]
                        S.dma("sp", xr_[:], I["xm"][go + t * 128:go + (t + 1) * 128, nb * 512:(nb + 1) * 512], writes=[Bxr_], semof=Bxr_)

                        def fo(e, wo_=wo_, po_=po_, t=t):
                            for kc in range(16):
                                ins = e.matmul(po_[:, :], merged[:, kc, t * 128:(t + 1) * 128], wo_[:, kc, :], start=(kc == 0), stop=(kc == 15))
                            return ins
                        S.op("pe", fo, reads=[Bwo_, Bmerged], writes=[Bpo_])
                        S.op("dve", lambda e, po_=po_, rs_=rs_, nb=nb: e.tensor_tensor(out=rs_[:], in0=po_[:, :], in1=gate_bc[:, nb * 512:(nb + 1) * 512], op=ALU.mult),
                             reads=[Bpo_, Bgate], writes=[Brs_])
                        S.op("pool", lambda e, rs_=rs_, xr_=xr_: e.tensor_tensor(out=rs_[:], in0=rs_[:], in1=xr_[:], op=ALU.add), reads=[Brs_, Bxr_], writes=[Brs_])
                        S.dma("sp", out[go + t * 128:go + (t + 1) * 128, nb * 512:(nb + 1) * 512], rs_[:], reads=[Brs_], semof=Brs_)
            S.barrier()
            S.emit()
            S.end_phase()
    return nc


_NC_CACHE = {}


def kernel(**inputs):
    maps = _host_inputs(inputs)
    if "nc" not in _NC_CACHE:
        _NC_CACHE["nc"] = build()
    nc = _NC_CACHE["nc"]
    res = run_bass_kernel_spmd(nc, maps, core_ids=list(range(8)))
    outp = np.zeros((2, NX, D), np.float32)
    for c in range(8):
        b, g = c // 4, c % 4
        outp[b, 2048 * g:2048 * g + 2048] = res.results[c]["out"]
    return outp
```

```python
import os
from contextlib import ExitStack
import numpy as np
import ml_dtypes
import concourse.bass as bass
import concourse.mybir as mybir
from concourse.bass_utils import run_bass_kernel_spmd

F32 = mybir.dt.float32
BF16 = mybir.dt.bfloat16
ALU = mybir.AluOpType
AF = mybir.ActivationFunctionType

NT = 8448
NX = 8192
NCTX = 256
D = 2048
WC = 2368
GS = 256
C0 = float(np.exp(-0.5))
EPS = 1e-6
GN_EPS = 64e-5


class Tok:
    __slots__ = ("sem", "val", "key")

    def __init__(self, sem, val, key):
        self.sem = sem
        self.val = val
        self.key = key


class DSem:
    _n = 0

    def __init__(self, sem, kind):
        DSem._n += 1
        self.uid = DSem._n
        self.sem = sem
        self.cnt = 0
        self.kind = kind


class Buf:
    def __init__(self, name, const=False):
        self.name = name
        self.w = None
        self.r = []
        self.const = const
        self.dsem = None
        self.dcnt = 0


class Sched:
    ENG = ["pe", "act", "dve", "pool", "sp"]

    def __init__(self, nc, stack):
        self.nc = nc
        self.stack = stack
        self.plan = {e: [] for e in self.ENG}
        self.ecnt = {e: 0 for e in self.ENG}
        self.esem = {}
        self.waited = {e: {} for e in self.ENG}
        self.nsem = 0
        for e in ("pe", "act", "dve", "pool"):
            self.esem[e] = self._newsem("e_" + e)
        self.dbufs = []
        self.free_dsems = {}
        self.ninst = 0

    def _newsem(self, name):
        self.nsem += 1
        return self.stack.enter_context(self.nc.semaphore(f"{name}_{self.nsem}"))

    def _waits(self, eng, toks):
        best = {}
        for t in toks:
            if t is not None and (t.key not in best or best[t.key].val < t.val):
                best[t.key] = t
        for t in best.values():
            if self.waited[eng].get(t.key, 0) >= t.val:
                continue
            if eng == "pe" and t.key == "e_pe":
                continue
            self.waited[eng][t.key] = t.val
            self.plan[eng].append(lambda e, sem=t.sem, v=t.val: e.wait_ge(sem, v))

    def _deps(self, reads, writes):
        deps = []
        for b in reads:
            deps.append(b.w)
        for b in writes:
            deps.append(b.w)
            deps.extend(b.r)
        return deps

    def _mark(self, tok, reads, writes):
        for b in reads:
            if not b.const:
                b.r.append(tok)
        for b in writes:
            b.w = tok
            b.r = []

    def op(self, eng, fn, reads=(), writes=()):
        self._waits(eng, self._deps(reads, writes))
        self.ecnt[eng] += 1
        self.ninst += 1
        tok = Tok(self.esem[eng], self.ecnt[eng], "e_" + eng)
        self.plan[eng].append(lambda e, fn=fn, sem=tok.sem: fn(e).then_inc(sem, 1))
        self._mark(tok, reads, writes)
        return tok

    def _dsem(self, b, kind):
        if b.dsem is None:
            fl = self.free_dsems.setdefault(kind, [])
            if fl:
                b.dsem = fl.pop()
            else:
                b.dsem = DSem(self._newsem("d" + kind), kind)
            self.dbufs.append(b)
        assert b.dsem.kind == kind, (b.name, b.dsem.kind, kind)

    def end_phase(self, recycle_hw=True):
        keep = []
        for b in self.dbufs:
            if b.dsem.kind == "cc":
                keep.append(b)
                continue
            b.dsem = None
            b.w = None
            b.r = []
        self.dbufs = keep

    def dma(self, q, out_ap, in_ap, reads=(), writes=(), semof=None, **kw):
        self._waits(q, self._deps(reads, writes))
        b = semof
        self._dsem(b, "sw" if q == "pool" else "hw")
        ds = b.dsem
        ds.cnt += 16
        tok = Tok(ds.sem, ds.cnt, "d%d" % ds.uid)
        self.plan[q].append(
            lambda e, o=out_ap, i=in_ap, sem=ds.sem, kw=kw: e.dma_start(out=o, in_=i, **kw).then_inc(sem, 16)
        )
        self._mark(tok, reads, writes)
        return tok

    def dma_group(self, q, pairs, reads=(), writes=(), semof=None):
        if os.environ.get("MK_SEQGROUP"):
            for (o, i) in pairs:
                tok = self.dma(q, o, i, reads=reads, writes=writes, semof=semof)
            return tok
        self._waits(q, self._deps(reads, writes))
        b = semof
        self._dsem(b, "sw" if q == "pool" else "hw")
        ds = b.dsem
        for (o, i) in pairs:
            ds.cnt += 16
            self.plan[q].append(lambda e, o=o, i=i, sem=ds.sem: e.dma_start(out=o, in_=i).then_inc(sem, 16))
        tok = Tok(ds.sem, ds.cnt, "d%d" % ds.uid)
        self._mark(tok, reads, writes)
        return tok

    def collective(self, kind, groups, in_ap, out_ap, reads, writes, semof):
        q = "pool"
        self._waits(q, self._deps(reads, writes))
        b = semof
        self._dsem(b, "cc")
        ds = b.dsem
        ds.cnt += 1
        tok = Tok(ds.sem, ds.cnt, "d%d" % ds.uid)
        self.plan[q].append(
            lambda e, sem=ds.sem: e.collective_compute(
                kind, ALU.bypass, replica_groups=groups, ins=[in_ap], outs=[out_ap]
            ).then_inc(sem, 1)
        )
        self._mark(tok, reads, writes)
        return tok

    def barrier(self):
        toks = []
        for e in ("pe", "act", "dve", "pool"):
            if self.ecnt[e] > 0:
                toks.append(Tok(self.esem[e], self.ecnt[e], "e_" + e))
        for b in self.dbufs:
            toks.append(Tok(b.dsem.sem, b.dsem.cnt, "d%d" % b.dsem.uid))
        for e in self.ENG:
            self._waits(e, toks)

    def emit(self):
        plan = self.plan
        with self.nc.Block() as block:

            @block.tensor
            def _(e):
                for f in plan["pe"]:
                    f(e)

            @block.scalar
            def _(e):
                for f in plan["act"]:
                    f(e)

            @block.vector
            def _(e):
                for f in plan["dve"]:
                    f(e)

            @block.gpsimd
            def _(e):
                for f in plan["pool"]:
                    f(e)

            @block.sync
            def _(e):
                for f in plan["sp"]:
                    f(e)

        self.plan = {e: [] for e in self.ENG}


class Ctx:
    def __init__(self, nc, stack):
        self.nc = nc
        self.stack = stack
        self.n = 0

    def sb(self, shape, dt, name=None):
        self.n += 1
        name = (name or "t") + f"_{self.n}"
        t = self.stack.enter_context(self.nc.sbuf_tensor(name, list(shape), dt))
        return t, Buf(name)

    def ps(self, shape, dt, name=None):
        self.n += 1
        name = (name or "p") + f"_{self.n}"
        t = self.stack.enter_context(self.nc.psum_tensor(name, list(shape), dt))
        return t, Buf(name)

    def sub(self):
        c = Ctx(self.nc, ExitStack())
        c.n = self.n + 1000
        return c


def _host_consts():
    idx = np.arange(64)
    us = (idx[:, None] < idx[None, :]).astype(np.float32)
    ui = (idx[:, None] <= idx[None, :]).astype(np.float32)
    ls = (idx[:, None] > idx[None, :]).astype(np.float32)
    li = (idx[:, None] >= idx[None, :]).astype(np.float32)
    masks = np.zeros((128, 4, 128), np.float32)
    for i, m in enumerate((us, ui, ls, li)):
        masks[0:64, i, 0:64] = m
        masks[64:128, i, 64:128] = m
    bones = np.zeros((128, 128), np.float32)
    bones[0:64, 0:64] = 1
    bones[64:128, 64:128] = 1
    sel = np.zeros((2, 2, 128), np.float32)
    sel[0, 0, :] = 1
    sel[1, 1, :] = 1
    reset = np.ones((128, 512), np.float32)
    reset[:, ::64] = 0
    rows = np.repeat(np.arange(128), 64).astype(np.float32)
    cols = np.tile(np.arange(64), 128).astype(np.float32)
    inv = np.power(np.float32(10000.0), -np.arange(0, 32, 2, dtype=np.float32) / np.float32(32)).astype(np.float32)
    ang = np.zeros((64, NX), np.float32)
    for d in range(64):
        pos = rows if d < 32 else cols
        ang[d] = pos * inv[d % 16]
    cos = np.cos(ang.astype(np.float64)).astype(np.float32)
    sin = np.sin(ang.astype(np.float64)).astype(np.float32)
    P = np.zeros((64, 64), np.float32)
    for d in range(64):
        if d % 32 < 16:
            P[d, d + 16] = -1
        else:
            P[d, d - 16] = 1
    return dict(
        ident=np.eye(128, dtype=np.float32), masks=masks, bones=bones, onesf=np.ones((128, 128), np.float32),
        sel=sel, reset=reset, rope_cos=cos, rope_sin=sin, ropePT=np.ascontiguousarray(P.T),
    )


def _host_inputs(inp):
    f = lambda a: np.ascontiguousarray(a, dtype=np.float32)
    consts = _host_consts()
    w_in = inp["w_in"][0]
    offs = np.cumsum([0, 3072, 1024, 128, 128, 512, 256, 64, 1024, 4096])
    o_rkv, o_zr, o_wd, o_ad, o_qd, o_kvd, o_kr, o_zm, o_mg = offs[:9]
    maps = []
    for c in range(8):
        b, g = c // 4, c % 4
        ch = slice(256 * g, 256 * g + 256)
        cols = np.concatenate([
            o_rkv + np.arange(256 * g, 256 * g + 256),
            o_rkv + 1024 + np.arange(256 * g, 256 * g + 256),
            o_rkv + 2048 + np.arange(256 * g, 256 * g + 256),
            o_zr + np.arange(256 * g, 256 * g + 256),
            o_wd + np.arange(128), o_ad + np.arange(128),
            o_qd + np.arange(512), o_kvd + np.arange(256), o_kr + np.arange(64),
            o_zm + np.arange(256 * g, 256 * g + 256),
        ])
        assert cols.size == WC
        conv = inp["conv_rkv"][0]
        conv_fm = np.zeros((128, 6, 3), np.float32)
        for kind in range(3):
            for cc in range(2):
                cidx = kind * 1024 + 256 * g + cc * 128 + np.arange(128)
                conv_fm[:, kind * 2 + cc, :] = conv[:, cidx].T
        chanv = np.zeros((128, 2, 10), np.float32)
        for cc in range(2):
            cidx = 256 * g + cc * 128 + np.arange(128)
            chanv[:, cc, 0] = inp["w0"][0, 0, cidx]
            chanv[:, cc, 1] = inp["w0"][0, 1, cidx]
            chanv[:, cc, 2] = inp["a0"][0, 0, cidx]
            chanv[:, cc, 3] = inp["a0"][0, 1, cidx]
            chanv[:, cc, 4] = inp["k_k"][0, cidx]
            chanv[:, cc, 5] = inp["k_a"][0, cidx]
            chanv[:, cc, 6] = inp["ln_x_w"][0, cidx]
            chanv[:, cc, 7] = inp["ln_x_b"][0, cidx]
            chanv[:, cc, 8] = inp["r_k"][0].reshape(-1)[cidx]
        wlora = np.zeros((128, 2, 256), np.float32)
        for d in range(2):
            wlora[64 * d:64 * d + 64, 0, :] = inp["w_decay_up"][0, d][:, ch]
            wlora[64 * d:64 * d + 64, 1, :] = inp["w_a_up"][0, d][:, ch]
        mlav = np.zeros((128, 10), np.float32)
        mlav[:, 0:4] = inp["q_norm_w"][0].reshape(4, 128).T
        mlav[:, 4:6] = inp["kv_norm_w"][0].reshape(2, 128).T
        mlav[:, 6] = inp["q_gain"][0][:128]
        mlav[:, 7] = inp["k_gain"][0][:128]
        mlav[0:64, 8] = inp["q_gain"][0][128:]
        mlav[0:64, 9] = inp["k_gain"][0][128:]
        hq = [2 * g, 2 * g + 1]
        w_uq_c = np.concatenate([inp["w_uq"][0][:, h * 192:(h + 1) * 192] for h in hq], axis=1)
        w_ukv_c = np.concatenate([inp["w_ukv"][0][:, h * 256:(h + 1) * 256] for h in hq], axis=1)
        cfm = np.stack([inp["c"][b].reshape(16, 128).T, inp["c_ctx"].reshape(16, 128).T], axis=-1)
        selq = np.zeros((128, 4), np.float32)
        selq[:, g] = 1
        m = dict(
            xs=f(np.concatenate([inp["ctx"][b], inp["x"][b]], axis=0)),
            xm=f(inp["x"][b, 2048 * g:2048 * g + 2048]),
            cfm=f(cfm), norm_w_row=f(np.stack([inp["norm_w"][0]] * 2)), b_mod_row=f(np.stack([inp["b_mod"][0]] * 2)),
            w_mod=f(inp["w_mod"][0]), w_in_core=f(w_in[:, cols]), w_in_mg=f(w_in[:, o_mg:o_mg + 4096]),
            conv_fm=f(conv_fm), chanv=f(chanv), wlora=f(wlora), mlav=f(mlav), w_uq_c=f(w_uq_c), w_ukv_c=f(w_ukv_c),
            w_br_r=f(inp["w_branch_rwkv"][0]), w_br_m=f(inp["w_branch_mla"][0]), w_out=f(inp["w_out"][0]),
            selq=f(selq),
        )
        m.update({k: f(v) for k, v in consts.items()})
        maps.append(m)
    return maps


INPUT_SHAPES = dict(
    xs=[NT, D], xm=[2048, D], cfm=[128, 16, 2], norm_w_row=[2, D], b_mod_row=[2, 3 * D], w_mod=[D, 3 * D],
    w_in_core=[D, WC], w_in_mg=[D, 4096], conv_fm=[128, 6, 3], chanv=[128, 2, 10], wlora=[128, 2, 256],
    mlav=[128, 10], w_uq_c=[512, 384], w_ukv_c=[256, 512], w_br_r=[1024, D], w_br_m=[1024, D], w_out=[D, D],
    selq=[128, 4], ident=[128, 128], masks=[128, 4, 128], bones=[128, 128], onesf=[128, 128], sel=[2, 2, 128],
    reset=[128, 512], rope_cos=[64, NX], rope_sin=[64, NX], ropePT=[64, 64],
)


def build(debug=(), upto="C"):
    nc = bass.Bass("TRN2", target_bir_lowering=False)
    I = {k: nc.dram_tensor(k, s, F32, kind="ExternalInput").ap() for k, s in INPUT_SHAPES.items()}
    out = nc.dram_tensor("out", [2048, D], F32, kind="ExternalOutput").ap()

    def scratch(name, shape, dt):
        if name in debug:
            return nc.dram_tensor(name, shape, dt, kind="ExternalOutput").ap()
        return nc.dram_tensor(name, shape, dt).ap()

    BC = scratch("BC", [5, 128, D], F32)
    U_rkv = scratch("U_rkv", [768, NT], F32)
    G_zr = scratch("G_zr", [256, NX], F32)
    SG = scratch("SG", [512, NT], F32)
    AA = scratch("AA", [512, NT], F32)
    QN = scratch("QN", [256, NX], BF16)
    QR = scratch("QR", [128, NX], BF16)
    KN = scratch("KN", [256, NT], BF16)
    KR = scratch("KR", [64, NT], BF16)
    VT = scratch("VT", [2, 128, NT], BF16)
    G_zm = scratch("G_zm", [256, NX], F32)
    KSD = scratch("KSD", [128, 132], F32)
    YD = [scratch("YD0", [256, NX], F32), scratch("YD1", [256, NX], F32)]
    BD = [scratch("BD0", [256, NX], F32), scratch("BD1", [256, NX], F32)]
    DBG = [scratch(f"DBG{i}", [128, 512], BF16 if i < 2 else F32) for i in range(4)]
    OXs = [scratch(f"OX{j}", [64, NX], BF16) for j in range(8)]
    OGs = [scratch(f"OG{j}", [256, NX], BF16) for j in range(8)]

    WMG_b = scratch("WMG_b", [16, 128, 16 * 256], BF16)
    WBR_b = scratch("WBR_b", [16, 128, 8 * 256], BF16)
    WOUT_b = scratch("WOUT_b", [4, 128, 16 * 512], BF16)
    v3 = lambda ap, p=128: ap.rearrange("(c p) n -> p c n", p=p)
    U_v, Gzr_v, SG_v, AA_v = v3(U_rkv), v3(G_zr), v3(SG), v3(AA)
    QN_v, QR_v, KN_v, Gzm_v = v3(QN), v3(QR, 64), v3(KN), v3(G_zm)

    with ExitStack() as top:
        S = Sched(nc, top)
        T = Ctx(nc, top)

        identb, Bidentb = T.sb([128, 128], BF16, "identb")
        msk, Bmsk = T.sb([128, 4, 128], BF16, "msk")
        bones, Bbones = T.sb([128, 128], F32, "bones")
        onesf, Bonesf = T.sb([128, 128], F32, "onesf")
        selt, Bselt = T.sb([2, 2, 128], F32, "selt")
        resetm, Bresetm = T.sb([128, 512], F32, "resetm")
        ropePT, BropePT = T.sb([64, 64], F32, "ropePT")
        epsc, Bepsc = T.sb([128, 4], F32, "epsc")
        convw, Bconvw = T.sb([128, 6, 3], F32, "convw")
        chanv, Bchanv = T.sb([128, 2, 10], F32, "chanv")
        mlav, Bmlav = T.sb([128, 10], F32, "mlav")
        KS, BKS = T.sb([128, 132], F32, "KS")
        S.dma("pool", identb[:], I["ident"], writes=[Bidentb], semof=Bidentb)
        S.dma("pool", msk[:], I["masks"], writes=[Bmsk], semof=Bmsk)
        for t_, b_, k_ in ((bones, Bbones, "bones"), (onesf, Bonesf, "onesf"), (selt, Bselt, "sel"), (resetm, Bresetm, "reset"),
                           (ropePT, BropePT, "ropePT"), (convw, Bconvw, "conv_fm"), (chanv, Bchanv, "chanv"), (mlav, Bmlav, "mlav")):
            S.dma("sp", t_[:], I[k_], writes=[b_], semof=b_)
        S.op("pool", lambda e: e.memset(epsc[:, 0:1], EPS), writes=[Bepsc])
        S.op("pool", lambda e: e.memset(epsc[:, 1:2], 1e-12), writes=[Bepsc])
        S.op("pool", lambda e: e.memset(epsc[:, 2:3], GN_EPS), writes=[Bepsc])
        S.op("pool", lambda e: e.memset(epsc[:, 3:4], 0.0), writes=[Bepsc])
        for b_ in (Bidentb, Bmsk, Bbones, Bonesf, Bselt, Bresetm, BropePT, Bepsc, Bconvw, Bchanv, Bmlav):
            b_.const = True

        with ExitStack() as p0s:
            P = Ctx(nc, p0s); P.n = 100
            cf, Bcf = P.sb([128, 16, 2], F32, "cf")
            sc, Bsc = P.sb([128, 16, 2], BF16, "sc")
            b2, Bb2 = P.sb([2, 3 * D], F32, "b2")
            nw2, Bnw2 = P.sb([2, D], F32, "nw2")
            mrow, Bmrow = P.sb([2, 3 * D], F32, "mrow")
            grow, Bgrow = P.sb([2, D], F32, "grow")
            wm = [P.sb([128, 16, 512], BF16, "wm") for _ in range(2)]
            bct = [P.sb([128, D], F32, "bct") for _ in range(2)]
            pm, Bpm = P.ps([128, 512], F32, "pm")
            pb = [P.ps([128, 512], F32, "pb") for _ in range(2)]
            S.dma("sp", cf[:], I["cfm"], writes=[Bcf], semof=Bcf)
            S.dma("sp", b2[:], I["b_mod_row"], writes=[Bb2], semof=Bb2)
            S.dma("sp", nw2[:], I["norm_w_row"], writes=[Bnw2], semof=Bnw2)
            S.op("act", lambda e: e.activation(out=sc[:], in_=cf[:], func=AF.Silu), reads=[Bcf], writes=[Bsc])
            wmod_v = I["w_mod"].rearrange("(kc p) n -> p kc n", p=128)
            for cb in range(12):
                wt, Bwt = wm[cb % 2]
                S.dma("pool", wt[:], wmod_v[:, :, cb * 512:(cb + 1) * 512], writes=[Bwt], semof=Bwt)

                def mmf(e, wt=wt):
                    for kc in range(16):
                        ins = e.matmul(pm[0:2, :], sc[:, kc, :], wt[:, kc, :], start=(kc == 0), stop=(kc == 15))
                    return ins
                S.op("pe", mmf, reads=[Bsc, Bwt], writes=[Bpm])
                S.op("act", lambda e, cb=cb: e.activation(out=mrow[0:2, cb * 512:(cb + 1) * 512], in_=pm[0:2, :], func=AF.Copy),
                     reads=[Bpm], writes=[Bmrow])
            S.op("pool", lambda e: e.tensor_tensor(out=mrow[:], in0=mrow[:], in1=b2[:], op=ALU.add), reads=[Bmrow, Bb2], writes=[Bmrow])
            S.op("dve", lambda e: e.scalar_tensor_tensor(out=grow[:], in0=mrow[:, D:2 * D], scalar=1.0, in1=nw2[:],
                                                          op0=ALU.add, op1=ALU.mult), reads=[Bmrow, Bnw2], writes=[Bgrow])
            plan0 = [(0, grow, Bgrow, 0, 0), (1, mrow, Bmrow, 0, 0), (2, mrow, Bmrow, 2 * D, 0), (3, grow, Bgrow, 0, 1), (4, mrow, Bmrow, 0, 1)]
            k = 0
            for (idx, src, Bsrc, off, si) in plan0:
                st, Bst = bct[idx % 2]
                for blk in range(4):
                    pt_, Bpt_ = pb[k % 2]
                    k += 1
                    S.op("pe", lambda e, pt_=pt_, src=src, off=off, blk=blk, si=si: e.matmul(
                        pt_[:, :], selt[0:2, si, :], src[0:2, off + blk * 512: off + (blk + 1) * 512], start=True, stop=True),
                        reads=[Bsrc, Bselt], writes=[Bpt_])
                    S.op("act", lambda e, pt_=pt_, st=st, blk=blk: e.activation(out=st[:, blk * 512:(blk + 1) * 512], in_=pt_[:, :], func=AF.Copy),
                         reads=[Bpt_], writes=[Bst])
                S.dma("sp", BC[idx], st[:], reads=[Bst], semof=Bst)
            S.barrier()
            S.emit()
            S.end_phase()
        if upto == "0":
            return nc

        with ExitStack() as pas:
            P = Ctx(nc, pas); P.n = 200
            W, BW = P.sb([128, 16, WC], BF16, "W")
            wlora, Bwlora = P.sb([128, 2, 256], BF16, "wlora")
            wuq, Bwuq = P.sb([128, 4, 384], BF16, "wuq")
            wukv, Bwukv = P.sb([128, 2, 512], BF16, "wukv")
            gain_bc, Bgain = P.sb([128, D], F32, "gain_bc")
            shift_bc, Bshift = P.sb([128, D], F32, "shift_bc")
            xt = [P.sb([128, D], F32, "xt") for _ in range(2)]
            hm = [P.sb([128, D], BF16, "hm") for _ in range(2)]
            ss = [P.sb([128, 4], F32, "ss") for _ in range(2)]
            hmT = [P.sb([128, 16, GS], BF16, "hmT") for _ in range(2)]
            urkv = [P.sb([128, GS], F32, "urkv") for _ in range(2)]
            gz = [P.sb([128, GS], F32, "gz") for _ in range(2)]
            sga = [P.sb([128, GS], F32, "sga") for _ in range(2)]
            twd2 = [P.sb([128, GS], BF16, "twd") for _ in range(2)]
            adb2 = [P.sb([128, GS], BF16, "adb") for _ in range(2)]
            qd2 = [P.sb([128, 4, GS], F32, "qd") for _ in range(2)]
            sqq, Bsqq = P.sb([128, 4, GS], F32, "sqq")
            qn, Bqn = P.sb([128, 4, GS], BF16, "qn")
            rq, Brq = P.sb([128, GS], F32, "rq")
            qno, Bqno = P.sb([128, GS], F32, "qno")
            qro, Bqro = P.sb([64, GS], F32, "qro")
            sqh, Bsqh = P.sb([128, 2, GS], F32, "sqh")
            rh, Brh = P.sb([128, GS], F32, "rh")
            qnf, Bqnf = P.sb([128, GS], BF16, "qnf")
            qrg, Bqrg = P.sb([64, GS], F32, "qrg")
            t1, Bt1 = P.sb([64, GS], F32, "t1")
            t2, Bt2 = P.sb([64, GS], F32, "t2")
            qrf, Bqrf = P.sb([64, GS], BF16, "qrf")
            cost, Bcost = P.sb([64, GS], F32, "cost")
            sint, Bsint = P.sb([64, GS], F32, "sint")
            kvd2 = [P.sb([128, 2, GS], F32, "kvd") for _ in range(2)]
            kvn, Bkvn = P.sb([128, 2, GS], BF16, "kvn")
            kro2 = [P.sb([64, GS], F32, "kro") for _ in range(2)]
            vts, Bvts = P.sb([128, 2, GS], BF16, "vts")
            kst, Bkst = P.sb([128, 8], F32, "kst")
            pT = [P.ps([128, 1024], BF16, "pT") for _ in range(2)]
            po = [P.ps([128, 512], F32, "po") for _ in range(3)]
            pst, Bpst = P.ps([128, 512], F32, "pst")
            pv, Bpv = P.ps([128, 512], F32, "pv")
            pks, Bpks = P.ps([128, 512], F32, "pks")
            cnt = {"po": 0, "tile": 0, "u": 0, "g": 0, "s": 0}

            def next_po():
                cnt["po"] += 1
                return po[cnt["po"] % 3]

            win_v = I["w_in_core"].rearrange("(kc p) n -> p kc n", p=128)
            S.dma("pool", W[:, :, :], win_v[:, :, :], writes=[BW], semof=BW)
            S.dma("pool", wlora[:], I["wlora"], writes=[Bwlora], semof=Bwlora)
            S.dma("pool", wuq[:], I["w_uq_c"].rearrange("(kc p) n -> p kc n", p=128), writes=[Bwuq], semof=Bwuq)
            S.dma("pool", wukv[:], I["w_ukv_c"].rearrange("(kc p) n -> p kc n", p=128), writes=[Bwukv], semof=Bwukv)
            S.dma("sp", gain_bc[:], BC[3], writes=[Bgain], semof=Bgain)
            S.dma("sp", shift_bc[:], BC[4], writes=[Bshift], semof=Bshift)

            def prep_a(row0):
                s = cnt["tile"] % 2
                cnt["tile"] += 1
                x_, Bx_ = xt[s]
                h_, Bh_ = hm[s]
                s_, Bs_ = ss[s]
                S.dma("sp", x_[:], I["xs"][row0:row0 + 128, :], writes=[Bx_], semof=Bx_)
                S.op("act", lambda e: e.activation(out=h_[:], in_=x_[:], func=AF.Square, accum_out=s_[:, 0:1]), reads=[Bx_], writes=[Bh_, Bs_])
                S.op("act", lambda e: e.activation(out=s_[:, 1:2], in_=s_[:, 0:1], func=AF.Sqrt, scale=1.0 / D, bias=epsc[:, 0:1]),
                     reads=[Bs_, Bepsc], writes=[Bs_])
                S.op("dve", lambda e: e.reciprocal(out=s_[:, 2:3], in_=s_[:, 1:2]), reads=[Bs_], writes=[Bs_])
                S.op("dve", lambda e: e.scalar_tensor_tensor(out=x_[:], in0=x_[:], scalar=s_[:, 2:3], in1=gain_bc[:], op0=ALU.mult, op1=ALU.mult),
                     reads=[Bx_, Bs_, Bgain], writes=[Bx_])
                S.op("pool", lambda e: e.tensor_tensor(out=h_[:], in0=x_[:], in1=shift_bc[:], op=ALU.add), reads=[Bx_, Bshift], writes=[Bh_])
                return s

            def prep_b(s, hslot, t):
                h_, Bh_ = hm[s]
                hT, BhT = hmT[hslot]
                for half in range(2):
                    p_, Bp_ = pT[half]

                    def trf(e, half=half, p_=p_):
                        for j in range(8):
                            kc = half * 8 + j
                            ins = e.transpose(p_[:, j * 128:(j + 1) * 128], h_[:, kc * 128:(kc + 1) * 128], identb[:])
                        return ins
                    S.op("pe", trf, reads=[Bh_, Bidentb], writes=[Bp_])
                    cp = (lambda e, half=half, p_=p_: e.activation(out=hT[:, half * 8:(half + 1) * 8, t * 128:(t + 1) * 128],
                                                                   in_=p_[:, :].rearrange("p (j n) -> p j n", n=128), func=AF.Copy)) if half == 0 else \
                         (lambda e, half=half, p_=p_: e.tensor_copy(out=hT[:, half * 8:(half + 1) * 8, t * 128:(t + 1) * 128],
                                                                    in_=p_[:, :].rearrange("p (j n) -> p j n", n=128)))
                    S.op("act" if half == 0 else "dve", cp, reads=[Bp_], writes=[BhT])

            def prep_tile(row0, hslot, t):
                prep_b(prep_a(row0), hslot, t)

            def rstd_from(psum_ap, Bps, out_t, Bout, npart, N, inv_n):
                S.op("act", lambda e: e.activation(out=out_t[0:npart, 0:N], in_=psum_ap, func=AF.Sqrt, scale=inv_n, bias=epsc[0:npart, 0:1]),
                     reads=[Bps, Bepsc], writes=[Bout])
                S.op("dve", lambda e: e.reciprocal(out=out_t[0:npart, 0:N], in_=out_t[0:npart, 0:N]), reads=[Bout], writes=[Bout])

            def rope_apply(src, Bsrc, N, dst_dram):
                pr, Bpr = next_po()
                S.op("pe", lambda e: e.matmul(pr[0:64, 0:N], ropePT[0:64, 0:64], src[0:64, 0:N], start=True, stop=True),
                     reads=[Bsrc, BropePT], writes=[Bpr])
                S.op("dve", lambda e: e.tensor_tensor(out=t1[0:64, 0:N], in0=src[0:64, 0:N], in1=cost[0:64, 0:N], op=ALU.mult),
                     reads=[Bsrc, Bcost], writes=[Bt1])
                S.op("dve", lambda e: e.tensor_tensor(out=t2[0:64, 0:N], in0=pr[0:64, 0:N], in1=sint[0:64, 0:N], op=ALU.mult),
                     reads=[Bpr, Bsint], writes=[Bt2])
                S.op("pool", lambda e: e.tensor_tensor(out=qrf[0:64, 0:N], in0=t1[0:64, 0:N], in1=t2[0:64, 0:N], op=ALU.add),
                     reads=[Bt1, Bt2], writes=[Bqrf])
                S.dma("sp", dst_dram, qrf[0:64, 0:N], reads=[Bqrf], semof=Bqrf)

            def grp_info(gi):
                isx = gi > 0
                N = GS
                n0 = 256 + (gi - 1) * GS if isx else 0
                xo = (gi - 1) * GS
                return isx, N, n0, xo

            def dense_gen(gi, hslot):
                isx, N, n0, xo = grp_info(gi)
                hT, BhT = hmT[hslot]
                s2 = gi % 2
                twd, Btwd = twd2[s2]
                adb, Badb = adb2[s2]
                qd, Bqd = qd2[s2]
                kvd, Bkvd = kvd2[s2]
                kro, Bkro = kro2[s2]

                def mm_chunk(col0, M):
                    p_, Bp_ = next_po()

                    def f(e):
                        for kc in range(16):
                            ins = e.matmul(p_[0:M, 0:N], W[:, kc, col0:col0 + M], hT[:, kc, 0:N], start=(kc == 0), stop=(kc == 15))
                        return ins
                    S.op("pe", f, reads=[BW, BhT], writes=[Bp_])
                    return p_, Bp_

                for j in range(6):
                    p_, Bp_ = mm_chunk(j * 128, 128)
                    u_, Bu_ = urkv[cnt["u"] % 2]
                    cnt["u"] += 1
                    S.op("act", lambda e, p_=p_, u_=u_: e.activation(out=u_[:, 0:N], in_=p_[:, 0:N], func=AF.Copy), reads=[Bp_], writes=[Bu_])
                    S.dma("sp", U_v[:, j, n0:n0 + N], u_[:, 0:N], reads=[Bu_], semof=Bu_)
                    yield
                p_, Bp_ = mm_chunk(1024, 128)
                S.op("act", lambda e, p_=p_: e.activation(out=twd[:, 0:N], in_=p_[:, 0:N], func=AF.Tanh), reads=[Bp_], writes=[Btwd])
                yield
                p_, Bp_ = mm_chunk(1152, 128)
                S.op("act", lambda e, p_=p_: e.activation(out=adb[:, 0:N], in_=p_[:, 0:N], func=AF.Copy), reads=[Bp_], writes=[Badb])
                yield
                if isx:
                    for j in range(4):
                        p_, Bp_ = mm_chunk(1280 + j * 128, 128)
                        S.op("act", lambda e, p_=p_, j=j: e.activation(out=qd[:, j, 0:N], in_=p_[:, 0:N], func=AF.Copy), reads=[Bp_], writes=[Bqd])
                        yield
                for j in range(2):
                    p_, Bp_ = mm_chunk(1792 + j * 128, 128)
                    S.op("act", lambda e, p_=p_, j=j: e.activation(out=kvd[:, j, 0:N], in_=p_[:, 0:N], func=AF.Copy), reads=[Bp_], writes=[Bkvd])
                    yield
                p_, Bp_ = mm_chunk(2048, 64)
                S.op("act", lambda e, p_=p_: e.activation(out=kro[0:64, 0:N], in_=p_[0:64, 0:N], func=AF.Copy), reads=[Bp_], writes=[Bkro])
                yield
                if isx:
                    for j in range(2):
                        p_, Bp_ = mm_chunk(768 + j * 128, 128)
                        g_, Bg_ = gz[cnt["g"] % 2]
                        cnt["g"] += 1
                        S.op("act", lambda e, p_=p_, g_=g_: e.activation(out=g_[:, 0:N], in_=p_[:, 0:N], func=AF.Silu), reads=[Bp_], writes=[Bg_])
                        S.dma("sp", Gzr_v[:, j, xo:xo + N], g_[:, 0:N], reads=[Bg_], semof=Bg_)
                        yield
                    for j in range(2):
                        p_, Bp_ = mm_chunk(2112 + j * 128, 128)
                        g_, Bg_ = gz[cnt["g"] % 2]
                        cnt["g"] += 1
                        S.op("act", lambda e, p_=p_, g_=g_: e.activation(out=g_[:, 0:N], in_=p_[:, 0:N], func=AF.Silu), reads=[Bp_], writes=[Bg_])
                        S.dma("sp", Gzm_v[:, j, xo:xo + N], g_[:, 0:N], reads=[Bg_], semof=Bg_)
                        yield

            def chain_gen(gi):
                isx, N, n0, xo = grp_info(gi)
                ntile = N // 128
                s2 = gi % 2
                twd, Btwd = twd2[s2]
                adb, Badb = adb2[s2]
                qd, Bqd = qd2[s2]
                kvd, Bkvd = kvd2[s2]
                kro, Bkro = kro2[s2]
                if isx:
                    S.dma("sp", cost[:, 0:N], I["rope_cos"][:, xo:xo + N], writes=[Bcost], semof=Bcost)
                    S.dma("sp", sint[:, 0:N], I["rope_sin"][:, xo:xo + N], writes=[Bsint], semof=Bsint)
                for which, (src, Bsrc, dst_v, cbase) in enumerate(((twd, Btwd, SG_v, 0), (adb, Badb, AA_v, 2))):
                    for d in range(2):
                        for cc in range(2):
                            p_, Bp_ = next_po()
                            S.op("pe", lambda e, p_=p_, src=src, d=d, cc=cc, which=which: e.matmul(
                                p_[:, 0:N], wlora[64 * d:64 * d + 64, which, cc * 128:(cc + 1) * 128], src[64 * d:64 * d + 64, 0:N],
                                start=True, stop=True), reads=[Bwlora, Bsrc], writes=[Bp_])
                            s_, Bs_ = sga[cnt["s"] % 2]
                            cnt["s"] += 1
                            S.op("act", lambda e, p_=p_, s_=s_, d=d, cc=cc, cbase=cbase: e.activation(
                                out=s_[:, 0:N], in_=p_[:, 0:N], func=AF.Sigmoid, bias=chanv[:, cc, cbase + d:cbase + d + 1]),
                                reads=[Bp_, Bchanv], writes=[Bs_])
                            S.dma("sp", dst_v[:, d * 2 + cc, n0:n0 + N], s_[:, 0:N], reads=[Bs_], semof=Bs_)
                            yield
                if isx:
                    S.op("pool", lambda e: e.tensor_tensor(out=sqq[:, :, 0:N], in0=qd[:, :, 0:N], in1=qd[:, :, 0:N], op=ALU.mult), reads=[Bqd], writes=[Bsqq])
                    yield

                    def ssq(e):
                        for j in range(4):
                            ins = e.matmul(pst[:, 0:N], onesf[:, :], sqq[:, j, 0:N], start=(j == 0), stop=(j == 3))
                        return ins
                    S.op("pe", ssq, reads=[Bsqq, Bonesf], writes=[Bpst])
                    S.op("act", lambda e: e.activation(out=rq[:, 0:N], in_=pst[:, 0:N], func=AF.Sqrt, scale=1.0 / 512, bias=epsc[:, 0:1]),
                         reads=[Bpst, Bepsc], writes=[Brq])
                    yield
                    S.op("dve", lambda e: e.reciprocal(out=rq[:, 0:N], in_=rq[:, 0:N]), reads=[Brq], writes=[Brq])
                    yield
                    for j in range(4):
                        S.op("dve", lambda e, j=j: e.scalar_tensor_tensor(out=qn[:, j, 0:N], in0=qd[:, j, 0:N], scalar=mlav[:, j:j + 1], in1=rq[:, 0:N],
                                                                          op0=ALU.mult, op1=ALU.mult), reads=[Bqd, Bmlav, Brq], writes=[Bqn])
                    yield
                    for h in range(2):
                        p1, Bp1 = next_po()
                        p2, Bp2 = next_po()

                        def qup(e, h=h, p1=p1, p2=p2):
                            for kc in range(4):
                                e.matmul(p1[:, 0:N], wuq[:, kc, h * 192:h * 192 + 128], qn[:, kc, 0:N], start=(kc == 0), stop=(kc == 3))
                            for kc in range(4):
                                ins = e.matmul(p2[0:64, 0:N], wuq[:, kc, h * 192 + 128:h * 192 + 192], qn[:, kc, 0:N], start=(kc == 0), stop=(kc == 3))
                            return ins
                        S.op("pe", qup, reads=[Bwuq, Bqn], writes=[Bp1, Bp2])
                        S.op("act", lambda e, p1=p1: e.activation(out=qno[:, 0:N], in_=p1[:, 0:N], func=AF.Copy), reads=[Bp1], writes=[Bqno])
                        S.op("act", lambda e, p2=p2: e.activation(out=qro[0:64, 0:N], in_=p2[0:64, 0:N], func=AF.Copy), reads=[Bp2], writes=[Bqro])
                        yield
                        S.op("pool", lambda e: e.tensor_tensor(out=sqh[:, 0, 0:N], in0=qno[:, 0:N], in1=qno[:, 0:N], op=ALU.mult), reads=[Bqno], writes=[Bsqh])
                        S.op("pool", lambda e: e.tensor_tensor(out=sqh[0:64, 1, 0:N], in0=qro[0:64, 0:N], in1=qro[0:64, 0:N], op=ALU.mult), reads=[Bqro], writes=[Bsqh])
                        yield

                        def ssh(e):
                            e.matmul(pst[:, 0:N], onesf[:, :], sqh[:, 0, 0:N], start=True, stop=False)
                            return e.matmul(pst[:, 0:N], onesf[0:64, :], sqh[0:64, 1, 0:N], start=False, stop=True)
                        S.op("pe", ssh, reads=[Bsqh, Bonesf], writes=[Bpst])
                        S.op("act", lambda e: e.activation(out=rh[:, 0:N], in_=pst[:, 0:N], func=AF.Sqrt, scale=1.0 / 192, bias=epsc[:, 0:1]),
                             reads=[Bpst, Bepsc], writes=[Brh])
                        yield
                        S.op("dve", lambda e: e.reciprocal(out=rh[:, 0:N], in_=rh[:, 0:N]), reads=[Brh], writes=[Brh])
                        yield
                        S.op("dve", lambda e: e.scalar_tensor_tensor(out=qnf[:, 0:N], in0=qno[:, 0:N], scalar=mlav[:, 6:7], in1=rh[:, 0:N],
                                                                      op0=ALU.mult, op1=ALU.mult), reads=[Bqno, Bmlav, Brh], writes=[Bqnf])
                        S.dma("sp", QN_v[:, h, xo:xo + N], qnf[:, 0:N], reads=[Bqnf], semof=Bqnf)
                        S.op("dve", lambda e: e.scalar_tensor_tensor(out=qrg[0:64, 0:N], in0=qro[0:64, 0:N], scalar=mlav[0:64, 8:9], in1=rh[0:64, 0:N],
                                                                      op0=ALU.mult, op1=ALU.mult), reads=[Bqro, Bmlav, Brh], writes=[Bqrg])
                        yield
                        rope_apply(qrg, Bqrg, N, QR_v[:, h, xo:xo + N])
                        yield
                S.op("pool", lambda e: e.tensor_tensor(out=sqq[:, 0:2, 0:N], in0=kvd[:, :, 0:N], in1=kvd[:, :, 0:N], op=ALU.mult), reads=[Bkvd], writes=[Bsqq])
                yield

                def sskv(e):
                    for j in range(2):
                        ins = e.matmul(pst[:, 0:N], onesf[:, :], sqq[:, j, 0:N], start=(j == 0), stop=(j == 1))
                    return ins
                S.op("pe", sskv, reads=[Bsqq, Bonesf], writes=[Bpst])
                S.op("act", lambda e: e.activation(out=rq[:, 0:N], in_=pst[:, 0:N], func=AF.Sqrt, scale=1.0 / 256, bias=epsc[:, 0:1]),
                     reads=[Bpst, Bepsc], writes=[Brq])
                yield
                S.op("dve", lambda e: e.reciprocal(out=rq[:, 0:N], in_=rq[:, 0:N]), reads=[Brq], writes=[Brq])
                yield
                for j in range(2):
                    S.op("dve", lambda e, j=j: e.scalar_tensor_tensor(out=kvn[:, j, 0:N], in0=kvd[:, j, 0:N], scalar=mlav[:, 4 + j:5 + j], in1=rq[:, 0:N],
                                                                      op0=ALU.mult, op1=ALU.mult), reads=[Bkvd, Bmlav, Brq], writes=[Bkvn])
                S.op("pool", lambda e: e.tensor_tensor(out=sqh[0:64, 1, 0:N], in0=kro[0:64, 0:N], in1=kro[0:64, 0:N], op=ALU.mult), reads=[Bkro], writes=[Bsqh])
                yield
                for h in range(2):
                    p1, Bp1 = next_po()

                    def kup(e, h=h, p1=p1):
                        for kc in range(2):
                            ins = e.matmul(p1[:, 0:N], wukv[:, kc, h * 256:h * 256 + 128], kvn[:, kc, 0:N], start=(kc == 0), stop=(kc == 1))
                        return ins
                    S.op("pe", kup, reads=[Bwukv, Bkvn], writes=[Bp1])
                    S.op("act", lambda e, p1=p1: e.activation(out=qno[:, 0:N], in_=p1[:, 0:N], func=AF.Copy), reads=[Bp1], writes=[Bqno])

                    def vup(e, h=h):
                        for t in range(ntile):
                            for kc in range(2):
                                ins = e.matmul(pv[:, t * 128:(t + 1) * 128], kvn[:, kc, t * 128:(t + 1) * 128], wukv[:, kc, h * 256 + 128:h * 256 + 256],
                                               start=(kc == 0), stop=(kc == 1))
                        return ins
                    S.op("pe", vup, reads=[Bwukv, Bkvn], writes=[Bpv])
                    S.op("act", lambda e, h=h: e.activation(out=vts[:, h, 0:N], in_=pv[:, 0:N], func=AF.Copy), reads=[Bpv], writes=[Bvts])
                    S.dma("sp", VT[h, :, n0:n0 + N], vts[:, h, 0:N], reads=[Bvts], semof=Bvts)
                    yield
                    S.op("pool", lambda e: e.tensor_tensor(out=sqh[:, 0, 0:N], in0=qno[:, 0:N], in1=qno[:, 0:N], op=ALU.mult), reads=[Bqno], writes=[Bsqh])
                    S.op("dve", lambda e: e.tensor_scalar(out=qnf[:, 0:N], in0=qno[:, 0:N], scalar1=mlav[:, 7:8], scalar2=None, op0=ALU.mult),
                         reads=[Bqno, Bmlav], writes=[Bqnf])
                    S.dma("sp", KN_v[:, h, n0:n0 + N], qnf[:, 0:N], reads=[Bqnf], semof=Bqnf)
                    yield

                    def kss(e, h=h):
                        for t in range(ntile):
                            c = t * 2 + h
                            e.matmul(pks[:, c:c + 1], sqh[:, 0, t * 128:(t + 1) * 128], onesf[:, 0:1], start=True, stop=False)
                            ins = e.matmul(pks[:, c:c + 1], sqh[0:64, 1, t * 128:(t + 1) * 128], onesf[0:64, 0:1], start=False, stop=True)
                        return ins
                    S.op("pe", kss, reads=[Bsqh, Bonesf], writes=[Bpks])
                    yield
                nk = ntile * 2
                t0 = (n0 // 128) * 2
                S.op("act", lambda e: e.activation(out=kst[:, 0:nk], in_=pks[:, 0:nk], func=AF.Sqrt, scale=1.0 / 192, bias=epsc[:, 0:1]),
                     reads=[Bpks, Bepsc], writes=[Bkst])
                S.op("dve", lambda e: e.tensor_scalar(out=qrg[0:64, 0:N], in0=kro[0:64, 0:N], scalar1=mlav[0:64, 9:10], scalar2=None, op0=ALU.mult),
                     reads=[Bkro, Bmlav], writes=[Bqrg])
                yield
                S.op("dve", lambda e: e.reciprocal(out=kst[:, 0:nk], in_=kst[:, 0:nk]), reads=[Bkst], writes=[Bkst])
                yield
                S.op("dve", lambda e: e.tensor_scalar(out=KS[:, t0:t0 + nk], in0=kst[:, 0:nk], scalar1=float(192 ** -0.5), scalar2=None, op0=ALU.mult),
                     reads=[Bkst], writes=[BKS])
                if isx:
                    rope_apply(qrg, Bqrg, N, KR[:, n0:n0 + N])
                else:
                    S.op("pool", lambda e: e.tensor_copy(out=qrf[0:64, 0:N], in_=qrg[0:64, 0:N]), reads=[Bqrg], writes=[Bqrf])
                    S.dma("sp", KR[:, n0:n0 + N], qrf[0:64, 0:N], reads=[Bqrf], semof=Bqrf)
                yield

            BWB = Buf("WB")
            wmg_v_ = I["w_in_mg"].rearrange("(kc p) n -> p kc n", p=128)
            wbr_r_v_ = I["w_br_r"].rearrange("(kc p) n -> p kc n", p=128)
            wbr_m_v_ = I["w_br_m"].rearrange("(kc p) n -> p kc n", p=128)
            wout_v_ = I["w_out"].rearrange("(kc p) n -> p kc n", p=128)
            castq = []
            for m in range(16):
                d_ = WMG_b[m].rearrange("p (k n) -> p k n", n=256)
                castq.append((d_[:, :, 0:128], wmg_v_[:, :, m * 128:(m + 1) * 128]))
                castq.append((d_[:, :, 128:256], wmg_v_[:, :, 2048 + m * 128:2048 + (m + 1) * 128]))
                d_ = WBR_b[m].rearrange("p (k n) -> p k n", n=256)
                castq.append((d_[:, :, 0:128], wbr_r_v_[:, :, m * 128:(m + 1) * 128]))
                castq.append((d_[:, :, 128:256], wbr_m_v_[:, :, m * 128:(m + 1) * 128]))
            for nb in range(4):
                castq.append((WOUT_b[nb].rearrange("p (k n) -> p k n", n=512), wout_v_[:, :, nb * 512:(nb + 1) * 512]))
            NGRP = 1 + NX // GS
            NGRP = int(os.environ.get("MK_NGRP", NGRP))
            for t in range(GS // 128):
                prep_tile(t * 128, 0, t)
            S.dma("sp", gain_bc[:], BC[0], writes=[Bgain], reads=[], semof=Bgain)
            S.dma("sp", shift_bc[:], BC[1], writes=[Bshift], reads=[], semof=Bshift)
            def gnext(g_):
                try:
                    next(g_)
                    return True
                except StopIteration:
                    return False

            chain = None
            for gi in range(NGRP):
                dense = dense_gen(gi, gi % 2)
                preps = []
                if gi + 1 < NGRP:
                    r0 = 256 + gi * GS
                    slots_ = {}
                    preps = [(lambda t=t, r0=r0: slots_.__setitem__(t, prep_a(r0 + t * 128))) for t in range(GS // 128)]
                    late = [(lambda t=t, hs=(gi + 1) % 2: prep_b(slots_[t], hs, t)) for t in range(GS // 128)]
                else:
                    late = []
                rnd = 0
                dalive = True
                while dalive or chain is not None:
                    if dalive:
                        dalive = gnext(dense)
                    for _ in range((3 if dalive else 1000) if not os.environ.get('MK_NOINTER') else (0 if dalive else 1000)):
                        if chain is None:
                            break
                        if not gnext(chain):
                            chain = None
                    rnd += 1
                    if preps and (rnd % 3 == 1 or not dalive):
                        preps.pop(0)()
                while preps:
                    preps.pop(0)()
                while late:
                    late.pop(0)()
                chain = chain_gen(gi)
                for _ in range(3):
                    if castq:
                        o_, i_ = castq.pop(0)
                        S.dma("pool", o_, i_, semof=BWB)
            while chain is not None:
                if not gnext(chain):
                    chain = None
            while castq:
                o_, i_ = castq.pop(0)
                S.dma("pool", o_, i_, semof=BWB)
            if "KSD" in debug:
                S.dma("sp", KSD, KS[:], reads=[BKS], semof=BKS)
            S.barrier()
            S.emit()
            S.end_phase(recycle_hw=False)
        if upto == "A":
            return nc

        GR = 256
        NCH = GR // 64
        NXG = NX // GR
        U_k = U_rkv.rearrange("(k c p) n -> p k c n", k=3, c=2, p=128)
        YD_v = [v3(YD[d]) for d in range(2)]
        BD_v = [v3(BD[d]) for d in range(2)]
        with ExitStack() as prs:
            P = Ctx(nc, prs); P.n = 300
            omka, Bomka = P.sb([128, 2], F32, "omka")
            S.op("dve", lambda e: e.tensor_scalar(out=omka[:, :], in0=chanv[:, :, 5], scalar1=-1.0, scalar2=1.0, op0=ALU.mult, op1=ALU.add),
                 reads=[Bchanv], writes=[Bomka])

            class CP:
                pass
            cps = []
            for cc in range(2):
                for d in range(2):
                    c_ = CP()
                    c_.cc, c_.d = cc, d
                    for nm, shp, dt in (("ub", [128, 3, GR + 2], F32), ("cv", [128, 3, GR], F32), ("sgt", [128, GR], F32), ("aat", [128, GR], F32),
                                        ("sq", [128, GR], F32), ("rs", [128, GR], F32), ("kk", [128, GR], F32), ("ff", [128, GR], F32),
                                        ("kmod", [128, GR], F32), ("akk", [128, GR], F32), ("Pc", [128, GR], F32), ("Ei", [128, GR], F32),
                                        ("Ee", [128, GR], F32), ("g", [128, GR], F32), ("gp", [128, GR], F32), ("gi", [128, GR], F32),
                                        ("NA", [128, 256], BF16),
                                        ("KA", [128, 256], BF16), ("A0", [128, 128], BF16), ("PW0", [128, 256], BF16), ("PW1", [128, 256], BF16),
                                        ("Tm0", [128, 128], BF16), ("Tm1", [128, 128], BF16), ("TR", [128, 384], BF16), ("Xb", [128, 128], BF16),
                                        ("Ub", [128, 128], BF16), ("H", [128, 128], F32), ("Hb", [128, 128], BF16), ("S1", [128, 128], F32),
                                        ("pr", [128, GR], F32), ("bon", [128, GR], F32)):
                        t_, b_ = P.sb(shp, dt, nm)
                        setattr(c_, nm, t_)
                        setattr(c_, "B" + nm, b_)
                    for nm, shp, dt in (("gtot", [128, NCH], F32), ("AR", [128, NCH, 256], BF16), ("BE", [128, NCH, 128], BF16),
                                        ("KT", [128, NCH, 128], BF16), ("VB", [128, NCH, 128], BF16), ("Yg", [128, GR], F32)):
                        lst = [P.sb(shp, dt, nm) for _ in range(2)]
                        setattr(c_, nm, [x[0] for x in lst])
                        setattr(c_, "B" + nm, [x[1] for x in lst])
                    c_.Bsgt = c_.Bub
                    c_.Baat = c_.Bub
                    c_.bk1, c_.Bbk1 = P.ps([128, 512], F32, "bk1")
                    c_.bk2, c_.BpAD = P.ps([128, 512], F32, "bk2")
                    c_.BpTT = c_.BpAD
                    for nm in ("H", "Hb"):
                        t_ = getattr(c_, nm)
                        b_ = getattr(c_, "B" + nm)
                        S.op("pool", lambda e, t_=t_: e.memset(t_[:], 0.0), writes=[b_])
                    for nm in ("AR", "BE", "KT", "VB"):
                        for sl_ in range(2):
                            t_ = getattr(c_, nm)[sl_]
                            b_ = getattr(c_, "B" + nm)[sl_]
                            S.op("pool", lambda e, t_=t_: e.memset(t_[:], 0.0), writes=[b_])
                    cps.append(c_)

            if os.environ.get("MK_WARM"):
                def warm(e):
                    for _ in range(400):
                        ins = e.matmul(cps[0].bk1[:, :], msk[:, 0, :], resetb[:, :], start=True, stop=True)
                    return ins
                resetb, Bresetb = P.sb([128, 512], BF16, "resetb")
                S.op("pool", lambda e: e.memset(resetb[:], 1.0), writes=[Bresetb])
                S.op("pe", warm, reads=[Bresetb, Bmsk], writes=[cps[0].Bbk1])
            c3 = lambda ap: ap.rearrange("p (c t) -> p c t", t=64)

            def prep_gen(c, sl, n0, N, s0, s1, xo):
                cc, d = c.cc, c.d
                AR, BAR = c.AR[sl], c.BAR[sl]
                BE, BBE = c.BE[sl], c.BBE[sl]
                KT, BKT = c.KT[sl], c.BKT[sl]
                VB, BVB = c.VB[sl], c.BVB[sl]
                gtot, Bgtot = c.gtot[sl], c.Bgtot[sl]
                lo, hi = n0 - 1, n0 + N + 1
                dl, dh = 0, N + 2
                if n0 == s0:
                    S.op("pool", lambda e: e.memset(c.ub[:, :, 0:1], 0.0), writes=[c.Bub])
                    lo, dl = n0, 1
                if n0 + N == s1:
                    S.op("pool", lambda e: e.memset(c.ub[:, :, N + 1:N + 2], 0.0), writes=[c.Bub])
                    hi, dh = n0 + N, N + 1
                S.dma_group("sp", [(c.ub[:, :, dl:dh], U_k[:, :, cc, lo:hi]),
                                   (c.sgt[:, 0:N], SG_v[:, d * 2 + cc, n0:n0 + N]),
                                   (c.aat[:, 0:N], AA_v[:, d * 2 + cc, n0:n0 + N])], writes=[c.Bub], semof=c.Bub)
                yield
                for kind in range(3):
                    ch = kind * 2 + cc
                    S.op("act", lambda e, kind=kind, ch=ch: e.activation(out=c.cv[:, kind, 0:N], in_=c.ub[:, kind, 1:N + 1], func=AF.Copy,
                                                                         scale=convw[:, ch, 1:2]), reads=[c.Bub, Bconvw], writes=[c.Bcv])
                S.op("dve", lambda e: e.tensor_tensor_scan(out=c.Pc[:, 0:N], data0=resetm[:, 0:N], data1=c.sgt[:, 0:N], initial=0.0, op0=ALU.mult, op1=ALU.add),
                     reads=[Bresetm, c.Bsgt], writes=[c.BPc])
                yield
                for tap in (0, 2):
                    for kind in range(3):
                        ch = kind * 2 + cc
                        S.op("dve", lambda e, kind=kind, ch=ch, tap=tap: e.scalar_tensor_tensor(
                            out=c.cv[:, kind, 0:N], in0=c.ub[:, kind, tap:tap + N], scalar=convw[:, ch, tap:tap + 1],
                            in1=c.cv[:, kind, 0:N], op0=ALU.mult, op1=ALU.add), reads=[c.Bub, Bconvw, c.Bcv], writes=[c.Bcv])
                    yield
                nch = N // 64
                tot = c3(c.Pc[:, 0:N])[:, :, 63]
                if d == 0:
                    S.op("pool", lambda e: e.tensor_tensor(out=c.Ee[:, 0:N], in0=c.Pc[:, 0:N], in1=c.sgt[:, 0:N], op=ALU.subtract), reads=[c.BPc, c.Bsgt], writes=[c.BEe])
                    Ei, BEi = c.Pc, c.BPc
                else:
                    for k_ in range(nch):
                        S.op("pool", lambda e, k_=k_: e.tensor_scalar(out=c.Ee[:, k_ * 64:(k_ + 1) * 64], in0=c.Pc[:, k_ * 64:(k_ + 1) * 64], scalar1=-1.0,
                                                                      scalar2=c.Pc[:, k_ * 64 + 63:k_ * 64 + 64], op0=ALU.mult, op1=ALU.add),
                             reads=[c.BPc], writes=[c.BEe])
                    S.op("pool", lambda e: e.tensor_tensor(out=c.Ei[:, 0:N], in0=c.Ee[:, 0:N], in1=c.sgt[:, 0:N], op=ALU.add), reads=[c.BEe, c.Bsgt], writes=[c.BEi])
                    Ei, BEi = c.Ei, c.BEi
                S.op("act", lambda e: e.activation(out=c.sq[:, 0:N], in_=c.cv[:, 1, 0:N], func=AF.Square, scale=chanv[:, cc, 4:5]),
                     reads=[c.Bcv, Bchanv], writes=[c.Bsq])
                yield
                st_ = c.bk1[:, 0:N]
                Bst_ = c.Bbk1
                S.op("pe", lambda e: e.matmul(st_, bones[:, :], c.sq[:, 0:N], start=True, stop=True), reads=[c.Bsq, Bbones], writes=[Bst_])
                S.op("act", lambda e: e.activation(out=c.rs[:, 0:N], in_=st_, func=AF.Sqrt, bias=epsc[:, 1:2], scale=1.0), reads=[Bepsc], writes=[c.Brs, Bst_])
                yield
                S.op("act", lambda e: e.activation(out=c.g[:, 0:N], in_=Ei[:, 0:N], func=AF.Exp, scale=-C0), reads=[BEi], writes=[c.Bg])
                S.op("act", lambda e: e.activation(out=c.gp[:, 0:N], in_=c.Ee[:, 0:N], func=AF.Exp, scale=-C0), reads=[c.BEe], writes=[c.Bgp])
                S.op("act", lambda e: e.activation(out=c.gi[:, 0:N], in_=Ei[:, 0:N], func=AF.Exp, scale=C0), reads=[BEi], writes=[c.Bgi])
                S.op("act", lambda e: e.activation(out=gtot[:, 0:nch], in_=tot, func=AF.Exp, scale=-C0), reads=[c.BPc], writes=[Bgtot])
                S.op("pool", lambda e: e.tensor_scalar(out=c.ff[:, 0:N], in0=c.aat[:, 0:N], scalar1=chanv[:, cc, 5:6], scalar2=omka[:, cc:cc + 1],
                                                        op0=ALU.mult, op1=ALU.add), reads=[c.Baat, Bchanv, Bomka], writes=[c.Bff])
                S.op("pool", lambda e: e.tensor_tensor(out=c.kmod[:, 0:N], in0=c.cv[:, 1, 0:N], in1=c.ff[:, 0:N], op=ALU.mult), reads=[c.Bcv, c.Bff], writes=[c.Bkmod])
                S.op("dve", lambda e: e.reciprocal(out=c.rs[:, 0:N], in_=c.rs[:, 0:N]), reads=[c.Brs], writes=[c.Brs])
                yield
                S.op("dve", lambda e: e.scalar_tensor_tensor(out=c.pr[:, 0:N], in0=c.cv[:, 0, 0:N], scalar=chanv[:, cc, 8:9], in1=c.kmod[:, 0:N],
                                                              op0=ALU.mult, op1=ALU.mult), reads=[c.Bcv, Bchanv, c.Bkmod], writes=[c.Bpr])
                S.op("dve", lambda e: e.scalar_tensor_tensor(out=c.kk[:, 0:N], in0=c.cv[:, 1, 0:N], scalar=chanv[:, cc, 4:5], in1=c.rs[:, 0:N],
                                                              op0=ALU.mult, op1=ALU.mult), reads=[c.Bcv, Bchanv, c.Brs], writes=[c.Bkk])
                yield
                S.op("pe", lambda e: e.matmul(st_, bones[:, :], c.pr[:, 0:N], start=True, stop=True), reads=[c.Bpr, Bbones], writes=[Bst_])
                S.op("dve", lambda e: e.tensor_tensor(out=c.bon[:, 0:N], in0=st_, in1=c.cv[:, 2, 0:N], op=ALU.mult), reads=[c.Bcv], writes=[c.Bbon, Bst_])
                if xo is not None:
                    S.dma("sp", BD_v[d][:, cc, xo:xo + N], c.bon[:, 0:N], reads=[c.Bbon], semof=c.Bbon)
                yield
                S.op("pool", lambda e: e.tensor_tensor(out=c.akk[:, 0:N], in0=c.aat[:, 0:N], in1=c.kk[:, 0:N], op=ALU.mult), reads=[c.Baat, c.Bkk], writes=[c.Bakk])
                for hh in range(2):
                    ps_ = slice(64 * hh, 64 * hh + 64)
                    o1 = slice(64 * hh, 64 * hh + 64)
                    o2 = slice(128 + 64 * hh, 128 + 64 * hh + 64)
                    S.op("dve", lambda e, ps_=ps_, o2=o2: e.tensor_tensor(out=AR[ps_, 0:nch, o2], in0=c3(c.cv[ps_, 0, 0:N]), in1=c3(c.g[ps_, 0:N]), op=ALU.mult),
                         reads=[c.Bcv, c.Bg], writes=[BAR])
                    S.op("dve", lambda e, ps_=ps_, o1=o1: e.scalar_tensor_tensor(out=AR[ps_, 0:nch, o1], in0=c3(c.kk[ps_, 0:N]), scalar=-1.0, in1=c3(c.gp[ps_, 0:N]),
                                                                                 op0=ALU.mult, op1=ALU.mult), reads=[c.Bkk, c.Bgp], writes=[BAR])
                    S.op("pool", lambda e, ps_=ps_, o1=o1: e.tensor_tensor(out=KT[ps_, 0:nch, o1], in0=c3(c.kmod[ps_, 0:N]), in1=c3(c.gi[ps_, 0:N]), op=ALU.mult),
                         reads=[c.Bkmod, c.Bgi], writes=[BKT])
                    S.op("pool", lambda e, ps_=ps_, o1=o1: e.tensor_copy(out=VB[ps_, 0:nch, o1], in_=c3(c.cv[ps_, 2, 0:N])), reads=[c.Bcv], writes=[BVB])
                yield
                for hh in range(2):
                    ps_ = slice(64 * hh, 64 * hh + 64)
                    o1 = slice(64 * hh, 64 * hh + 64)
                    S.op("pool", lambda e, ps_=ps_, o1=o1: e.tensor_tensor(out=BE[ps_, 0:nch, o1], in0=c3(c.akk[ps_, 0:N]), in1=c3(c.gi[ps_, 0:N]), op=ALU.mult),
                         reads=[c.Bakk, c.Bgi], writes=[BBE])
                yield

            def chunk_gen(c, sl, k, want_y):
                AR, BAR = c.AR[sl], c.BAR[sl]
                BE, BBE = c.BE[sl], c.BBE[sl]
                KT, BKT = c.KT[sl], c.BKT[sl]
                VB, BVB = c.VB[sl], c.BVB[sl]
                gtot, Bgtot = c.gtot[sl], c.Bgtot[sl]
                Yg, BYg = c.Yg[sl], c.BYg[sl]
                bk1, Bbk1, bk2, BpAD, BpTT = c.bk1, c.Bbk1, c.bk2, c.BpAD, c.BpTT
                mN = (msk[:, 0:2, :] if c.d == 0 else msk[:, 2:4, :]).rearrange("p a b -> p (a b)")
                mA = msk[:, 2, :] if c.d == 0 else msk[:, 0, :]

                def f1(e):
                    e.matmul(bk1[:, 0:256], BE[:, k, :], AR[:, k, :], start=True, stop=True)
                    return e.matmul(bk1[:, 256:512], KT[:, k, :], AR[:, k, :], start=True, stop=True)
                S.op("pe", f1, reads=[BBE, BKT, BAR], writes=[Bbk1])
                S.op("pe", lambda e: e.matmul(bk2[:, 0:128], AR[:, k, 0:128], BE[:, k, :], start=True, stop=True), reads=[BAR, BBE], writes=[BpAD])
                S.op("dve", lambda e: e.tensor_tensor(out=c.NA[:, :], in0=bk1[:, 0:256], in1=mN, op=ALU.mult), reads=[Bmsk], writes=[c.BNA, Bbk1])
                S.op("dve", lambda e: e.tensor_tensor(out=c.KA[:, :], in0=bk1[:, 256:512], in1=mN, op=ALU.mult), reads=[Bmsk], writes=[c.BKA, Bbk1])
                S.op("dve", lambda e: e.tensor_tensor(out=c.A0[:, :], in0=bk2[:, 0:128], in1=mA, op=ALU.mult), reads=[Bmsk], writes=[c.BA0, BpAD])
                yield
                def f2(e):
                    e.matmul(bk1[:, 0:128], BE[:, k, :], identb[:, :], start=True, stop=True)
                    e.matmul(bk1[:, 128:256], KT[:, k, :], identb[:, :], start=True, stop=True)
                    return e.matmul(bk1[:, 256:384], VB[:, k, :], identb[:, :], start=True, stop=True)
                S.op("pe", f2, reads=[BBE, BKT, BVB, Bidentb], writes=[Bbk1])
                S.op("act", lambda e: e.activation(out=c.TR[:, :], in_=bk1[:, 0:384], func=AF.Copy), reads=[], writes=[c.BTR, Bbk1])
                yield
                S.op("pool", lambda e: e.tensor_tensor(out=c.Tm0[:, :], in0=c.NA[:, 0:128], in1=identb[:, :], op=ALU.add), reads=[c.BNA, Bidentb], writes=[c.BTm0])
                Nk, BNk, Ak, BAk = c.NA[:, 0:128], c.BNA, c.A0[:, :], c.BA0
                Tc, BTc = c.Tm0, c.BTm0
                for lvl in range(5):
                    pw, Bpw = (c.PW0, c.BPW0) if lvl % 2 == 0 else (c.PW1, c.BPW1)
                    if lvl < 4:
                        def f3(e, Nk=Nk, Ak=Ak):
                            e.matmul(bk2[:, 0:128], Ak, Nk, start=True, stop=True)
                            return e.matmul(bk2[:, 128:256], Nk, Ak, start=True, stop=True)
                        S.op("pe", f3, reads=[BNk, BAk], writes=[BpAD])
                        S.op("act", lambda e, pw=pw: e.activation(out=pw[:, :], in_=bk2[:, 0:256], func=AF.Copy), reads=[BpAD], writes=[Bpw])
                    else:
                        S.op("pe", lambda e, Nk=Nk, Ak=Ak: e.matmul(bk2[:, 128:256], Nk, Ak, start=True, stop=True), reads=[BNk, BAk], writes=[BpAD])
                        S.op("act", lambda e, pw=pw: e.activation(out=pw[:, 128:256], in_=bk2[:, 128:256], func=AF.Copy), reads=[BpAD], writes=[Bpw])
                    yield
                    Nk, BNk, Ak, BAk = pw[:, 0:128], Bpw, pw[:, 128:256], Bpw
                    Tn, BTn = (c.Tm1, c.BTm1) if lvl % 2 == 0 else (c.Tm0, c.BTm0)
                    S.op("pe", lambda e, Ak=Ak, Tc=Tc: e.matmul(bk1[:, 384:512], Ak, Tc[:, :], start=True, stop=True), reads=[BAk, BTc], writes=[Bbk1])
                    S.op("dve", lambda e, Tc=Tc, Tn=Tn: e.tensor_tensor(out=Tn[:, :], in0=bk1[:, 384:512], in1=Tc[:, :], op=ALU.add), reads=[BTc], writes=[BTn, Bbk1])
                    Tc, BTc = Tn, BTn
                    yield
                Tf, BTf = Tc, BTc
                def fx(e):
                    e.matmul(bk1[:, 0:128], c.KA[:, 0:128], c.TR[:, 256:384], start=True, stop=False)
                    return e.matmul(bk1[:, 0:128], AR[:, k, 0:128], c.Hb[:, :], start=False, stop=True)
                S.op("pe", fx, reads=[c.BKA, c.BTR, BAR, c.BHb], writes=[Bbk1])
                S.op("act", lambda e: e.activation(out=c.Xb[:, :], in_=bk1[:, 0:128], func=AF.Copy), reads=[Bbk1], writes=[c.BXb])
                yield
                S.op("pe", lambda e: e.matmul(bk1[:, 128:256], Tf[:, :], c.Xb[:, :], start=True, stop=True), reads=[BTf, c.BXb], writes=[Bbk1])
                S.op("act", lambda e: e.activation(out=c.Ub[:, :], in_=bk1[:, 128:256], func=AF.Copy), reads=[Bbk1], writes=[c.BUb])
                yield

                def fh(e):
                    e.matmul(bk1[:, 256:384], c.TR[:, 128:256], c.TR[:, 256:384], start=True, stop=False)
                    ins = e.matmul(bk1[:, 256:384], c.TR[:, 0:128], c.Ub[:, :], start=False, stop=True)
                    if want_y:
                        e.matmul(bk1[:, 384:512], c.Hb[:, :], AR[:, k, 128:256], start=True, stop=False)
                        e.matmul(bk1[:, 384:512], c.Ub[:, :], c.NA[:, 128:256], start=False, stop=False)
                        ins = e.matmul(bk1[:, 384:512], c.TR[:, 256:384], c.KA[:, 128:256], start=False, stop=True)
                    return ins
                S.op("pe", fh, reads=[c.BTR, c.BUb, c.BHb, BAR, c.BNA, c.BKA], writes=[Bbk1])
                S.op("dve", lambda e: e.tensor_tensor(out=c.S1[:, :], in0=bk1[:, 256:384], in1=c.H[:, :], op=ALU.add), reads=[Bbk1, c.BH], writes=[c.BS1])
                if want_y:
                    for hh in range(2):
                        ps_ = slice(64 * hh, 64 * hh + 64)
                        S.op("act", lambda e, ps_=ps_, hh=hh: e.activation(out=Yg[ps_, k * 64:(k + 1) * 64], in_=bk1[ps_, 384 + 64 * hh:384 + 64 * hh + 64], func=AF.Copy),
                             reads=[], writes=[BYg, Bbk1])
                yield
                S.op("act", lambda e: e.activation(out=c.Hb[:, :], in_=c.S1[:, :], func=AF.Copy, scale=gtot[:, k:k + 1]), reads=[c.BS1, Bgtot], writes=[c.BHb])
                S.op("pool", lambda e: e.tensor_scalar(out=c.H[:, :], in0=c.S1[:, :], scalar1=gtot[:, k:k + 1], scalar2=None, op0=ALU.mult),
                     reads=[c.BS1, Bgtot], writes=[c.BH])
                yield

            def step_info(c, step):
                if step == 0:
                    return 0, 0, 256, None
                xg = (step - 1) if c.d == 0 else (NXG - step)
                return 256 + xg * GR, 256, NT, xg * GR

            def run_rr(gens):
                alive = list(gens)
                while alive:
                    nxt = []
                    for g_ in alive:
                        try:
                            next(g_)
                            nxt.append(g_)
                        except StopIteration:
                            pass
                    alive = nxt

            NSTEP = int(os.environ.get("MK_RSTEPS", 1 + NXG))

            def mk_prep(c, step):
                n0, s0, s1, xo = step_info(c, step)
                return prep_gen(c, step % 2, n0, GR, s0, s1, xo)

            run_rr([mk_prep(c, 0) for c in cps])
            for step in range(NSTEP):
                isx = step > 0
                sl = step % 2

                def seq(c):
                    for ci in range(NCH):
                        k = ci if c.d == 0 else NCH - 1 - ci
                        yield from chunk_gen(c, sl, k, isx)
                    if isx:
                        xo = step_info(c, step)[3]
                        S.dma("sp", YD_v[c.d][:, c.cc, xo:xo + GR], c.Yg[sl][:, :], reads=[c.BYg[sl]], semof=c.BYg[sl])

                gens = [seq(c) for c in cps]
                preps = [mk_prep(c, step + 1) for c in cps] if step + 1 < NSTEP else []
                rnd = 0
                alive = gens
                while alive or preps:
                    nxt = []
                    for g_ in alive:
                        try:
                            next(g_)
                            nxt.append(g_)
                        except StopIteration:
                            pass
                    alive = nxt
                    rnd += 1
                    if preps and (rnd % 4 == 0 or not alive):
                        np_ = []
                        for g_ in preps:
                            try:
                                next(g_)
                                np_.append(g_)
                            except StopIteration:
                                pass
                        preps = np_
            S.barrier()
            S.emit()
            S.end_phase()
        if upto == "R":
            return nc

        NF = 512
        BOX = Buf("OX")
        with ExitStack() as pfs:
            P = Ctx(nc, pfs); P.n = 400
            ld = [[P.sb([128, NF], F32, "fld") for _ in range(5)] for _ in range(2)]
            ld = [[(t_, grp[0][1]) for (t_, _) in grp] for grp in ld]
            tmp = [{nm: P.sb([128, NF], F32, nm) for nm in ("yy", "bs", "ysq", "mm", "msq", "var", "yc")} for _ in range(2)]
            ob = [P.sb([128, NF], BF16, "ob") for _ in range(2)]
            ps1 = [P.ps([128, 512], F32, "ps1") for _ in range(2)]
            ps2 = [P.ps([128, 512], F32, "ps2") for _ in range(2)]

            def ftile_gen(cc, ti, sl):
                xo = ti * NF
                (y0, By0), (y1, By1), (b0, Bb0), (b1, Bb1), (gzt, Bgzt) = ld[sl]
                T_ = tmp[sl]
                (yy, Byy), (bs, Bbs), (ysq, Bysq), (mm_, Bmm_), (msq, Bmsq), (var, Bvar), (yc, Byc) = (T_[k] for k in ("yy", "bs", "ysq", "mm", "msq", "var", "yc"))
                p1, Bp1 = ps1[sl]
                p2, Bp2 = ps2[sl]
                o_, Bo_ = ob[sl]
                S.dma_group("sp", [(y0[:], YD_v[0][:, cc, xo:xo + NF]), (y1[:], YD_v[1][:, cc, xo:xo + NF]),
                                   (b0[:], BD_v[0][:, cc, xo:xo + NF]), (b1[:], BD_v[1][:, cc, xo:xo + NF]),
                                   (gzt[:], Gzr_v[:, cc, xo:xo + NF])], writes=[By0], semof=By0)
                yield
                S.op("pool", lambda e: e.tensor_tensor(out=yy[:], in0=y0[:], in1=y1[:], op=ALU.add), reads=[By0, By1], writes=[Byy])
                S.op("pool", lambda e: e.tensor_tensor(out=bs[:], in0=b0[:], in1=b1[:], op=ALU.add), reads=[Bb0, Bb1], writes=[Bbs])
                yield
                S.op("pe", lambda e: e.matmul(p1[:, :], bones[:, :], yy[:], start=True, stop=True), reads=[Byy, Bbones], writes=[Bp1])
                S.op("act", lambda e: e.activation(out=ysq[:], in_=yy[:], func=AF.Square), reads=[Byy], writes=[Bysq])
                yield
                S.op("pe", lambda e: e.matmul(p2[:, :], bones[:, :], ysq[:], start=True, stop=True), reads=[Bysq, Bbones], writes=[Bp2])
                S.op("dve", lambda e: e.tensor_scalar(out=mm_[:], in0=p1[:, :], scalar1=1.0 / 64, scalar2=None, op0=ALU.mult), reads=[Bp1], writes=[Bmm_])
                yield
                S.op("pool", lambda e: e.tensor_tensor(out=msq[:], in0=mm_[:], in1=mm_[:], op=ALU.mult), reads=[Bmm_], writes=[Bmsq])
                S.op("pool", lambda e: e.tensor_tensor(out=yc[:], in0=yy[:], in1=mm_[:], op=ALU.subtract), reads=[Byy, Bmm_], writes=[Byc])
                yield
                S.op("dve", lambda e: e.scalar_tensor_tensor(out=var[:], in0=p2[:, :], scalar=1.0 / 64, in1=msq[:], op0=ALU.mult, op1=ALU.subtract),
                     reads=[Bp2, Bmsq], writes=[Bvar])
                yield
                S.op("act", lambda e: e.activation(out=var[:], in_=var[:], func=AF.Sqrt, bias=epsc[:, 2:3], scale=1.0), reads=[Bvar, Bepsc], writes=[Bvar])
                yield
                S.op("dve", lambda e: e.reciprocal(out=var[:], in_=var[:]), reads=[Bvar], writes=[Bvar])
                yield
                S.op("pool", lambda e: e.tensor_tensor(out=yc[:], in0=yc[:], in1=var[:], op=ALU.mult), reads=[Byc, Bvar], writes=[Byc])
                yield
                S.op("dve", lambda e: e.tensor_scalar(out=yc[:], in0=yc[:], scalar1=chanv[:, cc, 6:7], scalar2=chanv[:, cc, 7:8], op0=ALU.mult, op1=ALU.add),
                     reads=[Byc, Bchanv], writes=[Byc])
                yield
                S.op("pool", lambda e: e.tensor_tensor(out=yc[:], in0=yc[:], in1=bs[:], op=ALU.add), reads=[Byc, Bbs], writes=[Byc])
                yield
                S.op("dve", lambda e: e.tensor_tensor(out=o_[:], in0=yc[:], in1=gzt[:], op=ALU.mult), reads=[Byc, Bgzt], writes=[Bo_])
                S.dma("sp", OXs[2 * cc][:, xo:xo + NF], o_[0:64, :], reads=[Bo_], semof=Bo_)
                S.dma("sp", OXs[2 * cc + 1][:, xo:xo + NF], o_[64:128, :], reads=[Bo_], semof=Bo_)
                yield

            tiles = [(cc, ti) for cc in range(2) for ti in range(NX // NF)]
            for i in range(0, len(tiles), 2):
                gens = [ftile_gen(tiles[i + k][0], tiles[i + k][1], k) for k in range(2) if i + k < len(tiles)]
                while gens:
                    nxt = []
                    for g_ in gens:
                        try:
                            next(g_)
                            nxt.append(g_)
                        except StopIteration:
                            pass
                    gens = nxt
            BOG = Buf("OG")
            S.barrier()
            for j in range(4):
                if not os.environ.get("MK_NOCC"):
                    S.collective("AllGather", [[0, 1, 2, 3], [4, 5, 6, 7]], OXs[j], OGs[j], reads=[], writes=[BOG], semof=BOG)
            S.emit()
            S.end_phase()
        if upto == "F":
            return nc

        QG = 512
        NKT = NT // 128
        with ExitStack() as pms:
            P = Ctx(nc, pms); P.n = 500
            Kn, BKn = P.sb([128, NT], BF16, "Kn")
            Kr, BKr = P.sb([128, NT], BF16, "Kr")
            Vt, BVt = P.sb([128, NKT, 128], BF16, "Vt")
            onesb, Bonesb = P.sb([128, 128], BF16, "onesb")
            Qn = [P.sb([128, QG], BF16, "Qn") for _ in range(2)]
            Qr = [P.sb([128, QG], BF16, "Qr") for _ in range(2)]
            Pacc = [[P.sb([128, QG], F32, "Pacc") for _ in range(2)] for _ in range(2)]
            gmt = [P.sb([128, QG], F32, "gmt") for _ in range(2)]
            Pt = [P.sb([128, QG], BF16, "Pt") for _ in range(4)]
            rl, Brl = P.sb([128, QG], F32, "rl")
            oo, Boo = P.sb([128, QG], F32, "oo")
            om = [P.sb([128, QG], BF16, "om") for _ in range(2)]
            pS = [P.ps([128, 512], F32, "pS") for _ in range(4)]
            pO = [P.ps([128, 512], F32, "pO") for _ in range(2)]
            pL = [P.ps([128, 512], F32, "pL") for _ in range(2)]
            S.op("pool", lambda e: e.memset(onesb[:], 1.0), writes=[Bonesb])
            S.op("pool", lambda e: e.memset(Kr[64:128, :], 0.0), writes=[BKr])
            for q_, Bq_ in Qr:
                S.op("pool", lambda e, q_=q_: e.memset(q_[64:128, :], 0.0), writes=[Bq_])
            S.dma("sp", Kr[0:64, :], KR, writes=[BKr], semof=BKr)
            NQG = int(os.environ.get("MK_NQG", NX // QG))
            def attn_group(h, qg, sl):
                qo = qg * QG
                qn_, Bqn_ = Qn[sl]
                qr_, Bqr_ = Qr[sl]
                gm_, Bgm_ = gmt[sl]
                po_, Bpo_ = pO[sl]
                pl_, Bpl_ = pL[sl]
                o_, Bo_ = om[sl]
                S.dma("sp", qn_[:], QN_v[:, h, qo:qo + QG], writes=[Bqn_], semof=Bqn_)
                S.dma("sp", qr_[0:64, :], QR_v[:, h, qo:qo + QG], writes=[Bqr_], semof=Bqr_)
                (pa0, Bpa0), (pa1, Bpa1) = Pacc[sl]
                S.dma("sp", gm_[:], Gzm_v[:, h, qo:qo + QG], writes=[Bgm_], semof=Bgm_)

                def qk(kt):
                    ps_, Bps_ = pS[kt % 4]

                    def f(e):
                        e.matmul(ps_[:, :], Kn[:, kt * 128:(kt + 1) * 128], qn_[:, :], start=True, stop=False)
                        return e.matmul(ps_[:, :], Kr[:, kt * 128:(kt + 1) * 128], qr_[:, :], start=False, stop=True)
                    S.op("pe", f, reads=[BKn, BKr, Bqn_, Bqr_], writes=[Bps_])

                def ex_pv(kt):
                    ps_, Bps_ = pS[kt % 4]
                    pt_, Bpt_ = Pt[kt % 4]
                    S.op("act", lambda e: e.activation(out=pt_[:, :], in_=ps_[:, :], func=AF.Exp, scale=KS[:, kt * 2 + h:kt * 2 + h + 1]),
                         reads=[Bps_, BKS], writes=[Bpt_])

                    S.op("pe", lambda e: e.matmul(po_[:, :], Vt[:, kt, :], pt_[:, :], start=(kt == 0), stop=(kt == NKT - 1)),
                         reads=[BVt, Bpt_], writes=[Bpo_])
                    pa_, Bpa_ = (pa0, Bpa0) if kt % 2 == 0 else (pa1, Bpa1)
                    eng_ = "pool" if kt % 2 == 0 else "dve"
                    if kt < 2:
                        S.op(eng_, lambda e: e.tensor_copy(out=pa_[:, :], in_=pt_[:, :]), reads=[Bpt_], writes=[Bpa_])
                    else:
                        S.op(eng_, lambda e: e.tensor_tensor(out=pa_[:, :], in0=pa_[:, :], in1=pt_[:, :], op=ALU.add), reads=[Bpt_, Bpa_], writes=[Bpa_])

                qk(0)
                qk(1)
                for kt in range(NKT):
                    if kt + 2 < NKT:
                        qk(kt + 2)
                    ex_pv(kt)
                def lsum(e):
                    e.matmul(pl_[:, :], onesf[:, :], pa0[:, :], start=True, stop=False)
                    return e.matmul(pl_[:, :], onesf[:, :], pa1[:, :], start=False, stop=True)
                S.op("pe", lsum, reads=[Bonesf, Bpa0, Bpa1], writes=[Bpl_])
                S.op("dve", lambda e: e.reciprocal(out=rl[:, :], in_=pl_[:, :]), reads=[Bpl_], writes=[Brl])
                S.op("dve", lambda e: e.tensor_tensor(out=oo[:, :], in0=po_[:, :], in1=rl[:, :], op=ALU.mult), reads=[Bpo_, Brl], writes=[Boo])
                S.op("pool", lambda e: e.tensor_tensor(out=o_[:, :], in0=oo[:, :], in1=gm_[:, :], op=ALU.mult), reads=[Boo, Bgm_], writes=[Bo_])
                S.dma("sp", OXs[4 + 2 * h][:, qo:qo + QG], o_[0:64, :], reads=[Bo_], semof=Bo_)
                S.dma("sp", OXs[5 + 2 * h][:, qo:qo + QG], o_[64:128, :], reads=[Bo_], semof=Bo_)

            gcount = 0
            for h in range(2):
                S.dma("sp", Kn[:], KN_v[:, h, :], writes=[BKn], semof=BKn)
                S.dma("sp", Vt[:], VT[h].rearrange("p (t d) -> p t d", d=128), writes=[BVt], semof=BVt)
                for qg in range(NQG):
                    attn_group(h, qg, gcount % 2)
                    gcount += 1
            S.barrier()
            S.emit()
            S.end_phase()
        if upto == "M":
            return nc

        for j in range(4, 8):
            S.collective("AllGather", [[0, 1, 2, 3], [4, 5, 6, 7]], OXs[j], OGs[j], reads=[BOX], writes=[BOG], semof=BOG)
        S.barrier()
        S.emit()
        S.end_phase()

        CG = 512
        with ExitStack() as pcs:
            P = Ctx(nc, pcs); P.n = 600
            selq, Bselq = P.sb([128, 4], F32, "selq")
            gain_bc, Bgain = P.sb([128, D], F32, "gain_c")
            shift_bc, Bshift = P.sb([128, D], F32, "shift_c")
            gate_bc, Bgate = P.sb([128, D], F32, "gate_c")
            Bselq = Bshift = Bgate = Bgain
            xt = [P.sb([128, D], F32, "xtc") for _ in range(2)]
            hm = [P.sb([128, D], BF16, "hmc") for _ in range(2)]
            ss = [P.sb([128, 4], F32, "ssc") for _ in range(2)]
            hT, BhT = P.sb([128, 16, CG], BF16, "hTc")
            ldq = [P.sb([128, 16, CG], BF16, "ldq") for _ in range(2)]
            osel, Bosel = P.sb([128, 16, CG], BF16, "osel")
            wmg = [P.sb([128, 16, 256], BF16, "wmg") for _ in range(2)]
            wbr = [P.sb([128, 8, 256], BF16, "wbr") for _ in range(2)]
            sgr, Bsgr = P.sb([128, CG], F32, "sgr")
            sgm, Bsgm = P.sb([128, CG], F32, "sgm")
            tr_, Btr_ = P.sb([128, CG], F32, "tr")
            tm_, Btm_ = P.sb([128, CG], F32, "tm")
            merged, Bmerged = P.sb([128, 16, CG], BF16, "merged")
            wout = [P.sb([128, 16, 512], BF16, "wout") for _ in range(2)]
            xr = [P.sb([128, 512], F32, "xr") for _ in range(2)]
            res = [P.sb([128, 512], F32, "res") for _ in range(2)]
            pT = [P.ps([128, 1024], BF16, "pTc") for _ in range(2)]
            pg = [P.ps([128, 512], F32, "pg") for _ in range(4)]
            pout = [P.ps([128, 512], F32, "pout") for _ in range(2)]
            S.dma_group("sp", [(selq[:], I["selq"]), (gain_bc[:], BC[0]), (shift_bc[:], BC[1]), (gate_bc[:], BC[2])], writes=[Bgain], semof=Bgain)
            wmg_v = I["w_in_mg"].rearrange("(kc p) n -> p kc n", p=128)
            wbr_r_v = I["w_br_r"].rearrange("(kc p) n -> p kc n", p=128)
            wbr_m_v = I["w_br_m"].rearrange("(kc p) n -> p kc n", p=128)
            wout_v = I["w_out"].rearrange("(kc p) n -> p kc n", p=128)
            cnt = {"tile": 0, "w": 0, "o": 0, "r": 0}
            NCG = int(os.environ.get("MK_NCG", 2048 // CG))
            for gj in range(NCG):
                go = gj * CG
                for t in range(CG // 128):
                    s = cnt["tile"] % 2
                    cnt["tile"] += 1
                    x_, Bx_ = xt[s]
                    h_, Bh_ = hm[s]
                    s_, Bs_ = ss[s]
                    S.dma("sp", x_[:], I["xm"][go + t * 128:go + (t + 1) * 128, :], writes=[Bx_], semof=Bx_)
                    S.op("act", lambda e, x_=x_, h_=h_, s_=s_: e.activation(out=h_[:], in_=x_[:], func=AF.Square, accum_out=s_[:, 0:1]), reads=[Bx_], writes=[Bh_, Bs_])
                    S.op("act", lambda e, s_=s_: e.activation(out=s_[:, 1:2], in_=s_[:, 0:1], func=AF.Sqrt, scale=1.0 / D, bias=epsc[:, 0:1]), reads=[Bs_, Bepsc], writes=[Bs_])
                    S.op("dve", lambda e, s_=s_: e.reciprocal(out=s_[:, 2:3], in_=s_[:, 1:2]), reads=[Bs_], writes=[Bs_])
                    S.op("dve", lambda e, x_=x_, s_=s_: e.scalar_tensor_tensor(out=x_[:], in0=x_[:], scalar=s_[:, 2:3], in1=gain_bc[:], op0=ALU.mult, op1=ALU.mult),
                         reads=[Bx_, Bs_, Bgain], writes=[Bx_])
                    S.op("pool", lambda e, x_=x_, h_=h_: e.tensor_tensor(out=h_[:], in0=x_[:], in1=shift_bc[:], op=ALU.add), reads=[Bx_, Bshift], writes=[Bh_])
                    for half in range(2):
                        p_, Bp_ = pT[half]

                        def trf(e, half=half, p_=p_, h_=h_):
                            for j in range(8):
                                kc = half * 8 + j
                                ins = e.transpose(p_[:, j * 128:(j + 1) * 128], h_[:, kc * 128:(kc + 1) * 128], identb[:])
                            return ins
                        S.op("pe", trf, reads=[Bh_, Bidentb], writes=[Bp_])
                        S.op("act" if half == 0 else "dve",
                             (lambda e, half=half, p_=p_, t=t: e.activation(out=hT[:, half * 8:(half + 1) * 8, t * 128:(t + 1) * 128],
                                                                            in_=p_[:, :].rearrange("p (j n) -> p j n", n=128), func=AF.Copy)) if half == 0 else
                             (lambda e, half=half, p_=p_, t=t: e.tensor_copy(out=hT[:, half * 8:(half + 1) * 8, t * 128:(t + 1) * 128],
                                                                             in_=p_[:, :].rearrange("p (j n) -> p j n", n=128))),
                             reads=[Bp_], writes=[BhT])
                for q in range(4):
                    l_, Bl_ = ldq[q % 2]
                    S.dma_group("sp", [(l_[(j % 2) * 64:(j % 2) * 64 + 64, :, :].rearrange("p (r c) n -> p r c n", c=4)[:, :, j // 2, :],
                                        OGs[j].rearrange("(r p) n -> p r n", p=64)[:, :, q * 2048 + go:q * 2048 + go + CG]) for j in range(8)],
                                reads=[BOG], writes=[Bl_], semof=Bl_)
                    if q == 0:
                        S.op("dve", lambda e, l_=l_: e.tensor_scalar(out=osel[:], in0=l_[:], scalar1=selq[:, 0:1], scalar2=None, op0=ALU.mult),
                             reads=[Bl_, Bselq], writes=[Bosel])
                    else:
                        S.op("dve", lambda e, l_=l_, q=q: e.scalar_tensor_tensor(out=osel[:], in0=l_[:], scalar=selq[:, q:q + 1], in1=osel[:], op0=ALU.mult, op1=ALU.add),
                             reads=[Bl_, Bselq, Bosel], writes=[Bosel])
                for m in range(16):
                    w_, Bw_ = wmg[cnt["w"] % 2]
                    b_, Bb_ = wbr[cnt["w"] % 2]
                    cnt["w"] += 1
                    S.dma("sp", w_[:, :, :], WMG_b[m].rearrange("p (k n) -> p k n", n=256), writes=[Bw_], semof=Bw_)
                    S.dma("sp", b_[:, :, :], WBR_b[m].rearrange("p (k n) -> p k n", n=256), writes=[Bb_], semof=Bb_)
                    (pgr, Bpgr), (pgm, Bpgm), (ppr, Bppr), (ppm, Bppm) = pg

                    def fg(e, w_=w_):
                        for kc in range(16):
                            e.matmul(pgr[:, 0:CG], w_[:, kc, 0:128], hT[:, kc, :], start=(kc == 0), stop=(kc == 15))
                        for kc in range(16):
                            ins = e.matmul(pgm[:, 0:CG], w_[:, kc, 128:256], hT[:, kc, :], start=(kc == 0), stop=(kc == 15))
                        return ins
                    S.op("pe", fg, reads=[Bw_, BhT], writes=[Bpgr, Bpgm])
                    S.op("act", lambda e: e.activation(out=sgr[:, :], in_=pgr[:, 0:CG], func=AF.Sigmoid), reads=[Bpgr], writes=[Bsgr])
                    S.op("act", lambda e: e.activation(out=sgm[:, :], in_=pgm[:, 0:CG], func=AF.Sigmoid), reads=[Bpgm], writes=[Bsgm])

                    def fb(e, b_=b_):
                        for j in range(8):
                            kc = (j // 2) * 4 + (j % 2)
                            e.matmul(ppr[:, 0:CG], b_[:, j, 0:128], osel[:, kc, :], start=(j == 0), stop=(j == 7))
                        for j in range(8):
                            kc = (j // 2) * 4 + 2 + (j % 2)
                            ins = e.matmul(ppm[:, 0:CG], b_[:, j, 128:256], osel[:, kc, :], start=(j == 0), stop=(j == 7))
                        return ins
                    S.op("pe", fb, reads=[Bb_, Bosel], writes=[Bppr, Bppm])
                    S.op("dve", lambda e: e.tensor_tensor(out=tr_[:, :], in0=ppr[:, 0:CG], in1=sgr[:, :], op=ALU.mult), reads=[Bppr, Bsgr], writes=[Btr_])
                    S.op("dve", lambda e: e.tensor_tensor(out=tm_[:, :], in0=ppm[:, 0:CG], in1=sgm[:, :], op=ALU.mult), reads=[Bppm, Bsgm], writes=[Btm_])
                    S.op("pool", lambda e, m=m: e.tensor_tensor(out=merged[:, m, :], in0=tr_[:, :], in1=tm_[:, :], op=ALU.add), reads=[Btr_, Btm_], writes=[Bmerged])
                for nb in range(4):
                    wo_, Bwo_ = wout[cnt["o"] % 2]
                    cnt["o"] += 1
                    S.dma("sp", wo_[:], WOUT_b[nb].rearrange("p (k n) -> p k n", n=512), writes=[Bwo_], semof=Bwo_)
                    for t in range(CG // 128):
                        r_ = cnt["r"] % 2
                        cnt["r"] += 1
                        po_, Bpo_ = pout[r_]
                        xr_, Bxr_ = xr[r_]
                        rs_, Brs_ = res[r_]
                        S.dma("act", xr_[:], I["xm"][go + t * 128:go + (t + 1) * 128, nb * 512:(nb + 1) * 512], writes=[Bxr_], semof=Bxr_)

                        def fo(e, wo_=wo_, po_=po_, t=t):
                            for kc in range(16):
                                ins = e.matmul(po_[:, :], merged[:, kc, t * 128:(t + 1) * 128], wo_[:, kc, :], start=(kc == 0), stop=(kc == 15))
                            return ins
                        S.op("pe", fo, reads=[Bwo_, Bmerged], writes=[Bpo_])
                        S.op("dve", lambda e, po_=po_, rs_=rs_, nb=nb: e.tensor_tensor(out=rs_[:], in0=po_[:, :], in1=gate_bc[:, nb * 512:(nb + 1) * 512], op=ALU.mult),
                             reads=[Bpo_, Bgate], writes=[Brs_])
                        S.op("pool", lambda e, rs_=rs_, xr_=xr_: e.tensor_tensor(out=rs_[:], in0=rs_[:], in1=xr_[:], op=ALU.add), reads=[Brs_, Bxr_], writes=[Brs_])
                        S.dma("act", out[go + t * 128:go + (t + 1) * 128, nb * 512:(nb + 1) * 512], rs_[:], reads=[Brs_], semof=Brs_)
            S.barrier()
            S.emit()
            S.end_phase()
    return nc


_NC_CACHE = {}


def kernel(**inputs):
    maps = _host_inputs(inputs)
    if "nc" not in _NC_CACHE:
        _NC_CACHE["nc"] = build()
    nc = _NC_CACHE["nc"]
    res = run_bass_kernel_spmd(nc, maps, core_ids=list(range(8)))
    outp = np.zeros((2, NX, D), np.float32)
    for c in range(8):
        b, g = c // 4, c % 4
        outp[b, 2048 * g:2048 * g + 2048] = res.results[c]["out"]
    return outp
```
